# Optimizing a Trainium2 kernel written in Bass

```python
import jax, jax.numpy as jnp
from jax import lax
import numpy as np

D_MODEL = 1024
BATCH = 4
SEQ = 4096
DEPTH = 2

GRID_W = 64
NA_WIN_ROWS = 8
NA_WIN_COLS = 16
NA_HEADS = 16
NA_HEAD_DIM = D_MODEL // NA_HEADS
MLA_HEADS = 16
MLA_Q_RANK = 384
MLA_KV_RANK = 256
MLA_NOPE = 64
MLA_ROPE = 32
MLA_V = 64
ROPE_BASE = 10000.0
Q_BLOCK = 128
PLE_DIM = 256
EPS = 1e-6
N_MIXERS = 2
N_A = (DEPTH + 1) // 2
N_B = DEPTH // 2

kernel_name = "hybrid_natten_mla_encoder"


def rmsnorm(x, g):
    xf = x.astype(jnp.float32)
    y = xf * lax.rsqrt(jnp.mean(xf * xf, axis=-1, keepdims=True) + EPS)
    return (y * g.astype(jnp.float32)).astype(x.dtype)


def rope_tables(seq, dim):
    inv = 1.0 / (ROPE_BASE ** (jnp.arange(0, dim, 2, dtype=jnp.float32) / dim))
    ang = jnp.arange(seq, dtype=jnp.float32)[:, None] * inv[None, :]
    return jnp.cos(ang), jnp.sin(ang)


def apply_rope(x, cos, sin):
    x1, x2 = jnp.split(x, 2, axis=-1)
    c = cos[None, :, None, :].astype(x.dtype)
    s = sin[None, :, None, :].astype(x.dtype)
    return jnp.concatenate([x1 * c - x2 * s, x1 * s + x2 * c], axis=-1)


def neighbourhood_attention(q, k, v, rpb):
    B, S, H, dh = q.shape
    rows = S // GRID_W
    kr = min(NA_WIN_ROWS, rows)
    kc = NA_WIN_COLS
    qg = q.reshape(B, rows, GRID_W, H, dh)
    kg = k.reshape(B, rows, GRID_W, H, dh)
    vg = v.reshape(B, rows, GRID_W, H, dh)
    cols = jnp.arange(GRID_W)
    col_start = jnp.clip(cols - kc // 2, 0, GRID_W - kc)
    col_idx = col_start[:, None] + jnp.arange(kc)[None, :]
    col_off = col_idx - cols[:, None] + (NA_WIN_COLS - 1)
    scale = dh ** -0.5

    def row_block(r):
        rs = jnp.clip(r - kr // 2, 0, rows - kr)
        q_r = lax.dynamic_index_in_dim(qg, r, axis=1, keepdims=False)
        k_band = lax.dynamic_slice_in_dim(kg, rs, kr, axis=1)
        v_band = lax.dynamic_slice_in_dim(vg, rs, kr, axis=1)
        k_win = k_band[:, :, col_idx]
        v_win = v_band[:, :, col_idx]
        row_off = rs + jnp.arange(kr) - r + (NA_WIN_ROWS - 1)
        bias = rpb[:, row_off[:, None, None], col_off[None, :, :]]
        bias = bias.transpose(0, 2, 1, 3).astype(jnp.float32)
        s = jnp.einsum('bwhd,biwjhd->bhwij', q_r, k_win).astype(jnp.float32) * scale
        s = (s + bias[None]).reshape(B, H, GRID_W, kr * kc)
        pr = jax.nn.softmax(s, axis=-1).reshape(B, H, GRID_W, kr, kc).astype(v.dtype)
        return jnp.einsum('bhwij,biwjhd->bwhd', pr, v_win)

    o = lax.map(row_block, jnp.arange(rows))
    return o.transpose(1, 0, 2, 3, 4).reshape(B, S, H, dh)


def block_attention(q, k, v):
    B, S, H, dqk = q.shape
    nb = S // Q_BLOCK
    qb = q.reshape(B, nb, Q_BLOCK, H, dqk).transpose(1, 0, 2, 3, 4)
    scale = dqk ** -0.5

    def one(qblk):
        s = jnp.einsum('bqhd,bkhd->bhqk', qblk, k).astype(jnp.float32) * scale
        pr = jax.nn.softmax(s, axis=-1).astype(v.dtype)
        return jnp.einsum('bhqk,bkhd->bqhd', pr, v)

    o = lax.map(one, qb)
    return o.transpose(1, 0, 2, 3, 4).reshape(B, S, H, v.shape[-1])


def mixer_na(xn, w_in, rpb, w_out):
    B, S, _ = xn.shape
    hd = NA_HEADS * NA_HEAD_DIM
    q, k, v, z = jnp.split(xn @ w_in, 4, axis=-1)
    shp = (B, S, NA_HEADS, NA_HEAD_DIM)
    o = neighbourhood_attention(q.reshape(shp), k.reshape(shp), v.reshape(shp), rpb)
    return (o.reshape(B, S, hd) * jax.nn.silu(z)) @ w_out


def mixer_mla(xn, w_in, q_norm, w_qb, kv_norm, w_kvb, w_out):
    B, S, _ = xn.shape
    H = MLA_HEADS
    c_q, c_kv, k_rope, z = jnp.split(
        xn @ w_in, [MLA_Q_RANK, MLA_Q_RANK + MLA_KV_RANK, MLA_Q_RANK + MLA_KV_RANK + MLA_ROPE], axis=-1)
    cos, sin = rope_tables(S, MLA_ROPE)
    q = (rmsnorm(c_q, q_norm) @ w_qb).reshape(B, S, H, MLA_NOPE + MLA_ROPE)
    q_nope, q_pe = jnp.split(q, [MLA_NOPE], axis=-1)
    q_pe = apply_rope(q_pe, cos, sin)
    kv = (rmsnorm(c_kv, kv_norm) @ w_kvb).reshape(B, S, H, MLA_NOPE + MLA_V)
    k_nope, v = jnp.split(kv, [MLA_NOPE], axis=-1)
    k_pe = apply_rope(k_rope[:, :, None, :], cos, sin)
    k = jnp.concatenate([k_nope, jnp.broadcast_to(k_pe, (B, S, H, MLA_ROPE))], axis=-1)
    q = jnp.concatenate([q_nope, q_pe], axis=-1)
    o = block_attention(q, k, v)
    return (o.reshape(B, S, H * MLA_V) * jax.nn.silu(z)) @ w_out


def setup_inputs(seed: int = 0) -> dict:
    key = jax.random.key(seed)
    ks = jax.random.split(key, 20)
    f32 = jnp.float32

    def w(k, shape, fan_in):
        return jax.random.normal(k, shape, f32) * fan_in ** -0.5

    def gain(k, shape):
        return 1.0 + 0.05 * jax.random.normal(k, shape, f32)

    na_in = 4 * NA_HEADS * NA_HEAD_DIM
    mla_in = MLA_Q_RANK + MLA_KV_RANK + MLA_ROPE + MLA_HEADS * MLA_V
    return {
        "x": jax.random.normal(ks[0], (BATCH, SEQ, D_MODEL), f32),
        "p": jax.random.normal(ks[1], (DEPTH, BATCH, SEQ, PLE_DIM), f32),
        "norm_g": gain(ks[2], (DEPTH, D_MODEL)),
        "na_w_in": w(ks[3], (N_A, D_MODEL, na_in), D_MODEL),
        "na_rpb": 0.1 * jax.random.normal(ks[4], (N_A, NA_HEADS, 2 * NA_WIN_ROWS - 1, 2 * NA_WIN_COLS - 1), f32),
        "na_w_out": w(ks[5], (N_A, NA_HEADS * NA_HEAD_DIM, D_MODEL), NA_HEADS * NA_HEAD_DIM),
        "mla_w_in": w(ks[6], (N_B, D_MODEL, mla_in), D_MODEL),
        "mla_q_norm": gain(ks[7], (N_B, MLA_Q_RANK)),
        "mla_w_qb": w(ks[8], (N_B, MLA_Q_RANK, MLA_HEADS * (MLA_NOPE + MLA_ROPE)), MLA_Q_RANK),
        "mla_kv_norm": gain(ks[9], (N_B, MLA_KV_RANK)),
        "mla_w_kvb": w(ks[10], (N_B, MLA_KV_RANK, MLA_HEADS * (MLA_NOPE + MLA_V)), MLA_KV_RANK),
        "mla_w_out": w(ks[11], (N_B, MLA_HEADS * MLA_V, D_MODEL), MLA_HEADS * MLA_V),
        "ple_norm": gain(ks[12], (DEPTH, D_MODEL)),
        "ple_w_gate": w(ks[13], (DEPTH, D_MODEL, D_MODEL), D_MODEL),
        "ple_w_proj": w(ks[14], (DEPTH, PLE_DIM, D_MODEL), PLE_DIM),
        "final_norm": gain(ks[15], (D_MODEL,)),
    }


def reference(x, p, norm_g, na_w_in, na_rpb, na_w_out, mla_w_in, mla_q_norm, mla_w_qb,
              mla_kv_norm, mla_w_kvb, mla_w_out, ple_norm, ple_w_gate, ple_w_proj, final_norm):
    for i in range(DEPTH):
        xn = rmsnorm(x, norm_g[i])
        j = i // N_MIXERS
        if i % N_MIXERS == 0:
            y = mixer_na(xn, na_w_in[j], na_rpb[j], na_w_out[j])
        else:
            y = mixer_mla(xn, mla_w_in[j], mla_q_norm[j], mla_w_qb[j],
                          mla_kv_norm[j], mla_w_kvb[j], mla_w_out[j])
        h = x + y
        gate = jax.nn.sigmoid(rmsnorm(h, ple_norm[i]) @ ple_w_gate[i])
        x = h + gate * (p[i] @ ple_w_proj[i])
    return rmsnorm(x, final_norm)
```

```python
import contextlib
import numpy as np
import concourse.bass as bass
import concourse.mybir as mybir
from concourse.bass_utils import run_bass_kernel_spmd

F32 = mybir.dt.float32
BF16 = mybir.dt.bfloat16
ALU = mybir.AluOpType
AF = mybir.ActivationFunctionType
AX = mybir.AxisListType

D = 1024
NEG = -30000.0
EPS = 1e-6


class Prog:
    STREAMS = ("pe", "act", "dve", "pool", "sp")

    def __init__(self, nc, n_dma_sems=12, sem_es=None, sem_prefix=""):
        self.nc = nc
        self.sem_es = sem_es
        self.sem_prefix = sem_prefix
        self.ops = []
        self.last_w = {}
        self.readers = {}
        self.n_dma_sems = n_dma_sems

    ALIAS = {
        "ra": ["Rsb_hi", "Rsb_lo", "sig1"], "Rsb_hi": ["ra", "sig1"], "Rsb_lo": ["ra", "sig1"],
        "sig1": ["ra", "Rsb_hi", "Rsb_lo"],
        "rb": ["Rsw", "tmp1", "Oc_lo", "Oc_hi"], "Rsw": ["rb", "tmp1", "Oc_lo", "Oc_hi"],
        "tmp1": ["rb", "Rsw", "Oc_lo", "Oc_hi"], "Oc_lo": ["rb", "Rsw", "tmp1"], "Oc_hi": ["rb", "Rsw", "tmp1"],
        "sig0": ["etmp"], "etmp": ["sig0"],
        "tmp0": ["t1_lo", "t1_hi"], "t1_lo": ["tmp0"], "t1_hi": ["tmp0"],
    }

    def _expand(self, keys):
        out = []
        for k in keys:
            out.append(k)
            out.extend(self.ALIAS.get(k, ()))
        return list(dict.fromkeys(out))

    def op(self, stream, fn, reads=(), writes=(), dma=False):
        reads = self._expand(reads)
        writes = self._expand(writes)
        i = len(self.ops)
        deps = {}

        def add(j, raw):
            if j is None:
                return
            deps[j] = deps.get(j, False) or raw

        for k in reads:
            add(self.last_w.get(k), True)
        for k in writes:
            add(self.last_w.get(k), True)
            for r in self.readers.get(k, ()):
                add(r, False)
        for k in reads:
            lst = self.readers.setdefault(k, [])
            if not dma:
                lst[:] = [r for r in lst if self.ops[r]["dma"] or self.ops[r]["stream"] != stream]
            lst.append(i)
        for k in writes:
            self.last_w[k] = i
            self.readers[k] = []
        self.ops.append(dict(i=i, stream=stream, fn=fn, dma=dma, deps=deps, sig=None))
        return i

    def emit(self, final_wait_ops=()):
        nc = self.nc
        ops = self.ops
        need = [[] for _ in ops]
        signaled = set()
        for o in ops:
            for j, raw in o["deps"].items():
                p = ops[j]
                if not p["dma"] and not o["dma"] and p["stream"] == o["stream"]:
                    if o["stream"] == "pe" or not raw:
                        continue
                need[o["i"]].append(j)
                signaled.add(j)
        for j in final_wait_ops:
            signaled.add(j)
        cnt = {s: 0 for s in self.STREAMS}
        dcnt = {s: 0 for s in self.STREAMS}
        for o in ops:
            if o["dma"]:
                k = dcnt[o["stream"]]
                dcnt[o["stream"]] += 1
                o["sig"] = ("d_%s_%d" % (o["stream"], k % self.n_dma_sems), 16 * (k // self.n_dma_sems + 1))
                o["dk"] = k
            elif o["i"] in signaled:
                cnt[o["stream"]] += 1
                o["sig"] = ("c_" + o["stream"], cnt[o["stream"]])
        dma_by_stream = {s: [o for o in ops if o["dma"] and o["stream"] == s] for s in self.STREAMS}
        sem_names = set()
        for o in ops:
            if o["sig"] is not None:
                sem_names.add(o["sig"][0])
        sem_names = sorted(sem_names)
        with contextlib.ExitStack() as es:
            ses = self.sem_es if self.sem_es is not None else es
            sems = {n: ses.enter_context(nc.semaphore(self.sem_prefix + n)) for n in sem_names}
            block = es.enter_context(nc.Block())
            seen = {s: {} for s in self.STREAMS}

            def run_stream(stream, eng):
                sw = seen[stream]
                for o in ops:
                    if o["stream"] != stream:
                        continue
                    waits = {}
                    for j in need[o["i"]]:
                        sn, sv = ops[j]["sig"]
                        waits[sn] = max(waits.get(sn, 0), sv)
                    if o["dma"] and o["dk"] >= self.n_dma_sems:
                        sn, sv = o["sig"]
                        waits[sn] = max(waits.get(sn, 0), sv - 16)
                    for sn, sv in sorted(waits.items()):
                        if sw.get(sn, 0) >= sv:
                            continue
                        sw[sn] = sv
                        eng.wait_ge(sems[sn], sv)
                    ins = o["fn"](eng)
                    if o["sig"] is not None:
                        ins.then_inc(sems[o["sig"][0]], 16 if o["dma"] else 1)
                if stream == "sp":
                    for j in final_wait_ops:
                        sn, sv = ops[j]["sig"]
                        if sw.get(sn, 0) < sv:
                            sw[sn] = sv
                            eng.wait_ge(sems[sn], sv)

            @block.tensor
            def _(e):
                run_stream("pe", e)

            @block.scalar
            def _(e):
                run_stream("act", e)

            @block.vector
            def _(e):
                run_stream("dve", e)

            @block.gpsimd
            def _(e):
                run_stream("pool", e)

            @block.sync
            def _(e):
                run_stream("sp", e)


class Ctx:
    def __init__(self, nc, es, pre="", sem_es=None):
        self.nc = nc
        self.es = es
        self.pre = pre
        self.P = Prog(nc, sem_es=sem_es, sem_prefix=pre)
        self._n = 0

    def sb(self, name, shape, dt):
        return self.es.enter_context(self.nc.sbuf_tensor(self.pre + name, list(shape), dt))

    def ps(self, name, shape, dt):
        return self.es.enter_context(self.nc.psum_tensor(self.pre + name, list(shape), dt))

    def din(self, name, shape, dt=F32):
        return self.nc.dram_tensor(self.pre + name, list(shape), dt, kind="ExternalInput").ap()

    def dout(self, name, shape, dt=F32):
        return self.nc.dram_tensor(name, list(shape), dt, kind="ExternalOutput").ap()


class Ring:
    def __init__(self, C, n=2, cols=1024):
        self.C = C
        self.slots = [C.sb("wst%d" % i, [128, cols], F32) for i in range(n)]
        self.k = 0

    def load(self, src, parts, shape_free, dst, dst_key, conv="pool", extra_writes=(), gain=None):
        P = self.C.P
        i = self.k % len(self.slots)
        self.k += 1
        n = int(np.prod(shape_free))
        st = self.slots[i][0:parts, 0:n]
        if len(shape_free) == 2:
            st = st.rearrange("p (a b) -> p a b", b=shape_free[1])
        key = "wst%d" % i
        P.op("sp", lambda e: e.dma_start(out=st, in_=src), writes=[key], dma=True)
        wr = [dst_key] + list(extra_writes)
        if gain is not None:
            gt, gkey, k0 = gain
            gap = bass.AP(gt, k0, [[gt[:].shape[1], 128], [1, shape_free[0]], [0, shape_free[1]]])
            P.op("pool", lambda e: e.tensor_tensor(out=dst, in0=st, in1=gap, op=ALU.mult), reads=[key, gkey], writes=wr)
        elif conv == "pool":
            P.op("pool", lambda e: e.tensor_copy(out=dst, in_=st), reads=[key], writes=wr)
        elif conv == "exp":
            P.op("act", lambda e: e.activation(out=dst, in_=st, func=AF.Exp), reads=[key], writes=wr)


def run_pipeline(n, stages):
    k = len(stages)
    for step in range(n + k - 1):
        for d in range(k):
            t = step - d
            if 0 <= t < n:
                stages[d](t)


def emit_norm_stats(C, st, xt_ap, xt_key, col, sq, width=D):
    P = C.P
    ss, ms, nh = st
    P.op("act", lambda e: e.activation(out=sq[:, 0:width], in_=xt_ap, func=AF.Square, accum_out=ss[:, col:col + 1]),
         reads=[xt_key], writes=["sq", "ss%d" % col])
    P.op("dve", lambda e: e.tensor_scalar(out=ms[:, col:col + 1], in0=ss[:, col:col + 1], scalar1=1.0 / width, scalar2=EPS,
                                         op0=ALU.mult, op1=ALU.add), reads=["ss%d" % col], writes=["ms%d" % col])
    P.op("pool", lambda e: e.tensor_tensor(out=ms[:, col:col + 1], in0=ms[:, col:col + 1], in1=nh[:], op=ALU.pow),
         reads=["ms%d" % col, "nh"], writes=["rs%d" % col])


def emit_norm_apply_T(C, st, xt_ap, xt_key, col, gbc, gkey, xnb_ap, xnb_key, tp_ps, tp_key, idb, dstT, dstT_key, nch=8,
                      copy_eng="act"):
    P = C.P
    ss, ms, nh = st
    P.op("act", lambda e: e.activation(out=xnb_ap, in_=xt_ap, func=AF.Copy, scale=ms[:, col:col + 1]),
         reads=[xt_key, "rs%d" % col], writes=[xnb_key])
    for kc in range(nch):
        P.op("pe", lambda e, kc=kc: e.transpose(out=tp_ps[:, kc, :], in_=xnb_ap[:, kc * 128:(kc + 1) * 128], identity=idb[:]),
             reads=[xnb_key, "idb"], writes=[tp_key])
    if copy_eng == "act":
        P.op("act", lambda e: e.activation(out=dstT, in_=tp_ps[:, 0:nch, :], func=AF.Copy), reads=[tp_key], writes=[dstT_key])
    else:
        P.op("dve", lambda e: e.tensor_copy(out=dstT, in_=tp_ps[:, 0:nch, :]), reads=[tp_key], writes=[dstT_key])


def emit_norm_T(C, st, xt_ap, xt_key, col, gbc, gkey, xnb, xnb_key, tp_ps, tp_key, idb, dstT, dstT_key, sq):
    emit_norm_stats(C, st, xt_ap, xt_key, col, sq)
    emit_norm_apply_T(C, st, xt_ap, xt_key, col, gbc, gkey, xnb[:], xnb_key, tp_ps, tp_key, idb, dstT, dstT_key)


def emit_finalize_now(C, Oe, Oo, Rsb, Oc):
    P = C.P
    P.op("dve", lambda e: e.reciprocal(out=Rsb[64:128, :], in_=Oe[64:128, :]), reads=["Oe"], writes=["Rsb_hi"])
    P.op("act", lambda e: e.activation(out=Oc[0:64, :], in_=Oe[0:64, :], func=AF.Copy), reads=["Oe"], writes=["Oc_lo"])
    P.op("dve", lambda e: e.reciprocal(out=Rsb[0:64, :], in_=Oo[0:64, :]), reads=["Oo"], writes=["Rsb_lo"])
    P.op("act", lambda e: e.activation(out=Oc[64:128, :], in_=Oo[64:128, :], func=AF.Copy), reads=["Oo"], writes=["Oc_hi"])


def emit_finalize_later(C, Rsb, Oc, t1, pmf, Mps, Mkey, gz_ap, gz_key, og_ap, og_key):
    P = C.P
    P.op("pe", lambda e: e.matmul(Mps[:, 0:512], lhsT=pmf[:], rhs=Rsb[:], start=True, stop=True),
         reads=["Rsb_hi", "Rsb_lo", "pmf"], writes=[Mkey])
    P.op("dve", lambda e: e.tensor_tensor(out=t1[:], in0=Oc[:], in1=Mps[:, 0:512], op=ALU.mult),
         reads=["Oc_lo", "Oc_hi", Mkey], writes=["t1_lo", "t1_hi"])
    P.op("pool", lambda e: e.tensor_tensor(out=og_ap, in0=t1[:], in1=gz_ap, op=ALU.mult),
         reads=["t1_lo", "t1_hi", gz_key], writes=[og_key])


def emit_silu_gate(C, zps, zkey, etmp, gz_ap, gz_key):
    P = C.P
    P.op("act", lambda e: e.activation(out=etmp[:], in_=zps, func=AF.Exp, scale=-1.0), reads=[zkey], writes=["etmp"])
    P.op("dve", lambda e: e.tensor_scalar(out=etmp[:], in0=etmp[:], scalar1=1.0, scalar2=None, op0=ALU.add),
         reads=["etmp"], writes=["etmp"])
    P.op("dve", lambda e: e.reciprocal(out=etmp[:], in_=etmp[:]), reads=["etmp"], writes=["etmp"])
    P.op("dve", lambda e: e.tensor_tensor(out=gz_ap, in0=zps, in1=etmp[:], op=ALU.mult),
         reads=[zkey, "etmp"], writes=[gz_key])


def emit_ple(C, get_tile, n_tiles, pd, pgbc, wgb, wpb, st, col0, xnb, idb, sq, hnT, pst, pbf, pT, sig, tmp,
             Sps, Mps, out_dram, final_norm=None, pre=None):
    P = C.P
    outs = []
    ss, ms, nh_ = st

    def sA(i):
        pre(i)

    def sB(i):
        xt, xkey = get_tile(i)
        emit_norm_stats(C, st, xt, xkey, col0 + i, sq)

    def sC(i):
        j = i % 2
        xt, xkey = get_tile(i)
        col = col0 + i
        P.op("act", lambda e: e.activation(out=xnb[j][:], in_=xt, func=AF.Copy, scale=ms[:, col:col + 1]),
             reads=[xkey, "rs%d" % col], writes=["xnb%d" % j])
        P.op("sp", lambda e: e.dma_start(out=pst[j][:], in_=pd[i * 128:(i + 1) * 128, :]), writes=["pst%d" % j], dma=True)
        P.op("pool", lambda e: e.tensor_copy(out=pbf[j][:], in_=pst[j][:]), reads=["pst%d" % j], writes=["pbf%d" % j])

    def sD(i):
        j = i % 2
        tp = Mps[0][:].bitcast(BF16).rearrange("p (k t) -> p k t", t=128)
        for kc in range(8):
            P.op("pe", lambda e, kc=kc: e.transpose(out=tp[:, kc, :], in_=xnb[j][:, kc * 128:(kc + 1) * 128], identity=idb[:]),
                 reads=["xnb%d" % j, "idb"], writes=["M0"])
        P.op("act", lambda e: e.activation(out=hnT[j][:], in_=tp, func=AF.Copy), reads=["M0"], writes=["hnT%d" % j])
        tp2 = Mps[1][:].bitcast(BF16).rearrange("p (k t) -> p k t", t=128)
        for kc in range(2):
            P.op("pe", lambda e, kc=kc: e.transpose(out=tp2[:, kc, :], in_=pbf[j][:, kc * 128:(kc + 1) * 128], identity=idb[:]),
                 reads=["pbf%d" % j, "idb"], writes=["M1"])
        P.op("dve", lambda e: e.tensor_copy(out=pT[j][:], in_=tp2[:, 0:2, :]), reads=["M1"], writes=["pT%d" % j])

    def sE(i):
        j = i % 2
        xt, xkey = get_tile(i)
        for nh in range(2):
            gps = Sps[0][:, nh * 512:(nh + 1) * 512]
            pps = Sps[1][:, nh * 512:(nh + 1) * 512]
            for kc in range(8):
                P.op("pe", lambda e, kc=kc, nh=nh, gps=gps: e.matmul(
                    gps, lhsT=hnT[j][:, kc, :], rhs=wgb[:, kc, nh * 512:(nh + 1) * 512], start=(kc == 0), stop=(kc == 7)),
                    reads=["hnT%d" % j, "wgb"], writes=["S0_%d" % nh])
            for kc in range(2):
                P.op("pe", lambda e, kc=kc, nh=nh, pps=pps: e.matmul(
                    pps, lhsT=pT[j][:, kc, :], rhs=wpb[:, kc, nh * 512:(nh + 1) * 512], start=(kc == 0), stop=(kc == 1)),
                    reads=["pT%d" % j, "wpb"], writes=["S1_%d" % nh])
            sg = sig[nh]
            P.op("act", lambda e, gps=gps, sg=sg: e.activation(out=sg[:], in_=gps, func=AF.Exp, scale=-1.0),
                 reads=["S0_%d" % nh], writes=["sig%d" % nh])
            P.op("dve", lambda e, sg=sg: e.tensor_scalar(out=sg[:], in0=sg[:], scalar1=1.0, scalar2=None, op0=ALU.add),
                 reads=["sig%d" % nh], writes=["sig%d" % nh])
            P.op("dve", lambda e, sg=sg: e.reciprocal(out=sg[:], in_=sg[:]), reads=["sig%d" % nh], writes=["sig%d" % nh])
            tm = tmp[nh]
            P.op("dve", lambda e, pps=pps, sg=sg, tm=tm: e.tensor_tensor(out=tm[:], in0=pps, in1=sg[:], op=ALU.mult),
                 reads=["S1_%d" % nh, "sig%d" % nh], writes=["tmp%d" % nh])
            xh = xt[:, nh * 512:(nh + 1) * 512]
            P.op("pool", lambda e, xh=xh, tm=tm: e.tensor_tensor(out=xh, in0=xh, in1=tm[:], op=ALU.add),
                 reads=[xkey, "tmp%d" % nh], writes=[xkey])

    def sF(i):
        xt, xkey = get_tile(i)
        emit_norm_stats(C, st, xt, xkey, final_norm[2] + i, sq)

    def sG(i):
        xt, xkey = get_tile(i)
        if final_norm is not None:
            fgbc, fgkey, fcol0 = final_norm
            c2 = fcol0 + i
            P.op("act", lambda e: e.activation(out=xt, in_=xt, func=AF.Copy, scale=ms[:, c2:c2 + 1]),
                 reads=[xkey, "rs%d" % c2], writes=[xkey])
            P.op("dve", lambda e: e.tensor_tensor(out=xt, in0=xt, in1=fgbc[:], op=ALU.mult),
                 reads=[xkey, fgkey], writes=[xkey])
        outs.append(P.op("sp", lambda e: e.dma_start(out=out_dram[i * 128:(i + 1) * 128, :], in_=xt), reads=[xkey], dma=True))

    stages = ([sA] if pre is not None else []) + [sB, sC, sD, sE] + ([sF] if final_norm is not None else []) + [sG]
    run_pipeline(n_tiles, stages)
    return outs


def alloc_common(C, layer1=False):
    B = {}
    if not layer1:
        B["xres"] = C.sb("xres", [128, 16, 1024], F32)
        B["og"] = [C.sb("og%d" % i, [128, 512], BF16) for i in range(2)]
        B["yst"] = [C.sb("yst%d" % i, [128, 512], F32) for i in range(2)]
        B["wo"] = [C.sb("wo%d" % i, [128, 1024], BF16) for i in range(2)]
    B["ss"] = C.sb("ss", [128, 96], F32)
    B["ms"] = C.sb("ms", [128, 96], F32)
    B["nh"] = C.sb("nh", [128, 1], F32)
    B["sq"] = C.sb("sq", [128, 1024], BF16)
    B["xnb"] = [C.sb("xnb%d" % i, [128, 1024], BF16) for i in range(2)]
    B["idb"] = C.sb("idb", [128, 128], BF16)
    B["pmf"] = C.sb("pmf", [128, 128], F32)
    B["gnc"] = C.sb("gnc_sb", [128, 8], F32)
    B["gpc"] = C.sb("gpc_sb", [128, 8], F32)
    B["Rsb"] = C.sb("Rsb", [128, 512], F32)
    B["Rsw"] = C.sb("Rsw", [128, 512], F32)
    B["t1"] = C.sb("t1", [128, 512], F32)
    B["hnT"] = [C.sb("hnT%d" % i, [128, 8, 128], BF16) for i in range(2)]
    B["pst"] = [C.sb("pst%d" % i, [128, 256], F32) for i in range(2)]
    B["pbf"] = [C.sb("pbf%d" % i, [128, 256], BF16) for i in range(2)]
    B["pT"] = [C.sb("pT%d" % i, [128, 2, 128], BF16) for i in range(2)]
    B["ring"] = Ring(C, n=2, cols=1024)
    B["S"] = [C.ps("S%d" % i, [128, 1024], F32) for i in range(2)]
    B["Oe"] = C.ps("Oe", [128, 512], F32)
    B["Oo"] = C.ps("Oo", [128, 512], F32)
    B["M"] = [C.ps("M%d" % i, [128, 512], F32) for i in range(2)]
    return B


def emit_consts(C, B, ident, perm):
    P = C.P
    B["ring"].load(ident, 128, [128], B["idb"][:], "idb")
    P.op("sp", lambda e: e.dma_start(out=B["pmf"][:], in_=perm), writes=["pmf"], dma=True)
    P.op("pool", lambda e: e.memset(B["nh"][:], -0.5), writes=["nh"])


def emit_outproj(C, B, og, og_key, wo, wo_key, tiles, mctr):
    P = C.P
    xres = B["xres"]
    for tt, i in enumerate(tiles):
        for nh in range(2):
            m = mctr[0] % 2
            mctr[0] += 1
            Mp = B["M"][m]
            P.op("pe", lambda e, tt=tt, nh=nh, Mp=Mp: e.matmul(
                Mp[:, 0:512], lhsT=og[:, tt * 128:(tt + 1) * 128], rhs=wo[:, nh * 512:(nh + 1) * 512],
                start=True, stop=True), reads=[og_key, wo_key], writes=["M%d" % m])
            xh = xres[:, i, nh * 512:(nh + 1) * 512]
            yk = mctr[1] % 2
            mctr[1] += 1
            ys = B["yst"][yk]
            P.op("act", lambda e, ys=ys, Mp=Mp: e.activation(out=ys[:], in_=Mp[:, 0:512], func=AF.Copy),
                 reads=["M%d" % m], writes=["yst%d" % yk])
            P.op("pool", lambda e, xh=xh, ys=ys: e.tensor_tensor(out=xh, in0=xh, in1=ys[:], op=ALU.add),
                 reads=["xr%d" % i, "yst%d" % yk], writes=["xr%d" % i])


def build_l0(nc, es, pre="", xo=None, sem_es=None, shared=None):
    C = Ctx(nc, es, pre, sem_es)
    P = C.P
    shared = {} if shared is None else shared

    def sdin(name, shape):
        if name not in shared:
            shared[name] = C.din(name, shape)
        return shared[name]

    xk = C.din("xk", [2560, 1024])
    pd = C.din("pd", [2048, 256])
    bq = C.din("bq", [128, 2048])
    gn = sdin("gnc", [128, 8])
    gp = sdin("gpc", [128, 8])
    w_in = sdin("w_in", [1024, 4096])
    w_out = sdin("w_out", [1024, 1024])
    wg = sdin("wg", [1024, 1024])
    wp = sdin("wp", [256, 1024])
    tabB = sdin("tabB", [8, 128, 2816])
    aoh = sdin("aoh", [128, 1024])
    ident = sdin("ident", [128, 128])
    perm = sdin("perm", [128, 128])
    if xo is None:
        xo = C.dout("xo", [2048, 1024])

    B = alloc_common(C)
    ring = B["ring"]
    xres = B["xres"]
    xnT = C.sb("xnT", [128, 8, 2560], BF16)
    hx = [C.sb("hx%d" % i, [128, 1024], F32) for i in range(2)]
    AR_COLS, VP_OFF = 12544, 6656
    ar = C.sb("ar", [128, 12544], BF16)
    qTe = ar[:, 0:2048]
    qTo = ar[:, 2048:4096]
    kT = ar[:, 4096:6656]
    vP = ar[:, 6656:10496].rearrange("p (t c) -> p t c", c=192)
    gz = ar[:, 10496:12544]
    wgb = ar[:, 0:8192].rearrange("p (k n) -> p k n", n=1024)
    wpb = ar[:, 8192:10240].rearrange("p (k n) -> p k n", n=1024)
    wq = C.sb("wq", [128, 8, 128], BF16)
    wk = C.sb("wk", [128, 8, 128], BF16)
    wv = C.sb("wv", [128, 8, 128], BF16)
    wz = C.sb("wz", [128, 8, 128], BF16)
    tabE = C.sb("tabE", [128, 2816], BF16)
    bqb = C.sb("bqb", [128, 2048], BF16)
    aohb = C.sb("aohb", [128, 1024], BF16)
    PT = [C.sb("PT%d" % i, [128, 1024], BF16) for i in range(3)]
    etmp = C.sb("etmp", [128, 512], F32)
    sig = [etmp, B["Rsb"]]
    tmp = [B["t1"], B["Rsw"]]
    st = (B["ss"], B["ms"], B["nh"])

    emit_consts(C, B, ident, perm)
    P.op("sp", lambda e: e.dma_start(out=B["gnc"][:], in_=gn), writes=["gnc"], dma=True)
    P.op("sp", lambda e: e.dma_start(out=B["gpc"][:], in_=gp), writes=["gpc"], dma=True)
    ring.load(aoh, 128, [1024], aohb[:], "aohb")
    for h2 in range(2):
        ring.load(bq[:, h2 * 1024:(h2 + 1) * 1024], 128, [1024], bqb[:, h2 * 1024:(h2 + 1) * 1024], "bqb")
    P.op("pool", lambda e: e.memset(qTe[64:128, :], 0.0), writes=["qTe_z"])
    P.op("pool", lambda e: e.memset(qTo[0:64, :], 0.0), writes=["qTo_z"])
    P.op("pool", lambda e: e.memset(vP[:, :, 64:128], 1.0), writes=["vP_ones"])

    def p1_tile(bt):
        if 2 <= bt < 18:
            return xres[:, bt - 2, :], "xr%d" % (bt - 2)
        return hx[bt % 2][:], "hx%d" % (bt % 2)

    def p1_s0(bt):
        xt, xkey = p1_tile(bt)
        P.op("sp", lambda e: e.dma_start(out=xt, in_=xk[bt * 128:(bt + 1) * 128, :]), writes=[xkey], dma=True)
        emit_norm_stats(C, st, xt, xkey, bt, B["sq"])

    def p1_s1(bt):
        xt, xkey = p1_tile(bt)
        j = bt % 2
        tp = B["M"][j][:].bitcast(BF16).rearrange("p (k t) -> p k t", t=128)
        emit_norm_apply_T(C, st, xt, xkey, bt, None, None, B["xnb"][j][:], "xnb%d" % j, tp, "M%d" % j, B["idb"],
                          xnT[:, :, bt * 128:(bt + 1) * 128], "xnT%d" % bt)

    run_pipeline(20, [p1_s0, p1_s1])

    w_in_v = w_in.rearrange("(kc p) n -> p kc n", p=128)
    mctr = [0, 0]
    pending = []
    sctr = [0]
    pctr = [0]
    octr = [0]

    def next_m():
        m = mctr[0] % 2
        mctr[0] += 1
        return m

    def load_inproj(c_):
        for wt, wkey, off in ((wq, "wq", 0), (wk, "wk", 1024), (wv, "wv", 2048), (wz, "wz", 3072)):
            ring.load(w_in_v[:, :, off + c_ * 128: off + (c_ + 1) * 128], 128, [8, 128], wt[:], wkey,
                      gain=(B["gnc"], "gnc", 0))

    load_inproj(0)
    for c in range(8):
        wo = B["wo"][c % 2]
        ring.load(w_out[c * 128:(c + 1) * 128, :], 128, [1024], wo[:], "wo%d" % (c % 2))
        for ch, (lo, hi) in enumerate(((0, 1024), (1024, 2048), (2048, 2816))):
            ring.load(tabB[c, :, lo:hi], 128, [hi - lo], tabE[:, lo:hi], "tabE%d" % ch, conv="exp")

        def flush_one():
            _, g_, c_ = pending.pop(0)
            m = next_m()
            oj = octr[0] % 2
            octr[0] += 1
            og = B["og"][oj]
            emit_finalize_later(C, B["Rsb"], B["Rsw"], B["t1"], B["pmf"], B["M"][m], "M%d" % m,
                                gz[:, g_ * 512:(g_ + 1) * 512], "gz%d" % g_, og[:], "og%d" % oj)
            emit_outproj(C, B, og, "og%d" % oj, B["wo"][c_ % 2], "wo%d" % (c_ % 2), [4 * g_ + a_ for a_ in range(4)], mctr)

        for blk in range(4):
            m = next_m()
            Mp = B["M"][m]
            t0 = 256 + blk * 512
            rk = ["xnT%d" % (t0 // 128 + a) for a in range(4)]
            for kc in range(8):
                P.op("pe", lambda e, kc=kc, Mp=Mp, t0=t0: e.matmul(Mp[:, 0:512], lhsT=wq[:, kc, :], rhs=xnT[:, kc, t0:t0 + 512],
                                                                   start=(kc == 0), stop=(kc == 7)),
                     reads=rk + ["wq"], writes=["M%d" % m])
            P.op("act", lambda e, Mp=Mp, blk=blk: e.activation(out=qTe[0:64, blk * 512:(blk + 1) * 512], in_=Mp[0:64, 0:512],
                                                                func=AF.Copy), reads=["M%d" % m], writes=["qTe"])
            P.op("act", lambda e, Mp=Mp, blk=blk: e.activation(out=qTo[64:128, blk * 512:(blk + 1) * 512], in_=Mp[64:128, 0:512],
                                                                func=AF.Copy), reads=["M%d" % m], writes=["qTo"])
        for blk in range(5):
            m = next_m()
            Mp = B["M"][m]
            t0 = blk * 512
            rk = ["xnT%d" % (t0 // 128 + a) for a in range(4)]
            for kc in range(8):
                P.op("pe", lambda e, kc=kc, Mp=Mp, t0=t0: e.matmul(Mp[:, 0:512], lhsT=wk[:, kc, :], rhs=xnT[:, kc, t0:t0 + 512],
                                                                   start=(kc == 0), stop=(kc == 7)),
                     reads=rk + ["wk"], writes=["M%d" % m])
            if blk % 2 == 0:
                P.op("dve", lambda e, Mp=Mp, t0=t0: e.tensor_copy(out=kT[:, t0:t0 + 512], in_=Mp[:, 0:512]),
                     reads=["M%d" % m], writes=["kT%d" % blk])
            else:
                P.op("act", lambda e, Mp=Mp, t0=t0: e.activation(out=kT[:, t0:t0 + 512], in_=Mp[:, 0:512], func=AF.Copy),
                     reads=["M%d" % m], writes=["kT%d" % blk])
        while pending:
            flush_one()
        for blk in range(4):
            m = next_m()
            Mp = B["M"][m]
            t0 = 256 + blk * 512
            rk = ["xnT%d" % (t0 // 128 + a) for a in range(4)]
            for kc in range(8):
                P.op("pe", lambda e, kc=kc, Mp=Mp, t0=t0: e.matmul(Mp[:, 0:512], lhsT=wz[:, kc, :], rhs=xnT[:, kc, t0:t0 + 512],
                                                                   start=(kc == 0), stop=(kc == 7)),
                     reads=rk + ["wz"], writes=["M%d" % m])
            emit_silu_gate(C, Mp[:, 0:512], "M%d" % m, etmp, gz[:, blk * 512:(blk + 1) * 512], "gz%d" % blk)
        for t4 in range(5):
            m = next_m()
            Mp = B["M"][m]
            Mv = Mp[:, 0:512].rearrange("p (t c) -> p t c", c=128)
            for a in range(4):
                bt = t4 * 4 + a
                for kc in range(8):
                    P.op("pe", lambda e, kc=kc, Mv=Mv, a=a, bt=bt: e.matmul(
                        Mv[:, a, :], lhsT=xnT[:, kc, bt * 128:(bt + 1) * 128], rhs=wv[:, kc, :],
                        start=(kc == 0), stop=(kc == 7)), reads=["xnT%d" % bt, "wv"], writes=["M%d" % m])
            vdst = bass.AP(ar, VP_OFF + t4 * 4 * 192, [[AR_COLS, 128], [192, 4], [128, 2], [1, 64]])
            P.op("dve", lambda e, Mv=Mv, vdst=vdst: e.tensor_copy(out=vdst, in_=Mv.rearrange("p t (h c) -> p t h c", h=2)),
                 reads=["M%d" % m], writes=["vPe%d" % t4, "vPo%d" % t4])

        if c < 7:
            load_inproj(c + 1)
        its = [(g, hh, tp_) for g in range(4) for hh in range(2) for tp_ in range(4)]
        SKEW = 2

        def qk_stage(n):
            g, hh, tp_ = its[n]
            qm = qTe if hh == 0 else qTo
            qkeys = ["qTe", "qTe_z"] if hh == 0 else ["qTo", "qTo_z"]
            s_ = sctr[0] % 2
            sctr[0] += 1
            Sp = B["S"][s_]
            for u in range(2):
                t = 2 * tp_ + (1 - u)
                bt = 4 * g + t
                P.op("pe", lambda e, Sp=Sp, u=u, bt=bt, qm=qm, g=g: e.matmul(
                    Sp[:, u * 512:(u + 1) * 512], lhsT=kT[:, bt * 128:(bt + 1) * 128], rhs=qm[:, g * 512:(g + 1) * 512],
                    start=True, stop=False), reads=["kT%d" % (bt // 4)] + qkeys, writes=["S%d" % s_])
                P.op("pe", lambda e, Sp=Sp, u=u, t=t, g=g: e.matmul(
                    Sp[:, u * 512:(u + 1) * 512], lhsT=aohb[:, t * 128:(t + 1) * 128], rhs=bqb[:, g * 512:(g + 1) * 512],
                    start=False, stop=True), reads=["aohb", "bqb"], writes=["S%d" % s_])
            pj = pctr[0] % 3
            pctr[0] += 1
            Pt = PT[pj]
            P.op("act", lambda e, Pt=Pt, Sp=Sp: e.activation(out=Pt[:], in_=Sp[:], func=AF.Exp, scale=0.125),
                 reads=["S%d" % s_], writes=["PT%d" % pj])
            tab_ap = bass.AP(tabE, hh * 1408 + (12 - 4 * tp_) * 64, [[2816, 128], [128, 2], [64, 8], [1, 64]])
            Pv = Pt[:].rearrange("p (u q c) -> p u q c", u=2, q=8)
            P.op("dve", lambda e, Pv=Pv, tab_ap=tab_ap: e.tensor_tensor(out=Pv, in0=Pv, in1=tab_ap, op=ALU.mult),
                 reads=["PT%d" % pj, "tabE0", "tabE1", "tabE2"], writes=["PT%d" % pj])
            return pj

        pjs = {}
        for n in range(len(its) + SKEW):
            if n < len(its):
                pjs[n] = qk_stage(n)
            m_ = n - SKEW
            if m_ < 0:
                continue
            g, hh, tp_ = its[m_]
            pj = pjs[m_]
            Pt = PT[pj]
            Oh = B["Oe"] if hh == 0 else B["Oo"]
            okey = "Oe" if hh == 0 else "Oo"
            for u in range(2):
                t = 2 * tp_ + (1 - u)
                bt = 4 * g + t
                first = (tp_ == 0 and u == 0)
                last = (tp_ == 3 and u == 1)
                P.op("pe", lambda e, Oh=Oh, bt=bt, hh=hh, Pt=Pt, u=u, first=first, last=last: e.matmul(
                    Oh[:, 0:512], lhsT=vP[:, bt, hh * 64:hh * 64 + 128], rhs=Pt[:, u * 512:(u + 1) * 512],
                    start=first, stop=last),
                    reads=["vPe%d" % (bt // 4), "vPo%d" % (bt // 4), "vP_ones", "PT%d" % pj], writes=[okey])
            if hh == 1 and tp_ == 3:
                emit_finalize_now(C, B["Oe"], B["Oo"], B["Rsb"], B["Rsw"])
                pending.append((n + 3 if g < 3 else 10 ** 9, g, c))
            while pending and pending[0][0] <= n:
                flush_one()

    while pending:
        flush_one()
    alias = ["qTe", "qTo", "qTe_z", "qTo_z", "vP_ones"] + ["kT%d" % i for i in range(5)] + ["gz%d" % i for i in range(4)] + \
            ["vPe%d" % i for i in range(5)] + ["vPo%d" % i for i in range(5)]
    wg_v = wg.rearrange("(kc p) n -> p kc n", p=128)
    for nb in range(8):
        ring.load(wg_v[:, :, nb * 128:(nb + 1) * 128], 128, [8, 128], wgb[:, :, nb * 128:(nb + 1) * 128], "wgb",
                  extra_writes=alias if nb == 0 else (), gain=(B["gpc"], "gpc", 0))
    wp_v = wp.rearrange("(kc p) n -> p kc n", p=128)
    for nb in range(2):
        ring.load(wp_v[:, :, nb * 512:(nb + 1) * 512], 128, [2, 512], wpb[:, :, nb * 512:(nb + 1) * 512], "wpb",
                  extra_writes=alias if nb == 0 else ())
    outs = emit_ple(C, lambda i: (xres[:, i, :], "xr%d" % i), 16, pd, None, wgb, wpb, st, 20, B["xnb"], B["idb"],
                    B["sq"], B["hnT"], B["pst"], B["pbf"], B["pT"], sig, tmp, B["S"], B["M"], xo)
    P.emit(final_wait_ops=outs)
    return nc


def _consts():
    ident = np.eye(128, dtype=np.float32)
    perm = np.zeros((128, 128), np.float32)
    for i in range(64):
        perm[64 + i, i] = 1.0
        perm[i, 64 + i] = 1.0
    return ident, perm


def _na_tables(rpb):
    kc = np.arange(64)
    qc = np.arange(64)
    cs = np.clip(qc - 8, 0, 48)
    colvalid = (kc[:, None] >= cs[None, :]) & (kc[:, None] < cs[None, :] + 16)
    coff = np.clip(kc[:, None] - qc[None, :] + 15, 0, 30)
    tab = np.full((16, 128, 22, 64), NEG, np.float32)
    for e in range(22):
        for half in range(2):
            dr = 10 - e + half
            if -7 <= dr <= 7:
                vals = rpb[:, dr + 7][:, coff]
                tab[:, half * 64:(half + 1) * 64, e, :] = np.where(colvalid[None], vals, NEG)
    return np.ascontiguousarray(tab.reshape(8, 2, 128, 22, 64).transpose(0, 2, 1, 3, 4).reshape(8, 128, 2816))


def _na_rowmask(hf):
    bq = np.zeros((128, 4, 8, 64), np.float32)
    for g in range(4):
        for qi in range(8):
            r = 32 * hf + 8 * g + qi
            rs = min(max(r - 4, 0), 56)
            for j in range(16):
                R = 32 * hf - 4 + 8 * g + j
                ok = (0 <= R <= 63) and (rs <= R < rs + 8)
                bq[j, g, qi, :] = 0.0 if ok else NEG
    aoh = np.zeros((128, 8, 128), np.float32)
    for t in range(8):
        aoh[2 * t, t, 0:64] = 1.0
        aoh[2 * t + 1, t, 64:128] = 1.0
    return bq.reshape(128, 2048), aoh.reshape(128, 1024)


_NC_CACHE = {}


def _get_nc(name, builder):
    if name not in _NC_CACHE:
        nc = bass.Bass("TRN2", target_bir_lowering=False)
        with contextlib.ExitStack() as es:
            builder(nc, es)
        _NC_CACHE[name] = nc
    return _NC_CACHE[name]


def run_l0(x, p, norm_g, na_w_in, na_rpb, na_w_out, ple_norm, ple_w_gate, ple_w_proj):
    x = np.asarray(x, np.float32)
    ident, perm = _consts()
    tabB = _na_tables(np.asarray(na_rpb[0], np.float32))
    gn = np.ascontiguousarray(np.asarray(norm_g[0], np.float32).reshape(8, 128).T)
    gp = np.ascontiguousarray(np.asarray(ple_norm[0], np.float32).reshape(8, 128).T)
    shared = dict(gnc=gn, gpc=gp, w_in=np.ascontiguousarray(na_w_in[0], dtype=np.float32),
                  w_out=np.ascontiguousarray(na_w_out[0], dtype=np.float32),
                  wg=np.ascontiguousarray(ple_w_gate[0], dtype=np.float32),
                  wp=np.ascontiguousarray(ple_w_proj[0], dtype=np.float32), tabB=tabB, ident=ident, perm=perm)
    in_maps = []
    for core in range(8):
        b, hf = core // 2, core % 2
        xk = np.zeros((40, 64, 1024), np.float32)
        xb = x[b].reshape(64, 64, 1024)
        for lr in range(40):
            R = 32 * hf - 4 + lr
            if 0 <= R <= 63:
                xk[lr] = xb[R]
        bq, aoh = _na_rowmask(hf)
        m = dict(shared)
        m.update(xk=xk.reshape(2560, 1024), pd=np.ascontiguousarray(p[0, b, hf * 2048:(hf + 1) * 2048], dtype=np.float32),
                 bq=bq, aoh=aoh)
        in_maps.append(m)
    nc = _get_nc("l0", build_l0)
    res = run_bass_kernel_spmd(nc, in_maps, core_ids=list(range(8)))
    x1 = np.zeros((4, 4096, 1024), np.float32)
    for core in range(8):
        b, hf = core // 2, core % 2
        x1[b, hf * 2048:(hf + 1) * 2048] = res.results[core]["xo"]
    return x1


def build_l1(nc, es, pre="", xf=None, sem_es=None):
    C = Ctx(nc, es, pre, sem_es)
    P = C.P
    if xf is None:
        xoth = C.din("xoth", [2048, 1024])
        xown = C.din("xown", [2048, 1024])
    else:
        xown, xoth = xf

    def xf_tile(T):
        src = xown if T < 16 else xoth
        return src[(T % 16) * 128:(T % 16 + 1) * 128, :]
    pd = C.din("pd", [2048, 256])
    gn = C.din("gnc", [128, 8])
    gp = C.din("gpc", [128, 8])
    gf = C.din("gf", [128, 1024])
    qn = C.din("qnc", [128, 3])
    kvn = C.din("kvnc", [128, 2])
    w_in = C.din("w_in", [1024, 1696])
    wqa = C.din("wqa", [384, 1536])
    wqs = C.din("wqs", [384, 1536])
    w_kvb = C.din("w_kvb", [256, 2048])
    w_out = C.din("w_out", [1024, 1024])
    wg = C.din("wg", [1024, 1024])
    wp = C.din("wp", [256, 1024])
    costok = C.din("costok", [128, 512])
    sintok = C.din("sintok", [128, 512])
    cosTd = C.din("cosT", [128, 2048])
    sinTd = C.din("sinT", [128, 2048])
    ident = C.din("ident", [128, 128])
    perm = C.din("perm", [128, 128])
    xo = C.dout("xo", [2048, 1024])

    B = alloc_common(C, layer1=True)
    ring = B["ring"]
    st = (B["ss"], B["ms"], B["nh"])
    xnT = C.sb("xnT", [128, 8, 2048], BF16)
    ckvT = C.sb("ckvT", [128, 2, 4096], BF16)
    kpeT = C.sb("kpeT", [128, 4096], BF16)
    cqT = C.sb("cqT", [128, 3, 2048], BF16)
    cosT = C.sb("cosTb", [128, 2048], BF16)
    sinT = C.sb("sinTb", [128, 2048], BF16)
    ogT = C.sb("ogT", [128, 8, 2048], BF16)
    hx = [C.sb("hx%d" % i, [128, 1024], F32) for i in range(2)]
    qnc = C.sb("qnc_sb", [128, 3], F32)
    kvnc = C.sb("kvnc_sb", [128, 2], F32)
    xkT = B["hnT"]
    lat = [B["pst"][i][:].bitcast(BF16) for i in range(2)]
    wqA = [C.sb("wqA%d" % i, [128, 3, 96], BF16) for i in range(2)]
    wqB = [C.sb("wqB%d" % i, [128, 3, 96], BF16) for i in range(2)]
    wkn = [C.sb("wkn%d" % i, [128, 2, 64], BF16) for i in range(2)]
    wvp = C.sb("wvp", [128, 2, 128], BF16)
    wz = C.sb("wz", [128, 8, 128], BF16)
    PT = [C.sb("PT%d" % i, [128, 1024], BF16) for i in range(3)]
    etmp = C.sb("etmp", [128, 512], F32)
    AR_COLS, VP_OFF = 20480, 12288
    ar = C.sb("ar", [128, 20480], BF16)
    qT = [ar[:, 0:2048], ar[:, 2048:4096]]
    kT = [ar[:, 4096:8192], ar[:, 8192:12288]]
    vP = ar[:, 12288:18432].rearrange("p (t c) -> p t c", c=192)
    gz = ar[:, 18432:20480]
    krope = ar[:, 0:2048].bitcast(F32).rearrange("p (t c) -> p t c", c=32)
    kpe_tok = ar[:, 2048:5120].rearrange("p (t c) -> p t c", c=96)
    ctk = ar[:, 5120:6144].bitcast(F32).rearrange("p (t c) -> p t c", c=16)
    stk = ar[:, 6144:7168].bitcast(F32).rearrange("p (t c) -> p t c", c=16)
    wkvr = ar[:, 7168:9472].rearrange("p (k n) -> p k n", n=288)
    wcq = ar[:, 9472:12544].rearrange("p (k n) -> p k n", n=384)
    wgb = ar[:, 0:8192].rearrange("p (k n) -> p k n", n=1024)
    wpb = ar[:, 8192:10240].rearrange("p (k n) -> p k n", n=1024)
    woa = ar[:, 10240:18432].rearrange("p (k n) -> p k n", n=1024)
    ra = B["Rsb"]
    rb = B["Rsw"]
    sig = [etmp, B["Rsb"]]
    tmp = [B["t1"], B["Rsw"]]
    fgbc = C.sb("fgbc", [128, 1024], F32)

    emit_consts(C, B, ident, perm)
    P.op("sp", lambda e: e.dma_start(out=B["gnc"][:], in_=gn), writes=["gnc"], dma=True)
    P.op("sp", lambda e: e.dma_start(out=B["gpc"][:], in_=gp), writes=["gpc"], dma=True)
    P.op("sp", lambda e: e.dma_start(out=fgbc[:], in_=gf), writes=["fgbc"], dma=True)
    P.op("sp", lambda e: e.dma_start(out=qnc[:], in_=qn), writes=["qnc"], dma=True)
    P.op("sp", lambda e: e.dma_start(out=kvnc[:], in_=kvn), writes=["kvnc"], dma=True)
    P.op("sp", lambda e: e.dma_start(out=ctk.rearrange("p t c -> p (t c)"), in_=costok), writes=["ctk"], dma=True)
    P.op("sp", lambda e: e.dma_start(out=stk.rearrange("p t c -> p (t c)"), in_=sintok), writes=["stk"], dma=True)
    for h2 in range(2):
        ring.load(cosTd[:, h2 * 1024:(h2 + 1) * 1024], 128, [1024], cosT[:, h2 * 1024:(h2 + 1) * 1024], "cosT")
        ring.load(sinTd[:, h2 * 1024:(h2 + 1) * 1024], 128, [1024], sinT[:, h2 * 1024:(h2 + 1) * 1024], "sinT")
    w_in_v = w_in.rearrange("(kc p) n -> p kc n", p=128)
    for k4 in range(4):
        ring.load(w_in_v[:, 2 * k4:2 * k4 + 2, 384:672], 128, [2, 288], wkvr[:, 2 * k4:2 * k4 + 2, :], "wkvr",
                  gain=(B["gnc"], "gnc", 2 * k4))
    for k4 in range(4):
        ring.load(w_in_v[:, 2 * k4:2 * k4 + 2, 0:384], 128, [2, 384], wcq[:, 2 * k4:2 * k4 + 2, :], "wcq",
                  gain=(B["gnc"], "gnc", 2 * k4))
    P.op("pool", lambda e: e.memset(kpe_tok[:, :, 0:64], 0.0), writes=["kpe_z"])

    def psum_bf(t):
        return t[:].bitcast(BF16).rearrange("p (k t) -> p k t", t=128)

    ss, ms, nh_ = st

    def latent_stages(n_tiles, load, col_x, col_l, width, wmat, wkey, gvec, gvkey, nch, dstT_of, dst_of, dst_key, extra=None):
        def s0(T):
            j = T % 2
            load(T, j)
            emit_norm_stats(C, st, hx[j][:], "hx%d" % j, col_x + T, B["sq"])

        def s1(T):
            j = T % 2
            dT, dkey = dstT_of(T, j)
            emit_norm_apply_T(C, st, hx[j][:], "hx%d" % j, col_x + T, None, None, B["xnb"][j][:], "xnb%d" % j,
                              psum_bf(B["M"][j]), "M%d" % j, B["idb"], dT, dkey)

        def s2(T):
            j = T % 2
            dT, dkey = dstT_of(T, j)
            Sp = B["S"][j]
            tot = width + (32 if extra else 0)
            for kc in range(8):
                P.op("pe", lambda e, kc=kc: e.matmul(Sp[:, 0:tot], lhsT=dT[:, kc, :], rhs=wmat[:, kc, :],
                                                     start=(kc == 0), stop=(kc == 7)),
                     reads=[dkey, wkey], writes=["S%d" % j])
            emit_norm_stats(C, st, Sp[:, 0:width], "S%d" % j, col_l + T, B["sq"], width=width)

        def s3(T):
            j = T % 2
            Sp = B["S"][j]
            col = col_l + T
            Ob = B["Oe"] if j == 0 else B["Oo"]
            okey = "Oe" if j == 0 else "Oo"
            if extra:
                extra(T, Sp, "S%d" % j)
            emit_norm_apply_T(C, st, Sp[:, 0:width], "S%d" % j, col, gvec, gvkey, lat[j][:, 0:width], "pst%d" % j,
                              psum_bf(Ob), okey, B["idb"], dst_of(T), dst_key % T, nch=nch, copy_eng="dve")

        run_pipeline(n_tiles, [s0, s1, s2, s3])

    def load_own(T, j):
        P.op("sp", lambda e: e.dma_start(out=hx[j][:], in_=xown[T * 128:(T + 1) * 128, :]), writes=["hx%d" % j], dma=True)

    latent_stages(16, load_own, 64, 80, 384, wcq, "wcq", None, None, 3,
                  lambda T, j: (xnT[:, :, T * 128:(T + 1) * 128], "xnT%d" % T),
                  lambda T: cqT[:, :, T * 128:(T + 1) * 128], "cqT%d")

    def load_any(T, j):
        P.op("sp", lambda e: e.dma_start(out=hx[j][:], in_=xf_tile(T)), writes=["hx%d" % j], dma=True)

    def krope_copy(T, Sp, skey):
        P.op("act", lambda e: e.activation(out=krope[:, T, :], in_=Sp[:, 256:288], func=AF.Copy), reads=[skey], writes=["krope"])

    latent_stages(32, load_any, 0, 32, 256, wkvr, "wkvr", None, None, 2,
                  lambda T, j: (xkT[j][:], "hnT%d" % j),
                  lambda T: ckvT[:, :, T * 128:(T + 1) * 128], "ckvT%d", extra=krope_copy)

    x1v = krope[:, :, 0:16]
    x2v = krope[:, :, 16:32]
    rav = ra[:].rearrange("p (t c) -> p t c", c=16)
    rbv = rb[:].rearrange("p (t c) -> p t c", c=16)
    P.op("dve", lambda e: e.tensor_tensor(out=rav, in0=x1v, in1=ctk, op=ALU.mult), reads=["krope", "ctk"], writes=["ra"])
    P.op("dve", lambda e: e.tensor_tensor(out=rbv, in0=x2v, in1=stk, op=ALU.mult), reads=["krope", "stk"], writes=["rb"])
    P.op("dve", lambda e: e.tensor_tensor(out=kpe_tok[:, :, 64:80], in0=rav, in1=rbv, op=ALU.subtract),
         reads=["ra", "rb"], writes=["kpe1"])
    P.op("dve", lambda e: e.tensor_tensor(out=rav, in0=x1v, in1=stk, op=ALU.mult), reads=["krope", "stk", "kpe1"], writes=["ra"])
    P.op("dve", lambda e: e.tensor_tensor(out=rbv, in0=x2v, in1=ctk, op=ALU.mult), reads=["krope", "ctk", "kpe1"], writes=["rb"])
    P.op("dve", lambda e: e.tensor_tensor(out=kpe_tok[:, :, 80:96], in0=rav, in1=rbv, op=ALU.add),
         reads=["ra", "rb"], writes=["kpe2"])
    for T8 in range(4):
        s = T8 % 2
        tpv = B["S"][s][:, 0:512].bitcast(BF16).rearrange("p (k t) -> p k t", t=128)
        for a in range(8):
            T = T8 * 8 + a
            P.op("pe", lambda e, a=a, T=T, tpv=tpv: e.transpose(out=tpv[0:96, a, :], in_=kpe_tok[:, T, 0:96], identity=B["idb"][:]),
                 reads=["kpe1", "kpe2", "kpe_z", "idb"], writes=["S%d" % s])
        dst = kpeT[64:96, T8 * 1024:(T8 + 1) * 1024].rearrange("p (a t) -> p a t", t=128)
        if T8 % 2 == 0:
            P.op("dve", lambda e, dst=dst, tpv=tpv: e.tensor_copy(out=dst, in_=tpv[64:96, :, :]), reads=["S%d" % s],
                 writes=["kpeT%d" % T8])
        else:
            P.op("act", lambda e, dst=dst, tpv=tpv: e.activation(out=dst, in_=tpv[64:96, :, :], func=AF.Copy), reads=["S%d" % s],
                 writes=["kpeT%d" % T8])

    mctr = [0]
    sctr = [0]
    pctr = [0]

    def next_m():
        m = mctr[0] % 2
        mctr[0] += 1
        return m

    wqa_v = wqa.rearrange("(kc p) n -> p kc n", p=128)
    wqs_v = wqs.rearrange("(kc p) n -> p kc n", p=128)
    wkv_v = w_kvb.rearrange("(kc p) n -> p kc n", p=128)
    alias_ac = ["krope", "kpe1", "kpe2", "kpe_z", "ctk", "stk", "wkvr", "wcq"]
    scale = float(96.0 ** -0.5)
    for c in range(8):
        for hh in range(2):
            h = 2 * c + hh
            ring.load(wqa_v[:, :, h * 96:(h + 1) * 96], 128, [3, 96], wqA[hh][:], "wqA%d" % hh, gain=(qnc, "qnc", 0))
            ring.load(wqs_v[:, :, h * 96:(h + 1) * 96], 128, [3, 96], wqB[hh][:], "wqB%d" % hh, gain=(qnc, "qnc", 0))
            ring.load(wkv_v[:, :, h * 128:h * 128 + 64], 128, [2, 64], wkn[hh][:], "wkn%d" % hh, gain=(kvnc, "kvnc", 0))
            ring.load(wkv_v[:, :, h * 128 + 64:h * 128 + 128], 128, [2, 64], wvp[:, :, hh * 64:(hh + 1) * 64], "wvp%d" % hh,
                      gain=(kvnc, "kvnc", 0))
        ring.load(w_in_v[:, :, 672 + c * 128:672 + (c + 1) * 128], 128, [8, 128], wz[:], "wz", gain=(B["gnc"], "gnc", 0))
        first_alias = alias_ac if c == 0 else []

        for hh in range(2):
            for blk in range(4):
                mA = next_m()
                mB = next_m()
                MA = B["M"][mA]
                MB = B["M"][mB]
                rk = ["cqT%d" % (blk * 4 + a) for a in range(4)]
                for kc in range(3):
                    P.op("pe", lambda e, kc=kc, MA=MA, hh=hh, blk=blk: e.matmul(
                        MA[0:96, 0:512], lhsT=wqA[hh][:, kc, :], rhs=cqT[:, kc, blk * 512:(blk + 1) * 512],
                        start=(kc == 0), stop=(kc == 2)), reads=rk + ["wqA%d" % hh], writes=["M%d" % mA])
                for kc in range(3):
                    P.op("pe", lambda e, kc=kc, MB=MB, hh=hh, blk=blk: e.matmul(
                        MB[0:96, 0:512], lhsT=wqB[hh][:, kc, :], rhs=cqT[:, kc, blk * 512:(blk + 1) * 512],
                        start=(kc == 0), stop=(kc == 2)), reads=rk + ["wqB%d" % hh], writes=["M%d" % mB])
                bs = slice(blk * 512, (blk + 1) * 512)
                P.op("act", lambda e, MA=MA, hh=hh, bs=bs: e.activation(out=qT[hh][0:64, bs], in_=MA[0:64, 0:512], func=AF.Copy),
                     reads=["M%d" % mA], writes=["qT%dn" % hh] + first_alias)
                P.op("dve", lambda e, MA=MA, bs=bs: e.tensor_tensor(out=ra[64:96, :], in0=MA[64:96, 0:512], in1=cosT[64:96, bs],
                                                                    op=ALU.mult), reads=["M%d" % mA, "cosT"], writes=["ra"])
                P.op("dve", lambda e, MB=MB, bs=bs: e.tensor_tensor(out=rb[64:96, :], in0=MB[64:96, 0:512], in1=sinT[64:96, bs],
                                                                    op=ALU.mult), reads=["M%d" % mB, "sinT"], writes=["rb"])
                P.op("pool", lambda e, hh=hh, bs=bs: e.tensor_tensor(out=qT[hh][64:96, bs], in0=ra[64:96, :], in1=rb[64:96, :],
                                                                      op=ALU.add),
                     reads=["ra", "rb"], writes=["qT%dp" % hh] + first_alias)
                first_alias = []
            for blk in range(8):
                m = next_m()
                Mp = B["M"][m]
                rk = ["ckvT%d" % (blk * 4 + a) for a in range(4)]
                for kc in range(2):
                    P.op("pe", lambda e, kc=kc, Mp=Mp, hh=hh, blk=blk: e.matmul(
                        Mp[0:64, 0:512], lhsT=wkn[hh][:, kc, :], rhs=ckvT[:, kc, blk * 512:(blk + 1) * 512],
                        start=(kc == 0), stop=(kc == 1)), reads=rk + ["wkn%d" % hh], writes=["M%d" % m])
                bs = slice(blk * 512, (blk + 1) * 512)
                if blk % 2 == 0:
                    P.op("dve", lambda e, Mp=Mp, hh=hh, bs=bs: e.tensor_copy(out=kT[hh][0:64, bs], in_=Mp[0:64, 0:512]),
                         reads=["M%d" % m], writes=["kT%d_%d" % (hh, blk)])
                else:
                    P.op("act", lambda e, Mp=Mp, hh=hh, bs=bs: e.activation(out=kT[hh][0:64, bs], in_=Mp[0:64, 0:512], func=AF.Copy),
                         reads=["M%d" % m], writes=["kT%d_%d" % (hh, blk)])
            P.op("dve", lambda e, hh=hh: e.tensor_copy(out=kT[hh][64:96, :], in_=kpeT[64:96, :]),
                 reads=["kpeT%d" % a for a in range(4)], writes=["kT%dpe" % hh])
        if c == 0:
            P.op("pool", lambda e: e.memset(vP[:, :, 64:128], 1.0), writes=["vP_ones"])
        for t4 in range(8):
            m = next_m()
            Mp = B["M"][m]
            Mv = Mp[:, 0:512].rearrange("p (t c) -> p t c", c=128)
            for a in range(4):
                T = t4 * 4 + a
                for kc in range(2):
                    P.op("pe", lambda e, kc=kc, Mv=Mv, a=a, T=T: e.matmul(
                        Mv[:, a, :], lhsT=ckvT[:, kc, T * 128:(T + 1) * 128], rhs=wvp[:, kc, :],
                        start=(kc == 0), stop=(kc == 1)), reads=["ckvT%d" % T, "wvp0", "wvp1"], writes=["M%d" % m])
            vdst = bass.AP(ar, VP_OFF + t4 * 4 * 192, [[AR_COLS, 128], [192, 4], [128, 2], [1, 64]])
            P.op("dve", lambda e, Mv=Mv, vdst=vdst: e.tensor_copy(out=vdst, in_=Mv.rearrange("p t (h c) -> p t h c", h=2)),
                 reads=["M%d" % m], writes=["vPe%d" % t4, "vPo%d" % t4])
        for blk in range(4):
            m = next_m()
            Mp = B["M"][m]
            rk = ["xnT%d" % (blk * 4 + a) for a in range(4)]
            for kc in range(8):
                P.op("pe", lambda e, kc=kc, Mp=Mp, blk=blk: e.matmul(Mp[:, 0:512], lhsT=wz[:, kc, :],
                                                                     rhs=xnT[:, kc, blk * 512:(blk + 1) * 512],
                                                                     start=(kc == 0), stop=(kc == 7)),
                     reads=rk + ["wz"], writes=["M%d" % m])
            emit_silu_gate(C, Mp[:, 0:512], "M%d" % m, etmp, gz[:, blk * 512:(blk + 1) * 512], "gz%d" % blk)

        its = [(qb, hh, tp_) for qb in range(4) for hh in range(2) for tp_ in range(16)]
        SKEW = 2

        def qk_stage(n):
            qb, hh, tp_ = its[n]
            s_ = sctr[0] % 2
            sctr[0] += 1
            Sp = B["S"][s_]
            for u in range(2):
                kt = 2 * tp_ + u
                P.op("pe", lambda e, Sp=Sp, u=u, kt=kt, hh=hh, qb=qb: e.matmul(
                    Sp[:, u * 512:(u + 1) * 512], lhsT=kT[hh][0:96, kt * 128:(kt + 1) * 128],
                    rhs=qT[hh][0:96, qb * 512:(qb + 1) * 512], start=True, stop=True),
                    reads=["kT%d_%d" % (hh, kt // 4), "kT%dpe" % hh, "qT%dn" % hh, "qT%dp" % hh], writes=["S%d" % s_])
            pj = pctr[0] % 3
            pctr[0] += 1
            Pt = PT[pj]
            P.op("act", lambda e, Pt=Pt, Sp=Sp: e.activation(out=Pt[:], in_=Sp[:], func=AF.Exp, scale=scale),
                 reads=["S%d" % s_], writes=["PT%d" % pj])
            return pj

        pjs = {}
        pending = []
        for n in range(len(its) + SKEW):
            if n < len(its):
                pjs[n] = qk_stage(n)
            m_ = n - SKEW
            if m_ < 0:
                continue
            qb, hh, tp_ = its[m_]
            pj = pjs[m_]
            Pt = PT[pj]
            Oh = B["Oe"] if hh == 0 else B["Oo"]
            okey = "Oe" if hh == 0 else "Oo"
            for u in range(2):
                kt = 2 * tp_ + u
                P.op("pe", lambda e, Oh=Oh, kt=kt, hh=hh, Pt=Pt, u=u: e.matmul(
                    Oh[:, 0:512], lhsT=vP[:, kt, hh * 64:hh * 64 + 128], rhs=Pt[:, u * 512:(u + 1) * 512],
                    start=(kt == 0), stop=(kt == 31)),
                    reads=["vPe%d" % (kt // 4), "vPo%d" % (kt // 4), "vP_ones", "PT%d" % pj], writes=[okey])
            if hh == 1 and tp_ == 15:
                emit_finalize_now(C, B["Oe"], B["Oo"], B["Rsb"], B["Rsw"])
                pending.append((n + 3, qb))
            while pending and (pending[0][0] <= n or n == len(its) + SKEW - 1):
                _, qb_ = pending.pop(0)
                m = next_m()
                emit_finalize_later(C, B["Rsb"], B["Rsw"], B["t1"], B["pmf"], B["M"][m], "M%d" % m,
                                    gz[:, qb_ * 512:(qb_ + 1) * 512], "gz%d" % qb_, ogT[:, c, qb_ * 512:(qb_ + 1) * 512],
                                    "ogT%d_%d" % (c, qb_))

    alias = ["qT0n", "qT0p", "qT1n", "qT1p", "vP_ones", "kT0pe", "kT1pe"] + ["kT%d_%d" % (a, b_) for a in range(2) for b_ in range(8)] + \
            ["gz%d" % i for i in range(4)] + ["vPe%d" % i for i in range(8)] + ["vPo%d" % i for i in range(8)]
    wg_v = wg.rearrange("(kc p) n -> p kc n", p=128)
    for nb in range(8):
        ring.load(wg_v[:, :, nb * 128:(nb + 1) * 128], 128, [8, 128], wgb[:, :, nb * 128:(nb + 1) * 128], "wgb",
                  extra_writes=alias if nb == 0 else (), gain=(B["gpc"], "gpc", 0))
    wp_v = wp.rearrange("(kc p) n -> p kc n", p=128)
    for nb in range(2):
        ring.load(wp_v[:, :, nb * 512:(nb + 1) * 512], 128, [2, 512], wpb[:, :, nb * 512:(nb + 1) * 512], "wpb",
                  extra_writes=alias if nb == 0 else ())
    for c in range(8):
        ring.load(w_out[c * 128:(c + 1) * 128, :], 128, [1024], woa[:, c, :], "woa", extra_writes=alias if c == 0 else ())

    NHX = 8
    hx4 = [hx[0][:], hx[1][:]] + [xnT[:, k_, :].bitcast(F32) for k_ in range(NHX - 2)]
    xnT_keys = ["xnT%d" % a_ for a_ in range(16)]

    def pre(i):
        j = i % NHX
        P.op("sp", lambda e: e.dma_start(out=hx4[j], in_=xown[i * 128:(i + 1) * 128, :]),
             writes=["hx%d" % j] + (xnT_keys if 2 <= i < NHX else []), dma=True)
        for nh in range(2):
            Ob = B["Oe"] if nh == 0 else B["Oo"]
            okey = "Oe" if nh == 0 else "Oo"
            for c in range(8):
                P.op("pe", lambda e, c=c, Ob=Ob, nh=nh: e.matmul(Ob[:, 0:512], lhsT=ogT[:, c, i * 128:(i + 1) * 128],
                                                                 rhs=woa[:, c, nh * 512:(nh + 1) * 512],
                                                                 start=(c == 0), stop=(c == 7)),
                     reads=["ogT%d_%d" % (c, i // 4), "woa"], writes=[okey])
            xh = hx4[j][:, nh * 512:(nh + 1) * 512]
            P.op("dve", lambda e, xh=xh, Ob=Ob: e.tensor_tensor(out=xh, in0=xh, in1=Ob[:, 0:512], op=ALU.add),
                 reads=["hx%d" % j, okey], writes=["hx%d" % j])

    outs = emit_ple(C, lambda i: (hx4[i % NHX], "hx%d" % (i % NHX)), 16, pd, None, wgb, wpb, st, 0, B["xnb"], B["idb"], B["sq"],
                    B["hnT"], B["pst"], B["pbf"], B["pT"], sig, tmp, B["S"], B["M"], xo,
                    final_norm=(fgbc, "fgbc", 16), pre=pre)
    P.emit(final_wait_ops=outs)
    return nc


def _rope_np():
    inv = (1.0 / (np.float32(10000.0) ** (np.arange(0, 32, 2, dtype=np.float32) / np.float32(32)))).astype(np.float32)
    ang = (np.arange(4096, dtype=np.float32)[:, None] * inv[None, :]).astype(np.float32)
    return np.cos(ang).astype(np.float32), np.sin(ang).astype(np.float32)


def run_l1(x1, p, norm_g, mla_w_in, mla_q_norm, mla_w_qb, mla_kv_norm, mla_w_kvb, mla_w_out, ple_norm, ple_w_gate,
           ple_w_proj, final_norm):
    ident, perm = _consts()
    f = lambda a: np.ascontiguousarray(a, dtype=np.float32)
    bc = lambda v, n: np.ascontiguousarray(np.broadcast_to(np.asarray(v, np.float32), (128, n)))
    wqb = f(mla_w_qb[0]).reshape(384, 16, 96)
    wqs = np.zeros_like(wqb)
    wqs[:, :, 64:80] = wqb[:, :, 80:96]
    wqs[:, :, 80:96] = wqb[:, :, 64:80]
    cos, sin = _rope_np()
    costok = np.ascontiguousarray(cos.reshape(32, 128, 16).transpose(1, 0, 2).reshape(128, 512))
    sintok = np.ascontiguousarray(sin.reshape(32, 128, 16).transpose(1, 0, 2).reshape(128, 512))
    pm = lambda v, k: np.ascontiguousarray(np.asarray(v, np.float32).reshape(k, 128).T)
    shared = dict(gnc=pm(norm_g[1], 8), gpc=pm(ple_norm[1], 8), gf=bc(final_norm, 1024), qnc=pm(mla_q_norm[0], 3),
                  kvnc=pm(mla_kv_norm[0], 2), w_in=f(mla_w_in[0]), wqa=wqb.reshape(384, 1536), wqs=wqs.reshape(384, 1536),
                  w_kvb=f(mla_w_kvb[0]), w_out=f(mla_w_out[0]), wg=f(ple_w_gate[1]), wp=f(ple_w_proj[1]),
                  costok=costok, sintok=sintok, ident=ident, perm=perm)
    in_maps = []
    for core in range(8):
        b, hf = core // 2, core % 2
        cosT = np.zeros((128, 2048), np.float32)
        sinT = np.zeros((128, 2048), np.float32)
        cs = cos[hf * 2048:(hf + 1) * 2048].T
        sn = sin[hf * 2048:(hf + 1) * 2048].T
        cosT[64:80] = cs
        cosT[80:96] = cs
        sinT[64:80] = -sn
        sinT[80:96] = sn
        m = dict(shared)
        m.update(xf=f(x1[b]), xown=f(x1[b, hf * 2048:(hf + 1) * 2048]), pd=f(p[1, b, hf * 2048:(hf + 1) * 2048]),
                 cosT=cosT, sinT=sinT)
        in_maps.append(m)
    nc = _get_nc("l1", build_l1)
    res = run_bass_kernel_spmd(nc, in_maps, core_ids=list(range(8)))
    out = np.zeros((4, 4096, 1024), np.float32)
    for core in range(8):
        b, hf = core // 2, core % 2
        out[b, hf * 2048:(hf + 1) * 2048] = res.results[core]["xo"]
    return out


def _l0_inputs(x, p, norm_g, na_w_in, na_rpb, na_w_out, ple_norm, ple_w_gate, ple_w_proj, flip=False):
    x = np.asarray(x, np.float32)
    ident, perm = _consts()
    tabB = _na_tables(np.asarray(na_rpb[0], np.float32))
    gn = np.ascontiguousarray(np.asarray(norm_g[0], np.float32).reshape(8, 128).T)
    gp = np.ascontiguousarray(np.asarray(ple_norm[0], np.float32).reshape(8, 128).T)
    shared = dict(gnc=gn, gpc=gp, w_in=np.ascontiguousarray(na_w_in[0], dtype=np.float32),
                  w_out=np.ascontiguousarray(na_w_out[0], dtype=np.float32),
                  wg=np.ascontiguousarray(ple_w_gate[0], dtype=np.float32),
                  wp=np.ascontiguousarray(ple_w_proj[0], dtype=np.float32), tabB=tabB, ident=ident, perm=perm)
    in_maps = []
    for core in range(8):
        b, hf = core // 2, core % 2
        if flip:
            hf = 1 - hf
        xk = np.zeros((40, 64, 1024), np.float32)
        xb = x[b].reshape(64, 64, 1024)
        for lr in range(40):
            R = 32 * hf - 4 + lr
            if 0 <= R <= 63:
                xk[lr] = xb[R]
        bq, aoh = _na_rowmask(hf)
        m = dict(shared)
        m.update(xk=xk.reshape(2560, 1024), pd=np.ascontiguousarray(p[0, b, hf * 2048:(hf + 1) * 2048], dtype=np.float32),
                 bq=bq, aoh=aoh)
        in_maps.append(m)
    return in_maps


def _l1_inputs(x1, p, norm_g, mla_w_in, mla_q_norm, mla_w_qb, mla_kv_norm, mla_w_kvb, mla_w_out, ple_norm, ple_w_gate,
               ple_w_proj, final_norm):
    ident, perm = _consts()
    f = lambda a: np.ascontiguousarray(a, dtype=np.float32)
    bc = lambda v, n: np.ascontiguousarray(np.broadcast_to(np.asarray(v, np.float32), (128, n)))
    wqb = f(mla_w_qb[0]).reshape(384, 16, 96)
    wqs = np.zeros_like(wqb)
    wqs[:, :, 64:80] = wqb[:, :, 80:96]
    wqs[:, :, 80:96] = wqb[:, :, 64:80]
    cos, sin = _rope_np()
    costok = np.ascontiguousarray(cos.reshape(32, 128, 16).transpose(1, 0, 2).reshape(128, 512))
    sintok = np.ascontiguousarray(sin.reshape(32, 128, 16).transpose(1, 0, 2).reshape(128, 512))
    pm = lambda v, k: np.ascontiguousarray(np.asarray(v, np.float32).reshape(k, 128).T)
    shared = dict(gnc=pm(norm_g[1], 8), gpc=pm(ple_norm[1], 8), gf=bc(final_norm, 1024), qnc=pm(mla_q_norm[0], 3),
                  kvnc=pm(mla_kv_norm[0], 2), w_in=f(mla_w_in[0]), wqa=wqb.reshape(384, 1536), wqs=wqs.reshape(384, 1536),
                  w_kvb=f(mla_w_kvb[0]), w_out=f(mla_w_out[0]), wg=f(ple_w_gate[1]), wp=f(ple_w_proj[1]),
                  costok=costok, sintok=sintok, ident=ident, perm=perm)
    in_maps = []
    for core in range(8):
        b, hf = core // 2, core % 2
        cosT = np.zeros((128, 2048), np.float32)
        sinT = np.zeros((128, 2048), np.float32)
        cs = cos[hf * 2048:(hf + 1) * 2048].T
        sn = sin[hf * 2048:(hf + 1) * 2048].T
        cosT[64:80] = cs
        cosT[80:96] = cs
        sinT[64:80] = -sn
        sinT[80:96] = sn
        m = dict(shared)
        if x1 is not None:
            m.update(xown=f(x1[b, hf * 2048:(hf + 1) * 2048]), xoth=f(x1[b, (1 - hf) * 2048:(2 - hf) * 2048]))
        order = np.concatenate([np.arange(hf * 2048, (hf + 1) * 2048), np.arange((1 - hf) * 2048, (2 - hf) * 2048)])
        m.update(pd=f(p[1, b, hf * 2048:(hf + 1) * 2048]), cosT=cosT, sinT=sinT,
                 costok=np.ascontiguousarray(cos[order].reshape(32, 128, 16).transpose(1, 0, 2).reshape(128, 512)),
                 sintok=np.ascontiguousarray(sin[order].reshape(32, 128, 16).transpose(1, 0, 2).reshape(128, 512)))
        in_maps.append(m)
    return in_maps


def build_fused(nc, es):
    x1loc = nc.dram_tensor("x1loc", [2048, 1024], F32).ap()
    x1oth = nc.dram_tensor("x1oth", [2048, 1024], F32).ap()
    shared = {}
    with contextlib.ExitStack() as es0:
        build_l0(nc, es0, pre="a_", xo=x1loc, sem_es=es, shared=shared)
    nc.all_engine_barrier()
    with contextlib.ExitStack() as es0:
        build_l0(nc, es0, pre="c_", xo=x1oth, sem_es=es, shared=shared)
    nc.all_engine_barrier()
    with contextlib.ExitStack() as es1:
        build_l1(nc, es1, pre="b_", xf=(x1loc, x1oth), sem_es=es)
    return nc


def kernel(x, p, norm_g, na_w_in, na_rpb, na_w_out, mla_w_in, mla_q_norm, mla_w_qb, mla_kv_norm, mla_w_kvb, mla_w_out,
           ple_norm, ple_w_gate, ple_w_proj, final_norm):
    x = np.asarray(x)
    p = np.asarray(p)
    m0 = _l0_inputs(x, p, norm_g, na_w_in, na_rpb, na_w_out, ple_norm, ple_w_gate, ple_w_proj)
    m0f = _l0_inputs(x, p, norm_g, na_w_in, na_rpb, na_w_out, ple_norm, ple_w_gate, ple_w_proj, flip=True)
    m1 = _l1_inputs(None, p, norm_g, mla_w_in, mla_q_norm, mla_w_qb, mla_kv_norm, mla_w_kvb, mla_w_out, ple_norm,
                    ple_w_gate, ple_w_proj, final_norm)
    in_maps = []
    for core in range(8):
        m = {"a_" + k: v for k, v in m0[core].items()}
        m.update({"c_" + k: m0f[core][k] for k in ("xk", "pd", "bq")})
        m.update({"b_" + k: v for k, v in m1[core].items()})
        in_maps.append(m)
    nc = _get_nc("fused", build_fused)
    res = run_bass_kernel_spmd(nc, in_maps, core_ids=list(range(8)))
    out = np.zeros((4, 4096, 1024), np.float32)
    for core in range(8):
        b, hf = core // 2, core % 2
        out[b, hf * 2048:(hf + 1) * 2048] = res.results[core]["xo"]
    return out
```

```python
import contextlib
import numpy as np
import concourse.bass as bass
import concourse.mybir as mybir
from concourse.bass_utils import run_bass_kernel_spmd

F32 = mybir.dt.float32
BF16 = mybir.dt.bfloat16
ALU = mybir.AluOpType
AF = mybir.ActivationFunctionType
AX = mybir.AxisListType

D = 1024
NEG = -30000.0
EPS = 1e-6


class Prog:
    STREAMS = ("pe", "act", "dve", "pool", "sp")

    def __init__(self, nc, n_dma_sems=12, sem_es=None, sem_prefix=""):
        self.nc = nc
        self.sem_es = sem_es
        self.sem_prefix = sem_prefix
        self.ops = []
        self.last_w = {}
        self.readers = {}
        self.n_dma_sems = n_dma_sems

    ALIAS = {
        "ra": ["Rsb_hi", "Rsb_lo", "sig1"], "Rsb_hi": ["ra", "sig1"], "Rsb_lo": ["ra", "sig1"],
        "sig1": ["ra", "Rsb_hi", "Rsb_lo"],
        "rb": ["Rsw", "tmp1", "Oc_lo", "Oc_hi"], "Rsw": ["rb", "tmp1", "Oc_lo", "Oc_hi"],
        "tmp1": ["rb", "Rsw", "Oc_lo", "Oc_hi"], "Oc_lo": ["rb", "Rsw", "tmp1"], "Oc_hi": ["rb", "Rsw", "tmp1"],
        "sig0": ["etmp"], "etmp": ["sig0"],
        "tmp0": ["t1_lo", "t1_hi"], "t1_lo": ["tmp0"], "t1_hi": ["tmp0"],
    }

    def _expand(self, keys):
        out = []
        for k in keys:
            out.append(k)
            out.extend(self.ALIAS.get(k, ()))
        return list(dict.fromkeys(out))

    def op(self, stream, fn, reads=(), writes=(), dma=False):
        reads = self._expand(reads)
        writes = self._expand(writes)
        i = len(self.ops)
        deps = {}

        def add(j, raw):
            if j is None:
                return
            deps[j] = deps.get(j, False) or raw

        for k in reads:
            add(self.last_w.get(k), True)
        for k in writes:
            add(self.last_w.get(k), True)
            for r in self.readers.get(k, ()):
                add(r, False)
        for k in reads:
            lst = self.readers.setdefault(k, [])
            if not dma:
                lst[:] = [r for r in lst if self.ops[r]["dma"] or self.ops[r]["stream"] != stream]
            lst.append(i)
        for k in writes:
            self.last_w[k] = i
            self.readers[k] = []
        self.ops.append(dict(i=i, stream=stream, fn=fn, dma=dma, deps=deps, sig=None))
        return i

    def emit(self, final_wait_ops=()):
        nc = self.nc
        ops = self.ops
        need = [[] for _ in ops]
        signaled = set()
        for o in ops:
            for j, raw in o["deps"].items():
                p = ops[j]
                if not p["dma"] and not o["dma"] and p["stream"] == o["stream"]:
                    if o["stream"] == "pe" or not raw:
                        continue
                need[o["i"]].append(j)
                signaled.add(j)
        for j in final_wait_ops:
            signaled.add(j)
        cnt = {s: 0 for s in self.STREAMS}
        dcnt = {s: 0 for s in self.STREAMS}
        for o in ops:
            if o["dma"]:
                k = dcnt[o["stream"]]
                dcnt[o["stream"]] += 1
                o["sig"] = ("d_%s_%d" % (o["stream"], k % self.n_dma_sems), 16 * (k // self.n_dma_sems + 1))
                o["dk"] = k
            elif o["i"] in signaled:
                cnt[o["stream"]] += 1
                o["sig"] = ("c_" + o["stream"], cnt[o["stream"]])
        dma_by_stream = {s: [o for o in ops if o["dma"] and o["stream"] == s] for s in self.STREAMS}
        sem_names = set()
        for o in ops:
            if o["sig"] is not None:
                sem_names.add(o["sig"][0])
        sem_names = sorted(sem_names)
        with contextlib.ExitStack() as es:
            ses = self.sem_es if self.sem_es is not None else es
            sems = {n: ses.enter_context(nc.semaphore(self.sem_prefix + n)) for n in sem_names}
            block = es.enter_context(nc.Block())
            seen = {s: {} for s in self.STREAMS}

            def run_stream(stream, eng):
                sw = seen[stream]
                for o in ops:
                    if o["stream"] != stream:
                        continue
                    waits = {}
                    for j in need[o["i"]]:
                        sn, sv = ops[j]["sig"]
                        waits[sn] = max(waits.get(sn, 0), sv)
                    if o["dma"] and o["dk"] >= self.n_dma_sems:
                        sn, sv = o["sig"]
                        waits[sn] = max(waits.get(sn, 0), sv - 16)
                    for sn, sv in sorted(waits.items()):
                        if sw.get(sn, 0) >= sv:
                            continue
                        sw[sn] = sv
                        eng.wait_ge(sems[sn], sv)
                    ins = o["fn"](eng)
                    if o["sig"] is not None:
                        ins.then_inc(sems[o["sig"][0]], 16 if o["dma"] else 1)
                if stream == "sp":
                    for j in final_wait_ops:
                        sn, sv = ops[j]["sig"]
                        if sw.get(sn, 0) < sv:
                            sw[sn] = sv
                            eng.wait_ge(sems[sn], sv)

            @block.tensor
            def _(e):
                run_stream("pe", e)

            @block.scalar
            def _(e):
                run_stream("act", e)

            @block.vector
            def _(e):
                run_stream("dve", e)

            @block.gpsimd
            def _(e):
                run_stream("pool", e)

            @block.sync
            def _(e):
                run_stream("sp", e)


class Ctx:
    def __init__(self, nc, es, pre="", sem_es=None):
        self.nc = nc
        self.es = es
        self.pre = pre
        self.P = Prog(nc, sem_es=sem_es, sem_prefix=pre)
        self._n = 0

    def sb(self, name, shape, dt):
        return self.es.enter_context(self.nc.sbuf_tensor(self.pre + name, list(shape), dt))

    def ps(self, name, shape, dt):
        return self.es.enter_context(self.nc.psum_tensor(self.pre + name, list(shape), dt))

    def din(self, name, shape, dt=F32):
        return self.nc.dram_tensor(self.pre + name, list(shape), dt, kind="ExternalInput").ap()

    def dout(self, name, shape, dt=F32):
        return self.nc.dram_tensor(name, list(shape), dt, kind="ExternalOutput").ap()


class Ring:
    def __init__(self, C, n=2, cols=1024):
        self.C = C
        self.slots = [C.sb("wst%d" % i, [128, cols], F32) for i in range(n)]
        self.k = 0

    def load(self, src, parts, shape_free, dst, dst_key, conv="pool", extra_writes=(), gain=None):
        P = self.C.P
        i = self.k % len(self.slots)
        self.k += 1
        n = int(np.prod(shape_free))
        st = self.slots[i][0:parts, 0:n]
        if len(shape_free) == 2:
            st = st.rearrange("p (a b) -> p a b", b=shape_free[1])
        key = "wst%d" % i
        P.op("sp", lambda e: e.dma_start(out=st, in_=src), writes=[key], dma=True)
        wr = [dst_key] + list(extra_writes)
        if gain is not None:
            gt, gkey, k0 = gain
            gap = bass.AP(gt, k0, [[gt[:].shape[1], 128], [1, shape_free[0]], [0, shape_free[1]]])
            P.op("pool", lambda e: e.tensor_tensor(out=dst, in0=st, in1=gap, op=ALU.mult), reads=[key, gkey], writes=wr)
        elif conv == "pool":
            P.op("pool", lambda e: e.tensor_copy(out=dst, in_=st), reads=[key], writes=wr)
        elif conv == "exp":
            P.op("act", lambda e: e.activation(out=dst, in_=st, func=AF.Exp), reads=[key], writes=wr)


def run_pipeline(n, stages):
    k = len(stages)
    for step in range(n + k - 1):
        for d in range(k):
            t = step - d
            if 0 <= t < n:
                stages[d](t)


def emit_norm_stats(C, st, xt_ap, xt_key, col, sq, width=D):
    P = C.P
    ss, ms, nh = st
    P.op("act", lambda e: e.activation(out=sq[:, 0:width], in_=xt_ap, func=AF.Square, accum_out=ss[:, col:col + 1]),
         reads=[xt_key], writes=["sq", "ss%d" % col])
    P.op("dve", lambda e: e.tensor_scalar(out=ms[:, col:col + 1], in0=ss[:, col:col + 1], scalar1=1.0 / width, scalar2=EPS,
                                         op0=ALU.mult, op1=ALU.add), reads=["ss%d" % col], writes=["ms%d" % col])
    P.op("pool", lambda e: e.tensor_tensor(out=ms[:, col:col + 1], in0=ms[:, col:col + 1], in1=nh[:], op=ALU.pow),
         reads=["ms%d" % col, "nh"], writes=["rs%d" % col])


def emit_norm_apply_T(C, st, xt_ap, xt_key, col, gbc, gkey, xnb_ap, xnb_key, tp_ps, tp_key, idb, dstT, dstT_key, nch=8,
                      copy_eng="act"):
    P = C.P
    ss, ms, nh = st
    P.op("act", lambda e: e.activation(out=xnb_ap, in_=xt_ap, func=AF.Copy, scale=ms[:, col:col + 1]),
         reads=[xt_key, "rs%d" % col], writes=[xnb_key])
    for kc in range(nch):
        P.op("pe", lambda e, kc=kc: e.transpose(out=tp_ps[:, kc, :], in_=xnb_ap[:, kc * 128:(kc + 1) * 128], identity=idb[:]),
             reads=[xnb_key, "idb"], writes=[tp_key])
    if copy_eng == "act":
        P.op("act", lambda e: e.activation(out=dstT, in_=tp_ps[:, 0:nch, :], func=AF.Copy), reads=[tp_key], writes=[dstT_key])
    else:
        P.op("dve", lambda e: e.tensor_copy(out=dstT, in_=tp_ps[:, 0:nch, :]), reads=[tp_key], writes=[dstT_key])


def emit_norm_T(C, st, xt_ap, xt_key, col, gbc, gkey, xnb, xnb_key, tp_ps, tp_key, idb, dstT, dstT_key, sq):
    emit_norm_stats(C, st, xt_ap, xt_key, col, sq)
    emit_norm_apply_T(C, st, xt_ap, xt_key, col, gbc, gkey, xnb[:], xnb_key, tp_ps, tp_key, idb, dstT, dstT_key)


def emit_finalize_now(C, Oe, Oo, Rsb, Oc):
    P = C.P
    P.op("dve", lambda e: e.reciprocal(out=Rsb[64:128, :], in_=Oe[64:128, :]), reads=["Oe"], writes=["Rsb_hi"])
    P.op("act", lambda e: e.activation(out=Oc[0:64, :], in_=Oe[0:64, :], func=AF.Copy), reads=["Oe"], writes=["Oc_lo"])
    P.op("dve", lambda e: e.reciprocal(out=Rsb[0:64, :], in_=Oo[0:64, :]), reads=["Oo"], writes=["Rsb_lo"])
    P.op("act", lambda e: e.activation(out=Oc[64:128, :], in_=Oo[64:128, :], func=AF.Copy), reads=["Oo"], writes=["Oc_hi"])


def emit_finalize_later(C, Rsb, Oc, t1, pmf, Mps, Mkey, gz_ap, gz_key, og_ap, og_key):
    P = C.P
    P.op("pe", lambda e: e.matmul(Mps[:, 0:512], lhsT=pmf[:], rhs=Rsb[:], start=True, stop=True),
         reads=["Rsb_hi", "Rsb_lo", "pmf"], writes=[Mkey])
    P.op("dve", lambda e: e.tensor_tensor(out=t1[:], in0=Oc[:], in1=Mps[:, 0:512], op=ALU.mult),
         reads=["Oc_lo", "Oc_hi", Mkey], writes=["t1_lo", "t1_hi"])
    P.op("pool", lambda e: e.tensor_tensor(out=og_ap, in0=t1[:], in1=gz_ap, op=ALU.mult),
         reads=["t1_lo", "t1_hi", gz_key], writes=[og_key])


def emit_silu_gate(C, zps, zkey, etmp, gz_ap, gz_key):
    P = C.P
    P.op("act", lambda e: e.activation(out=etmp[:], in_=zps, func=AF.Exp, scale=-1.0), reads=[zkey], writes=["etmp"])
    P.op("dve", lambda e: e.tensor_scalar(out=etmp[:], in0=etmp[:], scalar1=1.0, scalar2=None, op0=ALU.add),
         reads=["etmp"], writes=["etmp"])
    P.op("dve", lambda e: e.reciprocal(out=etmp[:], in_=etmp[:]), reads=["etmp"], writes=["etmp"])
    P.op("dve", lambda e: e.tensor_tensor(out=gz_ap, in0=zps, in1=etmp[:], op=ALU.mult),
         reads=[zkey, "etmp"], writes=[gz_key])


def emit_ple(C, get_tile, n_tiles, pd, pgbc, wgb, wpb, st, col0, xnb, idb, sq, hnT, pst, pbf, pT, sig, tmp,
             Sps, Mps, out_dram, final_norm=None, pre=None):
    P = C.P
    outs = []
    ss, ms, nh_ = st

    def sA(i):
        pre(i)

    def sB(i):
        xt, xkey = get_tile(i)
        emit_norm_stats(C, st, xt, xkey, col0 + i, sq)

    def sC(i):
        j = i % 2
        xt, xkey = get_tile(i)
        col = col0 + i
        P.op("act", lambda e: e.activation(out=xnb[j][:], in_=xt, func=AF.Copy, scale=ms[:, col:col + 1]),
             reads=[xkey, "rs%d" % col], writes=["xnb%d" % j])
        P.op("sp", lambda e: e.dma_start(out=pst[j][:], in_=pd[i * 128:(i + 1) * 128, :]), writes=["pst%d" % j], dma=True)
        P.op("pool", lambda e: e.tensor_copy(out=pbf[j][:], in_=pst[j][:]), reads=["pst%d" % j], writes=["pbf%d" % j])

    def sD(i):
        j = i % 2
        tp = Mps[0][:].bitcast(BF16).rearrange("p (k t) -> p k t", t=128)
        for kc in range(8):
            P.op("pe", lambda e, kc=kc: e.transpose(out=tp[:, kc, :], in_=xnb[j][:, kc * 128:(kc + 1) * 128], identity=idb[:]),
                 reads=["xnb%d" % j, "idb"], writes=["M0"])
        P.op("act", lambda e: e.activation(out=hnT[j][:], in_=tp, func=AF.Copy), reads=["M0"], writes=["hnT%d" % j])
        tp2 = Mps[1][:].bitcast(BF16).rearrange("p (k t) -> p k t", t=128)
        for kc in range(2):
            P.op("pe", lambda e, kc=kc: e.transpose(out=tp2[:, kc, :], in_=pbf[j][:, kc * 128:(kc + 1) * 128], identity=idb[:]),
                 reads=["pbf%d" % j, "idb"], writes=["M1"])
        P.op("dve", lambda e: e.tensor_copy(out=pT[j][:], in_=tp2[:, 0:2, :]), reads=["M1"], writes=["pT%d" % j])

    def sE(i):
        j = i % 2
        xt, xkey = get_tile(i)
        for nh in range(2):
            gps = Sps[0][:, nh * 512:(nh + 1) * 512]
            pps = Sps[1][:, nh * 512:(nh + 1) * 512]
            for kc in range(8):
                P.op("pe", lambda e, kc=kc, nh=nh, gps=gps: e.matmul(
                    gps, lhsT=hnT[j][:, kc, :], rhs=wgb[:, kc, nh * 512:(nh + 1) * 512], start=(kc == 0), stop=(kc == 7)),
                    reads=["hnT%d" % j, "wgb"], writes=["S0_%d" % nh])
            for kc in range(2):
                P.op("pe", lambda e, kc=kc, nh=nh, pps=pps: e.matmul(
                    pps, lhsT=pT[j][:, kc, :], rhs=wpb[:, kc, nh * 512:(nh + 1) * 512], start=(kc == 0), stop=(kc == 1)),
                    reads=["pT%d" % j, "wpb"], writes=["S1_%d" % nh])
            sg = sig[nh]
            P.op("act", lambda e, gps=gps, sg=sg: e.activation(out=sg[:], in_=gps, func=AF.Exp, scale=-1.0),
                 reads=["S0_%d" % nh], writes=["sig%d" % nh])
            P.op("dve", lambda e, sg=sg: e.tensor_scalar(out=sg[:], in0=sg[:], scalar1=1.0, scalar2=None, op0=ALU.add),
                 reads=["sig%d" % nh], writes=["sig%d" % nh])
            P.op("dve", lambda e, sg=sg: e.reciprocal(out=sg[:], in_=sg[:]), reads=["sig%d" % nh], writes=["sig%d" % nh])
            tm = tmp[nh]
            P.op("dve", lambda e, pps=pps, sg=sg, tm=tm: e.tensor_tensor(out=tm[:], in0=pps, in1=sg[:], op=ALU.mult),
                 reads=["S1_%d" % nh, "sig%d" % nh], writes=["tmp%d" % nh])
            xh = xt[:, nh * 512:(nh + 1) * 512]
            P.op("pool", lambda e, xh=xh, tm=tm: e.tensor_tensor(out=xh, in0=xh, in1=tm[:], op=ALU.add),
                 reads=[xkey, "tmp%d" % nh], writes=[xkey])

    def sF(i):
        xt, xkey = get_tile(i)
        emit_norm_stats(C, st, xt, xkey, final_norm[2] + i, sq)

    def sG(i):
        xt, xkey = get_tile(i)
        if final_norm is not None:
            fgbc, fgkey, fcol0 = final_norm
            c2 = fcol0 + i
            P.op("act", lambda e: e.activation(out=xt, in_=xt, func=AF.Copy, scale=ms[:, c2:c2 + 1]),
                 reads=[xkey, "rs%d" % c2], writes=[xkey])
            P.op("dve", lambda e: e.tensor_tensor(out=xt, in0=xt, in1=fgbc[:], op=ALU.mult),
                 reads=[xkey, fgkey], writes=[xkey])
        outs.append(P.op("sp", lambda e: e.dma_start(out=out_dram[i * 128:(i + 1) * 128, :], in_=xt), reads=[xkey], dma=True))

    stages = ([sA] if pre is not None else []) + [sB, sC, sD, sE] + ([sF] if final_norm is not None else []) + [sG]
    run_pipeline(n_tiles, stages)
    return outs


def alloc_common(C, layer1=False):
    B = {}
    if not layer1:
        B["xres"] = C.sb("xres", [128, 16, 1024], F32)
        B["og"] = [C.sb("og%d" % i, [128, 512], BF16) for i in range(2)]
        B["wo"] = [C.sb("wo%d" % i, [128, 1024], BF16) for i in range(2)]
    B["ss"] = C.sb("ss", [128, 96], F32)
    B["ms"] = C.sb("ms", [128, 96], F32)
    B["nh"] = C.sb("nh", [128, 1], F32)
    B["sq"] = C.sb("sq", [128, 1024], BF16)
    B["xnb"] = [C.sb("xnb%d" % i, [128, 1024], BF16) for i in range(2)]
    B["idb"] = C.sb("idb", [128, 128], BF16)
    B["pmf"] = C.sb("pmf", [128, 128], F32)
    B["gnc"] = C.sb("gnc_sb", [128, 8], F32)
    B["gpc"] = C.sb("gpc_sb", [128, 8], F32)
    B["Rsb"] = C.sb("Rsb", [128, 512], F32)
    B["Rsw"] = C.sb("Rsw", [128, 512], F32)
    B["t1"] = C.sb("t1", [128, 512], F32)
    B["hnT"] = [C.sb("hnT%d" % i, [128, 8, 128], BF16) for i in range(2)]
    B["pst"] = [C.sb("pst%d" % i, [128, 256], F32) for i in range(2)]
    B["pbf"] = [C.sb("pbf%d" % i, [128, 256], BF16) for i in range(2)]
    B["pT"] = [C.sb("pT%d" % i, [128, 2, 128], BF16) for i in range(2)]
    B["ring"] = Ring(C, n=2, cols=1024)
    B["S"] = [C.ps("S%d" % i, [128, 1024], F32) for i in range(2)]
    B["Oe"] = C.ps("Oe", [128, 512], F32)
    B["Oo"] = C.ps("Oo", [128, 512], F32)
    B["M"] = [C.ps("M%d" % i, [128, 512], F32) for i in range(2)]
    return B


def emit_consts(C, B, ident, perm):
    P = C.P
    B["ring"].load(ident, 128, [128], B["idb"][:], "idb")
    P.op("sp", lambda e: e.dma_start(out=B["pmf"][:], in_=perm), writes=["pmf"], dma=True)
    P.op("pool", lambda e: e.memset(B["nh"][:], -0.5), writes=["nh"])


def emit_outproj(C, B, og, og_key, wo, wo_key, tiles, mctr):
    P = C.P
    xres = B["xres"]
    for tt, i in enumerate(tiles):
        for nh in range(2):
            m = mctr[0] % 2
            mctr[0] += 1
            Mp = B["M"][m]
            P.op("pe", lambda e, tt=tt, nh=nh, Mp=Mp: e.matmul(
                Mp[:, 0:512], lhsT=og[:, tt * 128:(tt + 1) * 128], rhs=wo[:, nh * 512:(nh + 1) * 512],
                start=True, stop=True), reads=[og_key, wo_key], writes=["M%d" % m])
            xh = xres[:, i, nh * 512:(nh + 1) * 512]
            P.op("dve", lambda e, xh=xh, Mp=Mp: e.tensor_tensor(out=xh, in0=xh, in1=Mp[:, 0:512], op=ALU.add),
                 reads=["xr%d" % i, "M%d" % m], writes=["xr%d" % i])


def build_l0(nc, es, pre="", xo=None, sem_es=None, shared=None):
    C = Ctx(nc, es, pre, sem_es)
    P = C.P
    shared = {} if shared is None else shared

    def sdin(name, shape):
        if name not in shared:
            shared[name] = C.din(name, shape)
        return shared[name]

    xk = C.din("xk", [2560, 1024])
    pd = C.din("pd", [2048, 256])
    bq = C.din("bq", [128, 2048])
    gn = sdin("gnc", [128, 8])
    gp = sdin("gpc", [128, 8])
    w_in = sdin("w_in", [1024, 4096])
    w_out = sdin("w_out", [1024, 1024])
    wg = sdin("wg", [1024, 1024])
    wp = sdin("wp", [256, 1024])
    tabB = sdin("tabB", [8, 128, 2816])
    tabBi = sdin("tabBi", [8, 128, 2816])
    aoh = sdin("aoh", [128, 1024])
    ident = sdin("ident", [128, 128])
    perm = sdin("perm", [128, 128])
    if xo is None:
        xo = C.dout("xo", [2048, 1024])

    B = alloc_common(C)
    ring = B["ring"]
    xres = B["xres"]
    xnT = C.sb("xnT", [128, 8, 2560], BF16)
    hx = [C.sb("hx%d" % i, [128, 1024], F32) for i in range(2)]
    AR_COLS, VP_OFF = 12544, 6656
    ar = C.sb("ar", [128, 12544], BF16)
    qTe = ar[:, 0:2048]
    qTo = ar[:, 2048:4096]
    kT = ar[:, 4096:6656]
    vP = ar[:, 6656:10496].rearrange("p (t c) -> p t c", c=192)
    gz = ar[:, 10496:12544]
    wgb = ar[:, 0:8192].rearrange("p (k n) -> p k n", n=1024)
    wpb = ar[:, 8192:10240].rearrange("p (k n) -> p k n", n=1024)
    wq = C.sb("wq", [128, 8, 128], BF16)
    wk = C.sb("wk", [128, 8, 128], BF16)
    wv = C.sb("wv", [128, 8, 128], BF16)
    wz = C.sb("wz", [128, 8, 128], BF16)
    tabE = C.sb("tabE", [128, 2816], BF16)
    tabEi = C.sb("tabEi", [128, 2816], BF16)
    bqb = C.sb("bqb", [128, 2048], BF16)
    aohb = C.sb("aohb", [128, 1024], BF16)
    PT = [C.sb("PT%d" % i, [128, 1024], BF16) for i in range(3)]
    etmp = C.sb("etmp", [128, 512], F32)
    sig = [etmp, B["Rsb"]]
    tmp = [B["t1"], B["Rsw"]]
    st = (B["ss"], B["ms"], B["nh"])

    emit_consts(C, B, ident, perm)
    P.op("sp", lambda e: e.dma_start(out=B["gnc"][:], in_=gn), writes=["gnc"], dma=True)
    P.op("sp", lambda e: e.dma_start(out=B["gpc"][:], in_=gp), writes=["gpc"], dma=True)
    ring.load(aoh, 128, [1024], aohb[:], "aohb")
    for h2 in range(2):
        ring.load(bq[:, h2 * 1024:(h2 + 1) * 1024], 128, [1024], bqb[:, h2 * 1024:(h2 + 1) * 1024], "bqb")
    P.op("pool", lambda e: e.memset(qTe[64:128, :], 0.0), writes=["qTe_z"])
    P.op("pool", lambda e: e.memset(qTo[0:64, :], 0.0), writes=["qTo_z"])
    P.op("pool", lambda e: e.memset(vP[:, :, 64:128], 1.0), writes=["vP_ones"])

    def p1_tile(bt):
        if 2 <= bt < 18:
            return xres[:, bt - 2, :], "xr%d" % (bt - 2)
        return hx[bt % 2][:], "hx%d" % (bt % 2)

    def p1_s0(bt):
        xt, xkey = p1_tile(bt)
        P.op("sp", lambda e: e.dma_start(out=xt, in_=xk[bt * 128:(bt + 1) * 128, :]), writes=[xkey], dma=True)
        emit_norm_stats(C, st, xt, xkey, bt, B["sq"])

    def p1_s1(bt):
        xt, xkey = p1_tile(bt)
        j = bt % 2
        tp = B["M"][j][:].bitcast(BF16).rearrange("p (k t) -> p k t", t=128)
        emit_norm_apply_T(C, st, xt, xkey, bt, None, None, B["xnb"][j][:], "xnb%d" % j, tp, "M%d" % j, B["idb"],
                          xnT[:, :, bt * 128:(bt + 1) * 128], "xnT%d" % bt)

    run_pipeline(20, [p1_s0, p1_s1])

    w_in_v = w_in.rearrange("(kc p) n -> p kc n", p=128)
    mctr = [0, 0]
    pending = []
    sctr = [0]
    pctr = [0]
    octr = [0]

    def next_m():
        m = mctr[0] % 2
        mctr[0] += 1
        return m

    def load_inproj(c_):
        for wt, wkey, off in ((wq, "wq", 0), (wk, "wk", 1024), (wv, "wv", 2048), (wz, "wz", 3072)):
            ring.load(w_in_v[:, :, off + c_ * 128: off + (c_ + 1) * 128], 128, [8, 128], wt[:], wkey,
                      gain=(B["gnc"], "gnc", 0))

    load_inproj(0)
    for c in range(8):
        wo = B["wo"][c % 2]
        ring.load(w_out[c * 128:(c + 1) * 128, :], 128, [1024], wo[:], "wo%d" % (c % 2))
        for ch, (lo, hi) in enumerate(((0, 1024), (1024, 2048), (2048, 2816))):
            ring.load(tabB[c, :, lo:hi], 128, [hi - lo], tabE[:, lo:hi], "tabE%d" % ch, conv="exp")
            ring.load(tabBi[c, :, lo:hi], 128, [hi - lo], tabEi[:, lo:hi], "tabEi%d" % ch, conv="exp")

        def flush_one():
            _, g_, c_ = pending.pop(0)
            m = next_m()
            oj = octr[0] % 2
            octr[0] += 1
            og = B["og"][oj]
            emit_finalize_later(C, B["Rsb"], B["Rsw"], B["t1"], B["pmf"], B["M"][m], "M%d" % m,
                                gz[:, g_ * 512:(g_ + 1) * 512], "gz%d" % g_, og[:], "og%d" % oj)
            emit_outproj(C, B, og, "og%d" % oj, B["wo"][c_ % 2], "wo%d" % (c_ % 2), [4 * g_ + a_ for a_ in range(4)], mctr)

        for blk in range(4):
            m = next_m()
            Mp = B["M"][m]
            t0 = 256 + blk * 512
            rk = ["xnT%d" % (t0 // 128 + a) for a in range(4)]
            for kc in range(8):
                P.op("pe", lambda e, kc=kc, Mp=Mp, t0=t0: e.matmul(Mp[:, 0:512], lhsT=wq[:, kc, :], rhs=xnT[:, kc, t0:t0 + 512],
                                                                   start=(kc == 0), stop=(kc == 7)),
                     reads=rk + ["wq"], writes=["M%d" % m])
            P.op("act", lambda e, Mp=Mp, blk=blk: e.activation(out=qTe[0:64, blk * 512:(blk + 1) * 512], in_=Mp[0:64, 0:512],
                                                                func=AF.Copy), reads=["M%d" % m], writes=["qTe"])
            P.op("act", lambda e, Mp=Mp, blk=blk: e.activation(out=qTo[64:128, blk * 512:(blk + 1) * 512], in_=Mp[64:128, 0:512],
                                                                func=AF.Copy), reads=["M%d" % m], writes=["qTo"])
        for blk in range(5):
            m = next_m()
            Mp = B["M"][m]
            t0 = blk * 512
            rk = ["xnT%d" % (t0 // 128 + a) for a in range(4)]
            for kc in range(8):
                P.op("pe", lambda e, kc=kc, Mp=Mp, t0=t0: e.matmul(Mp[:, 0:512], lhsT=wk[:, kc, :], rhs=xnT[:, kc, t0:t0 + 512],
                                                                   start=(kc == 0), stop=(kc == 7)),
                     reads=rk + ["wk"], writes=["M%d" % m])
            if blk % 2 == 0:
                P.op("dve", lambda e, Mp=Mp, t0=t0: e.tensor_copy(out=kT[:, t0:t0 + 512], in_=Mp[:, 0:512]),
                     reads=["M%d" % m], writes=["kT%d" % blk])
            else:
                P.op("act", lambda e, Mp=Mp, t0=t0: e.activation(out=kT[:, t0:t0 + 512], in_=Mp[:, 0:512], func=AF.Copy),
                     reads=["M%d" % m], writes=["kT%d" % blk])
        while pending:
            flush_one()
        for blk in range(4):
            m = next_m()
            Mp = B["M"][m]
            t0 = 256 + blk * 512
            rk = ["xnT%d" % (t0 // 128 + a) for a in range(4)]
            for kc in range(8):
                P.op("pe", lambda e, kc=kc, Mp=Mp, t0=t0: e.matmul(Mp[:, 0:512], lhsT=wz[:, kc, :], rhs=xnT[:, kc, t0:t0 + 512],
                                                                   start=(kc == 0), stop=(kc == 7)),
                     reads=rk + ["wz"], writes=["M%d" % m])
            emit_silu_gate(C, Mp[:, 0:512], "M%d" % m, etmp, gz[:, blk * 512:(blk + 1) * 512], "gz%d" % blk)
        for t4 in range(5):
            m = next_m()
            Mp = B["M"][m]
            Mv = Mp[:, 0:512].rearrange("p (t c) -> p t c", c=128)
            for a in range(4):
                bt = t4 * 4 + a
                for kc in range(8):
                    P.op("pe", lambda e, kc=kc, Mv=Mv, a=a, bt=bt: e.matmul(
                        Mv[:, a, :], lhsT=xnT[:, kc, bt * 128:(bt + 1) * 128], rhs=wv[:, kc, :],
                        start=(kc == 0), stop=(kc == 7)), reads=["xnT%d" % bt, "wv"], writes=["M%d" % m])
            vdst = bass.AP(ar, VP_OFF + t4 * 4 * 192, [[AR_COLS, 128], [192, 4], [128, 2], [1, 64]])
            P.op("dve", lambda e, Mv=Mv, vdst=vdst: e.tensor_copy(out=vdst, in_=Mv.rearrange("p t (h c) -> p t h c", h=2)),
                 reads=["M%d" % m], writes=["vPe%d" % t4, "vPo%d" % t4])

        if c < 7:
            load_inproj(c + 1)
        its = [(g, hh, tp_) for g in range(4) for hh in range(2) for tp_ in range(4)]
        SKEW = 2

        def qk_stage(n):
            g, hh, tp_ = its[n]
            qm = qTe if hh == 0 else qTo
            qkeys = ["qTe", "qTe_z"] if hh == 0 else ["qTo", "qTo_z"]
            s_ = sctr[0] % 2
            sctr[0] += 1
            Sp = B["S"][s_]
            for u in range(2):
                t = 2 * tp_ + (1 - u)
                bt = 4 * g + t
                interior = g in (1, 2)
                P.op("pe", lambda e, Sp=Sp, u=u, bt=bt, qm=qm, g=g, interior=interior: e.matmul(
                    Sp[:, u * 512:(u + 1) * 512], lhsT=kT[:, bt * 128:(bt + 1) * 128], rhs=qm[:, g * 512:(g + 1) * 512],
                    start=True, stop=interior), reads=["kT%d" % (bt // 4)] + qkeys, writes=["S%d" % s_])
                if not interior:
                    P.op("pe", lambda e, Sp=Sp, u=u, t=t, g=g: e.matmul(
                        Sp[:, u * 512:(u + 1) * 512], lhsT=aohb[:, t * 128:(t + 1) * 128], rhs=bqb[:, g * 512:(g + 1) * 512],
                        start=False, stop=True), reads=["aohb", "bqb"], writes=["S%d" % s_])
            pj = pctr[0] % 3
            pctr[0] += 1
            Pt = PT[pj]
            P.op("act", lambda e, Pt=Pt, Sp=Sp: e.activation(out=Pt[:], in_=Sp[:], func=AF.Exp, scale=0.125),
                 reads=["S%d" % s_], writes=["PT%d" % pj])
            tsel, tkey = (tabEi, "tabEi") if g in (1, 2) else (tabE, "tabE")
            tab_ap = bass.AP(tsel, hh * 1408 + (12 - 4 * tp_) * 64, [[2816, 128], [128, 2], [64, 8], [1, 64]])
            Pv = Pt[:].rearrange("p (u q c) -> p u q c", u=2, q=8)
            P.op("dve", lambda e, Pv=Pv, tab_ap=tab_ap: e.tensor_tensor(out=Pv, in0=Pv, in1=tab_ap, op=ALU.mult),
                 reads=["PT%d" % pj, tkey + "0", tkey + "1", tkey + "2"], writes=["PT%d" % pj])
            return pj

        pjs = {}
        for n in range(len(its) + SKEW):
            if n < len(its):
                pjs[n] = qk_stage(n)
            m_ = n - SKEW
            if m_ < 0:
                continue
            g, hh, tp_ = its[m_]
            pj = pjs[m_]
            Pt = PT[pj]
            Oh = B["Oe"] if hh == 0 else B["Oo"]
            okey = "Oe" if hh == 0 else "Oo"
            for u in range(2):
                t = 2 * tp_ + (1 - u)
                bt = 4 * g + t
                first = (tp_ == 0 and u == 0)
                last = (tp_ == 3 and u == 1)
                P.op("pe", lambda e, Oh=Oh, bt=bt, hh=hh, Pt=Pt, u=u, first=first, last=last: e.matmul(
                    Oh[:, 0:512], lhsT=vP[:, bt, hh * 64:hh * 64 + 128], rhs=Pt[:, u * 512:(u + 1) * 512],
                    start=first, stop=last),
                    reads=["vPe%d" % (bt // 4), "vPo%d" % (bt // 4), "vP_ones", "PT%d" % pj], writes=[okey])
            if hh == 1 and tp_ == 3:
                emit_finalize_now(C, B["Oe"], B["Oo"], B["Rsb"], B["Rsw"])
                pending.append((n + 3 if g < 3 else 10 ** 9, g, c))
            while pending and pending[0][0] <= n:
                flush_one()

    while pending:
        flush_one()
    alias = ["qTe", "qTo", "qTe_z", "qTo_z", "vP_ones"] + ["kT%d" % i for i in range(5)] + ["gz%d" % i for i in range(4)] + \
            ["vPe%d" % i for i in range(5)] + ["vPo%d" % i for i in range(5)]
    wg_v = wg.rearrange("(kc p) n -> p kc n", p=128)
    for nb in range(8):
        ring.load(wg_v[:, :, nb * 128:(nb + 1) * 128], 128, [8, 128], wgb[:, :, nb * 128:(nb + 1) * 128], "wgb",
                  extra_writes=alias if nb == 0 else (), gain=(B["gpc"], "gpc", 0))
    wp_v = wp.rearrange("(kc p) n -> p kc n", p=128)
    for nb in range(2):
        ring.load(wp_v[:, :, nb * 512:(nb + 1) * 512], 128, [2, 512], wpb[:, :, nb * 512:(nb + 1) * 512], "wpb",
                  extra_writes=alias if nb == 0 else ())
    outs = emit_ple(C, lambda i: (xres[:, i, :], "xr%d" % i), 16, pd, None, wgb, wpb, st, 20, B["xnb"], B["idb"],
                    B["sq"], B["hnT"], B["pst"], B["pbf"], B["pT"], sig, tmp, B["S"], B["M"], xo)
    P.emit(final_wait_ops=outs)
    return nc


def _consts():
    ident = np.eye(128, dtype=np.float32)
    perm = np.zeros((128, 128), np.float32)
    for i in range(64):
        perm[64 + i, i] = 1.0
        perm[i, 64 + i] = 1.0
    return ident, perm


def _na_tables(rpb, interior=False):
    kc = np.arange(64)
    qc = np.arange(64)
    cs = np.clip(qc - 8, 0, 48)
    colvalid = (kc[:, None] >= cs[None, :]) & (kc[:, None] < cs[None, :] + 16)
    coff = np.clip(kc[:, None] - qc[None, :] + 15, 0, 30)
    tab = np.full((16, 128, 22, 64), NEG, np.float32)
    for e in range(22):
        for half in range(2):
            dr = 10 - e + half
            if (-4 <= dr <= 3) if interior else (-7 <= dr <= 7):
                vals = rpb[:, dr + 7][:, coff]
                tab[:, half * 64:(half + 1) * 64, e, :] = np.where(colvalid[None], vals, NEG)
    return np.ascontiguousarray(tab.reshape(8, 2, 128, 22, 64).transpose(0, 2, 1, 3, 4).reshape(8, 128, 2816))


def _na_rowmask(hf):
    bq = np.zeros((128, 4, 8, 64), np.float32)
    for g in range(4):
        for qi in range(8):
            r = 32 * hf + 8 * g + qi
            rs = min(max(r - 4, 0), 56)
            for j in range(16):
                R = 32 * hf - 4 + 8 * g + j
                ok = (0 <= R <= 63) and (rs <= R < rs + 8)
                bq[j, g, qi, :] = 0.0 if ok else NEG
    aoh = np.zeros((128, 8, 128), np.float32)
    for t in range(8):
        aoh[2 * t, t, 0:64] = 1.0
        aoh[2 * t + 1, t, 64:128] = 1.0
    return bq.reshape(128, 2048), aoh.reshape(128, 1024)


_NC_CACHE = {}


def _get_nc(name, builder):
    if name not in _NC_CACHE:
        nc = bass.Bass("TRN2", target_bir_lowering=False)
        with contextlib.ExitStack() as es:
            builder(nc, es)
        _NC_CACHE[name] = nc
    return _NC_CACHE[name]


def run_l0(x, p, norm_g, na_w_in, na_rpb, na_w_out, ple_norm, ple_w_gate, ple_w_proj):
    x = np.asarray(x, np.float32)
    ident, perm = _consts()
    tabB = _na_tables(np.asarray(na_rpb[0], np.float32))
    tabBi = _na_tables(np.asarray(na_rpb[0], np.float32), interior=True)
    gn = np.ascontiguousarray(np.asarray(norm_g[0], np.float32).reshape(8, 128).T)
    gp = np.ascontiguousarray(np.asarray(ple_norm[0], np.float32).reshape(8, 128).T)
    shared = dict(gnc=gn, gpc=gp, w_in=np.ascontiguousarray(na_w_in[0], dtype=np.float32),
                  w_out=np.ascontiguousarray(na_w_out[0], dtype=np.float32),
                  wg=np.ascontiguousarray(ple_w_gate[0], dtype=np.float32),
                  wp=np.ascontiguousarray(ple_w_proj[0], dtype=np.float32), tabB=tabB, tabBi=tabBi, ident=ident, perm=perm)
    in_maps = []
    for core in range(8):
        b, hf = core // 2, core % 2
        xk = np.zeros((40, 64, 1024), np.float32)
        xb = x[b].reshape(64, 64, 1024)
        for lr in range(40):
            R = 32 * hf - 4 + lr
            if 0 <= R <= 63:
                xk[lr] = xb[R]
        bq, aoh = _na_rowmask(hf)
        m = dict(shared)
        m.update(xk=xk.reshape(2560, 1024), pd=np.ascontiguousarray(p[0, b, hf * 2048:(hf + 1) * 2048], dtype=np.float32),
                 bq=bq, aoh=aoh)
        in_maps.append(m)
    nc = _get_nc("l0", build_l0)
    res = run_bass_kernel_spmd(nc, in_maps, core_ids=list(range(8)))
    x1 = np.zeros((4, 4096, 1024), np.float32)
    for core in range(8):
        b, hf = core // 2, core % 2
        x1[b, hf * 2048:(hf + 1) * 2048] = res.results[core]["xo"]
    return x1


def build_l1(nc, es, pre="", xf=None, sem_es=None):
    C = Ctx(nc, es, pre, sem_es)
    P = C.P
    if xf is None:
        xoth = C.din("xoth", [2048, 1024])
        xown = C.din("xown", [2048, 1024])
    else:
        xown, xoth = xf

    def xf_tile(T):
        src = xown if T < 16 else xoth
        return src[(T % 16) * 128:(T % 16 + 1) * 128, :]
    pd = C.din("pd", [2048, 256])
    gn = C.din("gnc", [128, 8])
    gp = C.din("gpc", [128, 8])
    gf = C.din("gf", [128, 1024])
    qn = C.din("qnc", [128, 3])
    kvn = C.din("kvnc", [128, 2])
    w_in = C.din("w_in", [1024, 1696])
    wqa = C.din("wqa", [384, 1536])
    wqs = C.din("wqs", [384, 1536])
    w_kvb = C.din("w_kvb", [256, 2048])
    w_out = C.din("w_out", [1024, 1024])
    wg = C.din("wg", [1024, 1024])
    wp = C.din("wp", [256, 1024])
    costok = C.din("costok", [128, 512])
    sintok = C.din("sintok", [128, 512])
    cosTd = C.din("cosT", [128, 2048])
    sinTd = C.din("sinT", [128, 2048])
    ident = C.din("ident", [128, 128])
    perm = C.din("perm", [128, 128])
    xo = C.dout("xo", [2048, 1024])

    B = alloc_common(C, layer1=True)
    ring = B["ring"]
    st = (B["ss"], B["ms"], B["nh"])
    xnT = C.sb("xnT", [128, 8, 2048], BF16)
    ckvT = C.sb("ckvT", [128, 2, 4096], BF16)
    kpeT = C.sb("kpeT", [128, 4096], BF16)
    cqT = C.sb("cqT", [128, 3, 2048], BF16)
    cosT = C.sb("cosTb", [128, 2048], BF16)
    sinT = C.sb("sinTb", [128, 2048], BF16)
    ogT = C.sb("ogT", [128, 8, 2048], BF16)
    hx = [C.sb("hx%d" % i, [128, 1024], F32) for i in range(2)]
    qnc = C.sb("qnc_sb", [128, 3], F32)
    kvnc = C.sb("kvnc_sb", [128, 2], F32)
    xkT = B["hnT"]
    lat = [B["pst"][i][:].bitcast(BF16) for i in range(2)]
    wqA = [C.sb("wqA%d" % i, [128, 3, 96], BF16) for i in range(2)]
    wqB = [C.sb("wqB%d" % i, [128, 3, 96], BF16) for i in range(2)]
    wkn = [C.sb("wkn%d" % i, [128, 2, 64], BF16) for i in range(2)]
    wvp = C.sb("wvp", [128, 2, 128], BF16)
    wz = C.sb("wz", [128, 8, 128], BF16)
    PT = [C.sb("PT%d" % i, [128, 1024], BF16) for i in range(3)]
    etmp = C.sb("etmp", [128, 512], F32)
    AR_COLS, VP_OFF = 20480, 12288
    ar = C.sb("ar", [128, 20480], BF16)
    qT = [ar[:, 0:2048], ar[:, 2048:4096]]
    kT = [ar[:, 4096:8192], ar[:, 8192:12288]]
    vP = ar[:, 12288:18432].rearrange("p (t c) -> p t c", c=192)
    gz = ar[:, 18432:20480]
    krope = ar[:, 0:2048].bitcast(F32).rearrange("p (t c) -> p t c", c=32)
    kpe_tok = ar[:, 2048:5120].rearrange("p (t c) -> p t c", c=96)
    ctk = ar[:, 5120:6144].bitcast(F32).rearrange("p (t c) -> p t c", c=16)
    stk = ar[:, 6144:7168].bitcast(F32).rearrange("p (t c) -> p t c", c=16)
    wkvr = ar[:, 7168:9472].rearrange("p (k n) -> p k n", n=288)
    wcq = ar[:, 9472:12544].rearrange("p (k n) -> p k n", n=384)
    wgb = ar[:, 0:8192].rearrange("p (k n) -> p k n", n=1024)
    wpb = ar[:, 8192:10240].rearrange("p (k n) -> p k n", n=1024)
    woa = ar[:, 10240:18432].rearrange("p (k n) -> p k n", n=1024)
    ra = B["Rsb"]
    rb = B["Rsw"]
    sig = [etmp, B["Rsb"]]
    tmp = [B["t1"], B["Rsw"]]
    fgbc = C.sb("fgbc", [128, 1024], F32)

    emit_consts(C, B, ident, perm)
    P.op("sp", lambda e: e.dma_start(out=B["gnc"][:], in_=gn), writes=["gnc"], dma=True)
    P.op("sp", lambda e: e.dma_start(out=B["gpc"][:], in_=gp), writes=["gpc"], dma=True)
    P.op("sp", lambda e: e.dma_start(out=fgbc[:], in_=gf), writes=["fgbc"], dma=True)
    P.op("sp", lambda e: e.dma_start(out=qnc[:], in_=qn), writes=["qnc"], dma=True)
    P.op("sp", lambda e: e.dma_start(out=kvnc[:], in_=kvn), writes=["kvnc"], dma=True)
    P.op("sp", lambda e: e.dma_start(out=ctk.rearrange("p t c -> p (t c)"), in_=costok), writes=["ctk"], dma=True)
    P.op("sp", lambda e: e.dma_start(out=stk.rearrange("p t c -> p (t c)"), in_=sintok), writes=["stk"], dma=True)
    for h2 in range(2):
        ring.load(cosTd[:, h2 * 1024:(h2 + 1) * 1024], 128, [1024], cosT[:, h2 * 1024:(h2 + 1) * 1024], "cosT")
        ring.load(sinTd[:, h2 * 1024:(h2 + 1) * 1024], 128, [1024], sinT[:, h2 * 1024:(h2 + 1) * 1024], "sinT")
    w_in_v = w_in.rearrange("(kc p) n -> p kc n", p=128)
    for k4 in range(4):
        ring.load(w_in_v[:, 2 * k4:2 * k4 + 2, 384:672], 128, [2, 288], wkvr[:, 2 * k4:2 * k4 + 2, :], "wkvr",
                  gain=(B["gnc"], "gnc", 2 * k4))
    for k4 in range(4):
        ring.load(w_in_v[:, 2 * k4:2 * k4 + 2, 0:384], 128, [2, 384], wcq[:, 2 * k4:2 * k4 + 2, :], "wcq",
                  gain=(B["gnc"], "gnc", 2 * k4))
    P.op("pool", lambda e: e.memset(kpe_tok[:, :, 0:64], 0.0), writes=["kpe_z"])

    def psum_bf(t):
        return t[:].bitcast(BF16).rearrange("p (k t) -> p k t", t=128)

    ss, ms, nh_ = st

    def latent_stages(n_tiles, load, col_x, col_l, width, wmat, wkey, gvec, gvkey, nch, dstT_of, dst_of, dst_key, extra=None):
        def s0(T):
            j = T % 2
            load(T, j)
            emit_norm_stats(C, st, hx[j][:], "hx%d" % j, col_x + T, B["sq"])

        def s1(T):
            j = T % 2
            dT, dkey = dstT_of(T, j)
            emit_norm_apply_T(C, st, hx[j][:], "hx%d" % j, col_x + T, None, None, B["xnb"][j][:], "xnb%d" % j,
                              psum_bf(B["M"][j]), "M%d" % j, B["idb"], dT, dkey)

        def s2(T):
            j = T % 2
            dT, dkey = dstT_of(T, j)
            Sp = B["S"][j]
            tot = width + (32 if extra else 0)
            for kc in range(8):
                P.op("pe", lambda e, kc=kc: e.matmul(Sp[:, 0:tot], lhsT=dT[:, kc, :], rhs=wmat[:, kc, :],
                                                     start=(kc == 0), stop=(kc == 7)),
                     reads=[dkey, wkey], writes=["S%d" % j])
            emit_norm_stats(C, st, Sp[:, 0:width], "S%d" % j, col_l + T, B["sq"], width=width)

        def s3(T):
            j = T % 2
            Sp = B["S"][j]
            col = col_l + T
            Ob = B["Oe"] if j == 0 else B["Oo"]
            okey = "Oe" if j == 0 else "Oo"
            if extra:
                extra(T, Sp, "S%d" % j)
            emit_norm_apply_T(C, st, Sp[:, 0:width], "S%d" % j, col, gvec, gvkey, lat[j][:, 0:width], "pst%d" % j,
                              psum_bf(Ob), okey, B["idb"], dst_of(T), dst_key % T, nch=nch, copy_eng="dve")

        run_pipeline(n_tiles, [s0, s1, s2, s3])

    def load_own(T, j):
        P.op("sp", lambda e: e.dma_start(out=hx[j][:], in_=xown[T * 128:(T + 1) * 128, :]), writes=["hx%d" % j], dma=True)

    latent_stages(16, load_own, 64, 80, 384, wcq, "wcq", None, None, 3,
                  lambda T, j: (xnT[:, :, T * 128:(T + 1) * 128], "xnT%d" % T),
                  lambda T: cqT[:, :, T * 128:(T + 1) * 128], "cqT%d")

    def load_any(T, j):
        P.op("sp", lambda e: e.dma_start(out=hx[j][:], in_=xf_tile(T)), writes=["hx%d" % j], dma=True)

    def krope_copy(T, Sp, skey):
        P.op("act", lambda e: e.activation(out=krope[:, T, :], in_=Sp[:, 256:288], func=AF.Copy), reads=[skey], writes=["krope"])

    latent_stages(32, load_any, 0, 32, 256, wkvr, "wkvr", None, None, 2,
                  lambda T, j: (xkT[j][:], "hnT%d" % j),
                  lambda T: ckvT[:, :, T * 128:(T + 1) * 128], "ckvT%d", extra=krope_copy)

    x1v = krope[:, :, 0:16]
    x2v = krope[:, :, 16:32]
    rav = ra[:].rearrange("p (t c) -> p t c", c=16)
    rbv = rb[:].rearrange("p (t c) -> p t c", c=16)
    P.op("dve", lambda e: e.tensor_tensor(out=rav, in0=x1v, in1=ctk, op=ALU.mult), reads=["krope", "ctk"], writes=["ra"])
    P.op("dve", lambda e: e.tensor_tensor(out=rbv, in0=x2v, in1=stk, op=ALU.mult), reads=["krope", "stk"], writes=["rb"])
    P.op("dve", lambda e: e.tensor_tensor(out=kpe_tok[:, :, 64:80], in0=rav, in1=rbv, op=ALU.subtract),
         reads=["ra", "rb"], writes=["kpe1"])
    P.op("dve", lambda e: e.tensor_tensor(out=rav, in0=x1v, in1=stk, op=ALU.mult), reads=["krope", "stk", "kpe1"], writes=["ra"])
    P.op("dve", lambda e: e.tensor_tensor(out=rbv, in0=x2v, in1=ctk, op=ALU.mult), reads=["krope", "ctk", "kpe1"], writes=["rb"])
    P.op("dve", lambda e: e.tensor_tensor(out=kpe_tok[:, :, 80:96], in0=rav, in1=rbv, op=ALU.add),
         reads=["ra", "rb"], writes=["kpe2"])
    for T8 in range(4):
        s = T8 % 2
        tpv = B["S"][s][:, 0:512].bitcast(BF16).rearrange("p (k t) -> p k t", t=128)
        for a in range(8):
            T = T8 * 8 + a
            P.op("pe", lambda e, a=a, T=T, tpv=tpv: e.transpose(out=tpv[0:96, a, :], in_=kpe_tok[:, T, 0:96], identity=B["idb"][:]),
                 reads=["kpe1", "kpe2", "kpe_z", "idb"], writes=["S%d" % s])
        dst = kpeT[64:96, T8 * 1024:(T8 + 1) * 1024].rearrange("p (a t) -> p a t", t=128)
        if T8 % 2 == 0:
            P.op("dve", lambda e, dst=dst, tpv=tpv: e.tensor_copy(out=dst, in_=tpv[64:96, :, :]), reads=["S%d" % s],
                 writes=["kpeT%d" % T8])
        else:
            P.op("act", lambda e, dst=dst, tpv=tpv: e.activation(out=dst, in_=tpv[64:96, :, :], func=AF.Copy), reads=["S%d" % s],
                 writes=["kpeT%d" % T8])

    mctr = [0]
    sctr = [0]
    pctr = [0]

    def next_m():
        m = mctr[0] % 2
        mctr[0] += 1
        return m

    wqa_v = wqa.rearrange("(kc p) n -> p kc n", p=128)
    wqs_v = wqs.rearrange("(kc p) n -> p kc n", p=128)
    wkv_v = w_kvb.rearrange("(kc p) n -> p kc n", p=128)
    alias_ac = ["krope", "kpe1", "kpe2", "kpe_z", "ctk", "stk", "wkvr", "wcq"]
    scale = float(96.0 ** -0.5)
    for c in range(8):
        for hh in range(2):
            h = 2 * c + hh
            ring.load(wqa_v[:, :, h * 96:(h + 1) * 96], 128, [3, 96], wqA[hh][:], "wqA%d" % hh, gain=(qnc, "qnc", 0))
            ring.load(wqs_v[:, :, h * 96:(h + 1) * 96], 128, [3, 96], wqB[hh][:], "wqB%d" % hh, gain=(qnc, "qnc", 0))
            ring.load(wkv_v[:, :, h * 128:h * 128 + 64], 128, [2, 64], wkn[hh][:], "wkn%d" % hh, gain=(kvnc, "kvnc", 0))
            ring.load(wkv_v[:, :, h * 128 + 64:h * 128 + 128], 128, [2, 64], wvp[:, :, hh * 64:(hh + 1) * 64], "wvp%d" % hh,
                      gain=(kvnc, "kvnc", 0))
        ring.load(w_in_v[:, :, 672 + c * 128:672 + (c + 1) * 128], 128, [8, 128], wz[:], "wz", gain=(B["gnc"], "gnc", 0))
        first_alias = alias_ac if c == 0 else []

        for hh in range(2):
            for blk in range(4):
                mA = next_m()
                mB = next_m()
                MA = B["M"][mA]
                MB = B["M"][mB]
                rk = ["cqT%d" % (blk * 4 + a) for a in range(4)]
                for kc in range(3):
                    P.op("pe", lambda e, kc=kc, MA=MA, hh=hh, blk=blk: e.matmul(
                        MA[0:96, 0:512], lhsT=wqA[hh][:, kc, :], rhs=cqT[:, kc, blk * 512:(blk + 1) * 512],
                        start=(kc == 0), stop=(kc == 2)), reads=rk + ["wqA%d" % hh], writes=["M%d" % mA])
                for kc in range(3):
                    P.op("pe", lambda e, kc=kc, MB=MB, hh=hh, blk=blk: e.matmul(
                        MB[0:96, 0:512], lhsT=wqB[hh][:, kc, :], rhs=cqT[:, kc, blk * 512:(blk + 1) * 512],
                        start=(kc == 0), stop=(kc == 2)), reads=rk + ["wqB%d" % hh], writes=["M%d" % mB])
                bs = slice(blk * 512, (blk + 1) * 512)
                P.op("act", lambda e, MA=MA, hh=hh, bs=bs: e.activation(out=qT[hh][0:64, bs], in_=MA[0:64, 0:512], func=AF.Copy),
                     reads=["M%d" % mA], writes=["qT%dn" % hh] + first_alias)
                P.op("dve", lambda e, MA=MA, bs=bs: e.tensor_tensor(out=ra[64:96, :], in0=MA[64:96, 0:512], in1=cosT[64:96, bs],
                                                                    op=ALU.mult), reads=["M%d" % mA, "cosT"], writes=["ra"])
                P.op("dve", lambda e, MB=MB, bs=bs: e.tensor_tensor(out=rb[64:96, :], in0=MB[64:96, 0:512], in1=sinT[64:96, bs],
                                                                    op=ALU.mult), reads=["M%d" % mB, "sinT"], writes=["rb"])
                P.op("pool", lambda e, hh=hh, bs=bs: e.tensor_tensor(out=qT[hh][64:96, bs], in0=ra[64:96, :], in1=rb[64:96, :],
                                                                      op=ALU.add),
                     reads=["ra", "rb"], writes=["qT%dp" % hh] + first_alias)
                first_alias = []
            for blk in range(8):
                m = next_m()
                Mp = B["M"][m]
                rk = ["ckvT%d" % (blk * 4 + a) for a in range(4)]
                for kc in range(2):
                    P.op("pe", lambda e, kc=kc, Mp=Mp, hh=hh, blk=blk: e.matmul(
                        Mp[0:64, 0:512], lhsT=wkn[hh][:, kc, :], rhs=ckvT[:, kc, blk * 512:(blk + 1) * 512],
                        start=(kc == 0), stop=(kc == 1)), reads=rk + ["wkn%d" % hh], writes=["M%d" % m])
                bs = slice(blk * 512, (blk + 1) * 512)
                if blk % 2 == 0:
                    P.op("dve", lambda e, Mp=Mp, hh=hh, bs=bs: e.tensor_copy(out=kT[hh][0:64, bs], in_=Mp[0:64, 0:512]),
                         reads=["M%d" % m], writes=["kT%d_%d" % (hh, blk)])
                else:
                    P.op("act", lambda e, Mp=Mp, hh=hh, bs=bs: e.activation(out=kT[hh][0:64, bs], in_=Mp[0:64, 0:512], func=AF.Copy),
                         reads=["M%d" % m], writes=["kT%d_%d" % (hh, blk)])
            P.op("dve", lambda e, hh=hh: e.tensor_copy(out=kT[hh][64:96, :], in_=kpeT[64:96, :]),
                 reads=["kpeT%d" % a for a in range(4)], writes=["kT%dpe" % hh])
        if c == 0:
            P.op("pool", lambda e: e.memset(vP[:, :, 64:128], 1.0), writes=["vP_ones"])
        for t4 in range(8):
            m = next_m()
            Mp = B["M"][m]
            Mv = Mp[:, 0:512].rearrange("p (t c) -> p t c", c=128)
            for a in range(4):
                T = t4 * 4 + a
                for kc in range(2):
                    P.op("pe", lambda e, kc=kc, Mv=Mv, a=a, T=T: e.matmul(
                        Mv[:, a, :], lhsT=ckvT[:, kc, T * 128:(T + 1) * 128], rhs=wvp[:, kc, :],
                        start=(kc == 0), stop=(kc == 1)), reads=["ckvT%d" % T, "wvp0", "wvp1"], writes=["M%d" % m])
            vdst = bass.AP(ar, VP_OFF + t4 * 4 * 192, [[AR_COLS, 128], [192, 4], [128, 2], [1, 64]])
            P.op("dve", lambda e, Mv=Mv, vdst=vdst: e.tensor_copy(out=vdst, in_=Mv.rearrange("p t (h c) -> p t h c", h=2)),
                 reads=["M%d" % m], writes=["vPe%d" % t4, "vPo%d" % t4])
        for blk in range(4):
            m = next_m()
            Mp = B["M"][m]
            rk = ["xnT%d" % (blk * 4 + a) for a in range(4)]
            for kc in range(8):
                P.op("pe", lambda e, kc=kc, Mp=Mp, blk=blk: e.matmul(Mp[:, 0:512], lhsT=wz[:, kc, :],
                                                                     rhs=xnT[:, kc, blk * 512:(blk + 1) * 512],
                                                                     start=(kc == 0), stop=(kc == 7)),
                     reads=rk + ["wz"], writes=["M%d" % m])
            emit_silu_gate(C, Mp[:, 0:512], "M%d" % m, etmp, gz[:, blk * 512:(blk + 1) * 512], "gz%d" % blk)

        its = [(qb, hh, tp_) for qb in range(4) for hh in range(2) for tp_ in range(16)]
        SKEW = 2

        def qk_stage(n):
            qb, hh, tp_ = its[n]
            s_ = sctr[0] % 2
            sctr[0] += 1
            Sp = B["S"][s_]
            for u in range(2):
                kt = 2 * tp_ + u
                P.op("pe", lambda e, Sp=Sp, u=u, kt=kt, hh=hh, qb=qb: e.matmul(
                    Sp[:, u * 512:(u + 1) * 512], lhsT=kT[hh][0:96, kt * 128:(kt + 1) * 128],
                    rhs=qT[hh][0:96, qb * 512:(qb + 1) * 512], start=True, stop=True),
                    reads=["kT%d_%d" % (hh, kt // 4), "kT%dpe" % hh, "qT%dn" % hh, "qT%dp" % hh], writes=["S%d" % s_])
            pj = pctr[0] % 3
            pctr[0] += 1
            Pt = PT[pj]
            P.op("act", lambda e, Pt=Pt, Sp=Sp: e.activation(out=Pt[:], in_=Sp[:], func=AF.Exp, scale=scale),
                 reads=["S%d" % s_], writes=["PT%d" % pj])
            return pj

        pjs = {}
        pending = []
        for n in range(len(its) + SKEW):
            if n < len(its):
                pjs[n] = qk_stage(n)
            m_ = n - SKEW
            if m_ < 0:
                continue
            qb, hh, tp_ = its[m_]
            pj = pjs[m_]
            Pt = PT[pj]
            Oh = B["Oe"] if hh == 0 else B["Oo"]
            okey = "Oe" if hh == 0 else "Oo"
            for u in range(2):
                kt = 2 * tp_ + u
                P.op("pe", lambda e, Oh=Oh, kt=kt, hh=hh, Pt=Pt, u=u: e.matmul(
                    Oh[:, 0:512], lhsT=vP[:, kt, hh * 64:hh * 64 + 128], rhs=Pt[:, u * 512:(u + 1) * 512],
                    start=(kt == 0), stop=(kt == 31)),
                    reads=["vPe%d" % (kt // 4), "vPo%d" % (kt // 4), "vP_ones", "PT%d" % pj], writes=[okey])
            if hh == 1 and tp_ == 15:
                emit_finalize_now(C, B["Oe"], B["Oo"], B["Rsb"], B["Rsw"])
                pending.append((n + 3, qb))
            while pending and (pending[0][0] <= n or n == len(its) + SKEW - 1):
                _, qb_ = pending.pop(0)
                m = next_m()
                emit_finalize_later(C, B["Rsb"], B["Rsw"], B["t1"], B["pmf"], B["M"][m], "M%d" % m,
                                    gz[:, qb_ * 512:(qb_ + 1) * 512], "gz%d" % qb_, ogT[:, c, qb_ * 512:(qb_ + 1) * 512],
                                    "ogT%d_%d" % (c, qb_))

    alias = ["qT0n", "qT0p", "qT1n", "qT1p", "vP_ones", "kT0pe", "kT1pe"] + ["kT%d_%d" % (a, b_) for a in range(2) for b_ in range(8)] + \
            ["gz%d" % i for i in range(4)] + ["vPe%d" % i for i in range(8)] + ["vPo%d" % i for i in range(8)]
    wg_v = wg.rearrange("(kc p) n -> p kc n", p=128)
    for nb in range(8):
        ring.load(wg_v[:, :, nb * 128:(nb + 1) * 128], 128, [8, 128], wgb[:, :, nb * 128:(nb + 1) * 128], "wgb",
                  extra_writes=alias if nb == 0 else (), gain=(B["gpc"], "gpc", 0))
    wp_v = wp.rearrange("(kc p) n -> p kc n", p=128)
    for nb in range(2):
        ring.load(wp_v[:, :, nb * 512:(nb + 1) * 512], 128, [2, 512], wpb[:, :, nb * 512:(nb + 1) * 512], "wpb",
                  extra_writes=alias if nb == 0 else ())
    for c in range(8):
        ring.load(w_out[c * 128:(c + 1) * 128, :], 128, [1024], woa[:, c, :], "woa", extra_writes=alias if c == 0 else ())

    NHX = 8
    hx4 = [hx[0][:], hx[1][:]] + [xnT[:, k_, :].bitcast(F32) for k_ in range(NHX - 2)]
    xnT_keys = ["xnT%d" % a_ for a_ in range(16)]

    def pre(i):
        j = i % NHX
        P.op("sp", lambda e: e.dma_start(out=hx4[j], in_=xown[i * 128:(i + 1) * 128, :]),
             writes=["hx%d" % j] + (xnT_keys if 2 <= i < NHX else []), dma=True)
        for nh in range(2):
            Ob = B["Oe"] if nh == 0 else B["Oo"]
            okey = "Oe" if nh == 0 else "Oo"
            for c in range(8):
                P.op("pe", lambda e, c=c, Ob=Ob, nh=nh: e.matmul(Ob[:, 0:512], lhsT=ogT[:, c, i * 128:(i + 1) * 128],
                                                                 rhs=woa[:, c, nh * 512:(nh + 1) * 512],
                                                                 start=(c == 0), stop=(c == 7)),
                     reads=["ogT%d_%d" % (c, i // 4), "woa"], writes=[okey])
            xh = hx4[j][:, nh * 512:(nh + 1) * 512]
            P.op("dve", lambda e, xh=xh, Ob=Ob: e.tensor_tensor(out=xh, in0=xh, in1=Ob[:, 0:512], op=ALU.add),
                 reads=["hx%d" % j, okey], writes=["hx%d" % j])

    outs = emit_ple(C, lambda i: (hx4[i % NHX], "hx%d" % (i % NHX)), 16, pd, None, wgb, wpb, st, 0, B["xnb"], B["idb"], B["sq"],
                    B["hnT"], B["pst"], B["pbf"], B["pT"], sig, tmp, B["S"], B["M"], xo,
                    final_norm=(fgbc, "fgbc", 16), pre=pre)
    P.emit(final_wait_ops=outs)
    return nc


def _rope_np():
    inv = (1.0 / (np.float32(10000.0) ** (np.arange(0, 32, 2, dtype=np.float32) / np.float32(32)))).astype(np.float32)
    ang = (np.arange(4096, dtype=np.float32)[:, None] * inv[None, :]).astype(np.float32)
    return np.cos(ang).astype(np.float32), np.sin(ang).astype(np.float32)


def run_l1(x1, p, norm_g, mla_w_in, mla_q_norm, mla_w_qb, mla_kv_norm, mla_w_kvb, mla_w_out, ple_norm, ple_w_gate,
           ple_w_proj, final_norm):
    ident, perm = _consts()
    f = lambda a: np.ascontiguousarray(a, dtype=np.float32)
    bc = lambda v, n: np.ascontiguousarray(np.broadcast_to(np.asarray(v, np.float32), (128, n)))
    wqb = f(mla_w_qb[0]).reshape(384, 16, 96)
    wqs = np.zeros_like(wqb)
    wqs[:, :, 64:80] = wqb[:, :, 80:96]
    wqs[:, :, 80:96] = wqb[:, :, 64:80]
    cos, sin = _rope_np()
    costok = np.ascontiguousarray(cos.reshape(32, 128, 16).transpose(1, 0, 2).reshape(128, 512))
    sintok = np.ascontiguousarray(sin.reshape(32, 128, 16).transpose(1, 0, 2).reshape(128, 512))
    pm = lambda v, k: np.ascontiguousarray(np.asarray(v, np.float32).reshape(k, 128).T)
    shared = dict(gnc=pm(norm_g[1], 8), gpc=pm(ple_norm[1], 8), gf=bc(final_norm, 1024), qnc=pm(mla_q_norm[0], 3),
                  kvnc=pm(mla_kv_norm[0], 2), w_in=f(mla_w_in[0]), wqa=wqb.reshape(384, 1536), wqs=wqs.reshape(384, 1536),
                  w_kvb=f(mla_w_kvb[0]), w_out=f(mla_w_out[0]), wg=f(ple_w_gate[1]), wp=f(ple_w_proj[1]),
                  costok=costok, sintok=sintok, ident=ident, perm=perm)
    in_maps = []
    for core in range(8):
        b, hf = core // 2, core % 2
        cosT = np.zeros((128, 2048), np.float32)
        sinT = np.zeros((128, 2048), np.float32)
        cs = cos[hf * 2048:(hf + 1) * 2048].T
        sn = sin[hf * 2048:(hf + 1) * 2048].T
        cosT[64:80] = cs
        cosT[80:96] = cs
        sinT[64:80] = -sn
        sinT[80:96] = sn
        m = dict(shared)
        m.update(xf=f(x1[b]), xown=f(x1[b, hf * 2048:(hf + 1) * 2048]), pd=f(p[1, b, hf * 2048:(hf + 1) * 2048]),
                 cosT=cosT, sinT=sinT)
        in_maps.append(m)
    nc = _get_nc("l1", build_l1)
    res = run_bass_kernel_spmd(nc, in_maps, core_ids=list(range(8)))
    out = np.zeros((4, 4096, 1024), np.float32)
    for core in range(8):
        b, hf = core // 2, core % 2
        out[b, hf * 2048:(hf + 1) * 2048] = res.results[core]["xo"]
    return out


def _l0_inputs(x, p, norm_g, na_w_in, na_rpb, na_w_out, ple_norm, ple_w_gate, ple_w_proj, flip=False):
    x = np.asarray(x, np.float32)
    ident, perm = _consts()
    tabB = _na_tables(np.asarray(na_rpb[0], np.float32))
    tabBi = _na_tables(np.asarray(na_rpb[0], np.float32), interior=True)
    gn = np.ascontiguousarray(np.asarray(norm_g[0], np.float32).reshape(8, 128).T)
    gp = np.ascontiguousarray(np.asarray(ple_norm[0], np.float32).reshape(8, 128).T)
    shared = dict(gnc=gn, gpc=gp, w_in=np.ascontiguousarray(na_w_in[0], dtype=np.float32),
                  w_out=np.ascontiguousarray(na_w_out[0], dtype=np.float32),
                  wg=np.ascontiguousarray(ple_w_gate[0], dtype=np.float32),
                  wp=np.ascontiguousarray(ple_w_proj[0], dtype=np.float32), tabB=tabB, tabBi=tabBi, ident=ident, perm=perm)
    in_maps = []
    for core in range(8):
        b, hf = core // 2, core % 2
        if flip:
            hf = 1 - hf
        xk = np.zeros((40, 64, 1024), np.float32)
        xb = x[b].reshape(64, 64, 1024)
        for lr in range(40):
            R = 32 * hf - 4 + lr
            if 0 <= R <= 63:
                xk[lr] = xb[R]
        bq, aoh = _na_rowmask(hf)
        m = dict(shared)
        m.update(xk=xk.reshape(2560, 1024), pd=np.ascontiguousarray(p[0, b, hf * 2048:(hf + 1) * 2048], dtype=np.float32),
                 bq=bq, aoh=aoh)
        in_maps.append(m)
    return in_maps


def _l1_inputs(x1, p, norm_g, mla_w_in, mla_q_norm, mla_w_qb, mla_kv_norm, mla_w_kvb, mla_w_out, ple_norm, ple_w_gate,
               ple_w_proj, final_norm):
    ident, perm = _consts()
    f = lambda a: np.ascontiguousarray(a, dtype=np.float32)
    bc = lambda v, n: np.ascontiguousarray(np.broadcast_to(np.asarray(v, np.float32), (128, n)))
    wqb = f(mla_w_qb[0]).reshape(384, 16, 96)
    wqs = np.zeros_like(wqb)
    wqs[:, :, 64:80] = wqb[:, :, 80:96]
    wqs[:, :, 80:96] = wqb[:, :, 64:80]
    cos, sin = _rope_np()
    costok = np.ascontiguousarray(cos.reshape(32, 128, 16).transpose(1, 0, 2).reshape(128, 512))
    sintok = np.ascontiguousarray(sin.reshape(32, 128, 16).transpose(1, 0, 2).reshape(128, 512))
    pm = lambda v, k: np.ascontiguousarray(np.asarray(v, np.float32).reshape(k, 128).T)
    shared = dict(gnc=pm(norm_g[1], 8), gpc=pm(ple_norm[1], 8), gf=bc(final_norm, 1024), qnc=pm(mla_q_norm[0], 3),
                  kvnc=pm(mla_kv_norm[0], 2), w_in=f(mla_w_in[0]), wqa=wqb.reshape(384, 1536), wqs=wqs.reshape(384, 1536),
                  w_kvb=f(mla_w_kvb[0]), w_out=f(mla_w_out[0]), wg=f(ple_w_gate[1]), wp=f(ple_w_proj[1]),
                  costok=costok, sintok=sintok, ident=ident, perm=perm)
    in_maps = []
    for core in range(8):
        b, hf = core // 2, core % 2
        cosT = np.zeros((128, 2048), np.float32)
        sinT = np.zeros((128, 2048), np.float32)
        cs = cos[hf * 2048:(hf + 1) * 2048].T
        sn = sin[hf * 2048:(hf + 1) * 2048].T
        cosT[64:80] = cs
        cosT[80:96] = cs
        sinT[64:80] = -sn
        sinT[80:96] = sn
        m = dict(shared)
        if x1 is not None:
            m.update(xown=f(x1[b, hf * 2048:(hf + 1) * 2048]), xoth=f(x1[b, (1 - hf) * 2048:(2 - hf) * 2048]))
        order = np.concatenate([np.arange(hf * 2048, (hf + 1) * 2048), np.arange((1 - hf) * 2048, (2 - hf) * 2048)])
        m.update(pd=f(p[1, b, hf * 2048:(hf + 1) * 2048]), cosT=cosT, sinT=sinT,
                 costok=np.ascontiguousarray(cos[order].reshape(32, 128, 16).transpose(1, 0, 2).reshape(128, 512)),
                 sintok=np.ascontiguousarray(sin[order].reshape(32, 128, 16).transpose(1, 0, 2).reshape(128, 512)))
        in_maps.append(m)
    return in_maps


def build_fused(nc, es):
    x1loc = nc.dram_tensor("x1loc", [2048, 1024], F32).ap()
    x1oth = nc.dram_tensor("x1oth", [2048, 1024], F32).ap()
    shared = {}
    with contextlib.ExitStack() as es0:
        build_l0(nc, es0, pre="a_", xo=x1loc, sem_es=es, shared=shared)
    nc.all_engine_barrier()
    with contextlib.ExitStack() as es0:
        build_l0(nc, es0, pre="c_", xo=x1oth, sem_es=es, shared=shared)
    nc.all_engine_barrier()
    with contextlib.ExitStack() as es1:
        build_l1(nc, es1, pre="b_", xf=(x1loc, x1oth), sem_es=es)
    return nc


def kernel(x, p, norm_g, na_w_in, na_rpb, na_w_out, mla_w_in, mla_q_norm, mla_w_qb, mla_kv_norm, mla_w_kvb, mla_w_out,
           ple_norm, ple_w_gate, ple_w_proj, final_norm):
    x = np.asarray(x)
    p = np.asarray(p)
    m0 = _l0_inputs(x, p, norm_g, na_w_in, na_rpb, na_w_out, ple_norm, ple_w_gate, ple_w_proj)
    m0f = _l0_inputs(x, p, norm_g, na_w_in, na_rpb, na_w_out, ple_norm, ple_w_gate, ple_w_proj, flip=True)
    m1 = _l1_inputs(None, p, norm_g, mla_w_in, mla_q_norm, mla_w_qb, mla_kv_norm, mla_w_kvb, mla_w_out, ple_norm,
                    ple_w_gate, ple_w_proj, final_norm)
    in_maps = []
    for core in range(8):
        m = {"a_" + k: v for k, v in m0[core].items()}
        m.update({"c_" + k: m0f[core][k] for k in ("xk", "pd", "bq")})
        m.update({"b_" + k: v for k, v in m1[core].items()})
        in_maps.append(m)
    nc = _get_nc("fused", build_fused)
    res = run_bass_kernel_spmd(nc, in_maps, core_ids=list(range(8)))
    out = np.zeros((4, 4096, 1024), np.float32)
    for core in range(8):
        b, hf = core // 2, core % 2
        out[b, hf * 2048:(hf + 1) * 2048] = res.results[core]["xo"]
    return out
```

```python
import contextlib
import numpy as np
import concourse.bass as bass
import concourse.mybir as mybir
from concourse.bass_utils import run_bass_kernel_spmd

F32 = mybir.dt.float32
BF16 = mybir.dt.bfloat16
ALU = mybir.AluOpType
AF = mybir.ActivationFunctionType
AX = mybir.AxisListType

D = 1024
NEG = -30000.0
EPS = 1e-6


class Prog:
    STREAMS = ("pe", "act", "dve", "pool", "sp")

    def __init__(self, nc, n_dma_sems=12, sem_es=None, sem_prefix=""):
        self.nc = nc
        self.sem_es = sem_es
        self.sem_prefix = sem_prefix
        self.ops = []
        self.last_w = {}
        self.readers = {}
        self.n_dma_sems = n_dma_sems

    ALIAS = {
        "ra": ["Rsb_hi", "Rsb_lo", "sig1"], "Rsb_hi": ["ra", "sig1"], "Rsb_lo": ["ra", "sig1"],
        "sig1": ["ra", "Rsb_hi", "Rsb_lo"],
        "rb": ["Rsw", "tmp1", "Oc_lo", "Oc_hi"], "Rsw": ["rb", "tmp1", "Oc_lo", "Oc_hi"],
        "tmp1": ["rb", "Rsw", "Oc_lo", "Oc_hi"], "Oc_lo": ["rb", "Rsw", "tmp1"], "Oc_hi": ["rb", "Rsw", "tmp1"],
        "sig0": ["etmp"], "etmp": ["sig0"],
        "tmp0": ["t1_lo", "t1_hi"], "t1_lo": ["tmp0"], "t1_hi": ["tmp0"],
    }

    def _expand(self, keys):
        out = []
        for k in keys:
            out.append(k)
            out.extend(self.ALIAS.get(k, ()))
        return list(dict.fromkeys(out))

    def op(self, stream, fn, reads=(), writes=(), dma=False):
        reads = self._expand(reads)
        writes = self._expand(writes)
        i = len(self.ops)
        deps = {}

        def add(j, raw):
            if j is None:
                return
            deps[j] = deps.get(j, False) or raw

        for k in reads:
            add(self.last_w.get(k), True)
        for k in writes:
            add(self.last_w.get(k), True)
            for r in self.readers.get(k, ()):
                add(r, False)
        for k in reads:
            lst = self.readers.setdefault(k, [])
            if not dma:
                lst[:] = [r for r in lst if self.ops[r]["dma"] or self.ops[r]["stream"] != stream]
            lst.append(i)
        for k in writes:
            self.last_w[k] = i
            self.readers[k] = []
        self.ops.append(dict(i=i, stream=stream, fn=fn, dma=dma, deps=deps, sig=None))
        return i

    def emit(self, final_wait_ops=()):
        nc = self.nc
        ops = self.ops
        need = [[] for _ in ops]
        signaled = set()
        for o in ops:
            for j, raw in o["deps"].items():
                p = ops[j]
                if not p["dma"] and not o["dma"] and p["stream"] == o["stream"]:
                    if o["stream"] == "pe" or not raw:
                        continue
                need[o["i"]].append(j)
                signaled.add(j)
        for j in final_wait_ops:
            signaled.add(j)
        cnt = {s: 0 for s in self.STREAMS}
        dcnt = {s: 0 for s in self.STREAMS}
        for o in ops:
            if o["dma"]:
                k = dcnt[o["stream"]]
                dcnt[o["stream"]] += 1
                o["sig"] = ("d_%s_%d" % (o["stream"], k % self.n_dma_sems), 16 * (k // self.n_dma_sems + 1))
                o["dk"] = k
            elif o["i"] in signaled:
                cnt[o["stream"]] += 1
                o["sig"] = ("c_" + o["stream"], cnt[o["stream"]])
        dma_by_stream = {s: [o for o in ops if o["dma"] and o["stream"] == s] for s in self.STREAMS}
        sem_names = set()
        for o in ops:
            if o["sig"] is not None:
                sem_names.add(o["sig"][0])
        sem_names = sorted(sem_names)
        with contextlib.ExitStack() as es:
            ses = self.sem_es if self.sem_es is not None else es
            sems = {n: ses.enter_context(nc.semaphore(self.sem_prefix + n)) for n in sem_names}
            block = es.enter_context(nc.Block())
            seen = {s: {} for s in self.STREAMS}

            def run_stream(stream, eng):
                sw = seen[stream]
                for o in ops:
                    if o["stream"] != stream:
                        continue
                    waits = {}
                    for j in need[o["i"]]:
                        sn, sv = ops[j]["sig"]
                        waits[sn] = max(waits.get(sn, 0), sv)
                    if o["dma"] and o["dk"] >= self.n_dma_sems:
                        sn, sv = o["sig"]
                        waits[sn] = max(waits.get(sn, 0), sv - 16)
                    for sn, sv in sorted(waits.items()):
                        if sw.get(sn, 0) >= sv:
                            continue
                        sw[sn] = sv
                        eng.wait_ge(sems[sn], sv)
                    ins = o["fn"](eng)
                    if o["sig"] is not None:
                        ins.then_inc(sems[o["sig"][0]], 16 if o["dma"] else 1)
                if stream == "sp":
                    for j in final_wait_ops:
                        sn, sv = ops[j]["sig"]
                        if sw.get(sn, 0) < sv:
                            sw[sn] = sv
                            eng.wait_ge(sems[sn], sv)

            @block.tensor
            def _(e):
                run_stream("pe", e)

            @block.scalar
            def _(e):
                run_stream("act", e)

            @block.vector
            def _(e):
                run_stream("dve", e)

            @block.gpsimd
            def _(e):
                run_stream("pool", e)

            @block.sync
            def _(e):
                run_stream("sp", e)


class Ctx:
    def __init__(self, nc, es, pre="", sem_es=None):
        self.nc = nc
        self.es = es
        self.pre = pre
        self.P = Prog(nc, sem_es=sem_es, sem_prefix=pre)
        self._n = 0

    def sb(self, name, shape, dt):
        return self.es.enter_context(self.nc.sbuf_tensor(self.pre + name, list(shape), dt))

    def ps(self, name, shape, dt):
        return self.es.enter_context(self.nc.psum_tensor(self.pre + name, list(shape), dt))

    def din(self, name, shape, dt=F32):
        return self.nc.dram_tensor(self.pre + name, list(shape), dt, kind="ExternalInput").ap()

    def dout(self, name, shape, dt=F32):
        return self.nc.dram_tensor(name, list(shape), dt, kind="ExternalOutput").ap()


class Ring:
    def __init__(self, C, n=2, cols=1024):
        self.C = C
        self.slots = [C.sb("wst%d" % i, [128, cols], F32) for i in range(n)]
        self.k = 0

    def load(self, src, parts, shape_free, dst, dst_key, conv="pool", extra_writes=(), gain=None):
        P = self.C.P
        i = self.k % len(self.slots)
        self.k += 1
        n = int(np.prod(shape_free))
        st = self.slots[i][0:parts, 0:n]
        if len(shape_free) == 2:
            st = st.rearrange("p (a b) -> p a b", b=shape_free[1])
        key = "wst%d" % i
        P.op("sp", lambda e: e.dma_start(out=st, in_=src), writes=[key], dma=True)
        wr = [dst_key] + list(extra_writes)
        if gain is not None:
            gt, gkey, k0 = gain
            gap = bass.AP(gt, k0, [[gt[:].shape[1], 128], [1, shape_free[0]], [0, shape_free[1]]])
            P.op("pool", lambda e: e.tensor_tensor(out=dst, in0=st, in1=gap, op=ALU.mult), reads=[key, gkey], writes=wr)
        elif conv == "pool":
            P.op("pool", lambda e: e.tensor_copy(out=dst, in_=st), reads=[key], writes=wr)
        elif conv == "exp":
            P.op("act", lambda e: e.activation(out=dst, in_=st, func=AF.Exp), reads=[key], writes=wr)


def run_pipeline(n, stages):
    k = len(stages)
    for step in range(n + k - 1):
        for d in range(k):
            t = step - d
            if 0 <= t < n:
                stages[d](t)


def emit_norm_stats(C, st, xt_ap, xt_key, col, sq, width=D):
    P = C.P
    ss, ms, nh = st
    P.op("act", lambda e: e.activation(out=sq[:, 0:width], in_=xt_ap, func=AF.Square, accum_out=ss[:, col:col + 1]),
         reads=[xt_key], writes=["sq", "ss%d" % col])
    P.op("dve", lambda e: e.tensor_scalar(out=ms[:, col:col + 1], in0=ss[:, col:col + 1], scalar1=1.0 / width, scalar2=EPS,
                                         op0=ALU.mult, op1=ALU.add), reads=["ss%d" % col], writes=["ms%d" % col])
    P.op("pool", lambda e: e.tensor_tensor(out=ms[:, col:col + 1], in0=ms[:, col:col + 1], in1=nh[:], op=ALU.pow),
         reads=["ms%d" % col, "nh"], writes=["rs%d" % col])


def emit_norm_apply_T(C, st, xt_ap, xt_key, col, gbc, gkey, xnb_ap, xnb_key, tp_ps, tp_key, idb, dstT, dstT_key, nch=8,
                      copy_eng="act"):
    P = C.P
    ss, ms, nh = st
    P.op("act", lambda e: e.activation(out=xnb_ap, in_=xt_ap, func=AF.Copy, scale=ms[:, col:col + 1]),
         reads=[xt_key, "rs%d" % col], writes=[xnb_key])
    for kc in range(nch):
        P.op("pe", lambda e, kc=kc: e.transpose(out=tp_ps[:, kc, :], in_=xnb_ap[:, kc * 128:(kc + 1) * 128], identity=idb[:]),
             reads=[xnb_key, "idb"], writes=[tp_key])
    if copy_eng == "act":
        P.op("act", lambda e: e.activation(out=dstT, in_=tp_ps[:, 0:nch, :], func=AF.Copy), reads=[tp_key], writes=[dstT_key])
    else:
        P.op("dve", lambda e: e.tensor_copy(out=dstT, in_=tp_ps[:, 0:nch, :]), reads=[tp_key], writes=[dstT_key])


def emit_norm_T(C, st, xt_ap, xt_key, col, gbc, gkey, xnb, xnb_key, tp_ps, tp_key, idb, dstT, dstT_key, sq):
    emit_norm_stats(C, st, xt_ap, xt_key, col, sq)
    emit_norm_apply_T(C, st, xt_ap, xt_key, col, gbc, gkey, xnb[:], xnb_key, tp_ps, tp_key, idb, dstT, dstT_key)


def emit_finalize_now(C, Oe, Oo, Rsb, Oc):
    P = C.P
    P.op("dve", lambda e: e.reciprocal(out=Rsb[64:128, :], in_=Oe[64:128, :]), reads=["Oe"], writes=["Rsb_hi"])
    P.op("act", lambda e: e.activation(out=Oc[0:64, :], in_=Oe[0:64, :], func=AF.Copy), reads=["Oe"], writes=["Oc_lo"])
    P.op("dve", lambda e: e.reciprocal(out=Rsb[0:64, :], in_=Oo[0:64, :]), reads=["Oo"], writes=["Rsb_lo"])
    P.op("act", lambda e: e.activation(out=Oc[64:128, :], in_=Oo[64:128, :], func=AF.Copy), reads=["Oo"], writes=["Oc_hi"])


def emit_finalize_later(C, Rsb, Oc, t1, pmf, Mps, Mkey, gz_ap, gz_key, og_ap, og_key):
    P = C.P
    P.op("pe", lambda e: e.matmul(Mps[:, 0:512], lhsT=pmf[:], rhs=Rsb[:], start=True, stop=True),
         reads=["Rsb_hi", "Rsb_lo", "pmf"], writes=[Mkey])
    P.op("dve", lambda e: e.tensor_tensor(out=t1[:], in0=Oc[:], in1=Mps[:, 0:512], op=ALU.mult),
         reads=["Oc_lo", "Oc_hi", Mkey], writes=["t1_lo", "t1_hi"])
    P.op("pool", lambda e: e.tensor_tensor(out=og_ap, in0=t1[:], in1=gz_ap, op=ALU.mult),
         reads=["t1_lo", "t1_hi", gz_key], writes=[og_key])


def emit_silu_gate(C, zps, zkey, etmp, gz_ap, gz_key):
    P = C.P
    P.op("act", lambda e: e.activation(out=etmp[:], in_=zps, func=AF.Exp, scale=-1.0), reads=[zkey], writes=["etmp"])
    P.op("dve", lambda e: e.tensor_scalar(out=etmp[:], in0=etmp[:], scalar1=1.0, scalar2=None, op0=ALU.add),
         reads=["etmp"], writes=["etmp"])
    P.op("dve", lambda e: e.reciprocal(out=etmp[:], in_=etmp[:]), reads=["etmp"], writes=["etmp"])
    P.op("dve", lambda e: e.tensor_tensor(out=gz_ap, in0=zps, in1=etmp[:], op=ALU.mult),
         reads=[zkey, "etmp"], writes=[gz_key])


def emit_ple(C, get_tile, n_tiles, pd, pgbc, wgb, wpb, st, col0, xnb, idb, sq, hnT, pst, pbf, pT, sig, tmp,
             Sps, Mps, out_dram, final_norm=None, pre=None):
    P = C.P
    outs = []
    ss, ms, nh_ = st

    def sA(i):
        pre(i)

    def sB(i):
        xt, xkey = get_tile(i)
        emit_norm_stats(C, st, xt, xkey, col0 + i, sq)

    def sC(i):
        j = i % 2
        xt, xkey = get_tile(i)
        col = col0 + i
        P.op("act", lambda e: e.activation(out=xnb[j][:], in_=xt, func=AF.Copy, scale=ms[:, col:col + 1]),
             reads=[xkey, "rs%d" % col], writes=["xnb%d" % j])
        P.op("sp", lambda e: e.dma_start(out=pst[j][:], in_=pd[i * 128:(i + 1) * 128, :]), writes=["pst%d" % j], dma=True)
        P.op("pool", lambda e: e.tensor_copy(out=pbf[j][:], in_=pst[j][:]), reads=["pst%d" % j], writes=["pbf%d" % j])

    def sD(i):
        j = i % 2
        tp = Mps[0][:].bitcast(BF16).rearrange("p (k t) -> p k t", t=128)
        for kc in range(8):
            P.op("pe", lambda e, kc=kc: e.transpose(out=tp[:, kc, :], in_=xnb[j][:, kc * 128:(kc + 1) * 128], identity=idb[:]),
                 reads=["xnb%d" % j, "idb"], writes=["M0"])
        P.op("act", lambda e: e.activation(out=hnT[j][:], in_=tp, func=AF.Copy), reads=["M0"], writes=["hnT%d" % j])
        tp2 = Mps[1][:].bitcast(BF16).rearrange("p (k t) -> p k t", t=128)
        for kc in range(2):
            P.op("pe", lambda e, kc=kc: e.transpose(out=tp2[:, kc, :], in_=pbf[j][:, kc * 128:(kc + 1) * 128], identity=idb[:]),
                 reads=["pbf%d" % j, "idb"], writes=["M1"])
        P.op("dve", lambda e: e.tensor_copy(out=pT[j][:], in_=tp2[:, 0:2, :]), reads=["M1"], writes=["pT%d" % j])

    def sE(i):
        j = i % 2
        xt, xkey = get_tile(i)
        for nh in range(2):
            gps = Sps[0][:, nh * 512:(nh + 1) * 512]
            pps = Sps[1][:, nh * 512:(nh + 1) * 512]
            for kc in range(8):
                P.op("pe", lambda e, kc=kc, nh=nh, gps=gps: e.matmul(
                    gps, lhsT=hnT[j][:, kc, :], rhs=wgb[:, kc, nh * 512:(nh + 1) * 512], start=(kc == 0), stop=(kc == 7)),
                    reads=["hnT%d" % j, "wgb"], writes=["S0_%d" % nh])
            for kc in range(2):
                P.op("pe", lambda e, kc=kc, nh=nh, pps=pps: e.matmul(
                    pps, lhsT=pT[j][:, kc, :], rhs=wpb[:, kc, nh * 512:(nh + 1) * 512], start=(kc == 0), stop=(kc == 1)),
                    reads=["pT%d" % j, "wpb"], writes=["S1_%d" % nh])
            sg = sig[nh]
            P.op("act", lambda e, gps=gps, sg=sg: e.activation(out=sg[:], in_=gps, func=AF.Exp, scale=-1.0),
                 reads=["S0_%d" % nh], writes=["sig%d" % nh])
            P.op("dve", lambda e, sg=sg: e.tensor_scalar(out=sg[:], in0=sg[:], scalar1=1.0, scalar2=None, op0=ALU.add),
                 reads=["sig%d" % nh], writes=["sig%d" % nh])
            P.op("dve", lambda e, sg=sg: e.reciprocal(out=sg[:], in_=sg[:]), reads=["sig%d" % nh], writes=["sig%d" % nh])
            tm = tmp[nh]
            P.op("dve", lambda e, pps=pps, sg=sg, tm=tm: e.tensor_tensor(out=tm[:], in0=pps, in1=sg[:], op=ALU.mult),
                 reads=["S1_%d" % nh, "sig%d" % nh], writes=["tmp%d" % nh])
            xh = xt[:, nh * 512:(nh + 1) * 512]
            P.op("pool", lambda e, xh=xh, tm=tm: e.tensor_tensor(out=xh, in0=xh, in1=tm[:], op=ALU.add),
                 reads=[xkey, "tmp%d" % nh], writes=[xkey])

    def sF(i):
        xt, xkey = get_tile(i)
        emit_norm_stats(C, st, xt, xkey, final_norm[2] + i, sq)

    def sG(i):
        xt, xkey = get_tile(i)
        if final_norm is not None:
            fgbc, fgkey, fcol0 = final_norm
            c2 = fcol0 + i
            P.op("act", lambda e: e.activation(out=xt, in_=xt, func=AF.Copy, scale=ms[:, c2:c2 + 1]),
                 reads=[xkey, "rs%d" % c2], writes=[xkey])
            P.op("dve", lambda e: e.tensor_tensor(out=xt, in0=xt, in1=fgbc[:], op=ALU.mult),
                 reads=[xkey, fgkey], writes=[xkey])
        outs.append(P.op("sp", lambda e: e.dma_start(out=out_dram[i * 128:(i + 1) * 128, :], in_=xt), reads=[xkey], dma=True))

    stages = ([sA] if pre is not None else []) + [sB, sC, sD, sE] + ([sF] if final_norm is not None else []) + [sG]
    run_pipeline(n_tiles, stages)
    return outs


def alloc_common(C, layer1=False):
    B = {}
    if not layer1:
        B["xres"] = C.sb("xres", [128, 16, 1024], F32)
        B["og"] = [[C.sb("og%d_%d" % (i, g_), [128, 512], BF16) for g_ in range(4)] for i in range(2)]
        B["wo"] = [C.sb("wo%d" % i, [128, 1024], BF16) for i in range(2)]
    B["ss"] = C.sb("ss", [128, 96], F32)
    B["ms"] = C.sb("ms", [128, 96], F32)
    B["nh"] = C.sb("nh", [128, 1], F32)
    B["sq"] = C.sb("sq", [128, 1024], BF16)
    B["xnb"] = [C.sb("xnb%d" % i, [128, 1024], BF16) for i in range(2)]
    B["idb"] = C.sb("idb", [128, 128], BF16)
    B["pmf"] = C.sb("pmf", [128, 128], F32)
    B["gnc"] = C.sb("gnc_sb", [128, 8], F32)
    B["gpc"] = C.sb("gpc_sb", [128, 8], F32)
    B["Rsb"] = C.sb("Rsb", [128, 512], F32)
    B["Rsw"] = C.sb("Rsw", [128, 512], F32)
    B["t1"] = C.sb("t1", [128, 512], F32)
    B["hnT"] = [C.sb("hnT%d" % i, [128, 8, 128], BF16) for i in range(2)]
    B["pst"] = [C.sb("pst%d" % i, [128, 256], F32) for i in range(2)]
    B["pbf"] = [C.sb("pbf%d" % i, [128, 256], BF16) for i in range(2)]
    B["pT"] = [C.sb("pT%d" % i, [128, 2, 128], BF16) for i in range(2)]
    B["ring"] = Ring(C, n=2, cols=1024)
    B["S"] = [C.ps("S%d" % i, [128, 1024], F32) for i in range(2)]
    B["Oe"] = C.ps("Oe", [128, 512], F32)
    B["Oo"] = C.ps("Oo", [128, 512], F32)
    B["M"] = [C.ps("M%d" % i, [128, 512], F32) for i in range(2)]
    return B


def emit_consts(C, B, ident, perm):
    P = C.P
    B["ring"].load(ident, 128, [128], B["idb"][:], "idb")
    P.op("sp", lambda e: e.dma_start(out=B["pmf"][:], in_=perm), writes=["pmf"], dma=True)
    P.op("pool", lambda e: e.memset(B["nh"][:], -0.5), writes=["nh"])


def emit_outproj(C, B, ogs, og_keys, wos, wo_keys, tiles, mctr):
    P = C.P
    xres = B["xres"]
    for tt, i in enumerate(tiles):
        for nh in range(2):
            m = mctr[0] % 2
            mctr[0] += 1
            Mp = B["M"][m]
            for k_ in range(2):
                P.op("pe", lambda e, tt=tt, nh=nh, Mp=Mp, k_=k_: e.matmul(
                    Mp[:, 0:512], lhsT=ogs[k_][:, tt * 128:(tt + 1) * 128], rhs=wos[k_][:, nh * 512:(nh + 1) * 512],
                    start=(k_ == 0), stop=(k_ == 1)), reads=[og_keys[k_], wo_keys[k_]], writes=["M%d" % m])
            xh = xres[:, i, nh * 512:(nh + 1) * 512]
            P.op("dve", lambda e, xh=xh, Mp=Mp: e.tensor_tensor(out=xh, in0=xh, in1=Mp[:, 0:512], op=ALU.add),
                 reads=["xr%d" % i, "M%d" % m], writes=["xr%d" % i])


def build_l0(nc, es, pre="", xo=None, sem_es=None, shared=None):
    C = Ctx(nc, es, pre, sem_es)
    P = C.P
    shared = {} if shared is None else shared

    def sdin(name, shape):
        if name not in shared:
            shared[name] = C.din(name, shape)
        return shared[name]

    xk = C.din("xk", [2560, 1024])
    pd = C.din("pd", [2048, 256])
    bq = C.din("bq", [128, 2048])
    gn = sdin("gnc", [128, 8])
    gp = sdin("gpc", [128, 8])
    w_in = sdin("w_in", [1024, 4096])
    w_out = sdin("w_out", [1024, 1024])
    wg = sdin("wg", [1024, 1024])
    wp = sdin("wp", [256, 1024])
    tabB = sdin("tabB", [8, 128, 2816])
    aoh = sdin("aoh", [128, 1024])
    ident = sdin("ident", [128, 128])
    perm = sdin("perm", [128, 128])
    if xo is None:
        xo = C.dout("xo", [2048, 1024])

    B = alloc_common(C)
    ring = B["ring"]
    xres = B["xres"]
    xnT = C.sb("xnT", [128, 8, 2560], BF16)
    hx = [C.sb("hx%d" % i, [128, 1024], F32) for i in range(2)]
    AR_COLS, VP_OFF = 12544, 6656
    ar = C.sb("ar", [128, 12544], BF16)
    qTe = ar[:, 0:2048]
    qTo = ar[:, 2048:4096]
    kT = ar[:, 4096:6656]
    vP = ar[:, 6656:10496].rearrange("p (t c) -> p t c", c=192)
    gz = ar[:, 10496:12544]
    wgb = ar[:, 0:8192].rearrange("p (k n) -> p k n", n=1024)
    wpb = ar[:, 8192:10240].rearrange("p (k n) -> p k n", n=1024)
    wq = C.sb("wq", [128, 8, 128], BF16)
    wk = C.sb("wk", [128, 8, 128], BF16)
    wv = C.sb("wv", [128, 8, 128], BF16)
    wz = C.sb("wz", [128, 8, 128], BF16)
    tabE = C.sb("tabE", [128, 2816], BF16)
    bqb = C.sb("bqb", [128, 2048], BF16)
    aohb = C.sb("aohb", [128, 1024], BF16)
    PT = [C.sb("PT%d" % i, [128, 1024], BF16) for i in range(3)]
    etmp = C.sb("etmp", [128, 512], F32)
    sig = [etmp, B["Rsb"]]
    tmp = [B["t1"], B["Rsw"]]
    st = (B["ss"], B["ms"], B["nh"])

    emit_consts(C, B, ident, perm)
    P.op("sp", lambda e: e.dma_start(out=B["gnc"][:], in_=gn), writes=["gnc"], dma=True)
    P.op("sp", lambda e: e.dma_start(out=B["gpc"][:], in_=gp), writes=["gpc"], dma=True)
    ring.load(aoh, 128, [1024], aohb[:], "aohb")
    for h2 in range(2):
        ring.load(bq[:, h2 * 1024:(h2 + 1) * 1024], 128, [1024], bqb[:, h2 * 1024:(h2 + 1) * 1024], "bqb")
    P.op("pool", lambda e: e.memset(qTe[64:128, :], 0.0), writes=["qTe_z"])
    P.op("pool", lambda e: e.memset(qTo[0:64, :], 0.0), writes=["qTo_z"])
    P.op("pool", lambda e: e.memset(vP[:, :, 64:128], 1.0), writes=["vP_ones"])

    def p1_tile(bt):
        if 2 <= bt < 18:
            return xres[:, bt - 2, :], "xr%d" % (bt - 2)
        return hx[bt % 2][:], "hx%d" % (bt % 2)

    def p1_s0(bt):
        xt, xkey = p1_tile(bt)
        P.op("sp", lambda e: e.dma_start(out=xt, in_=xk[bt * 128:(bt + 1) * 128, :]), writes=[xkey], dma=True)
        emit_norm_stats(C, st, xt, xkey, bt, B["sq"])

    def p1_s1(bt):
        xt, xkey = p1_tile(bt)
        j = bt % 2
        tp = B["M"][j][:].bitcast(BF16).rearrange("p (k t) -> p k t", t=128)
        emit_norm_apply_T(C, st, xt, xkey, bt, None, None, B["xnb"][j][:], "xnb%d" % j, tp, "M%d" % j, B["idb"],
                          xnT[:, :, bt * 128:(bt + 1) * 128], "xnT%d" % bt)

    run_pipeline(20, [p1_s0, p1_s1])

    w_in_v = w_in.rearrange("(kc p) n -> p kc n", p=128)
    mctr = [0, 0]
    pending = []
    sctr = [0]
    pctr = [0]
    octr = [0]

    def next_m():
        m = mctr[0] % 2
        mctr[0] += 1
        return m

    def load_inproj(c_):
        for wt, wkey, off in ((wq, "wq", 0), (wk, "wk", 1024), (wv, "wv", 2048), (wz, "wz", 3072)):
            ring.load(w_in_v[:, :, off + c_ * 128: off + (c_ + 1) * 128], 128, [8, 128], wt[:], wkey,
                      gain=(B["gnc"], "gnc", 0))

    load_inproj(0)
    for c in range(8):
        wo = B["wo"][c % 2]
        for ch, (lo, hi) in enumerate(((0, 1024), (1024, 2048), (2048, 2816))):
            ring.load(tabB[c, :, lo:hi], 128, [hi - lo], tabE[:, lo:hi], "tabE%d" % ch, conv="exp")

        def flush_one():
            _, g_, c_ = pending.pop(0)
            m = next_m()
            og = B["og"][c_ % 2][g_]
            emit_finalize_later(C, B["Rsb"], B["Rsw"], B["t1"], B["pmf"], B["M"][m], "M%d" % m,
                                gz[:, g_ * 512:(g_ + 1) * 512], "gz%d" % g_, og[:], "og%d_%d" % (c_ % 2, g_))
            if c_ % 2 == 1:
                emit_outproj(C, B, [B["og"][0][g_], B["og"][1][g_]], ["og0_%d" % g_, "og1_%d" % g_],
                             B["wo"], ["wo0", "wo1"], [4 * g_ + a_ for a_ in range(4)], mctr)

        for blk in range(4):
            m = next_m()
            Mp = B["M"][m]
            t0 = 256 + blk * 512
            rk = ["xnT%d" % (t0 // 128 + a) for a in range(4)]
            for kc in range(8):
                P.op("pe", lambda e, kc=kc, Mp=Mp, t0=t0: e.matmul(Mp[:, 0:512], lhsT=wq[:, kc, :], rhs=xnT[:, kc, t0:t0 + 512],
                                                                   start=(kc == 0), stop=(kc == 7)),
                     reads=rk + ["wq"], writes=["M%d" % m])
            P.op("act", lambda e, Mp=Mp, blk=blk: e.activation(out=qTe[0:64, blk * 512:(blk + 1) * 512], in_=Mp[0:64, 0:512],
                                                                func=AF.Copy), reads=["M%d" % m], writes=["qTe"])
            P.op("act", lambda e, Mp=Mp, blk=blk: e.activation(out=qTo[64:128, blk * 512:(blk + 1) * 512], in_=Mp[64:128, 0:512],
                                                                func=AF.Copy), reads=["M%d" % m], writes=["qTo"])
        for blk in range(5):
            m = next_m()
            Mp = B["M"][m]
            t0 = blk * 512
            rk = ["xnT%d" % (t0 // 128 + a) for a in range(4)]
            for kc in range(8):
                P.op("pe", lambda e, kc=kc, Mp=Mp, t0=t0: e.matmul(Mp[:, 0:512], lhsT=wk[:, kc, :], rhs=xnT[:, kc, t0:t0 + 512],
                                                                   start=(kc == 0), stop=(kc == 7)),
                     reads=rk + ["wk"], writes=["M%d" % m])
            if blk % 2 == 0:
                P.op("dve", lambda e, Mp=Mp, t0=t0: e.tensor_copy(out=kT[:, t0:t0 + 512], in_=Mp[:, 0:512]),
                     reads=["M%d" % m], writes=["kT%d" % blk])
            else:
                P.op("act", lambda e, Mp=Mp, t0=t0: e.activation(out=kT[:, t0:t0 + 512], in_=Mp[:, 0:512], func=AF.Copy),
                     reads=["M%d" % m], writes=["kT%d" % blk])
        while pending:
            flush_one()
        ring.load(w_out[c * 128:(c + 1) * 128, :], 128, [1024], wo[:], "wo%d" % (c % 2))
        for blk in range(4):
            m = next_m()
            Mp = B["M"][m]
            t0 = 256 + blk * 512
            rk = ["xnT%d" % (t0 // 128 + a) for a in range(4)]
            for kc in range(8):
                P.op("pe", lambda e, kc=kc, Mp=Mp, t0=t0: e.matmul(Mp[:, 0:512], lhsT=wz[:, kc, :], rhs=xnT[:, kc, t0:t0 + 512],
                                                                   start=(kc == 0), stop=(kc == 7)),
                     reads=rk + ["wz"], writes=["M%d" % m])
            emit_silu_gate(C, Mp[:, 0:512], "M%d" % m, etmp, gz[:, blk * 512:(blk + 1) * 512], "gz%d" % blk)
        for t4 in range(5):
            m = next_m()
            Mp = B["M"][m]
            Mv = Mp[:, 0:512].rearrange("p (t c) -> p t c", c=128)
            for a in range(4):
                bt = t4 * 4 + a
                for kc in range(8):
                    P.op("pe", lambda e, kc=kc, Mv=Mv, a=a, bt=bt: e.matmul(
                        Mv[:, a, :], lhsT=xnT[:, kc, bt * 128:(bt + 1) * 128], rhs=wv[:, kc, :],
                        start=(kc == 0), stop=(kc == 7)), reads=["xnT%d" % bt, "wv"], writes=["M%d" % m])
            vdst = bass.AP(ar, VP_OFF + t4 * 4 * 192, [[AR_COLS, 128], [192, 4], [128, 2], [1, 64]])
            P.op("dve", lambda e, Mv=Mv, vdst=vdst: e.tensor_copy(out=vdst, in_=Mv.rearrange("p t (h c) -> p t h c", h=2)),
                 reads=["M%d" % m], writes=["vPe%d" % t4, "vPo%d" % t4])

        if c < 7:
            load_inproj(c + 1)
        its = [(g, hh, tp_) for g in range(4) for hh in range(2) for tp_ in range(4)]
        SKEW = 2

        def qk_stage(n):
            g, hh, tp_ = its[n]
            qm = qTe if hh == 0 else qTo
            qkeys = ["qTe", "qTe_z"] if hh == 0 else ["qTo", "qTo_z"]
            s_ = sctr[0] % 2
            sctr[0] += 1
            Sp = B["S"][s_]
            for u in range(2):
                t = 2 * tp_ + (1 - u)
                bt = 4 * g + t
                P.op("pe", lambda e, Sp=Sp, u=u, bt=bt, qm=qm, g=g: e.matmul(
                    Sp[:, u * 512:(u + 1) * 512], lhsT=kT[:, bt * 128:(bt + 1) * 128], rhs=qm[:, g * 512:(g + 1) * 512],
                    start=True, stop=False), reads=["kT%d" % (bt // 4)] + qkeys, writes=["S%d" % s_])
                P.op("pe", lambda e, Sp=Sp, u=u, t=t, g=g: e.matmul(
                    Sp[:, u * 512:(u + 1) * 512], lhsT=aohb[:, t * 128:(t + 1) * 128], rhs=bqb[:, g * 512:(g + 1) * 512],
                    start=False, stop=True), reads=["aohb", "bqb"], writes=["S%d" % s_])
            pj = pctr[0] % 3
            pctr[0] += 1
            Pt = PT[pj]
            P.op("act", lambda e, Pt=Pt, Sp=Sp: e.activation(out=Pt[:], in_=Sp[:], func=AF.Exp, scale=0.125),
                 reads=["S%d" % s_], writes=["PT%d" % pj])
            tab_ap = bass.AP(tabE, hh * 1408 + (12 - 4 * tp_) * 64, [[2816, 128], [128, 2], [64, 8], [1, 64]])
            Pv = Pt[:].rearrange("p (u q c) -> p u q c", u=2, q=8)
            P.op("dve", lambda e, Pv=Pv, tab_ap=tab_ap: e.tensor_tensor(out=Pv, in0=Pv, in1=tab_ap, op=ALU.mult),
                 reads=["PT%d" % pj, "tabE0", "tabE1", "tabE2"], writes=["PT%d" % pj])
            return pj

        pjs = {}
        for n in range(len(its) + SKEW):
            if n < len(its):
                pjs[n] = qk_stage(n)
            m_ = n - SKEW
            if m_ < 0:
                continue
            g, hh, tp_ = its[m_]
            pj = pjs[m_]
            Pt = PT[pj]
            Oh = B["Oe"] if hh == 0 else B["Oo"]
            okey = "Oe" if hh == 0 else "Oo"
            for u in range(2):
                t = 2 * tp_ + (1 - u)
                bt = 4 * g + t
                first = (tp_ == 0 and u == 0)
                last = (tp_ == 3 and u == 1)
                P.op("pe", lambda e, Oh=Oh, bt=bt, hh=hh, Pt=Pt, u=u, first=first, last=last: e.matmul(
                    Oh[:, 0:512], lhsT=vP[:, bt, hh * 64:hh * 64 + 128], rhs=Pt[:, u * 512:(u + 1) * 512],
                    start=first, stop=last),
                    reads=["vPe%d" % (bt // 4), "vPo%d" % (bt // 4), "vP_ones", "PT%d" % pj], writes=[okey])
            if hh == 1 and tp_ == 3:
                emit_finalize_now(C, B["Oe"], B["Oo"], B["Rsb"], B["Rsw"])
                pending.append((n + 3 if g < 3 else 10 ** 9, g, c))
            while pending and pending[0][0] <= n:
                flush_one()

    while pending:
        flush_one()
    alias = ["qTe", "qTo", "qTe_z", "qTo_z", "vP_ones"] + ["kT%d" % i for i in range(5)] + ["gz%d" % i for i in range(4)] + \
            ["vPe%d" % i for i in range(5)] + ["vPo%d" % i for i in range(5)]
    wg_v = wg.rearrange("(kc p) n -> p kc n", p=128)
    for nb in range(8):
        ring.load(wg_v[:, :, nb * 128:(nb + 1) * 128], 128, [8, 128], wgb[:, :, nb * 128:(nb + 1) * 128], "wgb",
                  extra_writes=alias if nb == 0 else (), gain=(B["gpc"], "gpc", 0))
    wp_v = wp.rearrange("(kc p) n -> p kc n", p=128)
    for nb in range(2):
        ring.load(wp_v[:, :, nb * 512:(nb + 1) * 512], 128, [2, 512], wpb[:, :, nb * 512:(nb + 1) * 512], "wpb",
                  extra_writes=alias if nb == 0 else ())
    outs = emit_ple(C, lambda i: (xres[:, i, :], "xr%d" % i), 16, pd, None, wgb, wpb, st, 20, B["xnb"], B["idb"],
                    B["sq"], B["hnT"], B["pst"], B["pbf"], B["pT"], sig, tmp, B["S"], B["M"], xo)
    P.emit(final_wait_ops=outs)
    return nc


def _consts():
    ident = np.eye(128, dtype=np.float32)
    perm = np.zeros((128, 128), np.float32)
    for i in range(64):
        perm[64 + i, i] = 1.0
        perm[i, 64 + i] = 1.0
    return ident, perm


def _na_tables(rpb):
    kc = np.arange(64)
    qc = np.arange(64)
    cs = np.clip(qc - 8, 0, 48)
    colvalid = (kc[:, None] >= cs[None, :]) & (kc[:, None] < cs[None, :] + 16)
    coff = np.clip(kc[:, None] - qc[None, :] + 15, 0, 30)
    tab = np.full((16, 128, 22, 64), NEG, np.float32)
    for e in range(22):
        for half in range(2):
            dr = 10 - e + half
            if -7 <= dr <= 7:
                vals = rpb[:, dr + 7][:, coff]
                tab[:, half * 64:(half + 1) * 64, e, :] = np.where(colvalid[None], vals, NEG)
    return np.ascontiguousarray(tab.reshape(8, 2, 128, 22, 64).transpose(0, 2, 1, 3, 4).reshape(8, 128, 2816))


def _na_rowmask(hf):
    bq = np.zeros((128, 4, 8, 64), np.float32)
    for g in range(4):
        for qi in range(8):
            r = 32 * hf + 8 * g + qi
            rs = min(max(r - 4, 0), 56)
            for j in range(16):
                R = 32 * hf - 4 + 8 * g + j
                ok = (0 <= R <= 63) and (rs <= R < rs + 8)
                bq[j, g, qi, :] = 0.0 if ok else NEG
    aoh = np.zeros((128, 8, 128), np.float32)
    for t in range(8):
        aoh[2 * t, t, 0:64] = 1.0
        aoh[2 * t + 1, t, 64:128] = 1.0
    return bq.reshape(128, 2048), aoh.reshape(128, 1024)


_NC_CACHE = {}


def _get_nc(name, builder):
    if name not in _NC_CACHE:
        nc = bass.Bass("TRN2", target_bir_lowering=False)
        with contextlib.ExitStack() as es:
            builder(nc, es)
        _NC_CACHE[name] = nc
    return _NC_CACHE[name]


def run_l0(x, p, norm_g, na_w_in, na_rpb, na_w_out, ple_norm, ple_w_gate, ple_w_proj):
    x = np.asarray(x, np.float32)
    ident, perm = _consts()
    tabB = _na_tables(np.asarray(na_rpb[0], np.float32))
    gn = np.ascontiguousarray(np.asarray(norm_g[0], np.float32).reshape(8, 128).T)
    gp = np.ascontiguousarray(np.asarray(ple_norm[0], np.float32).reshape(8, 128).T)
    shared = dict(gnc=gn, gpc=gp, w_in=np.ascontiguousarray(na_w_in[0], dtype=np.float32),
                  w_out=np.ascontiguousarray(na_w_out[0], dtype=np.float32),
                  wg=np.ascontiguousarray(ple_w_gate[0], dtype=np.float32),
                  wp=np.ascontiguousarray(ple_w_proj[0], dtype=np.float32), tabB=tabB, ident=ident, perm=perm)
    in_maps = []
    for core in range(8):
        b, hf = core // 2, core % 2
        xk = np.zeros((40, 64, 1024), np.float32)
        xb = x[b].reshape(64, 64, 1024)
        for lr in range(40):
            R = 32 * hf - 4 + lr
            if 0 <= R <= 63:
                xk[lr] = xb[R]
        bq, aoh = _na_rowmask(hf)
        m = dict(shared)
        m.update(xk=xk.reshape(2560, 1024), pd=np.ascontiguousarray(p[0, b, hf * 2048:(hf + 1) * 2048], dtype=np.float32),
                 bq=bq, aoh=aoh)
        in_maps.append(m)
    nc = _get_nc("l0", build_l0)
    res = run_bass_kernel_spmd(nc, in_maps, core_ids=list(range(8)))
    x1 = np.zeros((4, 4096, 1024), np.float32)
    for core in range(8):
        b, hf = core // 2, core % 2
        x1[b, hf * 2048:(hf + 1) * 2048] = res.results[core]["xo"]
    return x1


def build_l1(nc, es, pre="", xf=None, sem_es=None):
    C = Ctx(nc, es, pre, sem_es)
    P = C.P
    if xf is None:
        xoth = C.din("xoth", [2048, 1024])
        xown = C.din("xown", [2048, 1024])
    else:
        xown, xoth = xf

    def xf_tile(T):
        src = xown if T < 16 else xoth
        return src[(T % 16) * 128:(T % 16 + 1) * 128, :]
    pd = C.din("pd", [2048, 256])
    gn = C.din("gnc", [128, 8])
    gp = C.din("gpc", [128, 8])
    gf = C.din("gf", [128, 1024])
    qn = C.din("qnc", [128, 3])
    kvn = C.din("kvnc", [128, 2])
    w_in = C.din("w_in", [1024, 1696])
    wqa = C.din("wqa", [384, 1536])
    wqs = C.din("wqs", [384, 1536])
    w_kvb = C.din("w_kvb", [256, 2048])
    w_out = C.din("w_out", [1024, 1024])
    wg = C.din("wg", [1024, 1024])
    wp = C.din("wp", [256, 1024])
    costok = C.din("costok", [128, 512])
    sintok = C.din("sintok", [128, 512])
    cosTd = C.din("cosT", [128, 2048])
    sinTd = C.din("sinT", [128, 2048])
    ident = C.din("ident", [128, 128])
    perm = C.din("perm", [128, 128])
    xo = C.dout("xo", [2048, 1024])

    B = alloc_common(C, layer1=True)
    ring = B["ring"]
    st = (B["ss"], B["ms"], B["nh"])
    xnT = C.sb("xnT", [128, 8, 2048], BF16)
    ckvT = C.sb("ckvT", [128, 2, 4096], BF16)
    kpeT = C.sb("kpeT", [128, 4096], BF16)
    cqT = C.sb("cqT", [128, 3, 2048], BF16)
    cosT = C.sb("cosTb", [128, 2048], BF16)
    sinT = C.sb("sinTb", [128, 2048], BF16)
    ogT = C.sb("ogT", [128, 8, 2048], BF16)
    hx = [C.sb("hx%d" % i, [128, 1024], F32) for i in range(2)]
    qnc = C.sb("qnc_sb", [128, 3], F32)
    kvnc = C.sb("kvnc_sb", [128, 2], F32)
    xkT = B["hnT"]
    lat = [B["pst"][i][:].bitcast(BF16) for i in range(2)]
    wqA = [C.sb("wqA%d" % i, [128, 3, 96], BF16) for i in range(2)]
    wqB = [C.sb("wqB%d" % i, [128, 3, 96], BF16) for i in range(2)]
    wkn = [C.sb("wkn%d" % i, [128, 2, 64], BF16) for i in range(2)]
    wvp = C.sb("wvp", [128, 2, 128], BF16)
    wz = C.sb("wz", [128, 8, 128], BF16)
    PT = [C.sb("PT%d" % i, [128, 1024], BF16) for i in range(3)]
    etmp = C.sb("etmp", [128, 512], F32)
    AR_COLS, VP_OFF = 20480, 12288
    ar = C.sb("ar", [128, 20480], BF16)
    qT = [ar[:, 0:2048], ar[:, 2048:4096]]
    kT = [ar[:, 4096:8192], ar[:, 8192:12288]]
    vP = ar[:, 12288:18432].rearrange("p (t c) -> p t c", c=192)
    gz = ar[:, 18432:20480]
    krope = ar[:, 0:2048].bitcast(F32).rearrange("p (t c) -> p t c", c=32)
    kpe_tok = ar[:, 2048:5120].rearrange("p (t c) -> p t c", c=96)
    ctk = ar[:, 5120:6144].bitcast(F32).rearrange("p (t c) -> p t c", c=16)
    stk = ar[:, 6144:7168].bitcast(F32).rearrange("p (t c) -> p t c", c=16)
    wkvr = ar[:, 7168:9472].rearrange("p (k n) -> p k n", n=288)
    wcq = ar[:, 9472:12544].rearrange("p (k n) -> p k n", n=384)
    wgb = ar[:, 0:8192].rearrange("p (k n) -> p k n", n=1024)
    wpb = ar[:, 8192:10240].rearrange("p (k n) -> p k n", n=1024)
    woa = ar[:, 10240:18432].rearrange("p (k n) -> p k n", n=1024)
    ra = B["Rsb"]
    rb = B["Rsw"]
    sig = [etmp, B["Rsb"]]
    tmp = [B["t1"], B["Rsw"]]
    fgbc = C.sb("fgbc", [128, 1024], F32)

    emit_consts(C, B, ident, perm)
    P.op("sp", lambda e: e.dma_start(out=B["gnc"][:], in_=gn), writes=["gnc"], dma=True)
    P.op("sp", lambda e: e.dma_start(out=B["gpc"][:], in_=gp), writes=["gpc"], dma=True)
    P.op("sp", lambda e: e.dma_start(out=fgbc[:], in_=gf), writes=["fgbc"], dma=True)
    P.op("sp", lambda e: e.dma_start(out=qnc[:], in_=qn), writes=["qnc"], dma=True)
    P.op("sp", lambda e: e.dma_start(out=kvnc[:], in_=kvn), writes=["kvnc"], dma=True)
    P.op("sp", lambda e: e.dma_start(out=ctk.rearrange("p t c -> p (t c)"), in_=costok), writes=["ctk"], dma=True)
    P.op("sp", lambda e: e.dma_start(out=stk.rearrange("p t c -> p (t c)"), in_=sintok), writes=["stk"], dma=True)
    for h2 in range(2):
        ring.load(cosTd[:, h2 * 1024:(h2 + 1) * 1024], 128, [1024], cosT[:, h2 * 1024:(h2 + 1) * 1024], "cosT")
        ring.load(sinTd[:, h2 * 1024:(h2 + 1) * 1024], 128, [1024], sinT[:, h2 * 1024:(h2 + 1) * 1024], "sinT")
    w_in_v = w_in.rearrange("(kc p) n -> p kc n", p=128)
    for k4 in range(4):
        ring.load(w_in_v[:, 2 * k4:2 * k4 + 2, 384:672], 128, [2, 288], wkvr[:, 2 * k4:2 * k4 + 2, :], "wkvr",
                  gain=(B["gnc"], "gnc", 2 * k4))
    for k4 in range(4):
        ring.load(w_in_v[:, 2 * k4:2 * k4 + 2, 0:384], 128, [2, 384], wcq[:, 2 * k4:2 * k4 + 2, :], "wcq",
                  gain=(B["gnc"], "gnc", 2 * k4))
    P.op("pool", lambda e: e.memset(kpe_tok[:, :, 0:64], 0.0), writes=["kpe_z"])

    def psum_bf(t):
        return t[:].bitcast(BF16).rearrange("p (k t) -> p k t", t=128)

    ss, ms, nh_ = st

    def latent_stages(n_tiles, load, col_x, col_l, width, wmat, wkey, gvec, gvkey, nch, dstT_of, dst_of, dst_key, extra=None):
        def s0(T):
            j = T % 2
            load(T, j)
            emit_norm_stats(C, st, hx[j][:], "hx%d" % j, col_x + T, B["sq"])

        def s1(T):
            j = T % 2
            dT, dkey = dstT_of(T, j)
            emit_norm_apply_T(C, st, hx[j][:], "hx%d" % j, col_x + T, None, None, B["xnb"][j][:], "xnb%d" % j,
                              psum_bf(B["M"][j]), "M%d" % j, B["idb"], dT, dkey)

        def s2(T):
            j = T % 2
            dT, dkey = dstT_of(T, j)
            Sp = B["S"][j]
            tot = width + (32 if extra else 0)
            for kc in range(8):
                P.op("pe", lambda e, kc=kc: e.matmul(Sp[:, 0:tot], lhsT=dT[:, kc, :], rhs=wmat[:, kc, :],
                                                     start=(kc == 0), stop=(kc == 7)),
                     reads=[dkey, wkey], writes=["S%d" % j])
            emit_norm_stats(C, st, Sp[:, 0:width], "S%d" % j, col_l + T, B["sq"], width=width)

        def s3(T):
            j = T % 2
            Sp = B["S"][j]
            col = col_l + T
            Ob = B["Oe"] if j == 0 else B["Oo"]
            okey = "Oe" if j == 0 else "Oo"
            if extra:
                extra(T, Sp, "S%d" % j)
            emit_norm_apply_T(C, st, Sp[:, 0:width], "S%d" % j, col, gvec, gvkey, lat[j][:, 0:width], "pst%d" % j,
                              psum_bf(Ob), okey, B["idb"], dst_of(T), dst_key % T, nch=nch, copy_eng="dve")

        run_pipeline(n_tiles, [s0, s1, s2, s3])

    def load_own(T, j):
        P.op("sp", lambda e: e.dma_start(out=hx[j][:], in_=xown[T * 128:(T + 1) * 128, :]), writes=["hx%d" % j], dma=True)

    latent_stages(16, load_own, 64, 80, 384, wcq, "wcq", None, None, 3,
                  lambda T, j: (xnT[:, :, T * 128:(T + 1) * 128], "xnT%d" % T),
                  lambda T: cqT[:, :, T * 128:(T + 1) * 128], "cqT%d")

    def load_any(T, j):
        P.op("sp", lambda e: e.dma_start(out=hx[j][:], in_=xf_tile(T)), writes=["hx%d" % j], dma=True)

    def krope_copy(T, Sp, skey):
        P.op("act", lambda e: e.activation(out=krope[:, T, :], in_=Sp[:, 256:288], func=AF.Copy), reads=[skey], writes=["krope"])

    latent_stages(32, load_any, 0, 32, 256, wkvr, "wkvr", None, None, 2,
                  lambda T, j: (xkT[j][:], "hnT%d" % j),
                  lambda T: ckvT[:, :, T * 128:(T + 1) * 128], "ckvT%d", extra=krope_copy)

    x1v = krope[:, :, 0:16]
    x2v = krope[:, :, 16:32]
    rav = ra[:].rearrange("p (t c) -> p t c", c=16)
    rbv = rb[:].rearrange("p (t c) -> p t c", c=16)
    P.op("dve", lambda e: e.tensor_tensor(out=rav, in0=x1v, in1=ctk, op=ALU.mult), reads=["krope", "ctk"], writes=["ra"])
    P.op("dve", lambda e: e.tensor_tensor(out=rbv, in0=x2v, in1=stk, op=ALU.mult), reads=["krope", "stk"], writes=["rb"])
    P.op("dve", lambda e: e.tensor_tensor(out=kpe_tok[:, :, 64:80], in0=rav, in1=rbv, op=ALU.subtract),
         reads=["ra", "rb"], writes=["kpe1"])
    P.op("dve", lambda e: e.tensor_tensor(out=rav, in0=x1v, in1=stk, op=ALU.mult), reads=["krope", "stk", "kpe1"], writes=["ra"])
    P.op("dve", lambda e: e.tensor_tensor(out=rbv, in0=x2v, in1=ctk, op=ALU.mult), reads=["krope", "ctk", "kpe1"], writes=["rb"])
    P.op("dve", lambda e: e.tensor_tensor(out=kpe_tok[:, :, 80:96], in0=rav, in1=rbv, op=ALU.add),
         reads=["ra", "rb"], writes=["kpe2"])
    for T8 in range(4):
        s = T8 % 2
        tpv = B["S"][s][:, 0:512].bitcast(BF16).rearrange("p (k t) -> p k t", t=128)
        for a in range(8):
            T = T8 * 8 + a
            P.op("pe", lambda e, a=a, T=T, tpv=tpv: e.transpose(out=tpv[0:96, a, :], in_=kpe_tok[:, T, 0:96], identity=B["idb"][:]),
                 reads=["kpe1", "kpe2", "kpe_z", "idb"], writes=["S%d" % s])
        dst = kpeT[64:96, T8 * 1024:(T8 + 1) * 1024].rearrange("p (a t) -> p a t", t=128)
        if T8 % 2 == 0:
            P.op("dve", lambda e, dst=dst, tpv=tpv: e.tensor_copy(out=dst, in_=tpv[64:96, :, :]), reads=["S%d" % s],
                 writes=["kpeT%d" % T8])
        else:
            P.op("act", lambda e, dst=dst, tpv=tpv: e.activation(out=dst, in_=tpv[64:96, :, :], func=AF.Copy), reads=["S%d" % s],
                 writes=["kpeT%d" % T8])

    mctr = [0]
    sctr = [0]
    pctr = [0]

    def next_m():
        m = mctr[0] % 2
        mctr[0] += 1
        return m

    wqa_v = wqa.rearrange("(kc p) n -> p kc n", p=128)
    wqs_v = wqs.rearrange("(kc p) n -> p kc n", p=128)
    wkv_v = w_kvb.rearrange("(kc p) n -> p kc n", p=128)
    alias_ac = ["krope", "kpe1", "kpe2", "kpe_z", "ctk", "stk", "wkvr", "wcq"]
    scale = float(96.0 ** -0.5)
    for c in range(8):
        for hh in range(2):
            h = 2 * c + hh
            ring.load(wqa_v[:, :, h * 96:(h + 1) * 96], 128, [3, 96], wqA[hh][:], "wqA%d" % hh, gain=(qnc, "qnc", 0))
            ring.load(wqs_v[:, :, h * 96:(h + 1) * 96], 128, [3, 96], wqB[hh][:], "wqB%d" % hh, gain=(qnc, "qnc", 0))
            ring.load(wkv_v[:, :, h * 128:h * 128 + 64], 128, [2, 64], wkn[hh][:], "wkn%d" % hh, gain=(kvnc, "kvnc", 0))
            ring.load(wkv_v[:, :, h * 128 + 64:h * 128 + 128], 128, [2, 64], wvp[:, :, hh * 64:(hh + 1) * 64], "wvp%d" % hh,
                      gain=(kvnc, "kvnc", 0))
        ring.load(w_in_v[:, :, 672 + c * 128:672 + (c + 1) * 128], 128, [8, 128], wz[:], "wz", gain=(B["gnc"], "gnc", 0))
        first_alias = alias_ac if c == 0 else []

        for hh in range(2):
            for blk in range(4):
                mA = next_m()
                mB = next_m()
                MA = B["M"][mA]
                MB = B["M"][mB]
                rk = ["cqT%d" % (blk * 4 + a) for a in range(4)]
                for kc in range(3):
                    P.op("pe", lambda e, kc=kc, MA=MA, hh=hh, blk=blk: e.matmul(
                        MA[0:96, 0:512], lhsT=wqA[hh][:, kc, :], rhs=cqT[:, kc, blk * 512:(blk + 1) * 512],
                        start=(kc == 0), stop=(kc == 2)), reads=rk + ["wqA%d" % hh], writes=["M%d" % mA])
                for kc in range(3):
                    P.op("pe", lambda e, kc=kc, MB=MB, hh=hh, blk=blk: e.matmul(
                        MB[0:96, 0:512], lhsT=wqB[hh][:, kc, :], rhs=cqT[:, kc, blk * 512:(blk + 1) * 512],
                        start=(kc == 0), stop=(kc == 2)), reads=rk + ["wqB%d" % hh], writes=["M%d" % mB])
                bs = slice(blk * 512, (blk + 1) * 512)
                P.op("act", lambda e, MA=MA, hh=hh, bs=bs: e.activation(out=qT[hh][0:64, bs], in_=MA[0:64, 0:512], func=AF.Copy),
                     reads=["M%d" % mA], writes=["qT%dn" % hh] + first_alias)
                P.op("dve", lambda e, MA=MA, bs=bs: e.tensor_tensor(out=ra[64:96, :], in0=MA[64:96, 0:512], in1=cosT[64:96, bs],
                                                                    op=ALU.mult), reads=["M%d" % mA, "cosT"], writes=["ra"])
                P.op("dve", lambda e, MB=MB, bs=bs: e.tensor_tensor(out=rb[64:96, :], in0=MB[64:96, 0:512], in1=sinT[64:96, bs],
                                                                    op=ALU.mult), reads=["M%d" % mB, "sinT"], writes=["rb"])
                P.op("pool", lambda e, hh=hh, bs=bs: e.tensor_tensor(out=qT[hh][64:96, bs], in0=ra[64:96, :], in1=rb[64:96, :],
                                                                      op=ALU.add),
                     reads=["ra", "rb"], writes=["qT%dp" % hh] + first_alias)
                first_alias = []
            for blk in range(8):
                m = next_m()
                Mp = B["M"][m]
                rk = ["ckvT%d" % (blk * 4 + a) for a in range(4)]
                for kc in range(2):
                    P.op("pe", lambda e, kc=kc, Mp=Mp, hh=hh, blk=blk: e.matmul(
                        Mp[0:64, 0:512], lhsT=wkn[hh][:, kc, :], rhs=ckvT[:, kc, blk * 512:(blk + 1) * 512],
                        start=(kc == 0), stop=(kc == 1)), reads=rk + ["wkn%d" % hh], writes=["M%d" % m])
                bs = slice(blk * 512, (blk + 1) * 512)
                if blk % 2 == 0:
                    P.op("dve", lambda e, Mp=Mp, hh=hh, bs=bs: e.tensor_copy(out=kT[hh][0:64, bs], in_=Mp[0:64, 0:512]),
                         reads=["M%d" % m], writes=["kT%d_%d" % (hh, blk)])
                else:
                    P.op("act", lambda e, Mp=Mp, hh=hh, bs=bs: e.activation(out=kT[hh][0:64, bs], in_=Mp[0:64, 0:512], func=AF.Copy),
                         reads=["M%d" % m], writes=["kT%d_%d" % (hh, blk)])
            P.op("dve", lambda e, hh=hh: e.tensor_copy(out=kT[hh][64:96, :], in_=kpeT[64:96, :]),
                 reads=["kpeT%d" % a for a in range(4)], writes=["kT%dpe" % hh])
        if c == 0:
            P.op("pool", lambda e: e.memset(vP[:, :, 64:128], 1.0), writes=["vP_ones"])
        for t4 in range(8):
            m = next_m()
            Mp = B["M"][m]
            Mv = Mp[:, 0:512].rearrange("p (t c) -> p t c", c=128)
            for a in range(4):
                T = t4 * 4 + a
                for kc in range(2):
                    P.op("pe", lambda e, kc=kc, Mv=Mv, a=a, T=T: e.matmul(
                        Mv[:, a, :], lhsT=ckvT[:, kc, T * 128:(T + 1) * 128], rhs=wvp[:, kc, :],
                        start=(kc == 0), stop=(kc == 1)), reads=["ckvT%d" % T, "wvp0", "wvp1"], writes=["M%d" % m])
            vdst = bass.AP(ar, VP_OFF + t4 * 4 * 192, [[AR_COLS, 128], [192, 4], [128, 2], [1, 64]])
            P.op("dve", lambda e, Mv=Mv, vdst=vdst: e.tensor_copy(out=vdst, in_=Mv.rearrange("p t (h c) -> p t h c", h=2)),
                 reads=["M%d" % m], writes=["vPe%d" % t4, "vPo%d" % t4])
        for blk in range(4):
            m = next_m()
            Mp = B["M"][m]
            rk = ["xnT%d" % (blk * 4 + a) for a in range(4)]
            for kc in range(8):
                P.op("pe", lambda e, kc=kc, Mp=Mp, blk=blk: e.matmul(Mp[:, 0:512], lhsT=wz[:, kc, :],
                                                                     rhs=xnT[:, kc, blk * 512:(blk + 1) * 512],
                                                                     start=(kc == 0), stop=(kc == 7)),
                     reads=rk + ["wz"], writes=["M%d" % m])
            emit_silu_gate(C, Mp[:, 0:512], "M%d" % m, etmp, gz[:, blk * 512:(blk + 1) * 512], "gz%d" % blk)

        its = [(qb, hh, tp_) for qb in range(4) for hh in range(2) for tp_ in range(16)]
        SKEW = 2

        def qk_stage(n):
            qb, hh, tp_ = its[n]
            s_ = sctr[0] % 2
            sctr[0] += 1
            Sp = B["S"][s_]
            for u in range(2):
                kt = 2 * tp_ + u
                P.op("pe", lambda e, Sp=Sp, u=u, kt=kt, hh=hh, qb=qb: e.matmul(
                    Sp[:, u * 512:(u + 1) * 512], lhsT=kT[hh][0:96, kt * 128:(kt + 1) * 128],
                    rhs=qT[hh][0:96, qb * 512:(qb + 1) * 512], start=True, stop=True),
                    reads=["kT%d_%d" % (hh, kt // 4), "kT%dpe" % hh, "qT%dn" % hh, "qT%dp" % hh], writes=["S%d" % s_])
            pj = pctr[0] % 3
            pctr[0] += 1
            Pt = PT[pj]
            P.op("act", lambda e, Pt=Pt, Sp=Sp: e.activation(out=Pt[:], in_=Sp[:], func=AF.Exp, scale=scale),
                 reads=["S%d" % s_], writes=["PT%d" % pj])
            return pj

        pjs = {}
        pending = []
        for n in range(len(its) + SKEW):
            if n < len(its):
                pjs[n] = qk_stage(n)
            m_ = n - SKEW
            if m_ < 0:
                continue
            qb, hh, tp_ = its[m_]
            pj = pjs[m_]
            Pt = PT[pj]
            Oh = B["Oe"] if hh == 0 else B["Oo"]
            okey = "Oe" if hh == 0 else "Oo"
            for u in range(2):
                kt = 2 * tp_ + u
                P.op("pe", lambda e, Oh=Oh, kt=kt, hh=hh, Pt=Pt, u=u: e.matmul(
                    Oh[:, 0:512], lhsT=vP[:, kt, hh * 64:hh * 64 + 128], rhs=Pt[:, u * 512:(u + 1) * 512],
                    start=(kt == 0), stop=(kt == 31)),
                    reads=["vPe%d" % (kt // 4), "vPo%d" % (kt // 4), "vP_ones", "PT%d" % pj], writes=[okey])
            if hh == 1 and tp_ == 15:
                emit_finalize_now(C, B["Oe"], B["Oo"], B["Rsb"], B["Rsw"])
                pending.append((n + 3, qb))
            while pending and (pending[0][0] <= n or n == len(its) + SKEW - 1):
                _, qb_ = pending.pop(0)
                m = next_m()
                emit_finalize_later(C, B["Rsb"], B["Rsw"], B["t1"], B["pmf"], B["M"][m], "M%d" % m,
                                    gz[:, qb_ * 512:(qb_ + 1) * 512], "gz%d" % qb_, ogT[:, c, qb_ * 512:(qb_ + 1) * 512],
                                    "ogT%d_%d" % (c, qb_))

    alias = ["qT0n", "qT0p", "qT1n", "qT1p", "vP_ones", "kT0pe", "kT1pe"] + ["kT%d_%d" % (a, b_) for a in range(2) for b_ in range(8)] + \
            ["gz%d" % i for i in range(4)] + ["vPe%d" % i for i in range(8)] + ["vPo%d" % i for i in range(8)]
    wg_v = wg.rearrange("(kc p) n -> p kc n", p=128)
    for nb in range(8):
        ring.load(wg_v[:, :, nb * 128:(nb + 1) * 128], 128, [8, 128], wgb[:, :, nb * 128:(nb + 1) * 128], "wgb",
                  extra_writes=alias if nb == 0 else (), gain=(B["gpc"], "gpc", 0))
    wp_v = wp.rearrange("(kc p) n -> p kc n", p=128)
    for nb in range(2):
        ring.load(wp_v[:, :, nb * 512:(nb + 1) * 512], 128, [2, 512], wpb[:, :, nb * 512:(nb + 1) * 512], "wpb",
                  extra_writes=alias if nb == 0 else ())
    for c in range(8):
        ring.load(w_out[c * 128:(c + 1) * 128, :], 128, [1024], woa[:, c, :], "woa", extra_writes=alias if c == 0 else ())

    NHX = 8
    hx4 = [hx[0][:], hx[1][:]] + [xnT[:, k_, :].bitcast(F32) for k_ in range(NHX - 2)]
    xnT_keys = ["xnT%d" % a_ for a_ in range(16)]

    def pre(i):
        j = i % NHX
        P.op("sp", lambda e: e.dma_start(out=hx4[j], in_=xown[i * 128:(i + 1) * 128, :]),
             writes=["hx%d" % j] + (xnT_keys if 2 <= i < NHX else []), dma=True)
        for nh in range(2):
            Ob = B["Oe"] if nh == 0 else B["Oo"]
            okey = "Oe" if nh == 0 else "Oo"
            for c in range(8):
                P.op("pe", lambda e, c=c, Ob=Ob, nh=nh: e.matmul(Ob[:, 0:512], lhsT=ogT[:, c, i * 128:(i + 1) * 128],
                                                                 rhs=woa[:, c, nh * 512:(nh + 1) * 512],
                                                                 start=(c == 0), stop=(c == 7)),
                     reads=["ogT%d_%d" % (c, i // 4), "woa"], writes=[okey])
            xh = hx4[j][:, nh * 512:(nh + 1) * 512]
            P.op("dve", lambda e, xh=xh, Ob=Ob: e.tensor_tensor(out=xh, in0=xh, in1=Ob[:, 0:512], op=ALU.add),
                 reads=["hx%d" % j, okey], writes=["hx%d" % j])

    outs = emit_ple(C, lambda i: (hx4[i % NHX], "hx%d" % (i % NHX)), 16, pd, None, wgb, wpb, st, 0, B["xnb"], B["idb"], B["sq"],
                    B["hnT"], B["pst"], B["pbf"], B["pT"], sig, tmp, B["S"], B["M"], xo,
                    final_norm=(fgbc, "fgbc", 16), pre=pre)
    P.emit(final_wait_ops=outs)
    return nc


def _rope_np():
    inv = (1.0 / (np.float32(10000.0) ** (np.arange(0, 32, 2, dtype=np.float32) / np.float32(32)))).astype(np.float32)
    ang = (np.arange(4096, dtype=np.float32)[:, None] * inv[None, :]).astype(np.float32)
    return np.cos(ang).astype(np.float32), np.sin(ang).astype(np.float32)


def run_l1(x1, p, norm_g, mla_w_in, mla_q_norm, mla_w_qb, mla_kv_norm, mla_w_kvb, mla_w_out, ple_norm, ple_w_gate,
           ple_w_proj, final_norm):
    ident, perm = _consts()
    f = lambda a: np.ascontiguousarray(a, dtype=np.float32)
    bc = lambda v, n: np.ascontiguousarray(np.broadcast_to(np.asarray(v, np.float32), (128, n)))
    wqb = f(mla_w_qb[0]).reshape(384, 16, 96)
    wqs = np.zeros_like(wqb)
    wqs[:, :, 64:80] = wqb[:, :, 80:96]
    wqs[:, :, 80:96] = wqb[:, :, 64:80]
    cos, sin = _rope_np()
    costok = np.ascontiguousarray(cos.reshape(32, 128, 16).transpose(1, 0, 2).reshape(128, 512))
    sintok = np.ascontiguousarray(sin.reshape(32, 128, 16).transpose(1, 0, 2).reshape(128, 512))
    pm = lambda v, k: np.ascontiguousarray(np.asarray(v, np.float32).reshape(k, 128).T)
    shared = dict(gnc=pm(norm_g[1], 8), gpc=pm(ple_norm[1], 8), gf=bc(final_norm, 1024), qnc=pm(mla_q_norm[0], 3),
                  kvnc=pm(mla_kv_norm[0], 2), w_in=f(mla_w_in[0]), wqa=wqb.reshape(384, 1536), wqs=wqs.reshape(384, 1536),
                  w_kvb=f(mla_w_kvb[0]), w_out=f(mla_w_out[0]), wg=f(ple_w_gate[1]), wp=f(ple_w_proj[1]),
                  costok=costok, sintok=sintok, ident=ident, perm=perm)
    in_maps = []
    for core in range(8):
        b, hf = core // 2, core % 2
        cosT = np.zeros((128, 2048), np.float32)
        sinT = np.zeros((128, 2048), np.float32)
        cs = cos[hf * 2048:(hf + 1) * 2048].T
        sn = sin[hf * 2048:(hf + 1) * 2048].T
        cosT[64:80] = cs
        cosT[80:96] = cs
        sinT[64:80] = -sn
        sinT[80:96] = sn
        m = dict(shared)
        m.update(xf=f(x1[b]), xown=f(x1[b, hf * 2048:(hf + 1) * 2048]), pd=f(p[1, b, hf * 2048:(hf + 1) * 2048]),
                 cosT=cosT, sinT=sinT)
        in_maps.append(m)
    nc = _get_nc("l1", build_l1)
    res = run_bass_kernel_spmd(nc, in_maps, core_ids=list(range(8)))
    out = np.zeros((4, 4096, 1024), np.float32)
    for core in range(8):
        b, hf = core // 2, core % 2
        out[b, hf * 2048:(hf + 1) * 2048] = res.results[core]["xo"]
    return out


def _l0_inputs(x, p, norm_g, na_w_in, na_rpb, na_w_out, ple_norm, ple_w_gate, ple_w_proj, flip=False):
    x = np.asarray(x, np.float32)
    ident, perm = _consts()
    tabB = _na_tables(np.asarray(na_rpb[0], np.float32))
    gn = np.ascontiguousarray(np.asarray(norm_g[0], np.float32).reshape(8, 128).T)
    gp = np.ascontiguousarray(np.asarray(ple_norm[0], np.float32).reshape(8, 128).T)
    shared = dict(gnc=gn, gpc=gp, w_in=np.ascontiguousarray(na_w_in[0], dtype=np.float32),
                  w_out=np.ascontiguousarray(na_w_out[0], dtype=np.float32),
                  wg=np.ascontiguousarray(ple_w_gate[0], dtype=np.float32),
                  wp=np.ascontiguousarray(ple_w_proj[0], dtype=np.float32), tabB=tabB, ident=ident, perm=perm)
    in_maps = []
    for core in range(8):
        b, hf = core // 2, core % 2
        if flip:
            hf = 1 - hf
        xk = np.zeros((40, 64, 1024), np.float32)
        xb = x[b].reshape(64, 64, 1024)
        for lr in range(40):
            R = 32 * hf - 4 + lr
            if 0 <= R <= 63:
                xk[lr] = xb[R]
        bq, aoh = _na_rowmask(hf)
        m = dict(shared)
        m.update(xk=xk.reshape(2560, 1024), pd=np.ascontiguousarray(p[0, b, hf * 2048:(hf + 1) * 2048], dtype=np.float32),
                 bq=bq, aoh=aoh)
        in_maps.append(m)
    return in_maps


def _l1_inputs(x1, p, norm_g, mla_w_in, mla_q_norm, mla_w_qb, mla_kv_norm, mla_w_kvb, mla_w_out, ple_norm, ple_w_gate,
               ple_w_proj, final_norm):
    ident, perm = _consts()
    f = lambda a: np.ascontiguousarray(a, dtype=np.float32)
    bc = lambda v, n: np.ascontiguousarray(np.broadcast_to(np.asarray(v, np.float32), (128, n)))
    wqb = f(mla_w_qb[0]).reshape(384, 16, 96)
    wqs = np.zeros_like(wqb)
    wqs[:, :, 64:80] = wqb[:, :, 80:96]
    wqs[:, :, 80:96] = wqb[:, :, 64:80]
    cos, sin = _rope_np()
    costok = np.ascontiguousarray(cos.reshape(32, 128, 16).transpose(1, 0, 2).reshape(128, 512))
    sintok = np.ascontiguousarray(sin.reshape(32, 128, 16).transpose(1, 0, 2).reshape(128, 512))
    pm = lambda v, k: np.ascontiguousarray(np.asarray(v, np.float32).reshape(k, 128).T)
    shared = dict(gnc=pm(norm_g[1], 8), gpc=pm(ple_norm[1], 8), gf=bc(final_norm, 1024), qnc=pm(mla_q_norm[0], 3),
                  kvnc=pm(mla_kv_norm[0], 2), w_in=f(mla_w_in[0]), wqa=wqb.reshape(384, 1536), wqs=wqs.reshape(384, 1536),
                  w_kvb=f(mla_w_kvb[0]), w_out=f(mla_w_out[0]), wg=f(ple_w_gate[1]), wp=f(ple_w_proj[1]),
                  costok=costok, sintok=sintok, ident=ident, perm=perm)
    in_maps = []
    for core in range(8):
        b, hf = core // 2, core % 2
        cosT = np.zeros((128, 2048), np.float32)
        sinT = np.zeros((128, 2048), np.float32)
        cs = cos[hf * 2048:(hf + 1) * 2048].T
        sn = sin[hf * 2048:(hf + 1) * 2048].T
        cosT[64:80] = cs
        cosT[80:96] = cs
        sinT[64:80] = -sn
        sinT[80:96] = sn
        m = dict(shared)
        if x1 is not None:
            m.update(xown=f(x1[b, hf * 2048:(hf + 1) * 2048]), xoth=f(x1[b, (1 - hf) * 2048:(2 - hf) * 2048]))
        order = np.concatenate([np.arange(hf * 2048, (hf + 1) * 2048), np.arange((1 - hf) * 2048, (2 - hf) * 2048)])
        m.update(pd=f(p[1, b, hf * 2048:(hf + 1) * 2048]), cosT=cosT, sinT=sinT,
                 costok=np.ascontiguousarray(cos[order].reshape(32, 128, 16).transpose(1, 0, 2).reshape(128, 512)),
                 sintok=np.ascontiguousarray(sin[order].reshape(32, 128, 16).transpose(1, 0, 2).reshape(128, 512)))
        in_maps.append(m)
    return in_maps


def build_fused(nc, es):
    x1loc = nc.dram_tensor("x1loc", [2048, 1024], F32).ap()
    x1oth = nc.dram_tensor("x1oth", [2048, 1024], F32).ap()
    shared = {}
    with contextlib.ExitStack() as es0:
        build_l0(nc, es0, pre="a_", xo=x1loc, sem_es=es, shared=shared)
    nc.all_engine_barrier()
    with contextlib.ExitStack() as es0:
        build_l0(nc, es0, pre="c_", xo=x1oth, sem_es=es, shared=shared)
    nc.all_engine_barrier()
    with contextlib.ExitStack() as es1:
        build_l1(nc, es1, pre="b_", xf=(x1loc, x1oth), sem_es=es)
    return nc


def kernel(x, p, norm_g, na_w_in, na_rpb, na_w_out, mla_w_in, mla_q_norm, mla_w_qb, mla_kv_norm, mla_w_kvb, mla_w_out,
           ple_norm, ple_w_gate, ple_w_proj, final_norm):
    x = np.asarray(x)
    p = np.asarray(p)
    m0 = _l0_inputs(x, p, norm_g, na_w_in, na_rpb, na_w_out, ple_norm, ple_w_gate, ple_w_proj)
    m0f = _l0_inputs(x, p, norm_g, na_w_in, na_rpb, na_w_out, ple_norm, ple_w_gate, ple_w_proj, flip=True)
    m1 = _l1_inputs(None, p, norm_g, mla_w_in, mla_q_norm, mla_w_qb, mla_kv_norm, mla_w_kvb, mla_w_out, ple_norm,
                    ple_w_gate, ple_w_proj, final_norm)
    in_maps = []
    for core in range(8):
        m = {"a_" + k: v for k, v in m0[core].items()}
        m.update({"c_" + k: m0f[core][k] for k in ("xk", "pd", "bq")})
        m.update({"b_" + k: v for k, v in m1[core].items()})
        in_maps.append(m)
    nc = _get_nc("fused", build_fused)
    res = run_bass_kernel_spmd(nc, in_maps, core_ids=list(range(8)))
    out = np.zeros((4, 4096, 1024), np.float32)
    for core in range(8):
        b, hf = core // 2, core % 2
        out[b, hf * 2048:(hf + 1) * 2048] = res.results[core]["xo"]
    return out
```

```python
import contextlib
import numpy as np
import concourse.bass as bass
import concourse.mybir as mybir
from concourse.bass_utils import run_bass_kernel_spmd

F32 = mybir.dt.float32
BF16 = mybir.dt.bfloat16
ALU = mybir.AluOpType
AF = mybir.ActivationFunctionType
AX = mybir.AxisListType

D = 1024
NEG = -30000.0
EPS = 1e-6


class Prog:
    STREAMS = ("pe", "act", "dve", "pool", "sp")

    def __init__(self, nc, n_dma_sems=12, sem_es=None, sem_prefix=""):
        self.nc = nc
        self.sem_es = sem_es
        self.sem_prefix = sem_prefix
        self.ops = []
        self.last_w = {}
        self.readers = {}
        self.n_dma_sems = n_dma_sems

    ALIAS = {
        "ra": ["Rsb_hi", "Rsb_lo", "sig1"], "Rsb_hi": ["ra", "sig1"], "Rsb_lo": ["ra", "sig1"],
        "sig1": ["ra", "Rsb_hi", "Rsb_lo"],
        "rb": ["Rsw", "tmp1", "Oc_lo", "Oc_hi"], "Rsw": ["rb", "tmp1", "Oc_lo", "Oc_hi"],
        "tmp1": ["rb", "Rsw", "Oc_lo", "Oc_hi"], "Oc_lo": ["rb", "Rsw", "tmp1"], "Oc_hi": ["rb", "Rsw", "tmp1"],
        "sig0": ["etmp"], "etmp": ["sig0"],
        "tmp0": ["t1_lo", "t1_hi"], "t1_lo": ["tmp0"], "t1_hi": ["tmp0"],
    }

    def _expand(self, keys):
        out = []
        for k in keys:
            out.append(k)
            out.extend(self.ALIAS.get(k, ()))
        return list(dict.fromkeys(out))

    def op(self, stream, fn, reads=(), writes=(), dma=False):
        reads = self._expand(reads)
        writes = self._expand(writes)
        i = len(self.ops)
        deps = {}

        def add(j, raw):
            if j is None:
                return
            deps[j] = deps.get(j, False) or raw

        for k in reads:
            add(self.last_w.get(k), True)
        for k in writes:
            add(self.last_w.get(k), True)
            for r in self.readers.get(k, ()):
                add(r, False)
        for k in reads:
            lst = self.readers.setdefault(k, [])
            if not dma:
                lst[:] = [r for r in lst if self.ops[r]["dma"] or self.ops[r]["stream"] != stream]
            lst.append(i)
        for k in writes:
            self.last_w[k] = i
            self.readers[k] = []
        self.ops.append(dict(i=i, stream=stream, fn=fn, dma=dma, deps=deps, sig=None))
        return i

    def emit(self, final_wait_ops=()):
        nc = self.nc
        ops = self.ops
        need = [[] for _ in ops]
        signaled = set()
        for o in ops:
            for j, raw in o["deps"].items():
                p = ops[j]
                if not p["dma"] and not o["dma"] and p["stream"] == o["stream"]:
                    if o["stream"] == "pe" or not raw:
                        continue
                need[o["i"]].append(j)
                signaled.add(j)
        for j in final_wait_ops:
            signaled.add(j)
        cnt = {s: 0 for s in self.STREAMS}
        dcnt = {s: 0 for s in self.STREAMS}
        for o in ops:
            if o["dma"]:
                k = dcnt[o["stream"]]
                dcnt[o["stream"]] += 1
                o["sig"] = ("d_%s_%d" % (o["stream"], k % self.n_dma_sems), 16 * (k // self.n_dma_sems + 1))
                o["dk"] = k
            elif o["i"] in signaled:
                cnt[o["stream"]] += 1
                o["sig"] = ("c_" + o["stream"], cnt[o["stream"]])
        dma_by_stream = {s: [o for o in ops if o["dma"] and o["stream"] == s] for s in self.STREAMS}
        sem_names = set()
        for o in ops:
            if o["sig"] is not None:
                sem_names.add(o["sig"][0])
        sem_names = sorted(sem_names)
        with contextlib.ExitStack() as es:
            ses = self.sem_es if self.sem_es is not None else es
            sems = {n: ses.enter_context(nc.semaphore(self.sem_prefix + n)) for n in sem_names}
            block = es.enter_context(nc.Block())
            seen = {s: {} for s in self.STREAMS}

            def run_stream(stream, eng):
                sw = seen[stream]
                for o in ops:
                    if o["stream"] != stream:
                        continue
                    waits = {}
                    for j in need[o["i"]]:
                        sn, sv = ops[j]["sig"]
                        waits[sn] = max(waits.get(sn, 0), sv)
                    if o["dma"] and o["dk"] >= self.n_dma_sems:
                        sn, sv = o["sig"]
                        waits[sn] = max(waits.get(sn, 0), sv - 16)
                    for sn, sv in sorted(waits.items()):
                        if sw.get(sn, 0) >= sv:
                            continue
                        sw[sn] = sv
                        eng.wait_ge(sems[sn], sv)
                    ins = o["fn"](eng)
                    if o["sig"] is not None:
                        ins.then_inc(sems[o["sig"][0]], 16 if o["dma"] else 1)
                if stream == "sp":
                    for j in final_wait_ops:
                        sn, sv = ops[j]["sig"]
                        if sw.get(sn, 0) < sv:
                            sw[sn] = sv
                            eng.wait_ge(sems[sn], sv)

            @block.tensor
            def _(e):
                run_stream("pe", e)

            @block.scalar
            def _(e):
                run_stream("act", e)

            @block.vector
            def _(e):
                run_stream("dve", e)

            @block.gpsimd
            def _(e):
                run_stream("pool", e)

            @block.sync
            def _(e):
                run_stream("sp", e)


class Ctx:
    def __init__(self, nc, es, pre="", sem_es=None):
        self.nc = nc
        self.es = es
        self.pre = pre
        self.P = Prog(nc, sem_es=sem_es, sem_prefix=pre)
        self._n = 0

    def sb(self, name, shape, dt):
        return self.es.enter_context(self.nc.sbuf_tensor(self.pre + name, list(shape), dt))

    def ps(self, name, shape, dt):
        return self.es.enter_context(self.nc.psum_tensor(self.pre + name, list(shape), dt))

    def din(self, name, shape, dt=F32):
        return self.nc.dram_tensor(self.pre + name, list(shape), dt, kind="ExternalInput").ap()

    def dout(self, name, shape, dt=F32):
        return self.nc.dram_tensor(name, list(shape), dt, kind="ExternalOutput").ap()


class Ring:
    def __init__(self, C, n=2, cols=1024):
        self.C = C
        self.slots = [C.sb("wst%d" % i, [128, cols], F32) for i in range(n)]
        self.k = 0

    def load(self, src, parts, shape_free, dst, dst_key, conv="pool", extra_writes=(), gain=None):
        P = self.C.P
        i = self.k % len(self.slots)
        self.k += 1
        n = int(np.prod(shape_free))
        st = self.slots[i][0:parts, 0:n]
        if len(shape_free) == 2:
            st = st.rearrange("p (a b) -> p a b", b=shape_free[1])
        key = "wst%d" % i
        P.op("sp", lambda e: e.dma_start(out=st, in_=src), writes=[key], dma=True)
        wr = [dst_key] + list(extra_writes)
        if gain is not None:
            gt, gkey, k0 = gain
            gap = bass.AP(gt, k0, [[gt[:].shape[1], 128], [1, shape_free[0]], [0, shape_free[1]]])
            P.op("pool", lambda e: e.tensor_tensor(out=dst, in0=st, in1=gap, op=ALU.mult), reads=[key, gkey], writes=wr)
        elif conv == "pool":
            P.op("pool", lambda e: e.tensor_copy(out=dst, in_=st), reads=[key], writes=wr)
        elif conv == "exp":
            P.op("act", lambda e: e.activation(out=dst, in_=st, func=AF.Exp), reads=[key], writes=wr)


def run_pipeline(n, stages):
    k = len(stages)
    for step in range(n + k - 1):
        for d in range(k):
            t = step - d
            if 0 <= t < n:
                stages[d](t)


def emit_norm_stats(C, st, xt_ap, xt_key, col, sq, width=D):
    P = C.P
    ss, ms, nh = st
    P.op("act", lambda e: e.activation(out=sq[:, 0:width], in_=xt_ap, func=AF.Square, accum_out=ss[:, col:col + 1]),
         reads=[xt_key], writes=["sq", "ss%d" % col])
    P.op("dve", lambda e: e.tensor_scalar(out=ms[:, col:col + 1], in0=ss[:, col:col + 1], scalar1=1.0 / width, scalar2=EPS,
                                         op0=ALU.mult, op1=ALU.add), reads=["ss%d" % col], writes=["ms%d" % col])
    P.op("pool", lambda e: e.tensor_tensor(out=ms[:, col:col + 1], in0=ms[:, col:col + 1], in1=nh[:], op=ALU.pow),
         reads=["ms%d" % col, "nh"], writes=["rs%d" % col])


def emit_norm_apply_T(C, st, xt_ap, xt_key, col, gbc, gkey, xnb_ap, xnb_key, tp_ps, tp_key, idb, dstT, dstT_key, nch=8,
                      copy_eng="act", scale_eng="act"):
    P = C.P
    ss, ms, nh = st
    if scale_eng == "act":
        P.op("act", lambda e: e.activation(out=xnb_ap, in_=xt_ap, func=AF.Copy, scale=ms[:, col:col + 1]),
             reads=[xt_key, "rs%d" % col], writes=[xnb_key])
    else:
        P.op("dve", lambda e: e.tensor_scalar(out=xnb_ap, in0=xt_ap, scalar1=ms[:, col:col + 1], scalar2=None, op0=ALU.mult),
             reads=[xt_key, "rs%d" % col], writes=[xnb_key])
    for kc in range(nch):
        P.op("pe", lambda e, kc=kc: e.transpose(out=tp_ps[:, kc, :], in_=xnb_ap[:, kc * 128:(kc + 1) * 128], identity=idb[:]),
             reads=[xnb_key, "idb"], writes=[tp_key])
    if copy_eng == "act":
        P.op("act", lambda e: e.activation(out=dstT, in_=tp_ps[:, 0:nch, :], func=AF.Copy), reads=[tp_key], writes=[dstT_key])
    else:
        P.op("dve", lambda e: e.tensor_copy(out=dstT, in_=tp_ps[:, 0:nch, :]), reads=[tp_key], writes=[dstT_key])


def emit_norm_T(C, st, xt_ap, xt_key, col, gbc, gkey, xnb, xnb_key, tp_ps, tp_key, idb, dstT, dstT_key, sq):
    emit_norm_stats(C, st, xt_ap, xt_key, col, sq)
    emit_norm_apply_T(C, st, xt_ap, xt_key, col, gbc, gkey, xnb[:], xnb_key, tp_ps, tp_key, idb, dstT, dstT_key)


def emit_finalize_now(C, Oe, Oo, Rsb, Oc):
    P = C.P
    P.op("dve", lambda e: e.reciprocal(out=Rsb[64:128, :], in_=Oe[64:128, :]), reads=["Oe"], writes=["Rsb_hi"])
    P.op("act", lambda e: e.activation(out=Oc[0:64, :], in_=Oe[0:64, :], func=AF.Copy), reads=["Oe"], writes=["Oc_lo"])
    P.op("dve", lambda e: e.reciprocal(out=Rsb[0:64, :], in_=Oo[0:64, :]), reads=["Oo"], writes=["Rsb_lo"])
    P.op("act", lambda e: e.activation(out=Oc[64:128, :], in_=Oo[64:128, :], func=AF.Copy), reads=["Oo"], writes=["Oc_hi"])


def emit_finalize_later(C, Rsb, Oc, t1, pmf, Mps, Mkey, gz_ap, gz_key, og_ap, og_key):
    P = C.P
    P.op("pe", lambda e: e.matmul(Mps[:, 0:512], lhsT=pmf[:], rhs=Rsb[:], start=True, stop=True),
         reads=["Rsb_hi", "Rsb_lo", "pmf"], writes=[Mkey])
    P.op("dve", lambda e: e.tensor_tensor(out=t1[:], in0=Oc[:], in1=Mps[:, 0:512], op=ALU.mult),
         reads=["Oc_lo", "Oc_hi", Mkey], writes=["t1_lo", "t1_hi"])
    P.op("pool", lambda e: e.tensor_tensor(out=og_ap, in0=t1[:], in1=gz_ap, op=ALU.mult),
         reads=["t1_lo", "t1_hi", gz_key], writes=[og_key])


def emit_silu_gate(C, zps, zkey, etmp, gz_ap, gz_key):
    P = C.P
    P.op("act", lambda e: e.activation(out=etmp[:], in_=zps, func=AF.Exp, scale=-1.0), reads=[zkey], writes=["etmp"])
    P.op("dve", lambda e: e.tensor_scalar(out=etmp[:], in0=etmp[:], scalar1=1.0, scalar2=None, op0=ALU.add),
         reads=["etmp"], writes=["etmp"])
    P.op("dve", lambda e: e.reciprocal(out=etmp[:], in_=etmp[:]), reads=["etmp"], writes=["etmp"])
    P.op("dve", lambda e: e.tensor_tensor(out=gz_ap, in0=zps, in1=etmp[:], op=ALU.mult),
         reads=[zkey, "etmp"], writes=[gz_key])


def emit_ple(C, get_tile, n_tiles, pd, pgbc, wgb, wpb, st, col0, xnb, idb, sq, hnT, pst, pbf, pT, sig, tmp,
             Sps, Mps, out_dram, final_norm=None, pre=None):
    P = C.P
    outs = []
    ss, ms, nh_ = st

    def sA(i):
        pre(i)

    def sB(i):
        xt, xkey = get_tile(i)
        emit_norm_stats(C, st, xt, xkey, col0 + i, sq)

    def sC(i):
        j = i % 2
        xt, xkey = get_tile(i)
        col = col0 + i
        P.op("act", lambda e: e.activation(out=xnb[j][:], in_=xt, func=AF.Copy, scale=ms[:, col:col + 1]),
             reads=[xkey, "rs%d" % col], writes=["xnb%d" % j])
        P.op("sp", lambda e: e.dma_start(out=pst[j][:], in_=pd[i * 128:(i + 1) * 128, :]), writes=["pst%d" % j], dma=True)
        P.op("pool", lambda e: e.tensor_copy(out=pbf[j][:], in_=pst[j][:]), reads=["pst%d" % j], writes=["pbf%d" % j])

    def sD(i):
        j = i % 2
        tp = Mps[0][:].bitcast(BF16).rearrange("p (k t) -> p k t", t=128)
        for kc in range(8):
            P.op("pe", lambda e, kc=kc: e.transpose(out=tp[:, kc, :], in_=xnb[j][:, kc * 128:(kc + 1) * 128], identity=idb[:]),
                 reads=["xnb%d" % j, "idb"], writes=["M0"])
        P.op("act", lambda e: e.activation(out=hnT[j][:], in_=tp, func=AF.Copy), reads=["M0"], writes=["hnT%d" % j])
        tp2 = Mps[1][:].bitcast(BF16).rearrange("p (k t) -> p k t", t=128)
        for kc in range(2):
            P.op("pe", lambda e, kc=kc: e.transpose(out=tp2[:, kc, :], in_=pbf[j][:, kc * 128:(kc + 1) * 128], identity=idb[:]),
                 reads=["pbf%d" % j, "idb"], writes=["M1"])
        P.op("dve", lambda e: e.tensor_copy(out=pT[j][:], in_=tp2[:, 0:2, :]), reads=["M1"], writes=["pT%d" % j])

    def sE(i):
        j = i % 2
        xt, xkey = get_tile(i)
        for nh in range(2):
            gps = Sps[0][:, nh * 512:(nh + 1) * 512]
            pps = Sps[1][:, nh * 512:(nh + 1) * 512]
            for kc in range(8):
                P.op("pe", lambda e, kc=kc, nh=nh, gps=gps: e.matmul(
                    gps, lhsT=hnT[j][:, kc, :], rhs=wgb[:, kc, nh * 512:(nh + 1) * 512], start=(kc == 0), stop=(kc == 7)),
                    reads=["hnT%d" % j, "wgb"], writes=["S0_%d" % nh])
            for kc in range(2):
                P.op("pe", lambda e, kc=kc, nh=nh, pps=pps: e.matmul(
                    pps, lhsT=pT[j][:, kc, :], rhs=wpb[:, kc, nh * 512:(nh + 1) * 512], start=(kc == 0), stop=(kc == 1)),
                    reads=["pT%d" % j, "wpb"], writes=["S1_%d" % nh])
            sg = sig[nh]
            P.op("act", lambda e, gps=gps, sg=sg: e.activation(out=sg[:], in_=gps, func=AF.Exp, scale=-1.0),
                 reads=["S0_%d" % nh], writes=["sig%d" % nh])
            P.op("dve", lambda e, sg=sg: e.tensor_scalar(out=sg[:], in0=sg[:], scalar1=1.0, scalar2=None, op0=ALU.add),
                 reads=["sig%d" % nh], writes=["sig%d" % nh])
            P.op("dve", lambda e, sg=sg: e.reciprocal(out=sg[:], in_=sg[:]), reads=["sig%d" % nh], writes=["sig%d" % nh])
            tm = tmp[nh]
            P.op("dve", lambda e, pps=pps, sg=sg, tm=tm: e.tensor_tensor(out=tm[:], in0=pps, in1=sg[:], op=ALU.mult),
                 reads=["S1_%d" % nh, "sig%d" % nh], writes=["tmp%d" % nh])
            xh = xt[:, nh * 512:(nh + 1) * 512]
            P.op("pool", lambda e, xh=xh, tm=tm: e.tensor_tensor(out=xh, in0=xh, in1=tm[:], op=ALU.add),
                 reads=[xkey, "tmp%d" % nh], writes=[xkey])

    def sF(i):
        xt, xkey = get_tile(i)
        emit_norm_stats(C, st, xt, xkey, final_norm[2] + i, sq)

    def sG(i):
        xt, xkey = get_tile(i)
        if final_norm is not None:
            fgbc, fgkey, fcol0 = final_norm
            c2 = fcol0 + i
            P.op("act", lambda e: e.activation(out=xt, in_=xt, func=AF.Copy, scale=ms[:, c2:c2 + 1]),
                 reads=[xkey, "rs%d" % c2], writes=[xkey])
            P.op("dve", lambda e: e.tensor_tensor(out=xt, in0=xt, in1=fgbc[:], op=ALU.mult),
                 reads=[xkey, fgkey], writes=[xkey])
        outs.append(P.op("sp", lambda e: e.dma_start(out=out_dram[i * 128:(i + 1) * 128, :], in_=xt), reads=[xkey], dma=True))

    stages = ([sA] if pre is not None else []) + [sB, sC, sD, sE] + ([sF] if final_norm is not None else []) + [sG]
    run_pipeline(n_tiles, stages)
    return outs


def alloc_common(C, layer1=False):
    B = {}
    if not layer1:
        B["xres"] = C.sb("xres", [128, 16, 1024], F32)
        B["og"] = [[C.sb("og%d_%d" % (i, g_), [128, 512], BF16) for g_ in range(4)] for i in range(2)]
        B["wo"] = [C.sb("wo%d" % i, [128, 1024], BF16) for i in range(2)]
    B["ss"] = C.sb("ss", [128, 96], F32)
    B["ms"] = C.sb("ms", [128, 96], F32)
    B["nh"] = C.sb("nh", [128, 1], F32)
    B["sq"] = C.sb("sq", [128, 1024], BF16)
    B["xnb"] = [C.sb("xnb%d" % i, [128, 1024], BF16) for i in range(2)]
    B["idb"] = C.sb("idb", [128, 128], BF16)
    B["pmf"] = C.sb("pmf", [128, 128], F32)
    B["gnc"] = C.sb("gnc_sb", [128, 8], F32)
    B["gpc"] = C.sb("gpc_sb", [128, 8], F32)
    B["Rsb"] = C.sb("Rsb", [128, 512], F32)
    B["Rsw"] = C.sb("Rsw", [128, 512], F32)
    B["t1"] = C.sb("t1", [128, 512], F32)
    B["hnT"] = [C.sb("hnT%d" % i, [128, 8, 128], BF16) for i in range(2)]
    B["pst"] = [C.sb("pst%d" % i, [128, 256], F32) for i in range(2)]
    B["pbf"] = [C.sb("pbf%d" % i, [128, 256], BF16) for i in range(2)]
    B["pT"] = [C.sb("pT%d" % i, [128, 2, 128], BF16) for i in range(2)]
    B["ring"] = Ring(C, n=2, cols=1024)
    B["S"] = [C.ps("S%d" % i, [128, 1024], F32) for i in range(2)]
    B["Oe"] = C.ps("Oe", [128, 512], F32)
    B["Oo"] = C.ps("Oo", [128, 512], F32)
    B["M"] = [C.ps("M%d" % i, [128, 512], F32) for i in range(2)]
    return B


def emit_consts(C, B, ident, perm):
    P = C.P
    B["ring"].load(ident, 128, [128], B["idb"][:], "idb")
    P.op("sp", lambda e: e.dma_start(out=B["pmf"][:], in_=perm), writes=["pmf"], dma=True)
    P.op("pool", lambda e: e.memset(B["nh"][:], -0.5), writes=["nh"])


def emit_outproj(C, B, ogs, og_keys, wos, wo_keys, tiles, mctr):
    P = C.P
    xres = B["xres"]
    for tt, i in enumerate(tiles):
        for nh in range(2):
            m = mctr[0] % 2
            mctr[0] += 1
            Mp = B["M"][m]
            for k_ in range(2):
                P.op("pe", lambda e, tt=tt, nh=nh, Mp=Mp, k_=k_: e.matmul(
                    Mp[:, 0:512], lhsT=ogs[k_][:, tt * 128:(tt + 1) * 128], rhs=wos[k_][:, nh * 512:(nh + 1) * 512],
                    start=(k_ == 0), stop=(k_ == 1)), reads=[og_keys[k_], wo_keys[k_]], writes=["M%d" % m])
            xh = xres[:, i, nh * 512:(nh + 1) * 512]
            P.op("dve", lambda e, xh=xh, Mp=Mp: e.tensor_tensor(out=xh, in0=xh, in1=Mp[:, 0:512], op=ALU.add),
                 reads=["xr%d" % i, "M%d" % m], writes=["xr%d" % i])


def build_l0(nc, es, pre="", xo=None, sem_es=None, shared=None):
    C = Ctx(nc, es, pre, sem_es)
    P = C.P
    shared = {} if shared is None else shared

    def sdin(name, shape):
        if name not in shared:
            shared[name] = C.din(name, shape)
        return shared[name]

    xk = C.din("xk", [2560, 1024])
    pd = C.din("pd", [2048, 256])
    bq = C.din("bq", [128, 2048])
    gn = sdin("gnc", [128, 8])
    gp = sdin("gpc", [128, 8])
    w_in = sdin("w_in", [1024, 4096])
    w_out = sdin("w_out", [1024, 1024])
    wg = sdin("wg", [1024, 1024])
    wp = sdin("wp", [256, 1024])
    tabB = sdin("tabB", [8, 128, 2816])
    aoh = sdin("aoh", [128, 1024])
    ident = sdin("ident", [128, 128])
    perm = sdin("perm", [128, 128])
    if xo is None:
        xo = C.dout("xo", [2048, 1024])

    B = alloc_common(C)
    ring = B["ring"]
    xres = B["xres"]
    xnT = C.sb("xnT", [128, 8, 2560], BF16)
    hx = [C.sb("hx%d" % i, [128, 1024], F32) for i in range(2)]
    AR_COLS, VP_OFF = 12544, 6656
    ar = C.sb("ar", [128, 12544], BF16)
    qTe = ar[:, 0:2048]
    qTo = ar[:, 2048:4096]
    kT = ar[:, 4096:6656]
    vP = ar[:, 6656:10496].rearrange("p (t c) -> p t c", c=192)
    gz = ar[:, 10496:12544]
    wgb = ar[:, 0:8192].rearrange("p (k n) -> p k n", n=1024)
    wpb = ar[:, 8192:10240].rearrange("p (k n) -> p k n", n=1024)
    wq = C.sb("wq", [128, 8, 128], BF16)
    wk = C.sb("wk", [128, 8, 128], BF16)
    wv = C.sb("wv", [128, 8, 128], BF16)
    wz = C.sb("wz", [128, 8, 128], BF16)
    tabE = C.sb("tabE", [128, 2816], BF16)
    bqb = C.sb("bqb", [128, 2048], BF16)
    aohb = C.sb("aohb", [128, 1024], BF16)
    PT = [C.sb("PT%d" % i, [128, 1024], BF16) for i in range(3)]
    etmp = C.sb("etmp", [128, 512], F32)
    sig = [etmp, B["Rsb"]]
    tmp = [B["t1"], B["Rsw"]]
    st = (B["ss"], B["ms"], B["nh"])

    emit_consts(C, B, ident, perm)
    P.op("sp", lambda e: e.dma_start(out=B["gnc"][:], in_=gn), writes=["gnc"], dma=True)
    P.op("sp", lambda e: e.dma_start(out=B["gpc"][:], in_=gp), writes=["gpc"], dma=True)
    ring.load(aoh, 128, [1024], aohb[:], "aohb")
    for h2 in range(2):
        ring.load(bq[:, h2 * 1024:(h2 + 1) * 1024], 128, [1024], bqb[:, h2 * 1024:(h2 + 1) * 1024], "bqb")
    P.op("pool", lambda e: e.memset(qTe[64:128, :], 0.0), writes=["qTe_z"])
    P.op("pool", lambda e: e.memset(qTo[0:64, :], 0.0), writes=["qTo_z"])
    P.op("pool", lambda e: e.memset(vP[:, :, 64:128], 1.0), writes=["vP_ones"])

    def p1_tile(bt):
        if 2 <= bt < 18:
            return xres[:, bt - 2, :], "xr%d" % (bt - 2)
        return hx[bt % 2][:], "hx%d" % (bt % 2)

    def p1_s0(bt):
        xt, xkey = p1_tile(bt)
        P.op("sp", lambda e: e.dma_start(out=xt, in_=xk[bt * 128:(bt + 1) * 128, :]), writes=[xkey], dma=True)
        emit_norm_stats(C, st, xt, xkey, bt, B["sq"])

    def p1_s1(bt):
        xt, xkey = p1_tile(bt)
        j = bt % 2
        tp = B["M"][j][:].bitcast(BF16).rearrange("p (k t) -> p k t", t=128)
        emit_norm_apply_T(C, st, xt, xkey, bt, None, None, B["xnb"][j][:], "xnb%d" % j, tp, "M%d" % j, B["idb"],
                          xnT[:, :, bt * 128:(bt + 1) * 128], "xnT%d" % bt, copy_eng="dve")

    run_pipeline(20, [p1_s0, p1_s1])

    w_in_v = w_in.rearrange("(kc p) n -> p kc n", p=128)
    mctr = [0, 0]
    pending = []
    sctr = [0]
    pctr = [0]
    octr = [0]

    def next_m():
        m = mctr[0] % 2
        mctr[0] += 1
        return m

    def load_inproj(c_):
        for wt, wkey, off in ((wq, "wq", 0), (wk, "wk", 1024), (wv, "wv", 2048), (wz, "wz", 3072)):
            ring.load(w_in_v[:, :, off + c_ * 128: off + (c_ + 1) * 128], 128, [8, 128], wt[:], wkey,
                      gain=(B["gnc"], "gnc", 0))

    load_inproj(0)
    for c in range(8):
        wo = B["wo"][c % 2]
        for ch, (lo, hi) in enumerate(((0, 1024), (1024, 2048), (2048, 2816))):
            ring.load(tabB[c, :, lo:hi], 128, [hi - lo], tabE[:, lo:hi], "tabE%d" % ch, conv="exp")

        def flush_one():
            _, g_, c_ = pending.pop(0)
            m = next_m()
            og = B["og"][c_ % 2][g_]
            emit_finalize_later(C, B["Rsb"], B["Rsw"], B["t1"], B["pmf"], B["M"][m], "M%d" % m,
                                gz[:, g_ * 512:(g_ + 1) * 512], "gz%d" % g_, og[:], "og%d_%d" % (c_ % 2, g_))
            if c_ % 2 == 1:
                emit_outproj(C, B, [B["og"][0][g_], B["og"][1][g_]], ["og0_%d" % g_, "og1_%d" % g_],
                             B["wo"], ["wo0", "wo1"], [4 * g_ + a_ for a_ in range(4)], mctr)

        for blk in range(4):
            m = next_m()
            Mp = B["M"][m]
            t0 = 256 + blk * 512
            rk = ["xnT%d" % (t0 // 128 + a) for a in range(4)]
            for kc in range(8):
                P.op("pe", lambda e, kc=kc, Mp=Mp, t0=t0: e.matmul(Mp[:, 0:512], lhsT=wq[:, kc, :], rhs=xnT[:, kc, t0:t0 + 512],
                                                                   start=(kc == 0), stop=(kc == 7)),
                     reads=rk + ["wq"], writes=["M%d" % m])
            P.op("act", lambda e, Mp=Mp, blk=blk: e.activation(out=qTe[0:64, blk * 512:(blk + 1) * 512], in_=Mp[0:64, 0:512],
                                                                func=AF.Copy), reads=["M%d" % m], writes=["qTe"])
            P.op("act", lambda e, Mp=Mp, blk=blk: e.activation(out=qTo[64:128, blk * 512:(blk + 1) * 512], in_=Mp[64:128, 0:512],
                                                                func=AF.Copy), reads=["M%d" % m], writes=["qTo"])
        for blk in range(5):
            m = next_m()
            Mp = B["M"][m]
            t0 = blk * 512
            rk = ["xnT%d" % (t0 // 128 + a) for a in range(4)]
            for kc in range(8):
                P.op("pe", lambda e, kc=kc, Mp=Mp, t0=t0: e.matmul(Mp[:, 0:512], lhsT=wk[:, kc, :], rhs=xnT[:, kc, t0:t0 + 512],
                                                                   start=(kc == 0), stop=(kc == 7)),
                     reads=rk + ["wk"], writes=["M%d" % m])
            if blk % 2 == 0:
                P.op("dve", lambda e, Mp=Mp, t0=t0: e.tensor_copy(out=kT[:, t0:t0 + 512], in_=Mp[:, 0:512]),
                     reads=["M%d" % m], writes=["kT%d" % blk])
            else:
                P.op("act", lambda e, Mp=Mp, t0=t0: e.activation(out=kT[:, t0:t0 + 512], in_=Mp[:, 0:512], func=AF.Copy),
                     reads=["M%d" % m], writes=["kT%d" % blk])
        while pending:
            flush_one()
        ring.load(w_out[c * 128:(c + 1) * 128, :], 128, [1024], wo[:], "wo%d" % (c % 2))
        for blk in range(4):
            m = next_m()
            Mp = B["M"][m]
            t0 = 256 + blk * 512
            rk = ["xnT%d" % (t0 // 128 + a) for a in range(4)]
            for kc in range(8):
                P.op("pe", lambda e, kc=kc, Mp=Mp, t0=t0: e.matmul(Mp[:, 0:512], lhsT=wz[:, kc, :], rhs=xnT[:, kc, t0:t0 + 512],
                                                                   start=(kc == 0), stop=(kc == 7)),
                     reads=rk + ["wz"], writes=["M%d" % m])
            emit_silu_gate(C, Mp[:, 0:512], "M%d" % m, etmp, gz[:, blk * 512:(blk + 1) * 512], "gz%d" % blk)
        for t4 in range(5):
            m = next_m()
            Mp = B["M"][m]
            Mv = Mp[:, 0:512].rearrange("p (t c) -> p t c", c=128)
            for a in range(4):
                bt = t4 * 4 + a
                for kc in range(8):
                    P.op("pe", lambda e, kc=kc, Mv=Mv, a=a, bt=bt: e.matmul(
                        Mv[:, a, :], lhsT=xnT[:, kc, bt * 128:(bt + 1) * 128], rhs=wv[:, kc, :],
                        start=(kc == 0), stop=(kc == 7)), reads=["xnT%d" % bt, "wv"], writes=["M%d" % m])
            vdst = bass.AP(ar, VP_OFF + t4 * 4 * 192, [[AR_COLS, 128], [192, 4], [128, 2], [1, 64]])
            P.op("dve", lambda e, Mv=Mv, vdst=vdst: e.tensor_copy(out=vdst, in_=Mv.rearrange("p t (h c) -> p t h c", h=2)),
                 reads=["M%d" % m], writes=["vPe%d" % t4, "vPo%d" % t4])

        if c < 7:
            load_inproj(c + 1)
        its = [(g, hh, tp_) for g in range(4) for hh in range(2) for tp_ in range(4)]
        SKEW = 2

        def qk_stage(n):
            g, hh, tp_ = its[n]
            qm = qTe if hh == 0 else qTo
            qkeys = ["qTe", "qTe_z"] if hh == 0 else ["qTo", "qTo_z"]
            s_ = sctr[0] % 2
            sctr[0] += 1
            Sp = B["S"][s_]
            for u in range(2):
                t = 2 * tp_ + (1 - u)
                bt = 4 * g + t
                P.op("pe", lambda e, Sp=Sp, u=u, bt=bt, qm=qm, g=g: e.matmul(
                    Sp[:, u * 512:(u + 1) * 512], lhsT=kT[:, bt * 128:(bt + 1) * 128], rhs=qm[:, g * 512:(g + 1) * 512],
                    start=True, stop=False), reads=["kT%d" % (bt // 4)] + qkeys, writes=["S%d" % s_])
                P.op("pe", lambda e, Sp=Sp, u=u, t=t, g=g: e.matmul(
                    Sp[:, u * 512:(u + 1) * 512], lhsT=aohb[:, t * 128:(t + 1) * 128], rhs=bqb[:, g * 512:(g + 1) * 512],
                    start=False, stop=True), reads=["aohb", "bqb"], writes=["S%d" % s_])
            pj = pctr[0] % 3
            pctr[0] += 1
            Pt = PT[pj]
            P.op("act", lambda e, Pt=Pt, Sp=Sp: e.activation(out=Pt[:], in_=Sp[:], func=AF.Exp, scale=0.125),
                 reads=["S%d" % s_], writes=["PT%d" % pj])
            tab_ap = bass.AP(tabE, hh * 1408 + (12 - 4 * tp_) * 64, [[2816, 128], [128, 2], [64, 8], [1, 64]])
            Pv = Pt[:].rearrange("p (u q c) -> p u q c", u=2, q=8)
            P.op("dve", lambda e, Pv=Pv, tab_ap=tab_ap: e.tensor_tensor(out=Pv, in0=Pv, in1=tab_ap, op=ALU.mult),
                 reads=["PT%d" % pj, "tabE0", "tabE1", "tabE2"], writes=["PT%d" % pj])
            return pj

        pjs = {}
        for n in range(len(its) + SKEW):
            if n < len(its):
                pjs[n] = qk_stage(n)
            m_ = n - SKEW
            if m_ < 0:
                continue
            g, hh, tp_ = its[m_]
            pj = pjs[m_]
            Pt = PT[pj]
            Oh = B["Oe"] if hh == 0 else B["Oo"]
            okey = "Oe" if hh == 0 else "Oo"
            for u in range(2):
                t = 2 * tp_ + (1 - u)
                bt = 4 * g + t
                first = (tp_ == 0 and u == 0)
                last = (tp_ == 3 and u == 1)
                P.op("pe", lambda e, Oh=Oh, bt=bt, hh=hh, Pt=Pt, u=u, first=first, last=last: e.matmul(
                    Oh[:, 0:512], lhsT=vP[:, bt, hh * 64:hh * 64 + 128], rhs=Pt[:, u * 512:(u + 1) * 512],
                    start=first, stop=last),
                    reads=["vPe%d" % (bt // 4), "vPo%d" % (bt // 4), "vP_ones", "PT%d" % pj], writes=[okey])
            if hh == 1 and tp_ == 3:
                emit_finalize_now(C, B["Oe"], B["Oo"], B["Rsb"], B["Rsw"])
                pending.append((n + 3 if g < 3 else 10 ** 9, g, c))
            while pending and pending[0][0] <= n:
                flush_one()

    while pending:
        flush_one()
    alias = ["qTe", "qTo", "qTe_z", "qTo_z", "vP_ones"] + ["kT%d" % i for i in range(5)] + ["gz%d" % i for i in range(4)] + \
            ["vPe%d" % i for i in range(5)] + ["vPo%d" % i for i in range(5)]
    wg_v = wg.rearrange("(kc p) n -> p kc n", p=128)
    for nb in range(8):
        ring.load(wg_v[:, :, nb * 128:(nb + 1) * 128], 128, [8, 128], wgb[:, :, nb * 128:(nb + 1) * 128], "wgb",
                  extra_writes=alias if nb == 0 else (), gain=(B["gpc"], "gpc", 0))
    wp_v = wp.rearrange("(kc p) n -> p kc n", p=128)
    for nb in range(2):
        ring.load(wp_v[:, :, nb * 512:(nb + 1) * 512], 128, [2, 512], wpb[:, :, nb * 512:(nb + 1) * 512], "wpb",
                  extra_writes=alias if nb == 0 else ())
    outs = emit_ple(C, lambda i: (xres[:, i, :], "xr%d" % i), 16, pd, None, wgb, wpb, st, 20, B["xnb"], B["idb"],
                    B["sq"], B["hnT"], B["pst"], B["pbf"], B["pT"], sig, tmp, B["S"], B["M"], xo)
    P.emit(final_wait_ops=outs)
    return nc


def _consts():
    ident = np.eye(128, dtype=np.float32)
    perm = np.zeros((128, 128), np.float32)
    for i in range(64):
        perm[64 + i, i] = 1.0
        perm[i, 64 + i] = 1.0
    return ident, perm


def _na_tables(rpb):
    kc = np.arange(64)
    qc = np.arange(64)
    cs = np.clip(qc - 8, 0, 48)
    colvalid = (kc[:, None] >= cs[None, :]) & (kc[:, None] < cs[None, :] + 16)
    coff = np.clip(kc[:, None] - qc[None, :] + 15, 0, 30)
    tab = np.full((16, 128, 22, 64), NEG, np.float32)
    for e in range(22):
        for half in range(2):
            dr = 10 - e + half
            if -7 <= dr <= 7:
                vals = rpb[:, dr + 7][:, coff]
                tab[:, half * 64:(half + 1) * 64, e, :] = np.where(colvalid[None], vals, NEG)
    return np.ascontiguousarray(tab.reshape(8, 2, 128, 22, 64).transpose(0, 2, 1, 3, 4).reshape(8, 128, 2816))


def _na_rowmask(hf):
    bq = np.zeros((128, 4, 8, 64), np.float32)
    for g in range(4):
        for qi in range(8):
            r = 32 * hf + 8 * g + qi
            rs = min(max(r - 4, 0), 56)
            for j in range(16):
                R = 32 * hf - 4 + 8 * g + j
                ok = (0 <= R <= 63) and (rs <= R < rs + 8)
                bq[j, g, qi, :] = 0.0 if ok else NEG
    aoh = np.zeros((128, 8, 128), np.float32)
    for t in range(8):
        aoh[2 * t, t, 0:64] = 1.0
        aoh[2 * t + 1, t, 64:128] = 1.0
    return bq.reshape(128, 2048), aoh.reshape(128, 1024)


_NC_CACHE = {}


def _get_nc(name, builder):
    if name not in _NC_CACHE:
        nc = bass.Bass("TRN2", target_bir_lowering=False)
        with contextlib.ExitStack() as es:
            builder(nc, es)
        _NC_CACHE[name] = nc
    return _NC_CACHE[name]


def run_l0(x, p, norm_g, na_w_in, na_rpb, na_w_out, ple_norm, ple_w_gate, ple_w_proj):
    x = np.asarray(x, np.float32)
    ident, perm = _consts()
    tabB = _na_tables(np.asarray(na_rpb[0], np.float32))
    gn = np.ascontiguousarray(np.asarray(norm_g[0], np.float32).reshape(8, 128).T)
    gp = np.ascontiguousarray(np.asarray(ple_norm[0], np.float32).reshape(8, 128).T)
    shared = dict(gnc=gn, gpc=gp, w_in=np.ascontiguousarray(na_w_in[0], dtype=np.float32),
                  w_out=np.ascontiguousarray(na_w_out[0], dtype=np.float32),
                  wg=np.ascontiguousarray(ple_w_gate[0], dtype=np.float32),
                  wp=np.ascontiguousarray(ple_w_proj[0], dtype=np.float32), tabB=tabB, ident=ident, perm=perm)
    in_maps = []
    for core in range(8):
        b, hf = core // 2, core % 2
        xk = np.zeros((40, 64, 1024), np.float32)
        xb = x[b].reshape(64, 64, 1024)
        for lr in range(40):
            R = 32 * hf - 4 + lr
            if 0 <= R <= 63:
                xk[lr] = xb[R]
        bq, aoh = _na_rowmask(hf)
        m = dict(shared)
        m.update(xk=xk.reshape(2560, 1024), pd=np.ascontiguousarray(p[0, b, hf * 2048:(hf + 1) * 2048], dtype=np.float32),
                 bq=bq, aoh=aoh)
        in_maps.append(m)
    nc = _get_nc("l0", build_l0)
    res = run_bass_kernel_spmd(nc, in_maps, core_ids=list(range(8)))
    x1 = np.zeros((4, 4096, 1024), np.float32)
    for core in range(8):
        b, hf = core // 2, core % 2
        x1[b, hf * 2048:(hf + 1) * 2048] = res.results[core]["xo"]
    return x1


def build_l1(nc, es, pre="", xf=None, sem_es=None):
    C = Ctx(nc, es, pre, sem_es)
    P = C.P
    if xf is None:
        xoth = C.din("xoth", [2048, 1024])
        xown = C.din("xown", [2048, 1024])
    else:
        xown, xoth = xf

    def xf_tile(T):
        src = xown if T < 16 else xoth
        return src[(T % 16) * 128:(T % 16 + 1) * 128, :]
    pd = C.din("pd", [2048, 256])
    gn = C.din("gnc", [128, 8])
    gp = C.din("gpc", [128, 8])
    gf = C.din("gf", [128, 1024])
    qn = C.din("qnc", [128, 3])
    kvn = C.din("kvnc", [128, 2])
    w_in = C.din("w_in", [1024, 1696])
    wqa = C.din("wqa", [384, 1536])
    wqs = C.din("wqs", [384, 1536])
    w_kvb = C.din("w_kvb", [256, 2048])
    w_out = C.din("w_out", [1024, 1024])
    wg = C.din("wg", [1024, 1024])
    wp = C.din("wp", [256, 1024])
    costok = C.din("costok", [128, 512])
    sintok = C.din("sintok", [128, 512])
    cosTd = C.din("cosT", [128, 2048])
    sinTd = C.din("sinT", [128, 2048])
    ident = C.din("ident", [128, 128])
    perm = C.din("perm", [128, 128])
    xo = C.dout("xo", [2048, 1024])

    B = alloc_common(C, layer1=True)
    ring = B["ring"]
    st = (B["ss"], B["ms"], B["nh"])
    xnT = C.sb("xnT", [128, 8, 2048], BF16)
    ckvT = C.sb("ckvT", [128, 2, 4096], BF16)
    kpeT = C.sb("kpeT", [128, 4096], BF16)
    cqT = C.sb("cqT", [128, 3, 2048], BF16)
    cosT = C.sb("cosTb", [128, 2048], BF16)
    sinT = C.sb("sinTb", [128, 2048], BF16)
    ogT = C.sb("ogT", [128, 8, 2048], BF16)
    hx = [C.sb("hx%d" % i, [128, 1024], F32) for i in range(2)]
    qnc = C.sb("qnc_sb", [128, 3], F32)
    kvnc = C.sb("kvnc_sb", [128, 2], F32)
    xkT = B["hnT"]
    lat = [B["pst"][i][:].bitcast(BF16) for i in range(2)]
    wqA = [C.sb("wqA%d" % i, [128, 3, 96], BF16) for i in range(2)]
    wqB = [C.sb("wqB%d" % i, [128, 3, 96], BF16) for i in range(2)]
    wkn = [C.sb("wkn%d" % i, [128, 2, 64], BF16) for i in range(2)]
    wvp = C.sb("wvp", [128, 2, 128], BF16)
    wz = C.sb("wz", [128, 8, 128], BF16)
    PT = [C.sb("PT%d" % i, [128, 1024], BF16) for i in range(3)]
    etmp = C.sb("etmp", [128, 512], F32)
    AR_COLS, VP_OFF = 20480, 12288
    ar = C.sb("ar", [128, 20480], BF16)
    qT = [ar[:, 0:2048], ar[:, 2048:4096]]
    kT = [ar[:, 4096:8192], ar[:, 8192:12288]]
    vP = ar[:, 12288:18432].rearrange("p (t c) -> p t c", c=192)
    gz = ar[:, 18432:20480]
    krope = ar[:, 0:2048].bitcast(F32).rearrange("p (t c) -> p t c", c=32)
    kpe_tok = ar[:, 2048:5120].rearrange("p (t c) -> p t c", c=96)
    ctk = ar[:, 5120:6144].bitcast(F32).rearrange("p (t c) -> p t c", c=16)
    stk = ar[:, 6144:7168].bitcast(F32).rearrange("p (t c) -> p t c", c=16)
    wkvr = ar[:, 7168:9472].rearrange("p (k n) -> p k n", n=288)
    wcq = ar[:, 9472:12544].rearrange("p (k n) -> p k n", n=384)
    wgb = ar[:, 0:8192].rearrange("p (k n) -> p k n", n=1024)
    wpb = ar[:, 8192:10240].rearrange("p (k n) -> p k n", n=1024)
    woa = ar[:, 10240:18432].rearrange("p (k n) -> p k n", n=1024)
    ra = B["Rsb"]
    rb = B["Rsw"]
    sig = [etmp, B["Rsb"]]
    tmp = [B["t1"], B["Rsw"]]
    fgbc = C.sb("fgbc", [128, 1024], F32)

    emit_consts(C, B, ident, perm)
    P.op("sp", lambda e: e.dma_start(out=B["gnc"][:], in_=gn), writes=["gnc"], dma=True)
    P.op("sp", lambda e: e.dma_start(out=B["gpc"][:], in_=gp), writes=["gpc"], dma=True)
    P.op("sp", lambda e: e.dma_start(out=fgbc[:], in_=gf), writes=["fgbc"], dma=True)
    P.op("sp", lambda e: e.dma_start(out=qnc[:], in_=qn), writes=["qnc"], dma=True)
    P.op("sp", lambda e: e.dma_start(out=kvnc[:], in_=kvn), writes=["kvnc"], dma=True)
    P.op("sp", lambda e: e.dma_start(out=ctk.rearrange("p t c -> p (t c)"), in_=costok), writes=["ctk"], dma=True)
    P.op("sp", lambda e: e.dma_start(out=stk.rearrange("p t c -> p (t c)"), in_=sintok), writes=["stk"], dma=True)
    for h2 in range(2):
        ring.load(cosTd[:, h2 * 1024:(h2 + 1) * 1024], 128, [1024], cosT[:, h2 * 1024:(h2 + 1) * 1024], "cosT")
        ring.load(sinTd[:, h2 * 1024:(h2 + 1) * 1024], 128, [1024], sinT[:, h2 * 1024:(h2 + 1) * 1024], "sinT")
    w_in_v = w_in.rearrange("(kc p) n -> p kc n", p=128)
    for k4 in range(4):
        ring.load(w_in_v[:, 2 * k4:2 * k4 + 2, 384:672], 128, [2, 288], wkvr[:, 2 * k4:2 * k4 + 2, :], "wkvr",
                  gain=(B["gnc"], "gnc", 2 * k4))
    for k4 in range(4):
        ring.load(w_in_v[:, 2 * k4:2 * k4 + 2, 0:384], 128, [2, 384], wcq[:, 2 * k4:2 * k4 + 2, :], "wcq",
                  gain=(B["gnc"], "gnc", 2 * k4))
    P.op("pool", lambda e: e.memset(kpe_tok[:, :, 0:64], 0.0), writes=["kpe_z"])

    def psum_bf(t):
        return t[:].bitcast(BF16).rearrange("p (k t) -> p k t", t=128)

    ss, ms, nh_ = st

    def latent_stages(n_tiles, load, col_x, col_l, width, wmat, wkey, gvec, gvkey, nch, dstT_of, dst_of, dst_key, extra=None):
        def s0(T):
            j = T % 2
            load(T, j)
            emit_norm_stats(C, st, hx[j][:], "hx%d" % j, col_x + T, B["sq"])

        def s1(T):
            j = T % 2
            dT, dkey = dstT_of(T, j)
            emit_norm_apply_T(C, st, hx[j][:], "hx%d" % j, col_x + T, None, None, B["xnb"][j][:], "xnb%d" % j,
                              psum_bf(B["M"][j]), "M%d" % j, B["idb"], dT, dkey, copy_eng="dve")

        def s2(T):
            j = T % 2
            dT, dkey = dstT_of(T, j)
            Sp = B["S"][j]
            tot = width + (32 if extra else 0)
            for kc in range(8):
                P.op("pe", lambda e, kc=kc: e.matmul(Sp[:, 0:tot], lhsT=dT[:, kc, :], rhs=wmat[:, kc, :],
                                                     start=(kc == 0), stop=(kc == 7)),
                     reads=[dkey, wkey], writes=["S%d" % j])
            emit_norm_stats(C, st, Sp[:, 0:width], "S%d" % j, col_l + T, B["sq"], width=width)

        def s3(T):
            j = T % 2
            Sp = B["S"][j]
            col = col_l + T
            Ob = B["Oe"] if j == 0 else B["Oo"]
            okey = "Oe" if j == 0 else "Oo"
            if extra:
                extra(T, Sp, "S%d" % j, col)
            emit_norm_apply_T(C, st, Sp[:, 0:width], "S%d" % j, col, gvec, gvkey, lat[j][:, 0:width], "pst%d" % j,
                              psum_bf(Ob), okey, B["idb"], dst_of(T), dst_key % T, nch=nch, copy_eng="dve", scale_eng="dve")

        run_pipeline(n_tiles, [s0, s1, s2, s3])

    def load_own(T, j):
        P.op("sp", lambda e: e.dma_start(out=hx[j][:], in_=xown[T * 128:(T + 1) * 128, :]), writes=["hx%d" % j], dma=True)

    latent_stages(16, load_own, 64, 80, 384, wcq, "wcq", None, None, 3,
                  lambda T, j: (xnT[:, :, T * 128:(T + 1) * 128], "xnT%d" % T),
                  lambda T: cqT[:, :, T * 128:(T + 1) * 128], "cqT%d")

    def load_any(T, j):
        P.op("sp", lambda e: e.dma_start(out=hx[j][:], in_=xf_tile(T)), writes=["hx%d" % j], dma=True)

    def krope_copy(T, Sp, skey, col):
        P.op("dve", lambda e: e.tensor_copy(out=krope[:, T, :], in_=Sp[:, 256:288]), reads=[skey, "rs%d" % col], writes=["krope"])

    latent_stages(32, load_any, 0, 32, 256, wkvr, "wkvr", None, None, 2,
                  lambda T, j: (xkT[j][:], "hnT%d" % j),
                  lambda T: ckvT[:, :, T * 128:(T + 1) * 128], "ckvT%d", extra=krope_copy)

    x1v = krope[:, :, 0:16]
    x2v = krope[:, :, 16:32]
    rav = ra[:].rearrange("p (t c) -> p t c", c=16)
    rbv = rb[:].rearrange("p (t c) -> p t c", c=16)
    P.op("dve", lambda e: e.tensor_tensor(out=rav, in0=x1v, in1=ctk, op=ALU.mult), reads=["krope", "ctk"], writes=["ra"])
    P.op("dve", lambda e: e.tensor_tensor(out=rbv, in0=x2v, in1=stk, op=ALU.mult), reads=["krope", "stk"], writes=["rb"])
    P.op("dve", lambda e: e.tensor_tensor(out=kpe_tok[:, :, 64:80], in0=rav, in1=rbv, op=ALU.subtract),
         reads=["ra", "rb"], writes=["kpe1"])
    P.op("dve", lambda e: e.tensor_tensor(out=rav, in0=x1v, in1=stk, op=ALU.mult), reads=["krope", "stk", "kpe1"], writes=["ra"])
    P.op("dve", lambda e: e.tensor_tensor(out=rbv, in0=x2v, in1=ctk, op=ALU.mult), reads=["krope", "ctk", "kpe1"], writes=["rb"])
    P.op("dve", lambda e: e.tensor_tensor(out=kpe_tok[:, :, 80:96], in0=rav, in1=rbv, op=ALU.add),
         reads=["ra", "rb"], writes=["kpe2"])
    for T8 in range(4):
        s = T8 % 2
        tpv = B["S"][s][:, 0:512].bitcast(BF16).rearrange("p (k t) -> p k t", t=128)
        for a in range(8):
            T = T8 * 8 + a
            P.op("pe", lambda e, a=a, T=T, tpv=tpv: e.transpose(out=tpv[0:96, a, :], in_=kpe_tok[:, T, 0:96], identity=B["idb"][:]),
                 reads=["kpe1", "kpe2", "kpe_z", "idb"], writes=["S%d" % s])
        dst = kpeT[64:96, T8 * 1024:(T8 + 1) * 1024].rearrange("p (a t) -> p a t", t=128)
        if T8 % 2 == 0:
            P.op("dve", lambda e, dst=dst, tpv=tpv: e.tensor_copy(out=dst, in_=tpv[64:96, :, :]), reads=["S%d" % s],
                 writes=["kpeT%d" % T8])
        else:
            P.op("act", lambda e, dst=dst, tpv=tpv: e.activation(out=dst, in_=tpv[64:96, :, :], func=AF.Copy), reads=["S%d" % s],
                 writes=["kpeT%d" % T8])

    mctr = [0]
    sctr = [0]
    pctr = [0]

    def next_m():
        m = mctr[0] % 2
        mctr[0] += 1
        return m

    wqa_v = wqa.rearrange("(kc p) n -> p kc n", p=128)
    wqs_v = wqs.rearrange("(kc p) n -> p kc n", p=128)
    wkv_v = w_kvb.rearrange("(kc p) n -> p kc n", p=128)
    alias_ac = ["krope", "kpe1", "kpe2", "kpe_z", "ctk", "stk", "wkvr", "wcq"]
    scale = float(96.0 ** -0.5)
    for c in range(8):
        for hh in range(2):
            h = 2 * c + hh
            ring.load(wqa_v[:, :, h * 96:(h + 1) * 96], 128, [3, 96], wqA[hh][:], "wqA%d" % hh, gain=(qnc, "qnc", 0))
            ring.load(wqs_v[:, :, h * 96:(h + 1) * 96], 128, [3, 96], wqB[hh][:], "wqB%d" % hh, gain=(qnc, "qnc", 0))
            ring.load(wkv_v[:, :, h * 128:h * 128 + 64], 128, [2, 64], wkn[hh][:], "wkn%d" % hh, gain=(kvnc, "kvnc", 0))
            ring.load(wkv_v[:, :, h * 128 + 64:h * 128 + 128], 128, [2, 64], wvp[:, :, hh * 64:(hh + 1) * 64], "wvp%d" % hh,
                      gain=(kvnc, "kvnc", 0))
        ring.load(w_in_v[:, :, 672 + c * 128:672 + (c + 1) * 128], 128, [8, 128], wz[:], "wz", gain=(B["gnc"], "gnc", 0))
        first_alias = alias_ac if c == 0 else []

        for hh in range(2):
            for blk in range(4):
                mA = next_m()
                mB = next_m()
                MA = B["M"][mA]
                MB = B["M"][mB]
                rk = ["cqT%d" % (blk * 4 + a) for a in range(4)]
                for kc in range(3):
                    P.op("pe", lambda e, kc=kc, MA=MA, hh=hh, blk=blk: e.matmul(
                        MA[0:96, 0:512], lhsT=wqA[hh][:, kc, :], rhs=cqT[:, kc, blk * 512:(blk + 1) * 512],
                        start=(kc == 0), stop=(kc == 2)), reads=rk + ["wqA%d" % hh], writes=["M%d" % mA])
                for kc in range(3):
                    P.op("pe", lambda e, kc=kc, MB=MB, hh=hh, blk=blk: e.matmul(
                        MB[0:96, 0:512], lhsT=wqB[hh][:, kc, :], rhs=cqT[:, kc, blk * 512:(blk + 1) * 512],
                        start=(kc == 0), stop=(kc == 2)), reads=rk + ["wqB%d" % hh], writes=["M%d" % mB])
                bs = slice(blk * 512, (blk + 1) * 512)
                P.op("act", lambda e, MA=MA, hh=hh, bs=bs: e.activation(out=qT[hh][0:64, bs], in_=MA[0:64, 0:512], func=AF.Copy),
                     reads=["M%d" % mA], writes=["qT%dn" % hh] + first_alias)
                P.op("dve", lambda e, MA=MA, bs=bs: e.tensor_tensor(out=ra[64:96, :], in0=MA[64:96, 0:512], in1=cosT[64:96, bs],
                                                                    op=ALU.mult), reads=["M%d" % mA, "cosT"], writes=["ra"])
                P.op("dve", lambda e, MB=MB, bs=bs: e.tensor_tensor(out=rb[64:96, :], in0=MB[64:96, 0:512], in1=sinT[64:96, bs],
                                                                    op=ALU.mult), reads=["M%d" % mB, "sinT"], writes=["rb"])
                P.op("pool", lambda e, hh=hh, bs=bs: e.tensor_tensor(out=qT[hh][64:96, bs], in0=ra[64:96, :], in1=rb[64:96, :],
                                                                      op=ALU.add),
                     reads=["ra", "rb"], writes=["qT%dp" % hh] + first_alias)
                first_alias = []
            for blk in range(8):
                m = next_m()
                Mp = B["M"][m]
                rk = ["ckvT%d" % (blk * 4 + a) for a in range(4)]
                for kc in range(2):
                    P.op("pe", lambda e, kc=kc, Mp=Mp, hh=hh, blk=blk: e.matmul(
                        Mp[0:64, 0:512], lhsT=wkn[hh][:, kc, :], rhs=ckvT[:, kc, blk * 512:(blk + 1) * 512],
                        start=(kc == 0), stop=(kc == 1)), reads=rk + ["wkn%d" % hh], writes=["M%d" % m])
                bs = slice(blk * 512, (blk + 1) * 512)
                if blk % 2 == 0:
                    P.op("dve", lambda e, Mp=Mp, hh=hh, bs=bs: e.tensor_copy(out=kT[hh][0:64, bs], in_=Mp[0:64, 0:512]),
                         reads=["M%d" % m], writes=["kT%d_%d" % (hh, blk)])
                else:
                    P.op("act", lambda e, Mp=Mp, hh=hh, bs=bs: e.activation(out=kT[hh][0:64, bs], in_=Mp[0:64, 0:512], func=AF.Copy),
                         reads=["M%d" % m], writes=["kT%d_%d" % (hh, blk)])
            P.op("dve", lambda e, hh=hh: e.tensor_copy(out=kT[hh][64:96, :], in_=kpeT[64:96, :]),
                 reads=["kpeT%d" % a for a in range(4)], writes=["kT%dpe" % hh])
        if c == 0:
            P.op("pool", lambda e: e.memset(vP[:, :, 64:128], 1.0), writes=["vP_ones"])
        for t4 in range(8):
            m = next_m()
            Mp = B["M"][m]
            Mv = Mp[:, 0:512].rearrange("p (t c) -> p t c", c=128)
            for a in range(4):
                T = t4 * 4 + a
                for kc in range(2):
                    P.op("pe", lambda e, kc=kc, Mv=Mv, a=a, T=T: e.matmul(
                        Mv[:, a, :], lhsT=ckvT[:, kc, T * 128:(T + 1) * 128], rhs=wvp[:, kc, :],
                        start=(kc == 0), stop=(kc == 1)), reads=["ckvT%d" % T, "wvp0", "wvp1"], writes=["M%d" % m])
            vdst = bass.AP(ar, VP_OFF + t4 * 4 * 192, [[AR_COLS, 128], [192, 4], [128, 2], [1, 64]])
            P.op("dve", lambda e, Mv=Mv, vdst=vdst: e.tensor_copy(out=vdst, in_=Mv.rearrange("p t (h c) -> p t h c", h=2)),
                 reads=["M%d" % m], writes=["vPe%d" % t4, "vPo%d" % t4])
        for blk in range(4):
            m = next_m()
            Mp = B["M"][m]
            rk = ["xnT%d" % (blk * 4 + a) for a in range(4)]
            for kc in range(8):
                P.op("pe", lambda e, kc=kc, Mp=Mp, blk=blk: e.matmul(Mp[:, 0:512], lhsT=wz[:, kc, :],
                                                                     rhs=xnT[:, kc, blk * 512:(blk + 1) * 512],
                                                                     start=(kc == 0), stop=(kc == 7)),
                     reads=rk + ["wz"], writes=["M%d" % m])
            emit_silu_gate(C, Mp[:, 0:512], "M%d" % m, etmp, gz[:, blk * 512:(blk + 1) * 512], "gz%d" % blk)

        its = [(qb, hh, tp_) for qb in range(4) for hh in range(2) for tp_ in range(16)]
        SKEW = 2

        def qk_stage(n):
            qb, hh, tp_ = its[n]
            s_ = sctr[0] % 2
            sctr[0] += 1
            Sp = B["S"][s_]
            for u in range(2):
                kt = 2 * tp_ + u
                P.op("pe", lambda e, Sp=Sp, u=u, kt=kt, hh=hh, qb=qb: e.matmul(
                    Sp[:, u * 512:(u + 1) * 512], lhsT=kT[hh][0:96, kt * 128:(kt + 1) * 128],
                    rhs=qT[hh][0:96, qb * 512:(qb + 1) * 512], start=True, stop=True),
                    reads=["kT%d_%d" % (hh, kt // 4), "kT%dpe" % hh, "qT%dn" % hh, "qT%dp" % hh], writes=["S%d" % s_])
            pj = pctr[0] % 3
            pctr[0] += 1
            Pt = PT[pj]
            P.op("act", lambda e, Pt=Pt, Sp=Sp: e.activation(out=Pt[:], in_=Sp[:], func=AF.Exp, scale=scale),
                 reads=["S%d" % s_], writes=["PT%d" % pj])
            return pj

        pjs = {}
        pending = []
        for n in range(len(its) + SKEW):
            if n < len(its):
                pjs[n] = qk_stage(n)
            m_ = n - SKEW
            if m_ < 0:
                continue
            qb, hh, tp_ = its[m_]
            pj = pjs[m_]
            Pt = PT[pj]
            Oh = B["Oe"] if hh == 0 else B["Oo"]
            okey = "Oe" if hh == 0 else "Oo"
            for u in range(2):
                kt = 2 * tp_ + u
                P.op("pe", lambda e, Oh=Oh, kt=kt, hh=hh, Pt=Pt, u=u: e.matmul(
                    Oh[:, 0:512], lhsT=vP[:, kt, hh * 64:hh * 64 + 128], rhs=Pt[:, u * 512:(u + 1) * 512],
                    start=(kt == 0), stop=(kt == 31)),
                    reads=["vPe%d" % (kt // 4), "vPo%d" % (kt // 4), "vP_ones", "PT%d" % pj], writes=[okey])
            if hh == 1 and tp_ == 15:
                emit_finalize_now(C, B["Oe"], B["Oo"], B["Rsb"], B["Rsw"])
                pending.append((n + 3, qb))
            while pending and (pending[0][0] <= n or n == len(its) + SKEW - 1):
                _, qb_ = pending.pop(0)
                m = next_m()
                emit_finalize_later(C, B["Rsb"], B["Rsw"], B["t1"], B["pmf"], B["M"][m], "M%d" % m,
                                    gz[:, qb_ * 512:(qb_ + 1) * 512], "gz%d" % qb_, ogT[:, c, qb_ * 512:(qb_ + 1) * 512],
                                    "ogT%d_%d" % (c, qb_))

    alias = ["qT0n", "qT0p", "qT1n", "qT1p", "vP_ones", "kT0pe", "kT1pe"] + ["kT%d_%d" % (a, b_) for a in range(2) for b_ in range(8)] + \
            ["gz%d" % i for i in range(4)] + ["vPe%d" % i for i in range(8)] + ["vPo%d" % i for i in range(8)]
    wg_v = wg.rearrange("(kc p) n -> p kc n", p=128)
    for nb in range(8):
        ring.load(wg_v[:, :, nb * 128:(nb + 1) * 128], 128, [8, 128], wgb[:, :, nb * 128:(nb + 1) * 128], "wgb",
                  extra_writes=alias if nb == 0 else (), gain=(B["gpc"], "gpc", 0))
    wp_v = wp.rearrange("(kc p) n -> p kc n", p=128)
    for nb in range(2):
        ring.load(wp_v[:, :, nb * 512:(nb + 1) * 512], 128, [2, 512], wpb[:, :, nb * 512:(nb + 1) * 512], "wpb",
                  extra_writes=alias if nb == 0 else ())
    for c in range(8):
        ring.load(w_out[c * 128:(c + 1) * 128, :], 128, [1024], woa[:, c, :], "woa", extra_writes=alias if c == 0 else ())

    NHX = 8
    hx4 = [hx[0][:], hx[1][:]] + [xnT[:, k_, :].bitcast(F32) for k_ in range(NHX - 2)]
    xnT_keys = ["xnT%d" % a_ for a_ in range(16)]

    def pre(i):
        j = i % NHX
        P.op("sp", lambda e: e.dma_start(out=hx4[j], in_=xown[i * 128:(i + 1) * 128, :]),
             writes=["hx%d" % j] + (xnT_keys if 2 <= i < NHX else []), dma=True)
        for nh in range(2):
            Ob = B["Oe"] if nh == 0 else B["Oo"]
            okey = "Oe" if nh == 0 else "Oo"
            for c in range(8):
                P.op("pe", lambda e, c=c, Ob=Ob, nh=nh: e.matmul(Ob[:, 0:512], lhsT=ogT[:, c, i * 128:(i + 1) * 128],
                                                                 rhs=woa[:, c, nh * 512:(nh + 1) * 512],
                                                                 start=(c == 0), stop=(c == 7)),
                     reads=["ogT%d_%d" % (c, i // 4), "woa"], writes=[okey])
            xh = hx4[j][:, nh * 512:(nh + 1) * 512]
            P.op("dve", lambda e, xh=xh, Ob=Ob: e.tensor_tensor(out=xh, in0=xh, in1=Ob[:, 0:512], op=ALU.add),
                 reads=["hx%d" % j, okey], writes=["hx%d" % j])

    outs = emit_ple(C, lambda i: (hx4[i % NHX], "hx%d" % (i % NHX)), 16, pd, None, wgb, wpb, st, 0, B["xnb"], B["idb"], B["sq"],
                    B["hnT"], B["pst"], B["pbf"], B["pT"], sig, tmp, B["S"], B["M"], xo,
                    final_norm=(fgbc, "fgbc", 16), pre=pre)
    P.emit(final_wait_ops=outs)
    return nc


def _rope_np():
    inv = (1.0 / (np.float32(10000.0) ** (np.arange(0, 32, 2, dtype=np.float32) / np.float32(32)))).astype(np.float32)
    ang = (np.arange(4096, dtype=np.float32)[:, None] * inv[None, :]).astype(np.float32)
    return np.cos(ang).astype(np.float32), np.sin(ang).astype(np.float32)


def run_l1(x1, p, norm_g, mla_w_in, mla_q_norm, mla_w_qb, mla_kv_norm, mla_w_kvb, mla_w_out, ple_norm, ple_w_gate,
           ple_w_proj, final_norm):
    ident, perm = _consts()
    f = lambda a: np.ascontiguousarray(a, dtype=np.float32)
    bc = lambda v, n: np.ascontiguousarray(np.broadcast_to(np.asarray(v, np.float32), (128, n)))
    wqb = f(mla_w_qb[0]).reshape(384, 16, 96)
    wqs = np.zeros_like(wqb)
    wqs[:, :, 64:80] = wqb[:, :, 80:96]
    wqs[:, :, 80:96] = wqb[:, :, 64:80]
    cos, sin = _rope_np()
    costok = np.ascontiguousarray(cos.reshape(32, 128, 16).transpose(1, 0, 2).reshape(128, 512))
    sintok = np.ascontiguousarray(sin.reshape(32, 128, 16).transpose(1, 0, 2).reshape(128, 512))
    pm = lambda v, k: np.ascontiguousarray(np.asarray(v, np.float32).reshape(k, 128).T)
    shared = dict(gnc=pm(norm_g[1], 8), gpc=pm(ple_norm[1], 8), gf=bc(final_norm, 1024), qnc=pm(mla_q_norm[0], 3),
                  kvnc=pm(mla_kv_norm[0], 2), w_in=f(mla_w_in[0]), wqa=wqb.reshape(384, 1536), wqs=wqs.reshape(384, 1536),
                  w_kvb=f(mla_w_kvb[0]), w_out=f(mla_w_out[0]), wg=f(ple_w_gate[1]), wp=f(ple_w_proj[1]),
                  costok=costok, sintok=sintok, ident=ident, perm=perm)
    in_maps = []
    for core in range(8):
        b, hf = core // 2, core % 2
        cosT = np.zeros((128, 2048), np.float32)
        sinT = np.zeros((128, 2048), np.float32)
        cs = cos[hf * 2048:(hf + 1) * 2048].T
        sn = sin[hf * 2048:(hf + 1) * 2048].T
        cosT[64:80] = cs
        cosT[80:96] = cs
        sinT[64:80] = -sn
        sinT[80:96] = sn
        m = dict(shared)
        m.update(xf=f(x1[b]), xown=f(x1[b, hf * 2048:(hf + 1) * 2048]), pd=f(p[1, b, hf * 2048:(hf + 1) * 2048]),
                 cosT=cosT, sinT=sinT)
        in_maps.append(m)
    nc = _get_nc("l1", build_l1)
    res = run_bass_kernel_spmd(nc, in_maps, core_ids=list(range(8)))
    out = np.zeros((4, 4096, 1024), np.float32)
    for core in range(8):
        b, hf = core // 2, core % 2
        out[b, hf * 2048:(hf + 1) * 2048] = res.results[core]["xo"]
    return out


def _l0_inputs(x, p, norm_g, na_w_in, na_rpb, na_w_out, ple_norm, ple_w_gate, ple_w_proj, flip=False):
    x = np.asarray(x, np.float32)
    ident, perm = _consts()
    tabB = _na_tables(np.asarray(na_rpb[0], np.float32))
    gn = np.ascontiguousarray(np.asarray(norm_g[0], np.float32).reshape(8, 128).T)
    gp = np.ascontiguousarray(np.asarray(ple_norm[0], np.float32).reshape(8, 128).T)
    shared = dict(gnc=gn, gpc=gp, w_in=np.ascontiguousarray(na_w_in[0], dtype=np.float32),
                  w_out=np.ascontiguousarray(na_w_out[0], dtype=np.float32),
                  wg=np.ascontiguousarray(ple_w_gate[0], dtype=np.float32),
                  wp=np.ascontiguousarray(ple_w_proj[0], dtype=np.float32), tabB=tabB, ident=ident, perm=perm)
    in_maps = []
    for core in range(8):
        b, hf = core // 2, core % 2
        if flip:
            hf = 1 - hf
        xk = np.zeros((40, 64, 1024), np.float32)
        xb = x[b].reshape(64, 64, 1024)
        for lr in range(40):
            R = 32 * hf - 4 + lr
            if 0 <= R <= 63:
                xk[lr] = xb[R]
        bq, aoh = _na_rowmask(hf)
        m = dict(shared)
        m.update(xk=xk.reshape(2560, 1024), pd=np.ascontiguousarray(p[0, b, hf * 2048:(hf + 1) * 2048], dtype=np.float32),
                 bq=bq, aoh=aoh)
        in_maps.append(m)
    return in_maps


def _l1_inputs(x1, p, norm_g, mla_w_in, mla_q_norm, mla_w_qb, mla_kv_norm, mla_w_kvb, mla_w_out, ple_norm, ple_w_gate,
               ple_w_proj, final_norm):
    ident, perm = _consts()
    f = lambda a: np.ascontiguousarray(a, dtype=np.float32)
    bc = lambda v, n: np.ascontiguousarray(np.broadcast_to(np.asarray(v, np.float32), (128, n)))
    wqb = f(mla_w_qb[0]).reshape(384, 16, 96)
    wqs = np.zeros_like(wqb)
    wqs[:, :, 64:80] = wqb[:, :, 80:96]
    wqs[:, :, 80:96] = wqb[:, :, 64:80]
    cos, sin = _rope_np()
    costok = np.ascontiguousarray(cos.reshape(32, 128, 16).transpose(1, 0, 2).reshape(128, 512))
    sintok = np.ascontiguousarray(sin.reshape(32, 128, 16).transpose(1, 0, 2).reshape(128, 512))
    pm = lambda v, k: np.ascontiguousarray(np.asarray(v, np.float32).reshape(k, 128).T)
    shared = dict(gnc=pm(norm_g[1], 8), gpc=pm(ple_norm[1], 8), gf=bc(final_norm, 1024), qnc=pm(mla_q_norm[0], 3),
                  kvnc=pm(mla_kv_norm[0], 2), w_in=f(mla_w_in[0]), wqa=wqb.reshape(384, 1536), wqs=wqs.reshape(384, 1536),
                  w_kvb=f(mla_w_kvb[0]), w_out=f(mla_w_out[0]), wg=f(ple_w_gate[1]), wp=f(ple_w_proj[1]),
                  costok=costok, sintok=sintok, ident=ident, perm=perm)
    in_maps = []
    for core in range(8):
        b, hf = core // 2, core % 2
        cosT = np.zeros((128, 2048), np.float32)
        sinT = np.zeros((128, 2048), np.float32)
        cs = cos[hf * 2048:(hf + 1) * 2048].T
        sn = sin[hf * 2048:(hf + 1) * 2048].T
        cosT[64:80] = cs
        cosT[80:96] = cs
        sinT[64:80] = -sn
        sinT[80:96] = sn
        m = dict(shared)
        if x1 is not None:
            m.update(xown=f(x1[b, hf * 2048:(hf + 1) * 2048]), xoth=f(x1[b, (1 - hf) * 2048:(2 - hf) * 2048]))
        order = np.concatenate([np.arange(hf * 2048, (hf + 1) * 2048), np.arange((1 - hf) * 2048, (2 - hf) * 2048)])
        m.update(pd=f(p[1, b, hf * 2048:(hf + 1) * 2048]), cosT=cosT, sinT=sinT,
                 costok=np.ascontiguousarray(cos[order].reshape(32, 128, 16).transpose(1, 0, 2).reshape(128, 512)),
                 sintok=np.ascontiguousarray(sin[order].reshape(32, 128, 16).transpose(1, 0, 2).reshape(128, 512)))
        in_maps.append(m)
    return in_maps


def build_fused(nc, es):
    x1loc = nc.dram_tensor("x1loc", [2048, 1024], F32).ap()
    x1oth = nc.dram_tensor("x1oth", [2048, 1024], F32).ap()
    shared = {}
    with contextlib.ExitStack() as es0:
        build_l0(nc, es0, pre="a_", xo=x1loc, sem_es=es, shared=shared)
    nc.all_engine_barrier()
    with contextlib.ExitStack() as es0:
        build_l0(nc, es0, pre="c_", xo=x1oth, sem_es=es, shared=shared)
    nc.all_engine_barrier()
    with contextlib.ExitStack() as es1:
        build_l1(nc, es1, pre="b_", xf=(x1loc, x1oth), sem_es=es)
    return nc


def kernel(x, p, norm_g, na_w_in, na_rpb, na_w_out, mla_w_in, mla_q_norm, mla_w_qb, mla_kv_norm, mla_w_kvb, mla_w_out,
           ple_norm, ple_w_gate, ple_w_proj, final_norm):
    x = np.asarray(x)
    p = np.asarray(p)
    m0 = _l0_inputs(x, p, norm_g, na_w_in, na_rpb, na_w_out, ple_norm, ple_w_gate, ple_w_proj)
    m0f = _l0_inputs(x, p, norm_g, na_w_in, na_rpb, na_w_out, ple_norm, ple_w_gate, ple_w_proj, flip=True)
    m1 = _l1_inputs(None, p, norm_g, mla_w_in, mla_q_norm, mla_w_qb, mla_kv_norm, mla_w_kvb, mla_w_out, ple_norm,
                    ple_w_gate, ple_w_proj, final_norm)
    in_maps = []
    for core in range(8):
        m = {"a_" + k: v for k, v in m0[core].items()}
        m.update({"c_" + k: m0f[core][k] for k in ("xk", "pd", "bq")})
        m.update({"b_" + k: v for k, v in m1[core].items()})
        in_maps.append(m)
    nc = _get_nc("fused", build_fused)
    res = run_bass_kernel_spmd(nc, in_maps, core_ids=list(range(8)))
    out = np.zeros((4, 4096, 1024), np.float32)
    for core in range(8):
        b, hf = core // 2, core % 2
        out[b, hf * 2048:(hf + 1) * 2048] = res.results[core]["xo"]
    return out
```

```python
import contextlib
import numpy as np
import concourse.bass as bass
import concourse.mybir as mybir
from concourse.bass_utils import run_bass_kernel_spmd

F32 = mybir.dt.float32
BF16 = mybir.dt.bfloat16
ALU = mybir.AluOpType
AF = mybir.ActivationFunctionType
AX = mybir.AxisListType

D = 1024
NEG = -30000.0
EPS = 1e-6


class Prog:
    STREAMS = ("pe", "act", "dve", "pool", "sp")

    def __init__(self, nc, n_dma_sems=12, sem_es=None, sem_prefix=""):
        self.nc = nc
        self.sem_es = sem_es
        self.sem_prefix = sem_prefix
        self.ops = []
        self.last_w = {}
        self.readers = {}
        self.n_dma_sems = n_dma_sems

    ALIAS = {
        "ra": ["Rsb_hi", "Rsb_lo", "sig1"], "Rsb_hi": ["ra", "sig1"], "Rsb_lo": ["ra", "sig1"],
        "sig1": ["ra", "Rsb_hi", "Rsb_lo"],
        "rb": ["Rsw", "tmp1", "Oc_lo", "Oc_hi"], "Rsw": ["rb", "tmp1", "Oc_lo", "Oc_hi"],
        "tmp1": ["rb", "Rsw", "Oc_lo", "Oc_hi"], "Oc_lo": ["rb", "Rsw", "tmp1"], "Oc_hi": ["rb", "Rsw", "tmp1"],
        "sig0": ["etmp"], "etmp": ["sig0"],
        "tmp0": ["t1_lo", "t1_hi"], "t1_lo": ["tmp0"], "t1_hi": ["tmp0"],
    }

    def _expand(self, keys):
        out = []
        for k in keys:
            out.append(k)
            out.extend(self.ALIAS.get(k, ()))
        return list(dict.fromkeys(out))

    def op(self, stream, fn, reads=(), writes=(), dma=False):
        reads = self._expand(reads)
        writes = self._expand(writes)
        i = len(self.ops)
        deps = {}

        def add(j, raw):
            if j is None:
                return
            deps[j] = deps.get(j, False) or raw

        for k in reads:
            add(self.last_w.get(k), True)
        for k in writes:
            add(self.last_w.get(k), True)
            for r in self.readers.get(k, ()):
                add(r, False)
        for k in reads:
            lst = self.readers.setdefault(k, [])
            if not dma:
                lst[:] = [r for r in lst if self.ops[r]["dma"] or self.ops[r]["stream"] != stream]
            lst.append(i)
        for k in writes:
            self.last_w[k] = i
            self.readers[k] = []
        self.ops.append(dict(i=i, stream=stream, fn=fn, dma=dma, deps=deps, sig=None))
        return i

    def emit(self, final_wait_ops=()):
        nc = self.nc
        ops = self.ops
        need = [[] for _ in ops]
        signaled = set()
        for o in ops:
            for j, raw in o["deps"].items():
                p = ops[j]
                if not p["dma"] and not o["dma"] and p["stream"] == o["stream"]:
                    if o["stream"] == "pe" or not raw:
                        continue
                need[o["i"]].append(j)
                signaled.add(j)
        for j in final_wait_ops:
            signaled.add(j)
        cnt = {s: 0 for s in self.STREAMS}
        dcnt = {s: 0 for s in self.STREAMS}
        for o in ops:
            if o["dma"]:
                k = dcnt[o["stream"]]
                dcnt[o["stream"]] += 1
                o["sig"] = ("d_%s_%d" % (o["stream"], k % self.n_dma_sems), 16 * (k // self.n_dma_sems + 1))
                o["dk"] = k
            elif o["i"] in signaled:
                cnt[o["stream"]] += 1
                o["sig"] = ("c_" + o["stream"], cnt[o["stream"]])
        dma_by_stream = {s: [o for o in ops if o["dma"] and o["stream"] == s] for s in self.STREAMS}
        sem_names = set()
        for o in ops:
            if o["sig"] is not None:
                sem_names.add(o["sig"][0])
        sem_names = sorted(sem_names)
        with contextlib.ExitStack() as es:
            ses = self.sem_es if self.sem_es is not None else es
            sems = {n: ses.enter_context(nc.semaphore(self.sem_prefix + n)) for n in sem_names}
            block = es.enter_context(nc.Block())
            seen = {s: {} for s in self.STREAMS}

            def run_stream(stream, eng):
                sw = seen[stream]
                for o in ops:
                    if o["stream"] != stream:
                        continue
                    waits = {}
                    for j in need[o["i"]]:
                        sn, sv = ops[j]["sig"]
                        waits[sn] = max(waits.get(sn, 0), sv)
                    if o["dma"] and o["dk"] >= self.n_dma_sems:
                        sn, sv = o["sig"]
                        waits[sn] = max(waits.get(sn, 0), sv - 16)
                    for sn, sv in sorted(waits.items()):
                        if sw.get(sn, 0) >= sv:
                            continue
                        sw[sn] = sv
                        eng.wait_ge(sems[sn], sv)
                    ins = o["fn"](eng)
                    if o["sig"] is not None:
                        ins.then_inc(sems[o["sig"][0]], 16 if o["dma"] else 1)
                if stream == "sp":
                    for j in final_wait_ops:
                        sn, sv = ops[j]["sig"]
                        if sw.get(sn, 0) < sv:
                            sw[sn] = sv
                            eng.wait_ge(sems[sn], sv)

            @block.tensor
            def _(e):
                run_stream("pe", e)

            @block.scalar
            def _(e):
                run_stream("act", e)

            @block.vector
            def _(e):
                run_stream("dve", e)

            @block.gpsimd
            def _(e):
                run_stream("pool", e)

            @block.sync
            def _(e):
                run_stream("sp", e)


class Ctx:
    def __init__(self, nc, es, pre="", sem_es=None):
        self.nc = nc
        self.es = es
        self.pre = pre
        self.P = Prog(nc, sem_es=sem_es, sem_prefix=pre)
        self._n = 0

    def sb(self, name, shape, dt):
        return self.es.enter_context(self.nc.sbuf_tensor(self.pre + name, list(shape), dt))

    def ps(self, name, shape, dt):
        return self.es.enter_context(self.nc.psum_tensor(self.pre + name, list(shape), dt))

    def din(self, name, shape, dt=F32):
        return self.nc.dram_tensor(self.pre + name, list(shape), dt, kind="ExternalInput").ap()

    def dout(self, name, shape, dt=F32):
        return self.nc.dram_tensor(name, list(shape), dt, kind="ExternalOutput").ap()


class Ring:
    def __init__(self, C, n=2, cols=1024):
        self.C = C
        self.slots = [C.sb("wst%d" % i, [128, cols], F32) for i in range(n)]
        self.k = 0

    def load(self, src, parts, shape_free, dst, dst_key, conv="pool", extra_writes=(), gain=None):
        P = self.C.P
        i = self.k % len(self.slots)
        self.k += 1
        n = int(np.prod(shape_free))
        st = self.slots[i][0:parts, 0:n]
        if len(shape_free) == 2:
            st = st.rearrange("p (a b) -> p a b", b=shape_free[1])
        key = "wst%d" % i
        P.op("sp", lambda e: e.dma_start(out=st, in_=src), writes=[key], dma=True)
        wr = [dst_key] + list(extra_writes)
        if gain is not None:
            gt, gkey, k0 = gain
            gap = bass.AP(gt, k0, [[gt[:].shape[1], 128], [1, shape_free[0]], [0, shape_free[1]]])
            P.op("pool", lambda e: e.tensor_tensor(out=dst, in0=st, in1=gap, op=ALU.mult), reads=[key, gkey], writes=wr)
        elif conv == "pool":
            P.op("pool", lambda e: e.tensor_copy(out=dst, in_=st), reads=[key], writes=wr)
        elif conv == "exp":
            P.op("act", lambda e: e.activation(out=dst, in_=st, func=AF.Exp), reads=[key], writes=wr)


def run_pipeline(n, stages):
    k = len(stages)
    for step in range(n + k - 1):
        for d in range(k):
            t = step - d
            if 0 <= t < n:
                stages[d](t)


def emit_norm_stats(C, st, xt_ap, xt_key, col, sq, width=D):
    P = C.P
    ss, ms, nh = st
    P.op("act", lambda e: e.activation(out=sq[:, 0:width], in_=xt_ap, func=AF.Square, accum_out=ss[:, col:col + 1]),
         reads=[xt_key], writes=["sq", "ss%d" % col])
    P.op("dve", lambda e: e.tensor_scalar(out=ms[:, col:col + 1], in0=ss[:, col:col + 1], scalar1=1.0 / width, scalar2=EPS,
                                         op0=ALU.mult, op1=ALU.add), reads=["ss%d" % col], writes=["ms%d" % col])
    P.op("pool", lambda e: e.tensor_tensor(out=ms[:, col:col + 1], in0=ms[:, col:col + 1], in1=nh[:], op=ALU.pow),
         reads=["ms%d" % col, "nh"], writes=["rs%d" % col])


def emit_norm_apply_T(C, st, xt_ap, xt_key, col, gbc, gkey, xnb_ap, xnb_key, tp_ps, tp_key, idb, dstT, dstT_key, nch=8,
                      copy_eng="act", scale_eng="act"):
    P = C.P
    ss, ms, nh = st
    if scale_eng == "act":
        P.op("act", lambda e: e.activation(out=xnb_ap, in_=xt_ap, func=AF.Copy, scale=ms[:, col:col + 1]),
             reads=[xt_key, "rs%d" % col], writes=[xnb_key])
    else:
        P.op("dve", lambda e: e.tensor_scalar(out=xnb_ap, in0=xt_ap, scalar1=ms[:, col:col + 1], scalar2=None, op0=ALU.mult),
             reads=[xt_key, "rs%d" % col], writes=[xnb_key])
    for kc in range(nch):
        P.op("pe", lambda e, kc=kc: e.transpose(out=tp_ps[:, kc, :], in_=xnb_ap[:, kc * 128:(kc + 1) * 128], identity=idb[:]),
             reads=[xnb_key, "idb"], writes=[tp_key])
    if copy_eng == "act":
        P.op("act", lambda e: e.activation(out=dstT, in_=tp_ps[:, 0:nch, :], func=AF.Copy), reads=[tp_key], writes=[dstT_key])
    else:
        P.op("dve", lambda e: e.tensor_copy(out=dstT, in_=tp_ps[:, 0:nch, :]), reads=[tp_key], writes=[dstT_key])


def emit_norm_T(C, st, xt_ap, xt_key, col, gbc, gkey, xnb, xnb_key, tp_ps, tp_key, idb, dstT, dstT_key, sq):
    emit_norm_stats(C, st, xt_ap, xt_key, col, sq)
    emit_norm_apply_T(C, st, xt_ap, xt_key, col, gbc, gkey, xnb[:], xnb_key, tp_ps, tp_key, idb, dstT, dstT_key)


def emit_finalize_now(C, Oe, Oo, Rsb, Oc):
    P = C.P
    P.op("dve", lambda e: e.reciprocal(out=Rsb[64:128, :], in_=Oe[64:128, :]), reads=["Oe"], writes=["Rsb_hi"])
    P.op("act", lambda e: e.activation(out=Oc[0:64, :], in_=Oe[0:64, :], func=AF.Copy), reads=["Oe"], writes=["Oc_lo"])
    P.op("dve", lambda e: e.reciprocal(out=Rsb[0:64, :], in_=Oo[0:64, :]), reads=["Oo"], writes=["Rsb_lo"])
    P.op("act", lambda e: e.activation(out=Oc[64:128, :], in_=Oo[64:128, :], func=AF.Copy), reads=["Oo"], writes=["Oc_hi"])


def emit_finalize_later(C, Rsb, Oc, t1, pmf, Mps, Mkey, gz_ap, gz_key, og_ap, og_key):
    P = C.P
    P.op("pe", lambda e: e.matmul(Mps[:, 0:512], lhsT=pmf[:], rhs=Rsb[:], start=True, stop=True),
         reads=["Rsb_hi", "Rsb_lo", "pmf"], writes=[Mkey])
    P.op("dve", lambda e: e.tensor_tensor(out=t1[:], in0=Oc[:], in1=Mps[:, 0:512], op=ALU.mult),
         reads=["Oc_lo", "Oc_hi", Mkey], writes=["t1_lo", "t1_hi"])
    P.op("pool", lambda e: e.tensor_tensor(out=og_ap, in0=t1[:], in1=gz_ap, op=ALU.mult),
         reads=["t1_lo", "t1_hi", gz_key], writes=[og_key])


def emit_silu_gate(C, zps, zkey, etmp, gz_ap, gz_key):
    P = C.P
    P.op("act", lambda e: e.activation(out=etmp[:], in_=zps, func=AF.Exp, scale=-1.0), reads=[zkey], writes=["etmp"])
    P.op("act", lambda e: e.activation(out=gz_ap, in_=zps, func=AF.Copy), reads=[zkey], writes=[gz_key])
    P.op("dve", lambda e: e.tensor_scalar(out=etmp[:], in0=etmp[:], scalar1=1.0, scalar2=None, op0=ALU.add),
         reads=["etmp"], writes=["etmp"])
    P.op("dve", lambda e: e.reciprocal(out=etmp[:], in_=etmp[:]), reads=["etmp"], writes=["etmp"])
    P.op("dve", lambda e: e.tensor_tensor(out=gz_ap, in0=gz_ap, in1=etmp[:], op=ALU.mult),
         reads=[gz_key, "etmp"], writes=[gz_key])


def emit_ple(C, get_tile, n_tiles, pd, pgbc, wgb, wpb, st, col0, xnb, idb, sq, hnT, pst, pbf, pT, sig, tmp,
             Sps, Mps, out_dram, final_norm=None, pre=None):
    P = C.P
    outs = []
    ss, ms, nh_ = st

    def sA(i):
        pre(i)

    def sB(i):
        xt, xkey = get_tile(i)
        emit_norm_stats(C, st, xt, xkey, col0 + i, sq)

    def sC(i):
        j = i % 2
        xt, xkey = get_tile(i)
        col = col0 + i
        P.op("act", lambda e: e.activation(out=xnb[j][:], in_=xt, func=AF.Copy, scale=ms[:, col:col + 1]),
             reads=[xkey, "rs%d" % col], writes=["xnb%d" % j])
        P.op("sp", lambda e: e.dma_start(out=pst[j][:], in_=pd[i * 128:(i + 1) * 128, :]), writes=["pst%d" % j], dma=True)
        P.op("pool", lambda e: e.tensor_copy(out=pbf[j][:], in_=pst[j][:]), reads=["pst%d" % j], writes=["pbf%d" % j])

    def sD(i):
        j = i % 2
        tp = Mps[0][:].bitcast(BF16).rearrange("p (k t) -> p k t", t=128)
        for kc in range(8):
            P.op("pe", lambda e, kc=kc: e.transpose(out=tp[:, kc, :], in_=xnb[j][:, kc * 128:(kc + 1) * 128], identity=idb[:]),
                 reads=["xnb%d" % j, "idb"], writes=["M0"])
        P.op("act", lambda e: e.activation(out=hnT[j][:], in_=tp, func=AF.Copy), reads=["M0"], writes=["hnT%d" % j])
        tp2 = Mps[1][:].bitcast(BF16).rearrange("p (k t) -> p k t", t=128)
        for kc in range(2):
            P.op("pe", lambda e, kc=kc: e.transpose(out=tp2[:, kc, :], in_=pbf[j][:, kc * 128:(kc + 1) * 128], identity=idb[:]),
                 reads=["pbf%d" % j, "idb"], writes=["M1"])
        P.op("dve", lambda e: e.tensor_copy(out=pT[j][:], in_=tp2[:, 0:2, :]), reads=["M1"], writes=["pT%d" % j])

    def sE(i):
        j = i % 2
        xt, xkey = get_tile(i)
        for nh in range(2):
            gps = Sps[0][:, nh * 512:(nh + 1) * 512]
            pps = Sps[1][:, nh * 512:(nh + 1) * 512]
            for kc in range(8):
                P.op("pe", lambda e, kc=kc, nh=nh, gps=gps: e.matmul(
                    gps, lhsT=hnT[j][:, kc, :], rhs=wgb[:, kc, nh * 512:(nh + 1) * 512], start=(kc == 0), stop=(kc == 7)),
                    reads=["hnT%d" % j, "wgb"], writes=["S0_%d" % nh])
            for kc in range(2):
                P.op("pe", lambda e, kc=kc, nh=nh, pps=pps: e.matmul(
                    pps, lhsT=pT[j][:, kc, :], rhs=wpb[:, kc, nh * 512:(nh + 1) * 512], start=(kc == 0), stop=(kc == 1)),
                    reads=["pT%d" % j, "wpb"], writes=["S1_%d" % nh])
            sg = sig[nh]
            P.op("act", lambda e, gps=gps, sg=sg: e.activation(out=sg[:], in_=gps, func=AF.Exp, scale=-1.0),
                 reads=["S0_%d" % nh], writes=["sig%d" % nh])
            P.op("dve", lambda e, sg=sg: e.tensor_scalar(out=sg[:], in0=sg[:], scalar1=1.0, scalar2=None, op0=ALU.add),
                 reads=["sig%d" % nh], writes=["sig%d" % nh])
            P.op("dve", lambda e, sg=sg: e.reciprocal(out=sg[:], in_=sg[:]), reads=["sig%d" % nh], writes=["sig%d" % nh])
            tm = tmp[nh]
            P.op("dve", lambda e, pps=pps, sg=sg, tm=tm: e.tensor_tensor(out=tm[:], in0=pps, in1=sg[:], op=ALU.mult),
                 reads=["S1_%d" % nh, "sig%d" % nh], writes=["tmp%d" % nh])
            xh = xt[:, nh * 512:(nh + 1) * 512]
            P.op("pool", lambda e, xh=xh, tm=tm: e.tensor_tensor(out=xh, in0=xh, in1=tm[:], op=ALU.add),
                 reads=[xkey, "tmp%d" % nh], writes=[xkey])

    def sF(i):
        xt, xkey = get_tile(i)
        emit_norm_stats(C, st, xt, xkey, final_norm[2] + i, sq)

    def sG(i):
        xt, xkey = get_tile(i)
        if final_norm is not None:
            fgbc, fgkey, fcol0 = final_norm
            c2 = fcol0 + i
            P.op("act", lambda e: e.activation(out=xt, in_=xt, func=AF.Copy, scale=ms[:, c2:c2 + 1]),
                 reads=[xkey, "rs%d" % c2], writes=[xkey])
            P.op("dve", lambda e: e.tensor_tensor(out=xt, in0=xt, in1=fgbc[:], op=ALU.mult),
                 reads=[xkey, fgkey], writes=[xkey])
        outs.append(P.op("sp", lambda e: e.dma_start(out=out_dram[i * 128:(i + 1) * 128, :], in_=xt), reads=[xkey], dma=True))

    stages = ([sA] if pre is not None else []) + [sB, sC, sD, sE] + ([sF] if final_norm is not None else []) + [sG]
    run_pipeline(n_tiles, stages)
    return outs


def alloc_common(C, layer1=False):
    B = {}
    if not layer1:
        B["xres"] = C.sb("xres", [128, 16, 1024], F32)
        B["og"] = [[C.sb("og%d_%d" % (i, g_), [128, 512], BF16) for g_ in range(4)] for i in range(2)]
        B["wo"] = [C.sb("wo%d" % i, [128, 1024], BF16) for i in range(2)]
    B["ss"] = C.sb("ss", [128, 96], F32)
    B["ms"] = C.sb("ms", [128, 96], F32)
    B["nh"] = C.sb("nh", [128, 1], F32)
    B["sq"] = C.sb("sq", [128, 1024], BF16)
    B["xnb"] = [C.sb("xnb%d" % i, [128, 1024], BF16) for i in range(2)]
    B["idb"] = C.sb("idb", [128, 128], BF16)
    B["pmf"] = C.sb("pmf", [128, 128], F32)
    B["gnc"] = C.sb("gnc_sb", [128, 8], F32)
    B["gpc"] = C.sb("gpc_sb", [128, 8], F32)
    B["Rsb"] = C.sb("Rsb", [128, 512], F32)
    B["Rsw"] = C.sb("Rsw", [128, 512], F32)
    B["t1"] = C.sb("t1", [128, 512], F32)
    B["hnT"] = [C.sb("hnT%d" % i, [128, 8, 128], BF16) for i in range(2)]
    B["pst"] = [C.sb("pst%d" % i, [128, 256], F32) for i in range(2)]
    B["pbf"] = [C.sb("pbf%d" % i, [128, 256], BF16) for i in range(2)]
    B["pT"] = [C.sb("pT%d" % i, [128, 2, 128], BF16) for i in range(2)]
    B["ring"] = Ring(C, n=2, cols=1024)
    B["S"] = [C.ps("S%d" % i, [128, 1024], F32) for i in range(2)]
    B["Oe"] = C.ps("Oe", [128, 512], F32)
    B["Oo"] = C.ps("Oo", [128, 512], F32)
    B["M"] = [C.ps("M%d" % i, [128, 512], F32) for i in range(2)]
    return B


def emit_consts(C, B, ident, perm):
    P = C.P
    B["ring"].load(ident, 128, [128], B["idb"][:], "idb")
    P.op("sp", lambda e: e.dma_start(out=B["pmf"][:], in_=perm), writes=["pmf"], dma=True)
    P.op("pool", lambda e: e.memset(B["nh"][:], -0.5), writes=["nh"])


def emit_outproj(C, B, ogs, og_keys, wos, wo_keys, tiles, mctr):
    P = C.P
    xres = B["xres"]
    for tt, i in enumerate(tiles):
        for nh in range(2):
            m = mctr[0] % 2
            mctr[0] += 1
            Mp = B["M"][m]
            for k_ in range(2):
                P.op("pe", lambda e, tt=tt, nh=nh, Mp=Mp, k_=k_: e.matmul(
                    Mp[:, 0:512], lhsT=ogs[k_][:, tt * 128:(tt + 1) * 128], rhs=wos[k_][:, nh * 512:(nh + 1) * 512],
                    start=(k_ == 0), stop=(k_ == 1)), reads=[og_keys[k_], wo_keys[k_]], writes=["M%d" % m])
            xh = xres[:, i, nh * 512:(nh + 1) * 512]
            P.op("dve", lambda e, xh=xh, Mp=Mp: e.tensor_tensor(out=xh, in0=xh, in1=Mp[:, 0:512], op=ALU.add),
                 reads=["xr%d" % i, "M%d" % m], writes=["xr%d" % i])


def build_l0(nc, es, pre="", xo=None, sem_es=None, shared=None):
    C = Ctx(nc, es, pre, sem_es)
    P = C.P
    shared = {} if shared is None else shared

    def sdin(name, shape):
        if name not in shared:
            shared[name] = C.din(name, shape)
        return shared[name]

    xk = C.din("xk", [2560, 1024])
    pd = C.din("pd", [2048, 256])
    bq = C.din("bq", [128, 2048])
    gn = sdin("gnc", [128, 8])
    gp = sdin("gpc", [128, 8])
    w_in = sdin("w_in", [1024, 4096])
    w_out = sdin("w_out", [1024, 1024])
    wg = sdin("wg", [1024, 1024])
    wp = sdin("wp", [256, 1024])
    tabB = sdin("tabB", [8, 128, 2816])
    aoh = sdin("aoh", [128, 1024])
    ident = sdin("ident", [128, 128])
    perm = sdin("perm", [128, 128])
    if xo is None:
        xo = C.dout("xo", [2048, 1024])

    B = alloc_common(C)
    ring = B["ring"]
    xres = B["xres"]
    xnT = C.sb("xnT", [128, 8, 2560], BF16)
    hx = [C.sb("hx%d" % i, [128, 1024], F32) for i in range(2)]
    AR_COLS, VP_OFF = 12544, 6656
    ar = C.sb("ar", [128, 12544], BF16)
    qTe = ar[:, 0:2048]
    qTo = ar[:, 2048:4096]
    kT = ar[:, 4096:6656]
    vP = ar[:, 6656:10496].rearrange("p (t c) -> p t c", c=192)
    gz = ar[:, 10496:12544]
    wgb = ar[:, 0:8192].rearrange("p (k n) -> p k n", n=1024)
    wpb = ar[:, 8192:10240].rearrange("p (k n) -> p k n", n=1024)
    wq = C.sb("wq", [128, 8, 128], BF16)
    wk = C.sb("wk", [128, 8, 128], BF16)
    wv = C.sb("wv", [128, 8, 128], BF16)
    wz = C.sb("wz", [128, 8, 128], BF16)
    tabE = C.sb("tabE", [128, 2816], BF16)
    bqb = C.sb("bqb", [128, 2048], BF16)
    aohb = C.sb("aohb", [128, 1024], BF16)
    PT = [C.sb("PT%d" % i, [128, 1024], BF16) for i in range(3)]
    etmp = C.sb("etmp", [128, 512], F32)
    sig = [etmp, B["Rsb"]]
    tmp = [B["t1"], B["Rsw"]]
    st = (B["ss"], B["ms"], B["nh"])

    emit_consts(C, B, ident, perm)
    P.op("sp", lambda e: e.dma_start(out=B["gnc"][:], in_=gn), writes=["gnc"], dma=True)
    P.op("sp", lambda e: e.dma_start(out=B["gpc"][:], in_=gp), writes=["gpc"], dma=True)
    ring.load(aoh, 128, [1024], aohb[:], "aohb")
    for h2 in range(2):
        ring.load(bq[:, h2 * 1024:(h2 + 1) * 1024], 128, [1024], bqb[:, h2 * 1024:(h2 + 1) * 1024], "bqb")
    P.op("pool", lambda e: e.memset(qTe[64:128, :], 0.0), writes=["qTe_z"])
    P.op("pool", lambda e: e.memset(qTo[0:64, :], 0.0), writes=["qTo_z"])
    P.op("pool", lambda e: e.memset(vP[:, :, 64:128], 1.0), writes=["vP_ones"])

    def p1_tile(bt):
        if 2 <= bt < 18:
            return xres[:, bt - 2, :], "xr%d" % (bt - 2)
        return hx[bt % 2][:], "hx%d" % (bt % 2)

    def p1_s0(bt):
        xt, xkey = p1_tile(bt)
        P.op("sp", lambda e: e.dma_start(out=xt, in_=xk[bt * 128:(bt + 1) * 128, :]), writes=[xkey], dma=True)
        emit_norm_stats(C, st, xt, xkey, bt, B["sq"])

    def p1_s1(bt):
        xt, xkey = p1_tile(bt)
        j = bt % 2
        tp = B["M"][j][:].bitcast(BF16).rearrange("p (k t) -> p k t", t=128)
        emit_norm_apply_T(C, st, xt, xkey, bt, None, None, B["xnb"][j][:], "xnb%d" % j, tp, "M%d" % j, B["idb"],
                          xnT[:, :, bt * 128:(bt + 1) * 128], "xnT%d" % bt, copy_eng="dve")

    run_pipeline(20, [p1_s0, p1_s1])

    w_in_v = w_in.rearrange("(kc p) n -> p kc n", p=128)
    mctr = [0, 0]
    pending = []
    sctr = [0]
    pctr = [0]
    octr = [0]

    def next_m():
        m = mctr[0] % 2
        mctr[0] += 1
        return m

    def load_inproj(c_):
        for wt, wkey, off in ((wq, "wq", 0), (wk, "wk", 1024), (wv, "wv", 2048), (wz, "wz", 3072)):
            ring.load(w_in_v[:, :, off + c_ * 128: off + (c_ + 1) * 128], 128, [8, 128], wt[:], wkey,
                      gain=(B["gnc"], "gnc", 0))

    load_inproj(0)
    for c in range(8):
        wo = B["wo"][c % 2]
        for ch, (lo, hi) in enumerate(((0, 1024), (1024, 2048), (2048, 2816))):
            ring.load(tabB[c, :, lo:hi], 128, [hi - lo], tabE[:, lo:hi], "tabE%d" % ch, conv="exp")

        def flush_one():
            _, g_, c_ = pending.pop(0)
            m = next_m()
            og = B["og"][c_ % 2][g_]
            emit_finalize_later(C, B["Rsb"], B["Rsw"], B["t1"], B["pmf"], B["M"][m], "M%d" % m,
                                gz[:, g_ * 512:(g_ + 1) * 512], "gz%d" % g_, og[:], "og%d_%d" % (c_ % 2, g_))
            if c_ % 2 == 1:
                emit_outproj(C, B, [B["og"][0][g_], B["og"][1][g_]], ["og0_%d" % g_, "og1_%d" % g_],
                             B["wo"], ["wo0", "wo1"], [4 * g_ + a_ for a_ in range(4)], mctr)

        for blk in range(4):
            m = next_m()
            Mp = B["M"][m]
            t0 = 256 + blk * 512
            rk = ["xnT%d" % (t0 // 128 + a) for a in range(4)]
            for kc in range(8):
                P.op("pe", lambda e, kc=kc, Mp=Mp, t0=t0: e.matmul(Mp[:, 0:512], lhsT=wq[:, kc, :], rhs=xnT[:, kc, t0:t0 + 512],
                                                                   start=(kc == 0), stop=(kc == 7)),
                     reads=rk + ["wq"], writes=["M%d" % m])
            P.op("act", lambda e, Mp=Mp, blk=blk: e.activation(out=qTe[0:64, blk * 512:(blk + 1) * 512], in_=Mp[0:64, 0:512],
                                                                func=AF.Copy), reads=["M%d" % m], writes=["qTe"])
            P.op("act", lambda e, Mp=Mp, blk=blk: e.activation(out=qTo[64:128, blk * 512:(blk + 1) * 512], in_=Mp[64:128, 0:512],
                                                                func=AF.Copy), reads=["M%d" % m], writes=["qTo"])
        for blk in range(5):
            m = next_m()
            Mp = B["M"][m]
            t0 = blk * 512
            rk = ["xnT%d" % (t0 // 128 + a) for a in range(4)]
            for kc in range(8):
                P.op("pe", lambda e, kc=kc, Mp=Mp, t0=t0: e.matmul(Mp[:, 0:512], lhsT=wk[:, kc, :], rhs=xnT[:, kc, t0:t0 + 512],
                                                                   start=(kc == 0), stop=(kc == 7)),
                     reads=rk + ["wk"], writes=["M%d" % m])
            if blk % 2 == 0:
                P.op("dve", lambda e, Mp=Mp, t0=t0: e.tensor_copy(out=kT[:, t0:t0 + 512], in_=Mp[:, 0:512]),
                     reads=["M%d" % m], writes=["kT%d" % blk])
            else:
                P.op("act", lambda e, Mp=Mp, t0=t0: e.activation(out=kT[:, t0:t0 + 512], in_=Mp[:, 0:512], func=AF.Copy),
                     reads=["M%d" % m], writes=["kT%d" % blk])
        while pending:
            flush_one()
        ring.load(w_out[c * 128:(c + 1) * 128, :], 128, [1024], wo[:], "wo%d" % (c % 2))
        for blk in range(4):
            m = next_m()
            Mp = B["M"][m]
            t0 = 256 + blk * 512
            rk = ["xnT%d" % (t0 // 128 + a) for a in range(4)]
            for kc in range(8):
                P.op("pe", lambda e, kc=kc, Mp=Mp, t0=t0: e.matmul(Mp[:, 0:512], lhsT=wz[:, kc, :], rhs=xnT[:, kc, t0:t0 + 512],
                                                                   start=(kc == 0), stop=(kc == 7)),
                     reads=rk + ["wz"], writes=["M%d" % m])
            emit_silu_gate(C, Mp[:, 0:512], "M%d" % m, etmp, gz[:, blk * 512:(blk + 1) * 512], "gz%d" % blk)
        for t4 in range(5):
            m = next_m()
            Mp = B["M"][m]
            Mv = Mp[:, 0:512].rearrange("p (t c) -> p t c", c=128)
            for a in range(4):
                bt = t4 * 4 + a
                for kc in range(8):
                    P.op("pe", lambda e, kc=kc, Mv=Mv, a=a, bt=bt: e.matmul(
                        Mv[:, a, :], lhsT=xnT[:, kc, bt * 128:(bt + 1) * 128], rhs=wv[:, kc, :],
                        start=(kc == 0), stop=(kc == 7)), reads=["xnT%d" % bt, "wv"], writes=["M%d" % m])
            vdst = bass.AP(ar, VP_OFF + t4 * 4 * 192, [[AR_COLS, 128], [192, 4], [128, 2], [1, 64]])
            P.op("dve", lambda e, Mv=Mv, vdst=vdst: e.tensor_copy(out=vdst, in_=Mv.rearrange("p t (h c) -> p t h c", h=2)),
                 reads=["M%d" % m], writes=["vPe%d" % t4, "vPo%d" % t4])

        if c < 7:
            load_inproj(c + 1)
        its = [(g, hh, tp_) for g in range(4) for hh in range(2) for tp_ in range(4)]
        SKEW = 2

        def qk_stage(n):
            g, hh, tp_ = its[n]
            qm = qTe if hh == 0 else qTo
            qkeys = ["qTe", "qTe_z"] if hh == 0 else ["qTo", "qTo_z"]
            s_ = sctr[0] % 2
            sctr[0] += 1
            Sp = B["S"][s_]
            for u in range(2):
                t = 2 * tp_ + (1 - u)
                bt = 4 * g + t
                P.op("pe", lambda e, Sp=Sp, u=u, bt=bt, qm=qm, g=g: e.matmul(
                    Sp[:, u * 512:(u + 1) * 512], lhsT=kT[:, bt * 128:(bt + 1) * 128], rhs=qm[:, g * 512:(g + 1) * 512],
                    start=True, stop=False), reads=["kT%d" % (bt // 4)] + qkeys, writes=["S%d" % s_])
                P.op("pe", lambda e, Sp=Sp, u=u, t=t, g=g: e.matmul(
                    Sp[:, u * 512:(u + 1) * 512], lhsT=aohb[:, t * 128:(t + 1) * 128], rhs=bqb[:, g * 512:(g + 1) * 512],
                    start=False, stop=True), reads=["aohb", "bqb"], writes=["S%d" % s_])
            pj = pctr[0] % 3
            pctr[0] += 1
            Pt = PT[pj]
            P.op("act", lambda e, Pt=Pt, Sp=Sp: e.activation(out=Pt[:], in_=Sp[:], func=AF.Exp, scale=0.125),
                 reads=["S%d" % s_], writes=["PT%d" % pj])
            tab_ap = bass.AP(tabE, hh * 1408 + (12 - 4 * tp_) * 64, [[2816, 128], [128, 2], [64, 8], [1, 64]])
            Pv = Pt[:].rearrange("p (u q c) -> p u q c", u=2, q=8)
            P.op("dve", lambda e, Pv=Pv, tab_ap=tab_ap: e.tensor_tensor(out=Pv, in0=Pv, in1=tab_ap, op=ALU.mult),
                 reads=["PT%d" % pj, "tabE0", "tabE1", "tabE2"], writes=["PT%d" % pj])
            return pj

        pjs = {}
        for n in range(len(its) + SKEW):
            if n < len(its):
                pjs[n] = qk_stage(n)
            m_ = n - SKEW
            if m_ < 0:
                continue
            g, hh, tp_ = its[m_]
            pj = pjs[m_]
            Pt = PT[pj]
            Oh = B["Oe"] if hh == 0 else B["Oo"]
            okey = "Oe" if hh == 0 else "Oo"
            for u in range(2):
                t = 2 * tp_ + (1 - u)
                bt = 4 * g + t
                first = (tp_ == 0 and u == 0)
                last = (tp_ == 3 and u == 1)
                P.op("pe", lambda e, Oh=Oh, bt=bt, hh=hh, Pt=Pt, u=u, first=first, last=last: e.matmul(
                    Oh[:, 0:512], lhsT=vP[:, bt, hh * 64:hh * 64 + 128], rhs=Pt[:, u * 512:(u + 1) * 512],
                    start=first, stop=last),
                    reads=["vPe%d" % (bt // 4), "vPo%d" % (bt // 4), "vP_ones", "PT%d" % pj], writes=[okey])
            if hh == 1 and tp_ == 3:
                emit_finalize_now(C, B["Oe"], B["Oo"], B["Rsb"], B["Rsw"])
                pending.append((n + 3 if g < 3 else 10 ** 9, g, c))
            while pending and pending[0][0] <= n:
                flush_one()

    while pending:
        flush_one()
    alias = ["qTe", "qTo", "qTe_z", "qTo_z", "vP_ones"] + ["kT%d" % i for i in range(5)] + ["gz%d" % i for i in range(4)] + \
            ["vPe%d" % i for i in range(5)] + ["vPo%d" % i for i in range(5)]
    wg_v = wg.rearrange("(kc p) n -> p kc n", p=128)
    for nb in range(8):
        ring.load(wg_v[:, :, nb * 128:(nb + 1) * 128], 128, [8, 128], wgb[:, :, nb * 128:(nb + 1) * 128], "wgb",
                  extra_writes=alias if nb == 0 else (), gain=(B["gpc"], "gpc", 0))
    wp_v = wp.rearrange("(kc p) n -> p kc n", p=128)
    for nb in range(2):
        ring.load(wp_v[:, :, nb * 512:(nb + 1) * 512], 128, [2, 512], wpb[:, :, nb * 512:(nb + 1) * 512], "wpb",
                  extra_writes=alias if nb == 0 else ())
    outs = emit_ple(C, lambda i: (xres[:, i, :], "xr%d" % i), 16, pd, None, wgb, wpb, st, 20, B["xnb"], B["idb"],
                    B["sq"], B["hnT"], B["pst"], B["pbf"], B["pT"], sig, tmp, B["S"], B["M"], xo)
    P.emit(final_wait_ops=outs)
    return nc


def _consts():
    ident = np.eye(128, dtype=np.float32)
    perm = np.zeros((128, 128), np.float32)
    for i in range(64):
        perm[64 + i, i] = 1.0
        perm[i, 64 + i] = 1.0
    return ident, perm


def _na_tables(rpb):
    kc = np.arange(64)
    qc = np.arange(64)
    cs = np.clip(qc - 8, 0, 48)
    colvalid = (kc[:, None] >= cs[None, :]) & (kc[:, None] < cs[None, :] + 16)
    coff = np.clip(kc[:, None] - qc[None, :] + 15, 0, 30)
    tab = np.full((16, 128, 22, 64), NEG, np.float32)
    for e in range(22):
        for half in range(2):
            dr = 10 - e + half
            if -7 <= dr <= 7:
                vals = rpb[:, dr + 7][:, coff]
                tab[:, half * 64:(half + 1) * 64, e, :] = np.where(colvalid[None], vals, NEG)
    return np.ascontiguousarray(tab.reshape(8, 2, 128, 22, 64).transpose(0, 2, 1, 3, 4).reshape(8, 128, 2816))


def _na_rowmask(hf):
    bq = np.zeros((128, 4, 8, 64), np.float32)
    for g in range(4):
        for qi in range(8):
            r = 32 * hf + 8 * g + qi
            rs = min(max(r - 4, 0), 56)
            for j in range(16):
                R = 32 * hf - 4 + 8 * g + j
                ok = (0 <= R <= 63) and (rs <= R < rs + 8)
                bq[j, g, qi, :] = 0.0 if ok else NEG
    aoh = np.zeros((128, 8, 128), np.float32)
    for t in range(8):
        aoh[2 * t, t, 0:64] = 1.0
        aoh[2 * t + 1, t, 64:128] = 1.0
    return bq.reshape(128, 2048), aoh.reshape(128, 1024)


_NC_CACHE = {}


def _get_nc(name, builder):
    if name not in _NC_CACHE:
        nc = bass.Bass("TRN2", target_bir_lowering=False)
        with contextlib.ExitStack() as es:
            builder(nc, es)
        _NC_CACHE[name] = nc
    return _NC_CACHE[name]


def run_l0(x, p, norm_g, na_w_in, na_rpb, na_w_out, ple_norm, ple_w_gate, ple_w_proj):
    x = np.asarray(x, np.float32)
    ident, perm = _consts()
    tabB = _na_tables(np.asarray(na_rpb[0], np.float32))
    gn = np.ascontiguousarray(np.asarray(norm_g[0], np.float32).reshape(8, 128).T)
    gp = np.ascontiguousarray(np.asarray(ple_norm[0], np.float32).reshape(8, 128).T)
    shared = dict(gnc=gn, gpc=gp, w_in=np.ascontiguousarray(na_w_in[0], dtype=np.float32),
                  w_out=np.ascontiguousarray(na_w_out[0], dtype=np.float32),
                  wg=np.ascontiguousarray(ple_w_gate[0], dtype=np.float32),
                  wp=np.ascontiguousarray(ple_w_proj[0], dtype=np.float32), tabB=tabB, ident=ident, perm=perm)
    in_maps = []
    for core in range(8):
        b, hf = core // 2, core % 2
        xk = np.zeros((40, 64, 1024), np.float32)
        xb = x[b].reshape(64, 64, 1024)
        for lr in range(40):
            R = 32 * hf - 4 + lr
            if 0 <= R <= 63:
                xk[lr] = xb[R]
        bq, aoh = _na_rowmask(hf)
        m = dict(shared)
        m.update(xk=xk.reshape(2560, 1024), pd=np.ascontiguousarray(p[0, b, hf * 2048:(hf + 1) * 2048], dtype=np.float32),
                 bq=bq, aoh=aoh)
        in_maps.append(m)
    nc = _get_nc("l0", build_l0)
    res = run_bass_kernel_spmd(nc, in_maps, core_ids=list(range(8)))
    x1 = np.zeros((4, 4096, 1024), np.float32)
    for core in range(8):
        b, hf = core // 2, core % 2
        x1[b, hf * 2048:(hf + 1) * 2048] = res.results[core]["xo"]
    return x1


def build_l1(nc, es, pre="", xf=None, sem_es=None):
    C = Ctx(nc, es, pre, sem_es)
    P = C.P
    if xf is None:
        xoth = C.din("xoth", [2048, 1024])
        xown = C.din("xown", [2048, 1024])
    else:
        xown, xoth = xf

    def xf_tile(T):
        src = xown if T < 16 else xoth
        return src[(T % 16) * 128:(T % 16 + 1) * 128, :]
    pd = C.din("pd", [2048, 256])
    gn = C.din("gnc", [128, 8])
    gp = C.din("gpc", [128, 8])
    gf = C.din("gf", [128, 1024])
    qn = C.din("qnc", [128, 3])
    kvn = C.din("kvnc", [128, 2])
    w_in = C.din("w_in", [1024, 1696])
    wqa = C.din("wqa", [384, 1536])
    wqs = C.din("wqs", [384, 1536])
    w_kvb = C.din("w_kvb", [256, 2048])
    w_out = C.din("w_out", [1024, 1024])
    wg = C.din("wg", [1024, 1024])
    wp = C.din("wp", [256, 1024])
    costok = C.din("costok", [128, 512])
    sintok = C.din("sintok", [128, 512])
    cosTd = C.din("cosT", [128, 2048])
    sinTd = C.din("sinT", [128, 2048])
    ident = C.din("ident", [128, 128])
    perm = C.din("perm", [128, 128])
    xo = C.dout("xo", [2048, 1024])

    B = alloc_common(C, layer1=True)
    ring = B["ring"]
    st = (B["ss"], B["ms"], B["nh"])
    xnT = C.sb("xnT", [128, 8, 2048], BF16)
    ckvT = C.sb("ckvT", [128, 2, 4096], BF16)
    kpeT = C.sb("kpeT", [128, 4096], BF16)
    cqT = C.sb("cqT", [128, 3, 2048], BF16)
    cosT = C.sb("cosTb", [128, 2048], BF16)
    sinT = C.sb("sinTb", [128, 2048], BF16)
    ogT = C.sb("ogT", [128, 8, 2048], BF16)
    hx = [C.sb("hx%d" % i, [128, 1024], F32) for i in range(2)]
    qnc = C.sb("qnc_sb", [128, 3], F32)
    kvnc = C.sb("kvnc_sb", [128, 2], F32)
    xkT = B["hnT"]
    lat = [B["pst"][i][:].bitcast(BF16) for i in range(2)]
    wqA = [C.sb("wqA%d" % i, [128, 3, 96], BF16) for i in range(2)]
    wqB = [C.sb("wqB%d" % i, [128, 3, 96], BF16) for i in range(2)]
    wkn = [C.sb("wkn%d" % i, [128, 2, 64], BF16) for i in range(2)]
    wvp = C.sb("wvp", [128, 2, 128], BF16)
    wz = C.sb("wz", [128, 8, 128], BF16)
    PT = [C.sb("PT%d" % i, [128, 1024], BF16) for i in range(3)]
    etmp = C.sb("etmp", [128, 512], F32)
    AR_COLS, VP_OFF = 20480, 12288
    ar = C.sb("ar", [128, 20480], BF16)
    qT = [ar[:, 0:2048], ar[:, 2048:4096]]
    kT = [ar[:, 4096:8192], ar[:, 8192:12288]]
    vP = ar[:, 12288:18432].rearrange("p (t c) -> p t c", c=192)
    gz = ar[:, 18432:20480]
    krope = ar[:, 0:2048].bitcast(F32).rearrange("p (t c) -> p t c", c=32)
    kpe_tok = ar[:, 2048:5120].rearrange("p (t c) -> p t c", c=96)
    ctk = ar[:, 5120:6144].bitcast(F32).rearrange("p (t c) -> p t c", c=16)
    stk = ar[:, 6144:7168].bitcast(F32).rearrange("p (t c) -> p t c", c=16)
    wkvr = ar[:, 7168:9472].rearrange("p (k n) -> p k n", n=288)
    wcq = ar[:, 9472:12544].rearrange("p (k n) -> p k n", n=384)
    wgb = ar[:, 0:8192].rearrange("p (k n) -> p k n", n=1024)
    wpb = ar[:, 8192:10240].rearrange("p (k n) -> p k n", n=1024)
    woa = ar[:, 10240:18432].rearrange("p (k n) -> p k n", n=1024)
    ra = B["Rsb"]
    rb = B["Rsw"]
    sig = [etmp, B["Rsb"]]
    tmp = [B["t1"], B["Rsw"]]
    fgbc = C.sb("fgbc", [128, 1024], F32)

    emit_consts(C, B, ident, perm)
    P.op("sp", lambda e: e.dma_start(out=B["gnc"][:], in_=gn), writes=["gnc"], dma=True)
    P.op("sp", lambda e: e.dma_start(out=B["gpc"][:], in_=gp), writes=["gpc"], dma=True)
    P.op("sp", lambda e: e.dma_start(out=fgbc[:], in_=gf), writes=["fgbc"], dma=True)
    P.op("sp", lambda e: e.dma_start(out=qnc[:], in_=qn), writes=["qnc"], dma=True)
    P.op("sp", lambda e: e.dma_start(out=kvnc[:], in_=kvn), writes=["kvnc"], dma=True)
    P.op("sp", lambda e: e.dma_start(out=ctk.rearrange("p t c -> p (t c)"), in_=costok), writes=["ctk"], dma=True)
    P.op("sp", lambda e: e.dma_start(out=stk.rearrange("p t c -> p (t c)"), in_=sintok), writes=["stk"], dma=True)
    for h2 in range(2):
        ring.load(cosTd[:, h2 * 1024:(h2 + 1) * 1024], 128, [1024], cosT[:, h2 * 1024:(h2 + 1) * 1024], "cosT")
        ring.load(sinTd[:, h2 * 1024:(h2 + 1) * 1024], 128, [1024], sinT[:, h2 * 1024:(h2 + 1) * 1024], "sinT")
    w_in_v = w_in.rearrange("(kc p) n -> p kc n", p=128)
    for k4 in range(4):
        ring.load(w_in_v[:, 2 * k4:2 * k4 + 2, 384:672], 128, [2, 288], wkvr[:, 2 * k4:2 * k4 + 2, :], "wkvr",
                  gain=(B["gnc"], "gnc", 2 * k4))
    for k4 in range(4):
        ring.load(w_in_v[:, 2 * k4:2 * k4 + 2, 0:384], 128, [2, 384], wcq[:, 2 * k4:2 * k4 + 2, :], "wcq",
                  gain=(B["gnc"], "gnc", 2 * k4))
    P.op("pool", lambda e: e.memset(kpe_tok[:, :, 0:64], 0.0), writes=["kpe_z"])

    def psum_bf(t):
        return t[:].bitcast(BF16).rearrange("p (k t) -> p k t", t=128)

    ss, ms, nh_ = st

    def latent_stages(n_tiles, load, col_x, col_l, width, wmat, wkey, gvec, gvkey, nch, dstT_of, dst_of, dst_key, extra=None):
        def s0(T):
            j = T % 2
            load(T, j)
            emit_norm_stats(C, st, hx[j][:], "hx%d" % j, col_x + T, B["sq"])

        def s1(T):
            j = T % 2
            dT, dkey = dstT_of(T, j)
            emit_norm_apply_T(C, st, hx[j][:], "hx%d" % j, col_x + T, None, None, B["xnb"][j][:], "xnb%d" % j,
                              psum_bf(B["M"][j]), "M%d" % j, B["idb"], dT, dkey, copy_eng="dve")

        def s2(T):
            j = T % 2
            dT, dkey = dstT_of(T, j)
            Sp = B["S"][j]
            tot = width + (32 if extra else 0)
            for kc in range(8):
                P.op("pe", lambda e, kc=kc: e.matmul(Sp[:, 0:tot], lhsT=dT[:, kc, :], rhs=wmat[:, kc, :],
                                                     start=(kc == 0), stop=(kc == 7)),
                     reads=[dkey, wkey], writes=["S%d" % j])
            emit_norm_stats(C, st, Sp[:, 0:width], "S%d" % j, col_l + T, B["sq"], width=width)

        def s3(T):
            j = T % 2
            Sp = B["S"][j]
            col = col_l + T
            Ob = B["Oe"] if j == 0 else B["Oo"]
            okey = "Oe" if j == 0 else "Oo"
            if extra:
                extra(T, Sp, "S%d" % j, col)
            emit_norm_apply_T(C, st, Sp[:, 0:width], "S%d" % j, col, gvec, gvkey, lat[j][:, 0:width], "pst%d" % j,
                              psum_bf(Ob), okey, B["idb"], dst_of(T), dst_key % T, nch=nch, copy_eng="dve", scale_eng="dve")

        run_pipeline(n_tiles, [s0, s1, s2, s3])

    def load_own(T, j):
        P.op("sp", lambda e: e.dma_start(out=hx[j][:], in_=xown[T * 128:(T + 1) * 128, :]), writes=["hx%d" % j], dma=True)

    latent_stages(16, load_own, 64, 80, 384, wcq, "wcq", None, None, 3,
                  lambda T, j: (xnT[:, :, T * 128:(T + 1) * 128], "xnT%d" % T),
                  lambda T: cqT[:, :, T * 128:(T + 1) * 128], "cqT%d")

    def load_any(T, j):
        P.op("sp", lambda e: e.dma_start(out=hx[j][:], in_=xf_tile(T)), writes=["hx%d" % j], dma=True)

    def krope_copy(T, Sp, skey, col):
        P.op("dve", lambda e: e.tensor_copy(out=krope[:, T, :], in_=Sp[:, 256:288]), reads=[skey, "rs%d" % col], writes=["krope"])

    latent_stages(32, load_any, 0, 32, 256, wkvr, "wkvr", None, None, 2,
                  lambda T, j: (xkT[j][:], "hnT%d" % j),
                  lambda T: ckvT[:, :, T * 128:(T + 1) * 128], "ckvT%d", extra=krope_copy)

    x1v = krope[:, :, 0:16]
    x2v = krope[:, :, 16:32]
    rav = ra[:].rearrange("p (t c) -> p t c", c=16)
    rbv = rb[:].rearrange("p (t c) -> p t c", c=16)
    P.op("dve", lambda e: e.tensor_tensor(out=rav, in0=x1v, in1=ctk, op=ALU.mult), reads=["krope", "ctk"], writes=["ra"])
    P.op("dve", lambda e: e.tensor_tensor(out=rbv, in0=x2v, in1=stk, op=ALU.mult), reads=["krope", "stk"], writes=["rb"])
    P.op("dve", lambda e: e.tensor_tensor(out=kpe_tok[:, :, 64:80], in0=rav, in1=rbv, op=ALU.subtract),
         reads=["ra", "rb"], writes=["kpe1"])
    P.op("dve", lambda e: e.tensor_tensor(out=rav, in0=x1v, in1=stk, op=ALU.mult), reads=["krope", "stk", "kpe1"], writes=["ra"])
    P.op("dve", lambda e: e.tensor_tensor(out=rbv, in0=x2v, in1=ctk, op=ALU.mult), reads=["krope", "ctk", "kpe1"], writes=["rb"])
    P.op("dve", lambda e: e.tensor_tensor(out=kpe_tok[:, :, 80:96], in0=rav, in1=rbv, op=ALU.add),
         reads=["ra", "rb"], writes=["kpe2"])
    for T8 in range(4):
        s = T8 % 2
        tpv = B["S"][s][:, 0:512].bitcast(BF16).rearrange("p (k t) -> p k t", t=128)
        for a in range(8):
            T = T8 * 8 + a
            P.op("pe", lambda e, a=a, T=T, tpv=tpv: e.transpose(out=tpv[0:96, a, :], in_=kpe_tok[:, T, 0:96], identity=B["idb"][:]),
                 reads=["kpe1", "kpe2", "kpe_z", "idb"], writes=["S%d" % s])
        dst = kpeT[64:96, T8 * 1024:(T8 + 1) * 1024].rearrange("p (a t) -> p a t", t=128)
        if T8 % 2 == 0:
            P.op("dve", lambda e, dst=dst, tpv=tpv: e.tensor_copy(out=dst, in_=tpv[64:96, :, :]), reads=["S%d" % s],
                 writes=["kpeT%d" % T8])
        else:
            P.op("act", lambda e, dst=dst, tpv=tpv: e.activation(out=dst, in_=tpv[64:96, :, :], func=AF.Copy), reads=["S%d" % s],
                 writes=["kpeT%d" % T8])

    mctr = [0]
    sctr = [0]
    pctr = [0]

    def next_m():
        m = mctr[0] % 2
        mctr[0] += 1
        return m

    wqa_v = wqa.rearrange("(kc p) n -> p kc n", p=128)
    wqs_v = wqs.rearrange("(kc p) n -> p kc n", p=128)
    wkv_v = w_kvb.rearrange("(kc p) n -> p kc n", p=128)
    alias_ac = ["krope", "kpe1", "kpe2", "kpe_z", "ctk", "stk", "wkvr", "wcq"]
    scale = float(96.0 ** -0.5)
    for c in range(8):
        for hh in range(2):
            h = 2 * c + hh
            ring.load(wqa_v[:, :, h * 96:(h + 1) * 96], 128, [3, 96], wqA[hh][:], "wqA%d" % hh, gain=(qnc, "qnc", 0))
            ring.load(wqs_v[:, :, h * 96:(h + 1) * 96], 128, [3, 96], wqB[hh][:], "wqB%d" % hh, gain=(qnc, "qnc", 0))
            ring.load(wkv_v[:, :, h * 128:h * 128 + 64], 128, [2, 64], wkn[hh][:], "wkn%d" % hh, gain=(kvnc, "kvnc", 0))
            ring.load(wkv_v[:, :, h * 128 + 64:h * 128 + 128], 128, [2, 64], wvp[:, :, hh * 64:(hh + 1) * 64], "wvp%d" % hh,
                      gain=(kvnc, "kvnc", 0))
        ring.load(w_in_v[:, :, 672 + c * 128:672 + (c + 1) * 128], 128, [8, 128], wz[:], "wz", gain=(B["gnc"], "gnc", 0))
        first_alias = alias_ac if c == 0 else []

        for hh in range(2):
            for blk in range(4):
                mA = next_m()
                mB = next_m()
                MA = B["M"][mA]
                MB = B["M"][mB]
                rk = ["cqT%d" % (blk * 4 + a) for a in range(4)]
                for kc in range(3):
                    P.op("pe", lambda e, kc=kc, MA=MA, hh=hh, blk=blk: e.matmul(
                        MA[0:96, 0:512], lhsT=wqA[hh][:, kc, :], rhs=cqT[:, kc, blk * 512:(blk + 1) * 512],
                        start=(kc == 0), stop=(kc == 2)), reads=rk + ["wqA%d" % hh], writes=["M%d" % mA])
                for kc in range(3):
                    P.op("pe", lambda e, kc=kc, MB=MB, hh=hh, blk=blk: e.matmul(
                        MB[0:96, 0:512], lhsT=wqB[hh][:, kc, :], rhs=cqT[:, kc, blk * 512:(blk + 1) * 512],
                        start=(kc == 0), stop=(kc == 2)), reads=rk + ["wqB%d" % hh], writes=["M%d" % mB])
                bs = slice(blk * 512, (blk + 1) * 512)
                P.op("act", lambda e, MA=MA, hh=hh, bs=bs: e.activation(out=qT[hh][0:64, bs], in_=MA[0:64, 0:512], func=AF.Copy),
                     reads=["M%d" % mA], writes=["qT%dn" % hh] + first_alias)
                P.op("dve", lambda e, MA=MA, bs=bs: e.tensor_tensor(out=ra[64:96, :], in0=MA[64:96, 0:512], in1=cosT[64:96, bs],
                                                                    op=ALU.mult), reads=["M%d" % mA, "cosT"], writes=["ra"])
                P.op("dve", lambda e, MB=MB, bs=bs: e.tensor_tensor(out=rb[64:96, :], in0=MB[64:96, 0:512], in1=sinT[64:96, bs],
                                                                    op=ALU.mult), reads=["M%d" % mB, "sinT"], writes=["rb"])
                P.op("pool", lambda e, hh=hh, bs=bs: e.tensor_tensor(out=qT[hh][64:96, bs], in0=ra[64:96, :], in1=rb[64:96, :],
                                                                      op=ALU.add),
                     reads=["ra", "rb"], writes=["qT%dp" % hh] + first_alias)
                first_alias = []
            for blk in range(8):
                m = next_m()
                Mp = B["M"][m]
                rk = ["ckvT%d" % (blk * 4 + a) for a in range(4)]
                for kc in range(2):
                    P.op("pe", lambda e, kc=kc, Mp=Mp, hh=hh, blk=blk: e.matmul(
                        Mp[0:64, 0:512], lhsT=wkn[hh][:, kc, :], rhs=ckvT[:, kc, blk * 512:(blk + 1) * 512],
                        start=(kc == 0), stop=(kc == 1)), reads=rk + ["wkn%d" % hh], writes=["M%d" % m])
                bs = slice(blk * 512, (blk + 1) * 512)
                if blk % 2 == 0:
                    P.op("dve", lambda e, Mp=Mp, hh=hh, bs=bs: e.tensor_copy(out=kT[hh][0:64, bs], in_=Mp[0:64, 0:512]),
                         reads=["M%d" % m], writes=["kT%d_%d" % (hh, blk)])
                else:
                    P.op("act", lambda e, Mp=Mp, hh=hh, bs=bs: e.activation(out=kT[hh][0:64, bs], in_=Mp[0:64, 0:512], func=AF.Copy),
                         reads=["M%d" % m], writes=["kT%d_%d" % (hh, blk)])
            P.op("dve", lambda e, hh=hh: e.tensor_copy(out=kT[hh][64:96, :], in_=kpeT[64:96, :]),
                 reads=["kpeT%d" % a for a in range(4)], writes=["kT%dpe" % hh])
        if c == 0:
            P.op("pool", lambda e: e.memset(vP[:, :, 64:128], 1.0), writes=["vP_ones"])
        for t4 in range(8):
            m = next_m()
            Mp = B["M"][m]
            Mv = Mp[:, 0:512].rearrange("p (t c) -> p t c", c=128)
            for a in range(4):
                T = t4 * 4 + a
                for kc in range(2):
                    P.op("pe", lambda e, kc=kc, Mv=Mv, a=a, T=T: e.matmul(
                        Mv[:, a, :], lhsT=ckvT[:, kc, T * 128:(T + 1) * 128], rhs=wvp[:, kc, :],
                        start=(kc == 0), stop=(kc == 1)), reads=["ckvT%d" % T, "wvp0", "wvp1"], writes=["M%d" % m])
            vdst = bass.AP(ar, VP_OFF + t4 * 4 * 192, [[AR_COLS, 128], [192, 4], [128, 2], [1, 64]])
            P.op("dve", lambda e, Mv=Mv, vdst=vdst: e.tensor_copy(out=vdst, in_=Mv.rearrange("p t (h c) -> p t h c", h=2)),
                 reads=["M%d" % m], writes=["vPe%d" % t4, "vPo%d" % t4])
        for blk in range(4):
            m = next_m()
            Mp = B["M"][m]
            rk = ["xnT%d" % (blk * 4 + a) for a in range(4)]
            for kc in range(8):
                P.op("pe", lambda e, kc=kc, Mp=Mp, blk=blk: e.matmul(Mp[:, 0:512], lhsT=wz[:, kc, :],
                                                                     rhs=xnT[:, kc, blk * 512:(blk + 1) * 512],
                                                                     start=(kc == 0), stop=(kc == 7)),
                     reads=rk + ["wz"], writes=["M%d" % m])
            emit_silu_gate(C, Mp[:, 0:512], "M%d" % m, etmp, gz[:, blk * 512:(blk + 1) * 512], "gz%d" % blk)

        its = [(qb, hh, tp_) for qb in range(4) for hh in range(2) for tp_ in range(16)]
        SKEW = 2

        def qk_stage(n):
            qb, hh, tp_ = its[n]
            s_ = sctr[0] % 2
            sctr[0] += 1
            Sp = B["S"][s_]
            for u in range(2):
                kt = 2 * tp_ + u
                P.op("pe", lambda e, Sp=Sp, u=u, kt=kt, hh=hh, qb=qb: e.matmul(
                    Sp[:, u * 512:(u + 1) * 512], lhsT=kT[hh][0:96, kt * 128:(kt + 1) * 128],
                    rhs=qT[hh][0:96, qb * 512:(qb + 1) * 512], start=True, stop=True),
                    reads=["kT%d_%d" % (hh, kt // 4), "kT%dpe" % hh, "qT%dn" % hh, "qT%dp" % hh], writes=["S%d" % s_])
            pj = pctr[0] % 3
            pctr[0] += 1
            Pt = PT[pj]
            P.op("act", lambda e, Pt=Pt, Sp=Sp: e.activation(out=Pt[:], in_=Sp[:], func=AF.Exp, scale=scale),
                 reads=["S%d" % s_], writes=["PT%d" % pj])
            return pj

        pjs = {}
        pending = []
        for n in range(len(its) + SKEW):
            if n < len(its):
                pjs[n] = qk_stage(n)
            m_ = n - SKEW
            if m_ < 0:
                continue
            qb, hh, tp_ = its[m_]
            pj = pjs[m_]
            Pt = PT[pj]
            Oh = B["Oe"] if hh == 0 else B["Oo"]
            okey = "Oe" if hh == 0 else "Oo"
            for u in range(2):
                kt = 2 * tp_ + u
                P.op("pe", lambda e, Oh=Oh, kt=kt, hh=hh, Pt=Pt, u=u: e.matmul(
                    Oh[:, 0:512], lhsT=vP[:, kt, hh * 64:hh * 64 + 128], rhs=Pt[:, u * 512:(u + 1) * 512],
                    start=(kt == 0), stop=(kt == 31)),
                    reads=["vPe%d" % (kt // 4), "vPo%d" % (kt // 4), "vP_ones", "PT%d" % pj], writes=[okey])
            if hh == 1 and tp_ == 15:
                emit_finalize_now(C, B["Oe"], B["Oo"], B["Rsb"], B["Rsw"])
                pending.append((n + 3, qb))
            while pending and (pending[0][0] <= n or n == len(its) + SKEW - 1):
                _, qb_ = pending.pop(0)
                m = next_m()
                emit_finalize_later(C, B["Rsb"], B["Rsw"], B["t1"], B["pmf"], B["M"][m], "M%d" % m,
                                    gz[:, qb_ * 512:(qb_ + 1) * 512], "gz%d" % qb_, ogT[:, c, qb_ * 512:(qb_ + 1) * 512],
                                    "ogT%d_%d" % (c, qb_))

    alias = ["qT0n", "qT0p", "qT1n", "qT1p", "vP_ones", "kT0pe", "kT1pe"] + ["kT%d_%d" % (a, b_) for a in range(2) for b_ in range(8)] + \
            ["gz%d" % i for i in range(4)] + ["vPe%d" % i for i in range(8)] + ["vPo%d" % i for i in range(8)]
    wg_v = wg.rearrange("(kc p) n -> p kc n", p=128)
    for nb in range(8):
        ring.load(wg_v[:, :, nb * 128:(nb + 1) * 128], 128, [8, 128], wgb[:, :, nb * 128:(nb + 1) * 128], "wgb",
                  extra_writes=alias if nb == 0 else (), gain=(B["gpc"], "gpc", 0))
    wp_v = wp.rearrange("(kc p) n -> p kc n", p=128)
    for nb in range(2):
        ring.load(wp_v[:, :, nb * 512:(nb + 1) * 512], 128, [2, 512], wpb[:, :, nb * 512:(nb + 1) * 512], "wpb",
                  extra_writes=alias if nb == 0 else ())
    for c in range(8):
        ring.load(w_out[c * 128:(c + 1) * 128, :], 128, [1024], woa[:, c, :], "woa", extra_writes=alias if c == 0 else ())

    NHX = 8
    hx4 = [hx[0][:], hx[1][:]] + [xnT[:, k_, :].bitcast(F32) for k_ in range(NHX - 2)]
    xnT_keys = ["xnT%d" % a_ for a_ in range(16)]

    def pre(i):
        j = i % NHX
        P.op("sp", lambda e: e.dma_start(out=hx4[j], in_=xown[i * 128:(i + 1) * 128, :]),
             writes=["hx%d" % j] + (xnT_keys if 2 <= i < NHX else []), dma=True)
        for nh in range(2):
            Ob = B["Oe"] if nh == 0 else B["Oo"]
            okey = "Oe" if nh == 0 else "Oo"
            for c in range(8):
                P.op("pe", lambda e, c=c, Ob=Ob, nh=nh: e.matmul(Ob[:, 0:512], lhsT=ogT[:, c, i * 128:(i + 1) * 128],
                                                                 rhs=woa[:, c, nh * 512:(nh + 1) * 512],
                                                                 start=(c == 0), stop=(c == 7)),
                     reads=["ogT%d_%d" % (c, i // 4), "woa"], writes=[okey])
            xh = hx4[j][:, nh * 512:(nh + 1) * 512]
            P.op("dve", lambda e, xh=xh, Ob=Ob: e.tensor_tensor(out=xh, in0=xh, in1=Ob[:, 0:512], op=ALU.add),
                 reads=["hx%d" % j, okey], writes=["hx%d" % j])

    outs = emit_ple(C, lambda i: (hx4[i % NHX], "hx%d" % (i % NHX)), 16, pd, None, wgb, wpb, st, 0, B["xnb"], B["idb"], B["sq"],
                    B["hnT"], B["pst"], B["pbf"], B["pT"], sig, tmp, B["S"], B["M"], xo,
                    final_norm=(fgbc, "fgbc", 16), pre=pre)
    P.emit(final_wait_ops=outs)
    return nc


def _rope_np():
    inv = (1.0 / (np.float32(10000.0) ** (np.arange(0, 32, 2, dtype=np.float32) / np.float32(32)))).astype(np.float32)
    ang = (np.arange(4096, dtype=np.float32)[:, None] * inv[None, :]).astype(np.float32)
    return np.cos(ang).astype(np.float32), np.sin(ang).astype(np.float32)


def run_l1(x1, p, norm_g, mla_w_in, mla_q_norm, mla_w_qb, mla_kv_norm, mla_w_kvb, mla_w_out, ple_norm, ple_w_gate,
           ple_w_proj, final_norm):
    ident, perm = _consts()
    f = lambda a: np.ascontiguousarray(a, dtype=np.float32)
    bc = lambda v, n: np.ascontiguousarray(np.broadcast_to(np.asarray(v, np.float32), (128, n)))
    wqb = f(mla_w_qb[0]).reshape(384, 16, 96)
    wqs = np.zeros_like(wqb)
    wqs[:, :, 64:80] = wqb[:, :, 80:96]
    wqs[:, :, 80:96] = wqb[:, :, 64:80]
    cos, sin = _rope_np()
    costok = np.ascontiguousarray(cos.reshape(32, 128, 16).transpose(1, 0, 2).reshape(128, 512))
    sintok = np.ascontiguousarray(sin.reshape(32, 128, 16).transpose(1, 0, 2).reshape(128, 512))
    pm = lambda v, k: np.ascontiguousarray(np.asarray(v, np.float32).reshape(k, 128).T)
    shared = dict(gnc=pm(norm_g[1], 8), gpc=pm(ple_norm[1], 8), gf=bc(final_norm, 1024), qnc=pm(mla_q_norm[0], 3),
                  kvnc=pm(mla_kv_norm[0], 2), w_in=f(mla_w_in[0]), wqa=wqb.reshape(384, 1536), wqs=wqs.reshape(384, 1536),
                  w_kvb=f(mla_w_kvb[0]), w_out=f(mla_w_out[0]), wg=f(ple_w_gate[1]), wp=f(ple_w_proj[1]),
                  costok=costok, sintok=sintok, ident=ident, perm=perm)
    in_maps = []
    for core in range(8):
        b, hf = core // 2, core % 2
        cosT = np.zeros((128, 2048), np.float32)
        sinT = np.zeros((128, 2048), np.float32)
        cs = cos[hf * 2048:(hf + 1) * 2048].T
        sn = sin[hf * 2048:(hf + 1) * 2048].T
        cosT[64:80] = cs
        cosT[80:96] = cs
        sinT[64:80] = -sn
        sinT[80:96] = sn
        m = dict(shared)
        m.update(xf=f(x1[b]), xown=f(x1[b, hf * 2048:(hf + 1) * 2048]), pd=f(p[1, b, hf * 2048:(hf + 1) * 2048]),
                 cosT=cosT, sinT=sinT)
        in_maps.append(m)
    nc = _get_nc("l1", build_l1)
    res = run_bass_kernel_spmd(nc, in_maps, core_ids=list(range(8)))
    out = np.zeros((4, 4096, 1024), np.float32)
    for core in range(8):
        b, hf = core // 2, core % 2
        out[b, hf * 2048:(hf + 1) * 2048] = res.results[core]["xo"]
    return out


def _l0_inputs(x, p, norm_g, na_w_in, na_rpb, na_w_out, ple_norm, ple_w_gate, ple_w_proj, flip=False):
    x = np.asarray(x, np.float32)
    ident, perm = _consts()
    tabB = _na_tables(np.asarray(na_rpb[0], np.float32))
    gn = np.ascontiguousarray(np.asarray(norm_g[0], np.float32).reshape(8, 128).T)
    gp = np.ascontiguousarray(np.asarray(ple_norm[0], np.float32).reshape(8, 128).T)
    shared = dict(gnc=gn, gpc=gp, w_in=np.ascontiguousarray(na_w_in[0], dtype=np.float32),
                  w_out=np.ascontiguousarray(na_w_out[0], dtype=np.float32),
                  wg=np.ascontiguousarray(ple_w_gate[0], dtype=np.float32),
                  wp=np.ascontiguousarray(ple_w_proj[0], dtype=np.float32), tabB=tabB, ident=ident, perm=perm)
    in_maps = []
    for core in range(8):
        b, hf = core // 2, core % 2
        if flip:
            hf = 1 - hf
        xk = np.zeros((40, 64, 1024), np.float32)
        xb = x[b].reshape(64, 64, 1024)
        for lr in range(40):
            R = 32 * hf - 4 + lr
            if 0 <= R <= 63:
                xk[lr] = xb[R]
        bq, aoh = _na_rowmask(hf)
        m = dict(shared)
        m.update(xk=xk.reshape(2560, 1024), pd=np.ascontiguousarray(p[0, b, hf * 2048:(hf + 1) * 2048], dtype=np.float32),
                 bq=bq, aoh=aoh)
        in_maps.append(m)
    return in_maps


def _l1_inputs(x1, p, norm_g, mla_w_in, mla_q_norm, mla_w_qb, mla_kv_norm, mla_w_kvb, mla_w_out, ple_norm, ple_w_gate,
               ple_w_proj, final_norm):
    ident, perm = _consts()
    f = lambda a: np.ascontiguousarray(a, dtype=np.float32)
    bc = lambda v, n: np.ascontiguousarray(np.broadcast_to(np.asarray(v, np.float32), (128, n)))
    wqb = f(mla_w_qb[0]).reshape(384, 16, 96)
    wqs = np.zeros_like(wqb)
    wqs[:, :, 64:80] = wqb[:, :, 80:96]
    wqs[:, :, 80:96] = wqb[:, :, 64:80]
    cos, sin = _rope_np()
    costok = np.ascontiguousarray(cos.reshape(32, 128, 16).transpose(1, 0, 2).reshape(128, 512))
    sintok = np.ascontiguousarray(sin.reshape(32, 128, 16).transpose(1, 0, 2).reshape(128, 512))
    pm = lambda v, k: np.ascontiguousarray(np.asarray(v, np.float32).reshape(k, 128).T)
    shared = dict(gnc=pm(norm_g[1], 8), gpc=pm(ple_norm[1], 8), gf=bc(final_norm, 1024), qnc=pm(mla_q_norm[0], 3),
                  kvnc=pm(mla_kv_norm[0], 2), w_in=f(mla_w_in[0]), wqa=wqb.reshape(384, 1536), wqs=wqs.reshape(384, 1536),
                  w_kvb=f(mla_w_kvb[0]), w_out=f(mla_w_out[0]), wg=f(ple_w_gate[1]), wp=f(ple_w_proj[1]),
                  costok=costok, sintok=sintok, ident=ident, perm=perm)
    in_maps = []
    for core in range(8):
        b, hf = core // 2, core % 2
        cosT = np.zeros((128, 2048), np.float32)
        sinT = np.zeros((128, 2048), np.float32)
        cs = cos[hf * 2048:(hf + 1) * 2048].T
        sn = sin[hf * 2048:(hf + 1) * 2048].T
        cosT[64:80] = cs
        cosT[80:96] = cs
        sinT[64:80] = -sn
        sinT[80:96] = sn
        m = dict(shared)
        if x1 is not None:
            m.update(xown=f(x1[b, hf * 2048:(hf + 1) * 2048]), xoth=f(x1[b, (1 - hf) * 2048:(2 - hf) * 2048]))
        order = np.concatenate([np.arange(hf * 2048, (hf + 1) * 2048), np.arange((1 - hf) * 2048, (2 - hf) * 2048)])
        m.update(pd=f(p[1, b, hf * 2048:(hf + 1) * 2048]), cosT=cosT, sinT=sinT,
                 costok=np.ascontiguousarray(cos[order].reshape(32, 128, 16).transpose(1, 0, 2).reshape(128, 512)),
                 sintok=np.ascontiguousarray(sin[order].reshape(32, 128, 16).transpose(1, 0, 2).reshape(128, 512)))
        in_maps.append(m)
    return in_maps


def build_fused(nc, es):
    x1loc = nc.dram_tensor("x1loc", [2048, 1024], F32).ap()
    x1oth = nc.dram_tensor("x1oth", [2048, 1024], F32).ap()
    shared = {}
    with contextlib.ExitStack() as es0:
        build_l0(nc, es0, pre="a_", xo=x1loc, sem_es=es, shared=shared)
    nc.all_engine_barrier()
    with contextlib.ExitStack() as es0:
        build_l0(nc, es0, pre="c_", xo=x1oth, sem_es=es, shared=shared)
    nc.all_engine_barrier()
    with contextlib.ExitStack() as es1:
        build_l1(nc, es1, pre="b_", xf=(x1loc, x1oth), sem_es=es)
    return nc


def kernel(x, p, norm_g, na_w_in, na_rpb, na_w_out, mla_w_in, mla_q_norm, mla_w_qb, mla_kv_norm, mla_w_kvb, mla_w_out,
           ple_norm, ple_w_gate, ple_w_proj, final_norm):
    x = np.asarray(x)
    p = np.asarray(p)
    m0 = _l0_inputs(x, p, norm_g, na_w_in, na_rpb, na_w_out, ple_norm, ple_w_gate, ple_w_proj)
    m0f = _l0_inputs(x, p, norm_g, na_w_in, na_rpb, na_w_out, ple_norm, ple_w_gate, ple_w_proj, flip=True)
    m1 = _l1_inputs(None, p, norm_g, mla_w_in, mla_q_norm, mla_w_qb, mla_kv_norm, mla_w_kvb, mla_w_out, ple_norm,
                    ple_w_gate, ple_w_proj, final_norm)
    in_maps = []
    for core in range(8):
        m = {"a_" + k: v for k, v in m0[core].items()}
        m.update({"c_" + k: m0f[core][k] for k in ("xk", "pd", "bq")})
        m.update({"b_" + k: v for k, v in m1[core].items()})
        in_maps.append(m)
    nc = _get_nc("fused", build_fused)
    res = run_bass_kernel_spmd(nc, in_maps, core_ids=list(range(8)))
    out = np.zeros((4, 4096, 1024), np.float32)
    for core in range(8):
        b, hf = core // 2, core % 2
        out[b, hf * 2048:(hf + 1) * 2048] = res.results[core]["xo"]
    return out
```

```python
import contextlib
import numpy as np
import concourse.bass as bass
import concourse.mybir as mybir
from concourse.bass_utils import run_bass_kernel_spmd

F32 = mybir.dt.float32
BF16 = mybir.dt.bfloat16
ALU = mybir.AluOpType
AF = mybir.ActivationFunctionType
AX = mybir.AxisListType

D = 1024
NEG = -30000.0
EPS = 1e-6


class Prog:
    STREAMS = ("pe", "act", "dve", "pool", "sp")

    def __init__(self, nc, n_dma_sems=12, sem_es=None, sem_prefix=""):
        self.nc = nc
        self.sem_es = sem_es
        self.sem_prefix = sem_prefix
        self.ops = []
        self.last_w = {}
        self.readers = {}
        self.n_dma_sems = n_dma_sems

    ALIAS = {
        "ra": ["Rsb_hi", "Rsb_lo", "sig1"], "Rsb_hi": ["ra", "sig1"], "Rsb_lo": ["ra", "sig1"],
        "sig1": ["ra", "Rsb_hi", "Rsb_lo"],
        "rb": ["Rsw", "tmp1", "Oc_lo", "Oc_hi"], "Rsw": ["rb", "tmp1", "Oc_lo", "Oc_hi"],
        "tmp1": ["rb", "Rsw", "Oc_lo", "Oc_hi"], "Oc_lo": ["rb", "Rsw", "tmp1"], "Oc_hi": ["rb", "Rsw", "tmp1"],
        "sig0": ["etmp"], "etmp": ["sig0"],
        "tmp0": ["t1_lo", "t1_hi"], "t1_lo": ["tmp0"], "t1_hi": ["tmp0"],
    }

    def _expand(self, keys):
        out = []
        for k in keys:
            out.append(k)
            out.extend(self.ALIAS.get(k, ()))
        return list(dict.fromkeys(out))

    def op(self, stream, fn, reads=(), writes=(), dma=False):
        reads = self._expand(reads)
        writes = self._expand(writes)
        i = len(self.ops)
        deps = {}

        def add(j, raw):
            if j is None:
                return
            deps[j] = deps.get(j, False) or raw

        for k in reads:
            add(self.last_w.get(k), True)
        for k in writes:
            add(self.last_w.get(k), True)
            for r in self.readers.get(k, ()):
                add(r, False)
        for k in reads:
            lst = self.readers.setdefault(k, [])
            if not dma:
                lst[:] = [r for r in lst if self.ops[r]["dma"] or self.ops[r]["stream"] != stream]
            lst.append(i)
        for k in writes:
            self.last_w[k] = i
            self.readers[k] = []
        self.ops.append(dict(i=i, stream=stream, fn=fn, dma=dma, deps=deps, sig=None))
        return i

    def emit(self, final_wait_ops=()):
        nc = self.nc
        ops = self.ops
        need = [[] for _ in ops]
        signaled = set()
        for o in ops:
            for j, raw in o["deps"].items():
                p = ops[j]
                if not p["dma"] and not o["dma"] and p["stream"] == o["stream"]:
                    if o["stream"] == "pe" or not raw:
                        continue
                need[o["i"]].append(j)
                signaled.add(j)
        for j in final_wait_ops:
            signaled.add(j)
        cnt = {s: 0 for s in self.STREAMS}
        dcnt = {s: 0 for s in self.STREAMS}
        for o in ops:
            if o["dma"]:
                k = dcnt[o["stream"]]
                dcnt[o["stream"]] += 1
                o["sig"] = ("d_%s_%d" % (o["stream"], k % self.n_dma_sems), 16 * (k // self.n_dma_sems + 1))
                o["dk"] = k
            elif o["i"] in signaled:
                cnt[o["stream"]] += 1
                o["sig"] = ("c_" + o["stream"], cnt[o["stream"]])
        dma_by_stream = {s: [o for o in ops if o["dma"] and o["stream"] == s] for s in self.STREAMS}
        sem_names = set()
        for o in ops:
            if o["sig"] is not None:
                sem_names.add(o["sig"][0])
        sem_names = sorted(sem_names)
        with contextlib.ExitStack() as es:
            ses = self.sem_es if self.sem_es is not None else es
            sems = {n: ses.enter_context(nc.semaphore(self.sem_prefix + n)) for n in sem_names}
            block = es.enter_context(nc.Block())
            seen = {s: {} for s in self.STREAMS}

            def run_stream(stream, eng):
                sw = seen[stream]
                for o in ops:
                    if o["stream"] != stream:
                        continue
                    waits = {}
                    for j in need[o["i"]]:
                        sn, sv = ops[j]["sig"]
                        waits[sn] = max(waits.get(sn, 0), sv)
                    if o["dma"] and o["dk"] >= self.n_dma_sems:
                        sn, sv = o["sig"]
                        waits[sn] = max(waits.get(sn, 0), sv - 16)
                    for sn, sv in sorted(waits.items()):
                        if sw.get(sn, 0) >= sv:
                            continue
                        sw[sn] = sv
                        eng.wait_ge(sems[sn], sv)
                    ins = o["fn"](eng)
                    if o["sig"] is not None:
                        ins.then_inc(sems[o["sig"][0]], 16 if o["dma"] else 1)
                if stream == "sp":
                    for j in final_wait_ops:
                        sn, sv = ops[j]["sig"]
                        if sw.get(sn, 0) < sv:
                            sw[sn] = sv
                            eng.wait_ge(sems[sn], sv)

            @block.tensor
            def _(e):
                run_stream("pe", e)

            @block.scalar
            def _(e):
                run_stream("act", e)

            @block.vector
            def _(e):
                run_stream("dve", e)

            @block.gpsimd
            def _(e):
                run_stream("pool", e)

            @block.sync
            def _(e):
                run_stream("sp", e)


class Ctx:
    def __init__(self, nc, es, pre="", sem_es=None):
        self.nc = nc
        self.es = es
        self.pre = pre
        self.P = Prog(nc, sem_es=sem_es, sem_prefix=pre)
        self._n = 0

    def sb(self, name, shape, dt):
        return self.es.enter_context(self.nc.sbuf_tensor(self.pre + name, list(shape), dt))

    def ps(self, name, shape, dt):
        return self.es.enter_context(self.nc.psum_tensor(self.pre + name, list(shape), dt))

    def din(self, name, shape, dt=F32):
        return self.nc.dram_tensor(self.pre + name, list(shape), dt, kind="ExternalInput").ap()

    def dout(self, name, shape, dt=F32):
        return self.nc.dram_tensor(name, list(shape), dt, kind="ExternalOutput").ap()


class Ring:
    def __init__(self, C, n=2, cols=1024):
        self.C = C
        self.slots = [C.sb("wst%d" % i, [128, cols], F32) for i in range(n)]
        self.k = 0

    def load(self, src, parts, shape_free, dst, dst_key, conv="pool", extra_writes=(), gain=None):
        P = self.C.P
        i = self.k % len(self.slots)
        self.k += 1
        n = int(np.prod(shape_free))
        st = self.slots[i][0:parts, 0:n]
        if len(shape_free) == 2:
            st = st.rearrange("p (a b) -> p a b", b=shape_free[1])
        key = "wst%d" % i
        P.op("sp", lambda e: e.dma_start(out=st, in_=src), writes=[key], dma=True)
        wr = [dst_key] + list(extra_writes)
        if gain is not None:
            gt, gkey, k0 = gain
            gap = bass.AP(gt, k0, [[gt[:].shape[1], 128], [1, shape_free[0]], [0, shape_free[1]]])
            P.op("pool", lambda e: e.tensor_tensor(out=dst, in0=st, in1=gap, op=ALU.mult), reads=[key, gkey], writes=wr)
        elif conv == "pool":
            P.op("pool", lambda e: e.tensor_copy(out=dst, in_=st), reads=[key], writes=wr)
        elif conv == "exp":
            P.op("act", lambda e: e.activation(out=dst, in_=st, func=AF.Exp), reads=[key], writes=wr)


def run_pipeline(n, stages):
    k = len(stages)
    for step in range(n + k - 1):
        for d in range(k):
            t = step - d
            if 0 <= t < n:
                stages[d](t)


def emit_norm_stats(C, st, xt_ap, xt_key, col, sq, width=D):
    P = C.P
    ss, ms, nh = st
    P.op("act", lambda e: e.activation(out=sq[:, 0:width], in_=xt_ap, func=AF.Square, accum_out=ss[:, col:col + 1]),
         reads=[xt_key], writes=["sq", "ss%d" % col])
    P.op("dve", lambda e: e.tensor_scalar(out=ms[:, col:col + 1], in0=ss[:, col:col + 1], scalar1=1.0 / width, scalar2=EPS,
                                         op0=ALU.mult, op1=ALU.add), reads=["ss%d" % col], writes=["ms%d" % col])
    P.op("pool", lambda e: e.tensor_tensor(out=ms[:, col:col + 1], in0=ms[:, col:col + 1], in1=nh[:], op=ALU.pow),
         reads=["ms%d" % col, "nh"], writes=["rs%d" % col])


def emit_norm_apply_T(C, st, xt_ap, xt_key, col, gbc, gkey, xnb_ap, xnb_key, tp_ps, tp_key, idb, dstT, dstT_key, nch=8,
                      copy_eng="act", scale_eng="act"):
    P = C.P
    ss, ms, nh = st
    if scale_eng == "act":
        P.op("act", lambda e: e.activation(out=xnb_ap, in_=xt_ap, func=AF.Copy, scale=ms[:, col:col + 1]),
             reads=[xt_key, "rs%d" % col], writes=[xnb_key])
    else:
        P.op("dve", lambda e: e.tensor_scalar(out=xnb_ap, in0=xt_ap, scalar1=ms[:, col:col + 1], scalar2=None, op0=ALU.mult),
             reads=[xt_key, "rs%d" % col], writes=[xnb_key])
    for kc in range(nch):
        P.op("pe", lambda e, kc=kc: e.transpose(out=tp_ps[:, kc, :], in_=xnb_ap[:, kc * 128:(kc + 1) * 128], identity=idb[:]),
             reads=[xnb_key, "idb"], writes=[tp_key])
    if copy_eng == "act":
        P.op("act", lambda e: e.activation(out=dstT, in_=tp_ps[:, 0:nch, :], func=AF.Copy), reads=[tp_key], writes=[dstT_key])
    else:
        P.op("dve", lambda e: e.tensor_copy(out=dstT, in_=tp_ps[:, 0:nch, :]), reads=[tp_key], writes=[dstT_key])


def emit_norm_T(C, st, xt_ap, xt_key, col, gbc, gkey, xnb, xnb_key, tp_ps, tp_key, idb, dstT, dstT_key, sq):
    emit_norm_stats(C, st, xt_ap, xt_key, col, sq)
    emit_norm_apply_T(C, st, xt_ap, xt_key, col, gbc, gkey, xnb[:], xnb_key, tp_ps, tp_key, idb, dstT, dstT_key)


def emit_finalize_now(C, Oe, Oo, Rsb, Oc):
    P = C.P
    P.op("dve", lambda e: e.reciprocal(out=Rsb[64:128, :], in_=Oe[64:128, :]), reads=["Oe"], writes=["Rsb_hi"])
    P.op("act", lambda e: e.activation(out=Oc[0:64, :], in_=Oe[0:64, :], func=AF.Copy), reads=["Oe"], writes=["Oc_lo"])
    P.op("dve", lambda e: e.reciprocal(out=Rsb[0:64, :], in_=Oo[0:64, :]), reads=["Oo"], writes=["Rsb_lo"])
    P.op("act", lambda e: e.activation(out=Oc[64:128, :], in_=Oo[64:128, :], func=AF.Copy), reads=["Oo"], writes=["Oc_hi"])


def emit_finalize_later(C, Rsb, Oc, t1, pmf, Mps, Mkey, gz_ap, gz_key, og_ap, og_key):
    P = C.P
    P.op("pe", lambda e: e.matmul(Mps[:, 0:512], lhsT=pmf[:], rhs=Rsb[:], start=True, stop=True),
         reads=["Rsb_hi", "Rsb_lo", "pmf"], writes=[Mkey])
    P.op("dve", lambda e: e.tensor_tensor(out=t1[:], in0=Oc[:], in1=Mps[:, 0:512], op=ALU.mult),
         reads=["Oc_lo", "Oc_hi", Mkey], writes=["t1_lo", "t1_hi"])
    P.op("pool", lambda e: e.tensor_tensor(out=og_ap, in0=t1[:], in1=gz_ap, op=ALU.mult),
         reads=["t1_lo", "t1_hi", gz_key], writes=[og_key])


def emit_silu_gate(C, zps, zkey, etmp, gz_ap, gz_key):
    P = C.P
    P.op("act", lambda e: e.activation(out=etmp[:], in_=zps, func=AF.Exp, scale=-1.0), reads=[zkey], writes=["etmp"])
    P.op("act", lambda e: e.activation(out=gz_ap, in_=zps, func=AF.Copy), reads=[zkey], writes=[gz_key])
    P.op("dve", lambda e: e.tensor_scalar(out=etmp[:], in0=etmp[:], scalar1=1.0, scalar2=None, op0=ALU.add),
         reads=["etmp"], writes=["etmp"])
    P.op("dve", lambda e: e.reciprocal(out=etmp[:], in_=etmp[:]), reads=["etmp"], writes=["etmp"])
    P.op("dve", lambda e: e.tensor_tensor(out=gz_ap, in0=gz_ap, in1=etmp[:], op=ALU.mult),
         reads=[gz_key, "etmp"], writes=[gz_key])


def emit_ple(C, get_tile, n_tiles, pd, pgbc, wgb, wpb, st, col0, xnb, idb, sq, hnT, pst, pbf, pT, sig, tmp,
             Sps, Mps, out_dram, final_norm=None, pre=None, pre_load=None):
    P = C.P
    outs = []
    ss, ms, nh_ = st

    def sA(i):
        pre(i)

    def sB(i):
        xt, xkey = get_tile(i)
        emit_norm_stats(C, st, xt, xkey, col0 + i, sq)

    def sC(i):
        j = i % 2
        xt, xkey = get_tile(i)
        col = col0 + i
        P.op("act", lambda e: e.activation(out=xnb[j][:], in_=xt, func=AF.Copy, scale=ms[:, col:col + 1]),
             reads=[xkey, "rs%d" % col], writes=["xnb%d" % j])
        P.op("sp", lambda e: e.dma_start(out=pst[j][:], in_=pd[i * 128:(i + 1) * 128, :]), writes=["pst%d" % j], dma=True)
        P.op("pool", lambda e: e.tensor_copy(out=pbf[j][:], in_=pst[j][:]), reads=["pst%d" % j], writes=["pbf%d" % j])

    def sD(i):
        j = i % 2
        tp = Mps[0][:].bitcast(BF16).rearrange("p (k t) -> p k t", t=128)
        for kc in range(8):
            P.op("pe", lambda e, kc=kc: e.transpose(out=tp[:, kc, :], in_=xnb[j][:, kc * 128:(kc + 1) * 128], identity=idb[:]),
                 reads=["xnb%d" % j, "idb"], writes=["M0"])
        P.op("act", lambda e: e.activation(out=hnT[j][:], in_=tp, func=AF.Copy), reads=["M0"], writes=["hnT%d" % j])
        tp2 = Mps[1][:].bitcast(BF16).rearrange("p (k t) -> p k t", t=128)
        for kc in range(2):
            P.op("pe", lambda e, kc=kc: e.transpose(out=tp2[:, kc, :], in_=pbf[j][:, kc * 128:(kc + 1) * 128], identity=idb[:]),
                 reads=["pbf%d" % j, "idb"], writes=["M1"])
        P.op("dve", lambda e: e.tensor_copy(out=pT[j][:], in_=tp2[:, 0:2, :]), reads=["M1"], writes=["pT%d" % j])

    def sE(i):
        j = i % 2
        xt, xkey = get_tile(i)
        for nh in range(2):
            gps = Sps[0][:, nh * 512:(nh + 1) * 512]
            pps = Sps[1][:, nh * 512:(nh + 1) * 512]
            for kc in range(8):
                P.op("pe", lambda e, kc=kc, nh=nh, gps=gps: e.matmul(
                    gps, lhsT=hnT[j][:, kc, :], rhs=wgb[:, kc, nh * 512:(nh + 1) * 512], start=(kc == 0), stop=(kc == 7)),
                    reads=["hnT%d" % j, "wgb"], writes=["S0_%d" % nh])
            for kc in range(2):
                P.op("pe", lambda e, kc=kc, nh=nh, pps=pps: e.matmul(
                    pps, lhsT=pT[j][:, kc, :], rhs=wpb[:, kc, nh * 512:(nh + 1) * 512], start=(kc == 0), stop=(kc == 1)),
                    reads=["pT%d" % j, "wpb"], writes=["S1_%d" % nh])
            sg = sig[nh]
            P.op("act", lambda e, gps=gps, sg=sg: e.activation(out=sg[:], in_=gps, func=AF.Exp, scale=-1.0),
                 reads=["S0_%d" % nh], writes=["sig%d" % nh])
            P.op("dve", lambda e, sg=sg: e.tensor_scalar(out=sg[:], in0=sg[:], scalar1=1.0, scalar2=None, op0=ALU.add),
                 reads=["sig%d" % nh], writes=["sig%d" % nh])
            P.op("dve", lambda e, sg=sg: e.reciprocal(out=sg[:], in_=sg[:]), reads=["sig%d" % nh], writes=["sig%d" % nh])
            tm = tmp[nh]
            P.op("dve", lambda e, pps=pps, sg=sg, tm=tm: e.tensor_tensor(out=tm[:], in0=pps, in1=sg[:], op=ALU.mult),
                 reads=["S1_%d" % nh, "sig%d" % nh], writes=["tmp%d" % nh])
            xh = xt[:, nh * 512:(nh + 1) * 512]
            P.op("pool", lambda e, xh=xh, tm=tm: e.tensor_tensor(out=xh, in0=xh, in1=tm[:], op=ALU.add),
                 reads=[xkey, "tmp%d" % nh], writes=[xkey])

    def sF(i):
        xt, xkey = get_tile(i)
        emit_norm_stats(C, st, xt, xkey, final_norm[2] + i, sq)

    def sG(i):
        xt, xkey = get_tile(i)
        if final_norm is not None:
            fgbc, fgkey, fcol0 = final_norm
            c2 = fcol0 + i
            P.op("act", lambda e: e.activation(out=xt, in_=xt, func=AF.Copy, scale=ms[:, c2:c2 + 1]),
                 reads=[xkey, "rs%d" % c2], writes=[xkey])
            P.op("dve", lambda e: e.tensor_tensor(out=xt, in0=xt, in1=fgbc[:], op=ALU.mult),
                 reads=[xkey, fgkey], writes=[xkey])
        outs.append(P.op("sp", lambda e: e.dma_start(out=out_dram[i * 128:(i + 1) * 128, :], in_=xt), reads=[xkey], dma=True))

    stages = ([pre_load] if pre_load is not None else []) + ([sA] if pre is not None else []) + [sB, sC, sD, sE] + ([sF] if final_norm is not None else []) + [sG]
    run_pipeline(n_tiles, stages)
    return outs


def alloc_common(C, layer1=False):
    B = {}
    if not layer1:
        B["xres"] = C.sb("xres", [128, 16, 1024], F32)
        B["og"] = [[C.sb("og%d_%d" % (i, g_), [128, 512], BF16) for g_ in range(4)] for i in range(2)]
        B["wo"] = [C.sb("wo%d" % i, [128, 1024], BF16) for i in range(2)]
    B["ss"] = C.sb("ss", [128, 96], F32)
    B["ms"] = C.sb("ms", [128, 96], F32)
    B["nh"] = C.sb("nh", [128, 1], F32)
    B["sq"] = C.sb("sq", [128, 1024], BF16)
    B["xnb"] = [C.sb("xnb%d" % i, [128, 1024], BF16) for i in range(2)]
    B["idb"] = C.sb("idb", [128, 128], BF16)
    B["pmf"] = C.sb("pmf", [128, 128], F32)
    B["gnc"] = C.sb("gnc_sb", [128, 8], F32)
    B["gpc"] = C.sb("gpc_sb", [128, 8], F32)
    B["Rsb"] = C.sb("Rsb", [128, 512], F32)
    B["Rsw"] = C.sb("Rsw", [128, 512], F32)
    B["t1"] = C.sb("t1", [128, 512], F32)
    B["hnT"] = [C.sb("hnT%d" % i, [128, 8, 128], BF16) for i in range(2)]
    B["pst"] = [C.sb("pst%d" % i, [128, 256], F32) for i in range(2)]
    B["pbf"] = [C.sb("pbf%d" % i, [128, 256], BF16) for i in range(2)]
    B["pT"] = [C.sb("pT%d" % i, [128, 2, 128], BF16) for i in range(2)]
    B["ring"] = Ring(C, n=2, cols=1024)
    B["S"] = [C.ps("S%d" % i, [128, 1024], F32) for i in range(2)]
    B["Oe"] = C.ps("Oe", [128, 512], F32)
    B["Oo"] = C.ps("Oo", [128, 512], F32)
    B["M"] = [C.ps("M%d" % i, [128, 512], F32) for i in range(2)]
    return B


def emit_consts(C, B, ident, perm):
    P = C.P
    B["ring"].load(ident, 128, [128], B["idb"][:], "idb")
    P.op("sp", lambda e: e.dma_start(out=B["pmf"][:], in_=perm), writes=["pmf"], dma=True)
    P.op("pool", lambda e: e.memset(B["nh"][:], -0.5), writes=["nh"])


def emit_outproj(C, B, ogs, og_keys, wos, wo_keys, tiles, mctr):
    P = C.P
    xres = B["xres"]
    for tt, i in enumerate(tiles):
        for nh in range(2):
            m = mctr[0] % 2
            mctr[0] += 1
            Mp = B["M"][m]
            for k_ in range(2):
                P.op("pe", lambda e, tt=tt, nh=nh, Mp=Mp, k_=k_: e.matmul(
                    Mp[:, 0:512], lhsT=ogs[k_][:, tt * 128:(tt + 1) * 128], rhs=wos[k_][:, nh * 512:(nh + 1) * 512],
                    start=(k_ == 0), stop=(k_ == 1)), reads=[og_keys[k_], wo_keys[k_]], writes=["M%d" % m])
            xh = xres[:, i, nh * 512:(nh + 1) * 512]
            P.op("dve", lambda e, xh=xh, Mp=Mp: e.tensor_tensor(out=xh, in0=xh, in1=Mp[:, 0:512], op=ALU.add),
                 reads=["xr%d" % i, "M%d" % m], writes=["xr%d" % i])


def build_l0(nc, es, pre="", xo=None, sem_es=None, shared=None):
    C = Ctx(nc, es, pre, sem_es)
    P = C.P
    shared = {} if shared is None else shared

    def sdin(name, shape):
        if name not in shared:
            shared[name] = C.din(name, shape)
        return shared[name]

    xk = C.din("xk", [2560, 1024])
    pd = C.din("pd", [2048, 256])
    bq = C.din("bq", [128, 2048])
    gn = sdin("gnc", [128, 8])
    gp = sdin("gpc", [128, 8])
    w_in = sdin("w_in", [1024, 4096])
    w_out = sdin("w_out", [1024, 1024])
    wg = sdin("wg", [1024, 1024])
    wp = sdin("wp", [256, 1024])
    tabB = sdin("tabB", [8, 128, 2816])
    aoh = sdin("aoh", [128, 1024])
    ident = sdin("ident", [128, 128])
    perm = sdin("perm", [128, 128])
    if xo is None:
        xo = C.dout("xo", [2048, 1024])

    B = alloc_common(C)
    ring = B["ring"]
    xres = B["xres"]
    xnT = C.sb("xnT", [128, 8, 2560], BF16)
    hx = [C.sb("hx%d" % i, [128, 1024], F32) for i in range(2)]
    AR_COLS, VP_OFF = 12544, 6656
    ar = C.sb("ar", [128, 12544], BF16)
    qTe = ar[:, 0:2048]
    qTo = ar[:, 2048:4096]
    kT = ar[:, 4096:6656]
    vP = ar[:, 6656:10496].rearrange("p (t c) -> p t c", c=192)
    gz = ar[:, 10496:12544]
    wgb = ar[:, 0:8192].rearrange("p (k n) -> p k n", n=1024)
    wpb = ar[:, 8192:10240].rearrange("p (k n) -> p k n", n=1024)
    wq = C.sb("wq", [128, 8, 128], BF16)
    wk = C.sb("wk", [128, 8, 128], BF16)
    wv = C.sb("wv", [128, 8, 128], BF16)
    wz = C.sb("wz", [128, 8, 128], BF16)
    tabE = C.sb("tabE", [128, 2816], BF16)
    bqb = C.sb("bqb", [128, 2048], BF16)
    aohb = C.sb("aohb", [128, 1024], BF16)
    PT = [C.sb("PT%d" % i, [128, 1024], BF16) for i in range(3)]
    etmp = C.sb("etmp", [128, 512], F32)
    sig = [etmp, B["Rsb"]]
    tmp = [B["t1"], B["Rsw"]]
    st = (B["ss"], B["ms"], B["nh"])

    emit_consts(C, B, ident, perm)
    P.op("sp", lambda e: e.dma_start(out=B["gnc"][:], in_=gn), writes=["gnc"], dma=True)
    P.op("sp", lambda e: e.dma_start(out=B["gpc"][:], in_=gp), writes=["gpc"], dma=True)
    ring.load(aoh, 128, [1024], aohb[:], "aohb")
    for h2 in range(2):
        ring.load(bq[:, h2 * 1024:(h2 + 1) * 1024], 128, [1024], bqb[:, h2 * 1024:(h2 + 1) * 1024], "bqb")
    P.op("pool", lambda e: e.memset(qTe[64:128, :], 0.0), writes=["qTe_z"])
    P.op("pool", lambda e: e.memset(qTo[0:64, :], 0.0), writes=["qTo_z"])
    P.op("pool", lambda e: e.memset(vP[:, :, 64:128], 1.0), writes=["vP_ones"])

    def p1_tile(bt):
        if 2 <= bt < 18:
            return xres[:, bt - 2, :], "xr%d" % (bt - 2)
        return hx[bt % 2][:], "hx%d" % (bt % 2)

    def p1_s0(bt):
        xt, xkey = p1_tile(bt)
        P.op("sp", lambda e: e.dma_start(out=xt, in_=xk[bt * 128:(bt + 1) * 128, :]), writes=[xkey], dma=True)
        emit_norm_stats(C, st, xt, xkey, bt, B["sq"])

    def p1_s1(bt):
        xt, xkey = p1_tile(bt)
        j = bt % 2
        tp = B["M"][j][:].bitcast(BF16).rearrange("p (k t) -> p k t", t=128)
        emit_norm_apply_T(C, st, xt, xkey, bt, None, None, B["xnb"][j][:], "xnb%d" % j, tp, "M%d" % j, B["idb"],
                          xnT[:, :, bt * 128:(bt + 1) * 128], "xnT%d" % bt, copy_eng="dve")

    run_pipeline(20, [p1_s0, p1_s1])

    w_in_v = w_in.rearrange("(kc p) n -> p kc n", p=128)
    mctr = [0, 0]
    pending = []
    sctr = [0]
    pctr = [0]
    octr = [0]

    def next_m():
        m = mctr[0] % 2
        mctr[0] += 1
        return m

    def load_inproj(c_):
        for wt, wkey, off in ((wq, "wq", 0), (wk, "wk", 1024), (wv, "wv", 2048), (wz, "wz", 3072)):
            ring.load(w_in_v[:, :, off + c_ * 128: off + (c_ + 1) * 128], 128, [8, 128], wt[:], wkey,
                      gain=(B["gnc"], "gnc", 0))

    load_inproj(0)
    for c in range(8):
        wo = B["wo"][c % 2]
        for ch, (lo, hi) in enumerate(((0, 1024), (1024, 2048), (2048, 2816))):
            ring.load(tabB[c, :, lo:hi], 128, [hi - lo], tabE[:, lo:hi], "tabE%d" % ch, conv="exp")

        def flush_one():
            _, g_, c_ = pending.pop(0)
            m = next_m()
            og = B["og"][c_ % 2][g_]
            emit_finalize_later(C, B["Rsb"], B["Rsw"], B["t1"], B["pmf"], B["M"][m], "M%d" % m,
                                gz[:, g_ * 512:(g_ + 1) * 512], "gz%d" % g_, og[:], "og%d_%d" % (c_ % 2, g_))
            if c_ % 2 == 1:
                emit_outproj(C, B, [B["og"][0][g_], B["og"][1][g_]], ["og0_%d" % g_, "og1_%d" % g_],
                             B["wo"], ["wo0", "wo1"], [4 * g_ + a_ for a_ in range(4)], mctr)

        for blk in range(4):
            m = next_m()
            Mp = B["M"][m]
            t0 = 256 + blk * 512
            rk = ["xnT%d" % (t0 // 128 + a) for a in range(4)]
            for kc in range(8):
                P.op("pe", lambda e, kc=kc, Mp=Mp, t0=t0: e.matmul(Mp[:, 0:512], lhsT=wq[:, kc, :], rhs=xnT[:, kc, t0:t0 + 512],
                                                                   start=(kc == 0), stop=(kc == 7)),
                     reads=rk + ["wq"], writes=["M%d" % m])
            P.op("act", lambda e, Mp=Mp, blk=blk: e.activation(out=qTe[0:64, blk * 512:(blk + 1) * 512], in_=Mp[0:64, 0:512],
                                                                func=AF.Copy), reads=["M%d" % m], writes=["qTe"])
            P.op("act", lambda e, Mp=Mp, blk=blk: e.activation(out=qTo[64:128, blk * 512:(blk + 1) * 512], in_=Mp[64:128, 0:512],
                                                                func=AF.Copy), reads=["M%d" % m], writes=["qTo"])
        for blk in range(5):
            m = next_m()
            Mp = B["M"][m]
            t0 = blk * 512
            rk = ["xnT%d" % (t0 // 128 + a) for a in range(4)]
            for kc in range(8):
                P.op("pe", lambda e, kc=kc, Mp=Mp, t0=t0: e.matmul(Mp[:, 0:512], lhsT=wk[:, kc, :], rhs=xnT[:, kc, t0:t0 + 512],
                                                                   start=(kc == 0), stop=(kc == 7)),
                     reads=rk + ["wk"], writes=["M%d" % m])
            if blk % 2 == 0:
                P.op("dve", lambda e, Mp=Mp, t0=t0: e.tensor_copy(out=kT[:, t0:t0 + 512], in_=Mp[:, 0:512]),
                     reads=["M%d" % m], writes=["kT%d" % blk])
            else:
                P.op("act", lambda e, Mp=Mp, t0=t0: e.activation(out=kT[:, t0:t0 + 512], in_=Mp[:, 0:512], func=AF.Copy),
                     reads=["M%d" % m], writes=["kT%d" % blk])
        while pending:
            flush_one()
        ring.load(w_out[c * 128:(c + 1) * 128, :], 128, [1024], wo[:], "wo%d" % (c % 2))
        for blk in range(4):
            m = next_m()
            Mp = B["M"][m]
            t0 = 256 + blk * 512
            rk = ["xnT%d" % (t0 // 128 + a) for a in range(4)]
            for kc in range(8):
                P.op("pe", lambda e, kc=kc, Mp=Mp, t0=t0: e.matmul(Mp[:, 0:512], lhsT=wz[:, kc, :], rhs=xnT[:, kc, t0:t0 + 512],
                                                                   start=(kc == 0), stop=(kc == 7)),
                     reads=rk + ["wz"], writes=["M%d" % m])
            emit_silu_gate(C, Mp[:, 0:512], "M%d" % m, etmp, gz[:, blk * 512:(blk + 1) * 512], "gz%d" % blk)
        for t4 in range(5):
            m = next_m()
            Mp = B["M"][m]
            Mv = Mp[:, 0:512].rearrange("p (t c) -> p t c", c=128)
            for a in range(4):
                bt = t4 * 4 + a
                for kc in range(8):
                    P.op("pe", lambda e, kc=kc, Mv=Mv, a=a, bt=bt: e.matmul(
                        Mv[:, a, :], lhsT=xnT[:, kc, bt * 128:(bt + 1) * 128], rhs=wv[:, kc, :],
                        start=(kc == 0), stop=(kc == 7)), reads=["xnT%d" % bt, "wv"], writes=["M%d" % m])
            vdst = bass.AP(ar, VP_OFF + t4 * 4 * 192, [[AR_COLS, 128], [192, 4], [128, 2], [1, 64]])
            P.op("dve", lambda e, Mv=Mv, vdst=vdst: e.tensor_copy(out=vdst, in_=Mv.rearrange("p t (h c) -> p t h c", h=2)),
                 reads=["M%d" % m], writes=["vPe%d" % t4, "vPo%d" % t4])

        if c < 7:
            load_inproj(c + 1)
        its = [(g, hh, tp_) for g in range(4) for hh in range(2) for tp_ in range(4)]
        SKEW = 2

        def qk_stage(n):
            g, hh, tp_ = its[n]
            qm = qTe if hh == 0 else qTo
            qkeys = ["qTe", "qTe_z"] if hh == 0 else ["qTo", "qTo_z"]
            s_ = sctr[0] % 2
            sctr[0] += 1
            Sp = B["S"][s_]
            for u in range(2):
                t = 2 * tp_ + (1 - u)
                bt = 4 * g + t
                P.op("pe", lambda e, Sp=Sp, u=u, bt=bt, qm=qm, g=g: e.matmul(
                    Sp[:, u * 512:(u + 1) * 512], lhsT=kT[:, bt * 128:(bt + 1) * 128], rhs=qm[:, g * 512:(g + 1) * 512],
                    start=True, stop=False), reads=["kT%d" % (bt // 4)] + qkeys, writes=["S%d" % s_])
                P.op("pe", lambda e, Sp=Sp, u=u, t=t, g=g: e.matmul(
                    Sp[:, u * 512:(u + 1) * 512], lhsT=aohb[:, t * 128:(t + 1) * 128], rhs=bqb[:, g * 512:(g + 1) * 512],
                    start=False, stop=True), reads=["aohb", "bqb"], writes=["S%d" % s_])
            pj = pctr[0] % 3
            pctr[0] += 1
            Pt = PT[pj]
            P.op("act", lambda e, Pt=Pt, Sp=Sp: e.activation(out=Pt[:], in_=Sp[:], func=AF.Exp, scale=0.125),
                 reads=["S%d" % s_], writes=["PT%d" % pj])
            tab_ap = bass.AP(tabE, hh * 1408 + (12 - 4 * tp_) * 64, [[2816, 128], [128, 2], [64, 8], [1, 64]])
            Pv = Pt[:].rearrange("p (u q c) -> p u q c", u=2, q=8)
            P.op("dve", lambda e, Pv=Pv, tab_ap=tab_ap: e.tensor_tensor(out=Pv, in0=Pv, in1=tab_ap, op=ALU.mult),
                 reads=["PT%d" % pj, "tabE0", "tabE1", "tabE2"], writes=["PT%d" % pj])
            return pj

        pjs = {}
        for n in range(len(its) + SKEW):
            if n < len(its):
                pjs[n] = qk_stage(n)
            m_ = n - SKEW
            if m_ < 0:
                continue
            g, hh, tp_ = its[m_]
            pj = pjs[m_]
            Pt = PT[pj]
            Oh = B["Oe"] if hh == 0 else B["Oo"]
            okey = "Oe" if hh == 0 else "Oo"
            for u in range(2):
                t = 2 * tp_ + (1 - u)
                bt = 4 * g + t
                first = (tp_ == 0 and u == 0)
                last = (tp_ == 3 and u == 1)
                P.op("pe", lambda e, Oh=Oh, bt=bt, hh=hh, Pt=Pt, u=u, first=first, last=last: e.matmul(
                    Oh[:, 0:512], lhsT=vP[:, bt, hh * 64:hh * 64 + 128], rhs=Pt[:, u * 512:(u + 1) * 512],
                    start=first, stop=last),
                    reads=["vPe%d" % (bt // 4), "vPo%d" % (bt // 4), "vP_ones", "PT%d" % pj], writes=[okey])
            if hh == 1 and tp_ == 3:
                emit_finalize_now(C, B["Oe"], B["Oo"], B["Rsb"], B["Rsw"])
                pending.append((n + 3 if g < 3 else 10 ** 9, g, c))
            while pending and pending[0][0] <= n:
                flush_one()

    while pending:
        flush_one()
    alias = ["qTe", "qTo", "qTe_z", "qTo_z", "vP_ones"] + ["kT%d" % i for i in range(5)] + ["gz%d" % i for i in range(4)] + \
            ["vPe%d" % i for i in range(5)] + ["vPo%d" % i for i in range(5)]
    wg_v = wg.rearrange("(kc p) n -> p kc n", p=128)
    for nb in range(8):
        ring.load(wg_v[:, :, nb * 128:(nb + 1) * 128], 128, [8, 128], wgb[:, :, nb * 128:(nb + 1) * 128], "wgb",
                  extra_writes=alias if nb == 0 else (), gain=(B["gpc"], "gpc", 0))
    wp_v = wp.rearrange("(kc p) n -> p kc n", p=128)
    for nb in range(2):
        ring.load(wp_v[:, :, nb * 512:(nb + 1) * 512], 128, [2, 512], wpb[:, :, nb * 512:(nb + 1) * 512], "wpb",
                  extra_writes=alias if nb == 0 else ())
    outs = emit_ple(C, lambda i: (xres[:, i, :], "xr%d" % i), 16, pd, None, wgb, wpb, st, 20, B["xnb"], B["idb"],
                    B["sq"], B["hnT"], B["pst"], B["pbf"], B["pT"], sig, tmp, B["S"], B["M"], xo)
    P.emit(final_wait_ops=outs)
    return nc


def _consts():
    ident = np.eye(128, dtype=np.float32)
    perm = np.zeros((128, 128), np.float32)
    for i in range(64):
        perm[64 + i, i] = 1.0
        perm[i, 64 + i] = 1.0
    return ident, perm


def _na_tables(rpb):
    kc = np.arange(64)
    qc = np.arange(64)
    cs = np.clip(qc - 8, 0, 48)
    colvalid = (kc[:, None] >= cs[None, :]) & (kc[:, None] < cs[None, :] + 16)
    coff = np.clip(kc[:, None] - qc[None, :] + 15, 0, 30)
    tab = np.full((16, 128, 22, 64), NEG, np.float32)
    for e in range(22):
        for half in range(2):
            dr = 10 - e + half
            if -7 <= dr <= 7:
                vals = rpb[:, dr + 7][:, coff]
                tab[:, half * 64:(half + 1) * 64, e, :] = np.where(colvalid[None], vals, NEG)
    return np.ascontiguousarray(tab.reshape(8, 2, 128, 22, 64).transpose(0, 2, 1, 3, 4).reshape(8, 128, 2816))


def _na_rowmask(hf):
    bq = np.zeros((128, 4, 8, 64), np.float32)
    for g in range(4):
        for qi in range(8):
            r = 32 * hf + 8 * g + qi
            rs = min(max(r - 4, 0), 56)
            for j in range(16):
                R = 32 * hf - 4 + 8 * g + j
                ok = (0 <= R <= 63) and (rs <= R < rs + 8)
                bq[j, g, qi, :] = 0.0 if ok else NEG
    aoh = np.zeros((128, 8, 128), np.float32)
    for t in range(8):
        aoh[2 * t, t, 0:64] = 1.0
        aoh[2 * t + 1, t, 64:128] = 1.0
    return bq.reshape(128, 2048), aoh.reshape(128, 1024)


_NC_CACHE = {}


def _get_nc(name, builder):
    if name not in _NC_CACHE:
        nc = bass.Bass("TRN2", target_bir_lowering=False)
        with contextlib.ExitStack() as es:
            builder(nc, es)
        _NC_CACHE[name] = nc
    return _NC_CACHE[name]


def run_l0(x, p, norm_g, na_w_in, na_rpb, na_w_out, ple_norm, ple_w_gate, ple_w_proj):
    x = np.asarray(x, np.float32)
    ident, perm = _consts()
    tabB = _na_tables(np.asarray(na_rpb[0], np.float32))
    gn = np.ascontiguousarray(np.asarray(norm_g[0], np.float32).reshape(8, 128).T)
    gp = np.ascontiguousarray(np.asarray(ple_norm[0], np.float32).reshape(8, 128).T)
    shared = dict(gnc=gn, gpc=gp, w_in=np.ascontiguousarray(na_w_in[0], dtype=np.float32),
                  w_out=np.ascontiguousarray(na_w_out[0], dtype=np.float32),
                  wg=np.ascontiguousarray(ple_w_gate[0], dtype=np.float32),
                  wp=np.ascontiguousarray(ple_w_proj[0], dtype=np.float32), tabB=tabB, ident=ident, perm=perm)
    in_maps = []
    for core in range(8):
        b, hf = core // 2, core % 2
        xk = np.zeros((40, 64, 1024), np.float32)
        xb = x[b].reshape(64, 64, 1024)
        for lr in range(40):
            R = 32 * hf - 4 + lr
            if 0 <= R <= 63:
                xk[lr] = xb[R]
        bq, aoh = _na_rowmask(hf)
        m = dict(shared)
        m.update(xk=xk.reshape(2560, 1024), pd=np.ascontiguousarray(p[0, b, hf * 2048:(hf + 1) * 2048], dtype=np.float32),
                 bq=bq, aoh=aoh)
        in_maps.append(m)
    nc = _get_nc("l0", build_l0)
    res = run_bass_kernel_spmd(nc, in_maps, core_ids=list(range(8)))
    x1 = np.zeros((4, 4096, 1024), np.float32)
    for core in range(8):
        b, hf = core // 2, core % 2
        x1[b, hf * 2048:(hf + 1) * 2048] = res.results[core]["xo"]
    return x1


def build_l1(nc, es, pre="", xf=None, sem_es=None):
    C = Ctx(nc, es, pre, sem_es)
    P = C.P
    if xf is None:
        xoth = C.din("xoth", [2048, 1024])
        xown = C.din("xown", [2048, 1024])
    else:
        xown, xoth = xf

    def xf_tile(T):
        src = xown if T < 16 else xoth
        return src[(T % 16) * 128:(T % 16 + 1) * 128, :]
    pd = C.din("pd", [2048, 256])
    gn = C.din("gnc", [128, 8])
    gp = C.din("gpc", [128, 8])
    gf = C.din("gf", [128, 1024])
    qn = C.din("qnc", [128, 3])
    kvn = C.din("kvnc", [128, 2])
    w_in = C.din("w_in", [1024, 1696])
    wqa = C.din("wqa", [384, 1536])
    wqs = C.din("wqs", [384, 1536])
    w_kvb = C.din("w_kvb", [256, 2048])
    w_out = C.din("w_out", [1024, 1024])
    wg = C.din("wg", [1024, 1024])
    wp = C.din("wp", [256, 1024])
    costok = C.din("costok", [128, 512])
    sintok = C.din("sintok", [128, 512])
    cosTd = C.din("cosT", [128, 2048])
    sinTd = C.din("sinT", [128, 2048])
    ident = C.din("ident", [128, 128])
    perm = C.din("perm", [128, 128])
    xo = C.dout("xo", [2048, 1024])

    B = alloc_common(C, layer1=True)
    ring = B["ring"]
    st = (B["ss"], B["ms"], B["nh"])
    xnT = C.sb("xnT", [128, 8, 2048], BF16)
    ckvT = C.sb("ckvT", [128, 2, 4096], BF16)
    kpeT = C.sb("kpeT", [128, 4096], BF16)
    cqT = C.sb("cqT", [128, 3, 2048], BF16)
    cosT = C.sb("cosTb", [128, 2048], BF16)
    sinT = C.sb("sinTb", [128, 2048], BF16)
    ogT = C.sb("ogT", [128, 8, 2048], BF16)
    hx = [C.sb("hx%d" % i, [128, 1024], F32) for i in range(2)]
    qnc = C.sb("qnc_sb", [128, 3], F32)
    kvnc = C.sb("kvnc_sb", [128, 2], F32)
    xkT = B["hnT"]
    lat = [B["pst"][i][:].bitcast(BF16) for i in range(2)]
    wqA = [C.sb("wqA%d" % i, [128, 3, 96], BF16) for i in range(2)]
    wqB = [C.sb("wqB%d" % i, [128, 3, 96], BF16) for i in range(2)]
    wkn = [C.sb("wkn%d" % i, [128, 2, 64], BF16) for i in range(2)]
    wvp = C.sb("wvp", [128, 2, 128], BF16)
    wz = C.sb("wz", [128, 8, 128], BF16)
    PT = [C.sb("PT%d" % i, [128, 1024], BF16) for i in range(3)]
    etmp = C.sb("etmp", [128, 512], F32)
    AR_COLS, VP_OFF = 20480, 12288
    ar = C.sb("ar", [128, 20480], BF16)
    qT = [ar[:, 0:2048], ar[:, 2048:4096]]
    kT = [ar[:, 4096:8192], ar[:, 8192:12288]]
    vP = ar[:, 12288:18432].rearrange("p (t c) -> p t c", c=192)
    gz = ar[:, 18432:20480]
    krope = ar[:, 0:2048].bitcast(F32).rearrange("p (t c) -> p t c", c=32)
    kpe_tok = ar[:, 2048:5120].rearrange("p (t c) -> p t c", c=96)
    ctk = ar[:, 5120:6144].bitcast(F32).rearrange("p (t c) -> p t c", c=16)
    stk = ar[:, 6144:7168].bitcast(F32).rearrange("p (t c) -> p t c", c=16)
    wkvr = ar[:, 7168:9472].rearrange("p (k n) -> p k n", n=288)
    wcq = ar[:, 9472:12544].rearrange("p (k n) -> p k n", n=384)
    wgb = ar[:, 0:8192].rearrange("p (k n) -> p k n", n=1024)
    wpb = ar[:, 8192:10240].rearrange("p (k n) -> p k n", n=1024)
    woa = ar[:, 10240:18432].rearrange("p (k n) -> p k n", n=1024)
    ra = B["Rsb"]
    rb = B["Rsw"]
    sig = [etmp, B["Rsb"]]
    tmp = [B["t1"], B["Rsw"]]
    fgbc = C.sb("fgbc", [128, 1024], F32)

    emit_consts(C, B, ident, perm)
    P.op("sp", lambda e: e.dma_start(out=B["gnc"][:], in_=gn), writes=["gnc"], dma=True)
    P.op("sp", lambda e: e.dma_start(out=B["gpc"][:], in_=gp), writes=["gpc"], dma=True)
    P.op("sp", lambda e: e.dma_start(out=fgbc[:], in_=gf), writes=["fgbc"], dma=True)
    P.op("sp", lambda e: e.dma_start(out=qnc[:], in_=qn), writes=["qnc"], dma=True)
    P.op("sp", lambda e: e.dma_start(out=kvnc[:], in_=kvn), writes=["kvnc"], dma=True)
    P.op("sp", lambda e: e.dma_start(out=ctk.rearrange("p t c -> p (t c)"), in_=costok), writes=["ctk"], dma=True)
    P.op("sp", lambda e: e.dma_start(out=stk.rearrange("p t c -> p (t c)"), in_=sintok), writes=["stk"], dma=True)
    for h2 in range(2):
        ring.load(cosTd[:, h2 * 1024:(h2 + 1) * 1024], 128, [1024], cosT[:, h2 * 1024:(h2 + 1) * 1024], "cosT")
        ring.load(sinTd[:, h2 * 1024:(h2 + 1) * 1024], 128, [1024], sinT[:, h2 * 1024:(h2 + 1) * 1024], "sinT")
    w_in_v = w_in.rearrange("(kc p) n -> p kc n", p=128)
    for k4 in range(4):
        ring.load(w_in_v[:, 2 * k4:2 * k4 + 2, 384:672], 128, [2, 288], wkvr[:, 2 * k4:2 * k4 + 2, :], "wkvr",
                  gain=(B["gnc"], "gnc", 2 * k4))
    for k4 in range(4):
        ring.load(w_in_v[:, 2 * k4:2 * k4 + 2, 0:384], 128, [2, 384], wcq[:, 2 * k4:2 * k4 + 2, :], "wcq",
                  gain=(B["gnc"], "gnc", 2 * k4))
    P.op("pool", lambda e: e.memset(kpe_tok[:, :, 0:64], 0.0), writes=["kpe_z"])

    def psum_bf(t):
        return t[:].bitcast(BF16).rearrange("p (k t) -> p k t", t=128)

    ss, ms, nh_ = st

    def latent_stages(n_tiles, load, col_x, col_l, width, wmat, wkey, gvec, gvkey, nch, dstT_of, dst_of, dst_key, extra=None):
        def s0(T):
            j = T % 2
            load(T, j)
            emit_norm_stats(C, st, hx[j][:], "hx%d" % j, col_x + T, B["sq"])

        def s1(T):
            j = T % 2
            dT, dkey = dstT_of(T, j)
            emit_norm_apply_T(C, st, hx[j][:], "hx%d" % j, col_x + T, None, None, B["xnb"][j][:], "xnb%d" % j,
                              psum_bf(B["M"][j]), "M%d" % j, B["idb"], dT, dkey, copy_eng="dve")

        def s2(T):
            j = T % 2
            dT, dkey = dstT_of(T, j)
            Sp = B["S"][j]
            tot = width + (32 if extra else 0)
            for kc in range(8):
                P.op("pe", lambda e, kc=kc: e.matmul(Sp[:, 0:tot], lhsT=dT[:, kc, :], rhs=wmat[:, kc, :],
                                                     start=(kc == 0), stop=(kc == 7)),
                     reads=[dkey, wkey], writes=["S%d" % j])
            emit_norm_stats(C, st, Sp[:, 0:width], "S%d" % j, col_l + T, B["sq"], width=width)

        def s3(T):
            j = T % 2
            Sp = B["S"][j]
            col = col_l + T
            Ob = B["Oe"] if j == 0 else B["Oo"]
            okey = "Oe" if j == 0 else "Oo"
            if extra:
                extra(T, Sp, "S%d" % j, col)
            emit_norm_apply_T(C, st, Sp[:, 0:width], "S%d" % j, col, gvec, gvkey, lat[j][:, 0:width], "pst%d" % j,
                              psum_bf(Ob), okey, B["idb"], dst_of(T), dst_key % T, nch=nch, copy_eng="dve", scale_eng="dve")

        run_pipeline(n_tiles, [s0, s1, s2, s3])

    def load_own(T, j):
        P.op("sp", lambda e: e.dma_start(out=hx[j][:], in_=xown[T * 128:(T + 1) * 128, :]), writes=["hx%d" % j], dma=True)

    latent_stages(16, load_own, 64, 80, 384, wcq, "wcq", None, None, 3,
                  lambda T, j: (xnT[:, :, T * 128:(T + 1) * 128], "xnT%d" % T),
                  lambda T: cqT[:, :, T * 128:(T + 1) * 128], "cqT%d")

    def load_any(T, j):
        P.op("sp", lambda e: e.dma_start(out=hx[j][:], in_=xf_tile(T)), writes=["hx%d" % j], dma=True)

    def krope_copy(T, Sp, skey, col):
        P.op("dve", lambda e: e.tensor_copy(out=krope[:, T, :], in_=Sp[:, 256:288]), reads=[skey, "rs%d" % col], writes=["krope"])

    latent_stages(32, load_any, 0, 32, 256, wkvr, "wkvr", None, None, 2,
                  lambda T, j: (xkT[j][:], "hnT%d" % j),
                  lambda T: ckvT[:, :, T * 128:(T + 1) * 128], "ckvT%d", extra=krope_copy)

    x1v = krope[:, :, 0:16]
    x2v = krope[:, :, 16:32]
    rav = ra[:].rearrange("p (t c) -> p t c", c=16)
    rbv = rb[:].rearrange("p (t c) -> p t c", c=16)
    P.op("dve", lambda e: e.tensor_tensor(out=rav, in0=x1v, in1=ctk, op=ALU.mult), reads=["krope", "ctk"], writes=["ra"])
    P.op("dve", lambda e: e.tensor_tensor(out=rbv, in0=x2v, in1=stk, op=ALU.mult), reads=["krope", "stk"], writes=["rb"])
    P.op("dve", lambda e: e.tensor_tensor(out=kpe_tok[:, :, 64:80], in0=rav, in1=rbv, op=ALU.subtract),
         reads=["ra", "rb"], writes=["kpe1"])
    P.op("dve", lambda e: e.tensor_tensor(out=rav, in0=x1v, in1=stk, op=ALU.mult), reads=["krope", "stk", "kpe1"], writes=["ra"])
    P.op("dve", lambda e: e.tensor_tensor(out=rbv, in0=x2v, in1=ctk, op=ALU.mult), reads=["krope", "ctk", "kpe1"], writes=["rb"])
    P.op("dve", lambda e: e.tensor_tensor(out=kpe_tok[:, :, 80:96], in0=rav, in1=rbv, op=ALU.add),
         reads=["ra", "rb"], writes=["kpe2"])
    for T8 in range(4):
        s = T8 % 2
        tpv = B["S"][s][:, 0:512].bitcast(BF16).rearrange("p (k t) -> p k t", t=128)
        for a in range(8):
            T = T8 * 8 + a
            P.op("pe", lambda e, a=a, T=T, tpv=tpv: e.transpose(out=tpv[0:96, a, :], in_=kpe_tok[:, T, 0:96], identity=B["idb"][:]),
                 reads=["kpe1", "kpe2", "kpe_z", "idb"], writes=["S%d" % s])
        dst = kpeT[64:96, T8 * 1024:(T8 + 1) * 1024].rearrange("p (a t) -> p a t", t=128)
        if T8 % 2 == 0:
            P.op("dve", lambda e, dst=dst, tpv=tpv: e.tensor_copy(out=dst, in_=tpv[64:96, :, :]), reads=["S%d" % s],
                 writes=["kpeT%d" % T8])
        else:
            P.op("act", lambda e, dst=dst, tpv=tpv: e.activation(out=dst, in_=tpv[64:96, :, :], func=AF.Copy), reads=["S%d" % s],
                 writes=["kpeT%d" % T8])

    mctr = [0]
    sctr = [0]
    pctr = [0]

    def next_m():
        m = mctr[0] % 2
        mctr[0] += 1
        return m

    wqa_v = wqa.rearrange("(kc p) n -> p kc n", p=128)
    wqs_v = wqs.rearrange("(kc p) n -> p kc n", p=128)
    wkv_v = w_kvb.rearrange("(kc p) n -> p kc n", p=128)
    alias_ac = ["krope", "kpe1", "kpe2", "kpe_z", "ctk", "stk", "wkvr", "wcq"]
    scale = float(96.0 ** -0.5)
    for c in range(8):
        for hh in range(2):
            h = 2 * c + hh
            ring.load(wqa_v[:, :, h * 96:(h + 1) * 96], 128, [3, 96], wqA[hh][:], "wqA%d" % hh, gain=(qnc, "qnc", 0))
            ring.load(wqs_v[:, :, h * 96:(h + 1) * 96], 128, [3, 96], wqB[hh][:], "wqB%d" % hh, gain=(qnc, "qnc", 0))
            ring.load(wkv_v[:, :, h * 128:h * 128 + 64], 128, [2, 64], wkn[hh][:], "wkn%d" % hh, gain=(kvnc, "kvnc", 0))
            ring.load(wkv_v[:, :, h * 128 + 64:h * 128 + 128], 128, [2, 64], wvp[:, :, hh * 64:(hh + 1) * 64], "wvp%d" % hh,
                      gain=(kvnc, "kvnc", 0))
        ring.load(w_in_v[:, :, 672 + c * 128:672 + (c + 1) * 128], 128, [8, 128], wz[:], "wz", gain=(B["gnc"], "gnc", 0))
        first_alias = alias_ac if c == 0 else []

        for hh in range(2):
            for blk in range(4):
                mA = next_m()
                mB = next_m()
                MA = B["M"][mA]
                MB = B["M"][mB]
                rk = ["cqT%d" % (blk * 4 + a) for a in range(4)]
                for kc in range(3):
                    P.op("pe", lambda e, kc=kc, MA=MA, hh=hh, blk=blk: e.matmul(
                        MA[0:96, 0:512], lhsT=wqA[hh][:, kc, :], rhs=cqT[:, kc, blk * 512:(blk + 1) * 512],
                        start=(kc == 0), stop=(kc == 2)), reads=rk + ["wqA%d" % hh], writes=["M%d" % mA])
                for kc in range(3):
                    P.op("pe", lambda e, kc=kc, MB=MB, hh=hh, blk=blk: e.matmul(
                        MB[0:96, 0:512], lhsT=wqB[hh][:, kc, :], rhs=cqT[:, kc, blk * 512:(blk + 1) * 512],
                        start=(kc == 0), stop=(kc == 2)), reads=rk + ["wqB%d" % hh], writes=["M%d" % mB])
                bs = slice(blk * 512, (blk + 1) * 512)
                P.op("act", lambda e, MA=MA, hh=hh, bs=bs: e.activation(out=qT[hh][0:64, bs], in_=MA[0:64, 0:512], func=AF.Copy),
                     reads=["M%d" % mA], writes=["qT%dn" % hh] + first_alias)
                P.op("dve", lambda e, MA=MA, bs=bs: e.tensor_tensor(out=ra[64:96, :], in0=MA[64:96, 0:512], in1=cosT[64:96, bs],
                                                                    op=ALU.mult), reads=["M%d" % mA, "cosT"], writes=["ra"])
                P.op("dve", lambda e, MB=MB, bs=bs: e.tensor_tensor(out=rb[64:96, :], in0=MB[64:96, 0:512], in1=sinT[64:96, bs],
                                                                    op=ALU.mult), reads=["M%d" % mB, "sinT"], writes=["rb"])
                P.op("pool", lambda e, hh=hh, bs=bs: e.tensor_tensor(out=qT[hh][64:96, bs], in0=ra[64:96, :], in1=rb[64:96, :],
                                                                      op=ALU.add),
                     reads=["ra", "rb"], writes=["qT%dp" % hh] + first_alias)
                first_alias = []
            for blk in range(8):
                m = next_m()
                Mp = B["M"][m]
                rk = ["ckvT%d" % (blk * 4 + a) for a in range(4)]
                for kc in range(2):
                    P.op("pe", lambda e, kc=kc, Mp=Mp, hh=hh, blk=blk: e.matmul(
                        Mp[0:64, 0:512], lhsT=wkn[hh][:, kc, :], rhs=ckvT[:, kc, blk * 512:(blk + 1) * 512],
                        start=(kc == 0), stop=(kc == 1)), reads=rk + ["wkn%d" % hh], writes=["M%d" % m])
                bs = slice(blk * 512, (blk + 1) * 512)
                if blk % 2 == 0:
                    P.op("dve", lambda e, Mp=Mp, hh=hh, bs=bs: e.tensor_copy(out=kT[hh][0:64, bs], in_=Mp[0:64, 0:512]),
                         reads=["M%d" % m], writes=["kT%d_%d" % (hh, blk)])
                else:
                    P.op("act", lambda e, Mp=Mp, hh=hh, bs=bs: e.activation(out=kT[hh][0:64, bs], in_=Mp[0:64, 0:512], func=AF.Copy),
                         reads=["M%d" % m], writes=["kT%d_%d" % (hh, blk)])
            P.op("dve", lambda e, hh=hh: e.tensor_copy(out=kT[hh][64:96, :], in_=kpeT[64:96, :]),
                 reads=["kpeT%d" % a for a in range(4)], writes=["kT%dpe" % hh])
        if c == 0:
            P.op("pool", lambda e: e.memset(vP[:, :, 64:128], 1.0), writes=["vP_ones"])
        for t4 in range(8):
            m = next_m()
            Mp = B["M"][m]
            Mv = Mp[:, 0:512].rearrange("p (t c) -> p t c", c=128)
            for a in range(4):
                T = t4 * 4 + a
                for kc in range(2):
                    P.op("pe", lambda e, kc=kc, Mv=Mv, a=a, T=T: e.matmul(
                        Mv[:, a, :], lhsT=ckvT[:, kc, T * 128:(T + 1) * 128], rhs=wvp[:, kc, :],
                        start=(kc == 0), stop=(kc == 1)), reads=["ckvT%d" % T, "wvp0", "wvp1"], writes=["M%d" % m])
            vdst = bass.AP(ar, VP_OFF + t4 * 4 * 192, [[AR_COLS, 128], [192, 4], [128, 2], [1, 64]])
            P.op("dve", lambda e, Mv=Mv, vdst=vdst: e.tensor_copy(out=vdst, in_=Mv.rearrange("p t (h c) -> p t h c", h=2)),
                 reads=["M%d" % m], writes=["vPe%d" % t4, "vPo%d" % t4])
        for blk in range(4):
            m = next_m()
            Mp = B["M"][m]
            rk = ["xnT%d" % (blk * 4 + a) for a in range(4)]
            for kc in range(8):
                P.op("pe", lambda e, kc=kc, Mp=Mp, blk=blk: e.matmul(Mp[:, 0:512], lhsT=wz[:, kc, :],
                                                                     rhs=xnT[:, kc, blk * 512:(blk + 1) * 512],
                                                                     start=(kc == 0), stop=(kc == 7)),
                     reads=rk + ["wz"], writes=["M%d" % m])
            emit_silu_gate(C, Mp[:, 0:512], "M%d" % m, etmp, gz[:, blk * 512:(blk + 1) * 512], "gz%d" % blk)

        its = [(qb, hh, tp_) for qb in range(4) for hh in range(2) for tp_ in range(16)]
        SKEW = 2

        def qk_stage(n):
            qb, hh, tp_ = its[n]
            s_ = sctr[0] % 2
            sctr[0] += 1
            Sp = B["S"][s_]
            for u in range(2):
                kt = 2 * tp_ + u
                P.op("pe", lambda e, Sp=Sp, u=u, kt=kt, hh=hh, qb=qb: e.matmul(
                    Sp[:, u * 512:(u + 1) * 512], lhsT=kT[hh][0:96, kt * 128:(kt + 1) * 128],
                    rhs=qT[hh][0:96, qb * 512:(qb + 1) * 512], start=True, stop=True),
                    reads=["kT%d_%d" % (hh, kt // 4), "kT%dpe" % hh, "qT%dn" % hh, "qT%dp" % hh], writes=["S%d" % s_])
            pj = pctr[0] % 3
            pctr[0] += 1
            Pt = PT[pj]
            P.op("act", lambda e, Pt=Pt, Sp=Sp: e.activation(out=Pt[:], in_=Sp[:], func=AF.Exp, scale=scale),
                 reads=["S%d" % s_], writes=["PT%d" % pj])
            return pj

        pjs = {}
        pending = []
        for n in range(len(its) + SKEW):
            if n < len(its):
                pjs[n] = qk_stage(n)
            m_ = n - SKEW
            if m_ < 0:
                continue
            qb, hh, tp_ = its[m_]
            pj = pjs[m_]
            Pt = PT[pj]
            Oh = B["Oe"] if hh == 0 else B["Oo"]
            okey = "Oe" if hh == 0 else "Oo"
            for u in range(2):
                kt = 2 * tp_ + u
                P.op("pe", lambda e, Oh=Oh, kt=kt, hh=hh, Pt=Pt, u=u: e.matmul(
                    Oh[:, 0:512], lhsT=vP[:, kt, hh * 64:hh * 64 + 128], rhs=Pt[:, u * 512:(u + 1) * 512],
                    start=(kt == 0), stop=(kt == 31)),
                    reads=["vPe%d" % (kt // 4), "vPo%d" % (kt // 4), "vP_ones", "PT%d" % pj], writes=[okey])
            if hh == 1 and tp_ == 15:
                emit_finalize_now(C, B["Oe"], B["Oo"], B["Rsb"], B["Rsw"])
                pending.append((n + 3, qb))
            while pending and (pending[0][0] <= n or n == len(its) + SKEW - 1):
                _, qb_ = pending.pop(0)
                m = next_m()
                emit_finalize_later(C, B["Rsb"], B["Rsw"], B["t1"], B["pmf"], B["M"][m], "M%d" % m,
                                    gz[:, qb_ * 512:(qb_ + 1) * 512], "gz%d" % qb_, ogT[:, c, qb_ * 512:(qb_ + 1) * 512],
                                    "ogT%d_%d" % (c, qb_))

    alias = ["qT0n", "qT0p", "qT1n", "qT1p", "vP_ones", "kT0pe", "kT1pe"] + ["kT%d_%d" % (a, b_) for a in range(2) for b_ in range(8)] + \
            ["gz%d" % i for i in range(4)] + ["vPe%d" % i for i in range(8)] + ["vPo%d" % i for i in range(8)]
    wg_v = wg.rearrange("(kc p) n -> p kc n", p=128)
    for nb in range(8):
        ring.load(wg_v[:, :, nb * 128:(nb + 1) * 128], 128, [8, 128], wgb[:, :, nb * 128:(nb + 1) * 128], "wgb",
                  extra_writes=alias if nb == 0 else (), gain=(B["gpc"], "gpc", 0))
    wp_v = wp.rearrange("(kc p) n -> p kc n", p=128)
    for nb in range(2):
        ring.load(wp_v[:, :, nb * 512:(nb + 1) * 512], 128, [2, 512], wpb[:, :, nb * 512:(nb + 1) * 512], "wpb",
                  extra_writes=alias if nb == 0 else ())
    for c in range(8):
        ring.load(w_out[c * 128:(c + 1) * 128, :], 128, [1024], woa[:, c, :], "woa", extra_writes=alias if c == 0 else ())

    NHX = 8
    hx4 = [hx[0][:], hx[1][:]] + [xnT[:, k_, :].bitcast(F32) for k_ in range(NHX - 2)]
    xnT_keys = ["xnT%d" % a_ for a_ in range(16)]

    def pre_load(i):
        j = i % NHX
        P.op("sp", lambda e: e.dma_start(out=hx4[j], in_=xown[i * 128:(i + 1) * 128, :]),
             writes=["hx%d" % j] + (xnT_keys if 2 <= i < NHX else []), dma=True)

    def pre(i):
        j = i % NHX
        for nh in range(2):
            Ob = B["Oe"] if nh == 0 else B["Oo"]
            okey = "Oe" if nh == 0 else "Oo"
            for c in range(8):
                P.op("pe", lambda e, c=c, Ob=Ob, nh=nh: e.matmul(Ob[:, 0:512], lhsT=ogT[:, c, i * 128:(i + 1) * 128],
                                                                 rhs=woa[:, c, nh * 512:(nh + 1) * 512],
                                                                 start=(c == 0), stop=(c == 7)),
                     reads=["ogT%d_%d" % (c, i // 4), "woa"], writes=[okey])
            xh = hx4[j][:, nh * 512:(nh + 1) * 512]
            P.op("dve", lambda e, xh=xh, Ob=Ob: e.tensor_tensor(out=xh, in0=xh, in1=Ob[:, 0:512], op=ALU.add),
                 reads=["hx%d" % j, okey], writes=["hx%d" % j])

    outs = emit_ple(C, lambda i: (hx4[i % NHX], "hx%d" % (i % NHX)), 16, pd, None, wgb, wpb, st, 0, B["xnb"], B["idb"], B["sq"],
                    B["hnT"], B["pst"], B["pbf"], B["pT"], sig, tmp, B["S"], B["M"], xo,
                    final_norm=(fgbc, "fgbc", 16), pre=pre, pre_load=pre_load)
    P.emit(final_wait_ops=outs)
    return nc


def _rope_np():
    inv = (1.0 / (np.float32(10000.0) ** (np.arange(0, 32, 2, dtype=np.float32) / np.float32(32)))).astype(np.float32)
    ang = (np.arange(4096, dtype=np.float32)[:, None] * inv[None, :]).astype(np.float32)
    return np.cos(ang).astype(np.float32), np.sin(ang).astype(np.float32)


def run_l1(x1, p, norm_g, mla_w_in, mla_q_norm, mla_w_qb, mla_kv_norm, mla_w_kvb, mla_w_out, ple_norm, ple_w_gate,
           ple_w_proj, final_norm):
    ident, perm = _consts()
    f = lambda a: np.ascontiguousarray(a, dtype=np.float32)
    bc = lambda v, n: np.ascontiguousarray(np.broadcast_to(np.asarray(v, np.float32), (128, n)))
    wqb = f(mla_w_qb[0]).reshape(384, 16, 96)
    wqs = np.zeros_like(wqb)
    wqs[:, :, 64:80] = wqb[:, :, 80:96]
    wqs[:, :, 80:96] = wqb[:, :, 64:80]
    cos, sin = _rope_np()
    costok = np.ascontiguousarray(cos.reshape(32, 128, 16).transpose(1, 0, 2).reshape(128, 512))
    sintok = np.ascontiguousarray(sin.reshape(32, 128, 16).transpose(1, 0, 2).reshape(128, 512))
    pm = lambda v, k: np.ascontiguousarray(np.asarray(v, np.float32).reshape(k, 128).T)
    shared = dict(gnc=pm(norm_g[1], 8), gpc=pm(ple_norm[1], 8), gf=bc(final_norm, 1024), qnc=pm(mla_q_norm[0], 3),
                  kvnc=pm(mla_kv_norm[0], 2), w_in=f(mla_w_in[0]), wqa=wqb.reshape(384, 1536), wqs=wqs.reshape(384, 1536),
                  w_kvb=f(mla_w_kvb[0]), w_out=f(mla_w_out[0]), wg=f(ple_w_gate[1]), wp=f(ple_w_proj[1]),
                  costok=costok, sintok=sintok, ident=ident, perm=perm)
    in_maps = []
    for core in range(8):
        b, hf = core // 2, core % 2
        cosT = np.zeros((128, 2048), np.float32)
        sinT = np.zeros((128, 2048), np.float32)
        cs = cos[hf * 2048:(hf + 1) * 2048].T
        sn = sin[hf * 2048:(hf + 1) * 2048].T
        cosT[64:80] = cs
        cosT[80:96] = cs
        sinT[64:80] = -sn
        sinT[80:96] = sn
        m = dict(shared)
        m.update(xf=f(x1[b]), xown=f(x1[b, hf * 2048:(hf + 1) * 2048]), pd=f(p[1, b, hf * 2048:(hf + 1) * 2048]),
                 cosT=cosT, sinT=sinT)
        in_maps.append(m)
    nc = _get_nc("l1", build_l1)
    res = run_bass_kernel_spmd(nc, in_maps, core_ids=list(range(8)))
    out = np.zeros((4, 4096, 1024), np.float32)
    for core in range(8):
        b, hf = core // 2, core % 2
        out[b, hf * 2048:(hf + 1) * 2048] = res.results[core]["xo"]
    return out


def _l0_inputs(x, p, norm_g, na_w_in, na_rpb, na_w_out, ple_norm, ple_w_gate, ple_w_proj, flip=False):
    x = np.asarray(x, np.float32)
    ident, perm = _consts()
    tabB = _na_tables(np.asarray(na_rpb[0], np.float32))
    gn = np.ascontiguousarray(np.asarray(norm_g[0], np.float32).reshape(8, 128).T)
    gp = np.ascontiguousarray(np.asarray(ple_norm[0], np.float32).reshape(8, 128).T)
    shared = dict(gnc=gn, gpc=gp, w_in=np.ascontiguousarray(na_w_in[0], dtype=np.float32),
                  w_out=np.ascontiguousarray(na_w_out[0], dtype=np.float32),
                  wg=np.ascontiguousarray(ple_w_gate[0], dtype=np.float32),
                  wp=np.ascontiguousarray(ple_w_proj[0], dtype=np.float32), tabB=tabB, ident=ident, perm=perm)
    in_maps = []
    for core in range(8):
        b, hf = core // 2, core % 2
        if flip:
            hf = 1 - hf
        xk = np.zeros((40, 64, 1024), np.float32)
        xb = x[b].reshape(64, 64, 1024)
        for lr in range(40):
            R = 32 * hf - 4 + lr
            if 0 <= R <= 63:
                xk[lr] = xb[R]
        bq, aoh = _na_rowmask(hf)
        m = dict(shared)
        m.update(xk=xk.reshape(2560, 1024), pd=np.ascontiguousarray(p[0, b, hf * 2048:(hf + 1) * 2048], dtype=np.float32),
                 bq=bq, aoh=aoh)
        in_maps.append(m)
    return in_maps


def _l1_inputs(x1, p, norm_g, mla_w_in, mla_q_norm, mla_w_qb, mla_kv_norm, mla_w_kvb, mla_w_out, ple_norm, ple_w_gate,
               ple_w_proj, final_norm):
    ident, perm = _consts()
    f = lambda a: np.ascontiguousarray(a, dtype=np.float32)
    bc = lambda v, n: np.ascontiguousarray(np.broadcast_to(np.asarray(v, np.float32), (128, n)))
    wqb = f(mla_w_qb[0]).reshape(384, 16, 96)
    wqs = np.zeros_like(wqb)
    wqs[:, :, 64:80] = wqb[:, :, 80:96]
    wqs[:, :, 80:96] = wqb[:, :, 64:80]
    cos, sin = _rope_np()
    costok = np.ascontiguousarray(cos.reshape(32, 128, 16).transpose(1, 0, 2).reshape(128, 512))
    sintok = np.ascontiguousarray(sin.reshape(32, 128, 16).transpose(1, 0, 2).reshape(128, 512))
    pm = lambda v, k: np.ascontiguousarray(np.asarray(v, np.float32).reshape(k, 128).T)
    shared = dict(gnc=pm(norm_g[1], 8), gpc=pm(ple_norm[1], 8), gf=bc(final_norm, 1024), qnc=pm(mla_q_norm[0], 3),
                  kvnc=pm(mla_kv_norm[0], 2), w_in=f(mla_w_in[0]), wqa=wqb.reshape(384, 1536), wqs=wqs.reshape(384, 1536),
                  w_kvb=f(mla_w_kvb[0]), w_out=f(mla_w_out[0]), wg=f(ple_w_gate[1]), wp=f(ple_w_proj[1]),
                  costok=costok, sintok=sintok, ident=ident, perm=perm)
    in_maps = []
    for core in range(8):
        b, hf = core // 2, core % 2
        cosT = np.zeros((128, 2048), np.float32)
        sinT = np.zeros((128, 2048), np.float32)
        cs = cos[hf * 2048:(hf + 1) * 2048].T
        sn = sin[hf * 2048:(hf + 1) * 2048].T
        cosT[64:80] = cs
        cosT[80:96] = cs
        sinT[64:80] = -sn
        sinT[80:96] = sn
        m = dict(shared)
        if x1 is not None:
            m.update(xown=f(x1[b, hf * 2048:(hf + 1) * 2048]), xoth=f(x1[b, (1 - hf) * 2048:(2 - hf) * 2048]))
        order = np.concatenate([np.arange(hf * 2048, (hf + 1) * 2048), np.arange((1 - hf) * 2048, (2 - hf) * 2048)])
        m.update(pd=f(p[1, b, hf * 2048:(hf + 1) * 2048]), cosT=cosT, sinT=sinT,
                 costok=np.ascontiguousarray(cos[order].reshape(32, 128, 16).transpose(1, 0, 2).reshape(128, 512)),
                 sintok=np.ascontiguousarray(sin[order].reshape(32, 128, 16).transpose(1, 0, 2).reshape(128, 512)))
        in_maps.append(m)
    return in_maps


def build_fused(nc, es):
    x1loc = nc.dram_tensor("x1loc", [2048, 1024], F32).ap()
    x1oth = nc.dram_tensor("x1oth", [2048, 1024], F32).ap()
    shared = {}
    with contextlib.ExitStack() as es0:
        build_l0(nc, es0, pre="a_", xo=x1loc, sem_es=es, shared=shared)
    nc.all_engine_barrier()
    with contextlib.ExitStack() as es0:
        build_l0(nc, es0, pre="c_", xo=x1oth, sem_es=es, shared=shared)
    nc.all_engine_barrier()
    with contextlib.ExitStack() as es1:
        build_l1(nc, es1, pre="b_", xf=(x1loc, x1oth), sem_es=es)
    return nc


def kernel(x, p, norm_g, na_w_in, na_rpb, na_w_out, mla_w_in, mla_q_norm, mla_w_qb, mla_kv_norm, mla_w_kvb, mla_w_out,
           ple_norm, ple_w_gate, ple_w_proj, final_norm):
    x = np.asarray(x)
    p = np.asarray(p)
    m0 = _l0_inputs(x, p, norm_g, na_w_in, na_rpb, na_w_out, ple_norm, ple_w_gate, ple_w_proj)
    m0f = _l0_inputs(x, p, norm_g, na_w_in, na_rpb, na_w_out, ple_norm, ple_w_gate, ple_w_proj, flip=True)
    m1 = _l1_inputs(None, p, norm_g, mla_w_in, mla_q_norm, mla_w_qb, mla_kv_norm, mla_w_kvb, mla_w_out, ple_norm,
                    ple_w_gate, ple_w_proj, final_norm)
    in_maps = []
    for core in range(8):
        m = {"a_" + k: v for k, v in m0[core].items()}
        m.update({"c_" + k: m0f[core][k] for k in ("xk", "pd", "bq")})
        m.update({"b_" + k: v for k, v in m1[core].items()})
        in_maps.append(m)
    nc = _get_nc("fused", build_fused)
    res = run_bass_kernel_spmd(nc, in_maps, core_ids=list(range(8)))
    out = np.zeros((4, 4096, 1024), np.float32)
    for core in range(8):
        b, hf = core // 2, core % 2
        out[b, hf * 2048:(hf + 1) * 2048] = res.results[core]["xo"]
    return out
```
